# Optimizing a Trainium2 kernel written in Bass

```python
import math
import jax, jax.numpy as jnp
from jax import lax
import numpy as np

D_MODEL = 1024
BATCH = 4
SEQ = 4096
DEPTH = 2

N_BRANCH = 4
D_MIX = D_MODEL // N_BRANCH
HEAD_A = 64
H_A = D_MIX // HEAD_A
LORA_W = 32
LORA_A = 32
LORA_V = 32
LORA_G = 64
LNX_EPS = 64e-5
CHUNK = 128
GROUP_B = 64
G_B = D_MIX // GROUP_B
CONV_K = 3
GROUP_D = 16
G_D = D_MIX // GROUP_D
N_STATE = 64
D_FF = 4 * D_MODEL
EPS = 1e-6
LN_EPS = 1e-5

A_COLS = 3 * D_MIX + LORA_W + LORA_A + LORA_G
B_COLS = 2 * D_MIX
C_COLS = 3 * D_MIX
D_COLS = D_MIX
GATE_COLS = N_BRANCH * D_MODEL
OFF_B = A_COLS
OFF_C = OFF_B + B_COLS
OFF_D = OFF_C + C_COLS
OFF_G = OFF_D + D_COLS
N_IN = OFF_G + GATE_COLS

kernel_name = "hybrid_rwkv7_sgu_conv_s5_block"


def rms_norm(x, g):
    xf = x.astype(jnp.float32)
    return xf * lax.rsqrt(jnp.mean(xf * xf, axis=-1, keepdims=True) + EPS) * g.astype(jnp.float32)


def layer_norm(x, w, b, eps):
    xf = x.astype(jnp.float32)
    mu = jnp.mean(xf, axis=-1, keepdims=True)
    var = jnp.mean(jnp.square(xf - mu), axis=-1, keepdims=True)
    return (xf - mu) * lax.rsqrt(var + eps) * w.astype(jnp.float32) + b.astype(jnp.float32)


def shift_prev(p):
    return jnp.pad(p, ((0, 0), (1, 0), (0, 0)))[:, :-1]


def wkv7(r, w, k, v, a, b):
    bsz, _, nh, n = r.shape

    def step(state, inp):
        r_t, w_t, k_t, v_t, a_t, b_t = inp
        sa = jnp.einsum('bhij,bhj->bhi', state, a_t)
        state = (state * w_t[:, :, None, :] + sa[..., None] * b_t[:, :, None, :]
                 + v_t[..., None] * k_t[:, :, None, :])
        return state, jnp.einsum('bhij,bhj->bhi', state, r_t)

    xs = tuple(jnp.moveaxis(t, 1, 0) for t in (r, w, k, v, a, b))
    s0 = jnp.zeros((bsz, nh, n, n), jnp.float32)
    _, out = lax.scan(step, s0, xs)
    return jnp.moveaxis(out, 0, 1)


def rwkv7_mixer(p, v_first, v_mix, mu, w0, w2, a0, a2, g2, k_k, k_a, r_k, lnx_w, lnx_b, w_out):
    bsz, seq, _ = p.shape
    p = p + (shift_prev(p) - p) * mu
    r = p[..., :D_MIX]
    k = p[..., D_MIX:2 * D_MIX]
    v = p[..., 2 * D_MIX:3 * D_MIX]
    wd = p[..., 3 * D_MIX:3 * D_MIX + LORA_W]
    ad = p[..., 3 * D_MIX + LORA_W:3 * D_MIX + LORA_W + LORA_A]
    gd = p[..., 3 * D_MIX + LORA_W + LORA_A:]
    log_w = -jax.nn.softplus(-(w0 + jnp.tanh(wd) @ w2)) - 0.5
    decay = jnp.exp(-jnp.exp(log_w))
    if v_mix is None:
        v_first = v
    else:
        v0, v1, v2 = v_mix
        v = v + (v_first - v) * jax.nn.sigmoid(v0 + (v @ v1) @ v2)
    a = jax.nn.sigmoid(a0 + ad @ a2)
    g = jax.nn.sigmoid(gd) @ g2

    def heads(t):
        return t.reshape(bsz, seq, H_A, HEAD_A).astype(jnp.float32)

    kk = heads(k * k_k)
    kk = kk / jnp.maximum(jnp.sqrt(jnp.sum(kk * kk, axis=-1, keepdims=True)), 1e-12)
    k = k * (1.0 + (a - 1.0) * k_a)
    rh, kh, vh, ah = heads(r), heads(k), heads(v), heads(a)
    o = wkv7(rh, heads(decay), kh, vh, -kk, kk * ah)
    o = layer_norm(o, lnx_w.reshape(H_A, HEAD_A), lnx_b.reshape(H_A, HEAD_A), LNX_EPS)
    o = o + jnp.sum(rh * kh * r_k, axis=-1, keepdims=True) * vh
    return (o.reshape(bsz, seq, D_MIX) * g) @ w_out, v_first


def spatial_gating_mixer(p, ln_w, ln_b, w_s, b_s, w_out):
    bsz, seq, _ = p.shape
    z = jax.nn.gelu(p)
    u = z[..., :D_MIX]
    v = layer_norm(z[..., D_MIX:], ln_w, ln_b, LN_EPS)
    v = v.reshape(bsz, seq // CHUNK, CHUNK, G_B, GROUP_B)
    mask = jnp.tril(jnp.ones((CHUNK, CHUNK), jnp.float32))
    mixed = jnp.einsum('gts,bnsgc->bntgc', w_s * mask, v) + b_s.T[None, None, :, :, None]
    return (u * mixed.reshape(bsz, seq, D_MIX)) @ w_out


def short_conv_mixer(p, conv_w, w_out):
    seq = p.shape[1]
    bg = p[..., :D_MIX]
    cg = p[..., D_MIX:2 * D_MIX]
    xin = p[..., 2 * D_MIX:]
    zp = jnp.pad(cg * xin, ((0, 0), (CONV_K - 1, 0), (0, 0)))
    y = sum(conv_w[j] * zp[:, j:j + seq] for j in range(CONV_K))
    return (bg * y) @ w_out


def _complex_affine_combine(e1, e2):
    a1r, a1i, b1r, b1i = e1
    a2r, a2i, b2r, b2i = e2
    return (a2r * a1r - a2i * a1i,
            a2r * a1i + a2i * a1r,
            a2r * b1r - a2i * b1i + b2r,
            a2r * b1i + a2i * b1r + b2i)


def s5_mixer(u, a_re, a_im, b_re, b_im, c_re, c_im, d, log_dt, glu_w):
    f32 = jnp.float32
    bsz, seq, _ = u.shape
    u32 = u.astype(f32)
    ug = u32.reshape(bsz, seq, G_D, GROUP_D)
    lam_re = jnp.minimum(a_re.astype(f32), -1e-4)
    lam_im = a_im.astype(f32)
    dt = jnp.exp(log_dt.astype(f32))[:, None]
    mag = jnp.exp(lam_re * dt)
    ab_re = mag * jnp.cos(lam_im * dt)
    ab_im = mag * jnp.sin(lam_im * dt)
    den = lam_re * lam_re + lam_im * lam_im
    q_re = ((ab_re - 1.0) * lam_re + ab_im * lam_im) / den
    q_im = (ab_im * lam_re - (ab_re - 1.0) * lam_im) / den
    bb_re = q_re[..., None] * b_re - q_im[..., None] * b_im
    bb_im = q_re[..., None] * b_im + q_im[..., None] * b_re
    bu_re = jnp.einsum('gnc,bsgc->bsgn', bb_re, ug)
    bu_im = jnp.einsum('gnc,bsgc->bsgn', bb_im, ug)
    ar = jnp.broadcast_to(ab_re, bu_re.shape)
    ai = jnp.broadcast_to(ab_im, bu_im.shape)
    _, _, x_re, x_im = lax.associative_scan(_complex_affine_combine, (ar, ai, bu_re, bu_im), axis=1)
    y = (jnp.einsum('gcn,bsgn->bsgc', c_re, x_re) - jnp.einsum('gcn,bsgn->bsgc', c_im, x_im))
    y = y.reshape(bsz, seq, D_MIX) + d * u32
    h = jax.nn.gelu(y) @ glu_w
    return h[..., :D_MODEL] * jax.nn.sigmoid(h[..., D_MODEL:])


def setup_inputs(seed: int = 0) -> dict:
    key = jax.random.key(seed)
    ks = iter(jax.random.split(key, 64))
    f32 = jnp.float32
    L = DEPTH

    def nrm(shape, scale):
        return scale * jax.random.normal(next(ks), shape, f32)

    def uni(shape, lo, hi):
        return jax.random.uniform(next(ks), shape, f32, lo, hi)

    return {
        'x': nrm((BATCH, SEQ, D_MODEL), 1.0),
        'c': nrm((BATCH, D_MODEL), 1.0),
        'ada_w': nrm((L, D_MODEL, 6 * D_MODEL), 0.5 * D_MODEL ** -0.5),
        'ada_b': nrm((L, 6 * D_MODEL), 0.02),
        'norm_mix_g': 1.0 + nrm((L, D_MODEL), 0.1),
        'w_in': nrm((L, D_MODEL, N_IN), D_MODEL ** -0.5),
        'rwkv_mu': uni((L, A_COLS), 0.0, 1.0),
        'rwkv_w0': uni((L, D_MIX), -6.0, -1.0),
        'rwkv_w2': nrm((L, LORA_W, D_MIX), 0.1),
        'rwkv_a0': nrm((L, D_MIX), 0.1),
        'rwkv_a2': nrm((L, LORA_A, D_MIX), 0.1),
        'rwkv_g2': nrm((L, LORA_G, D_MIX), LORA_G ** -0.5),
        'rwkv_v0': 1.0 + nrm((L - 1, D_MIX), 0.1),
        'rwkv_v1': nrm((L - 1, D_MIX, LORA_V), D_MIX ** -0.5),
        'rwkv_v2': nrm((L - 1, LORA_V, D_MIX), 0.1),
        'rwkv_kk': 0.85 + nrm((L, D_MIX), 0.1),
        'rwkv_ka': 1.0 + nrm((L, D_MIX), 0.1),
        'rwkv_rk': nrm((L, H_A, HEAD_A), 0.1),
        'rwkv_lnx_w': 1.0 + nrm((L, D_MIX), 0.1),
        'rwkv_lnx_b': nrm((L, D_MIX), 0.02),
        'rwkv_out': nrm((L, D_MIX, D_MODEL), D_MIX ** -0.5),
        'sg_ln_w': 1.0 + nrm((L, D_MIX), 0.1),
        'sg_ln_b': nrm((L, D_MIX), 0.02),
        'sg_ws': nrm((L, G_B, CHUNK, CHUNK), CHUNK ** -0.5),
        'sg_bs': 1.0 + nrm((L, G_B, CHUNK), 0.1),
        'sg_out': nrm((L, D_MIX, D_MODEL), D_MIX ** -0.5),
        'conv_w': nrm((L, CONV_K, D_MIX), CONV_K ** -0.5),
        'conv_out': nrm((L, D_MIX, D_MODEL), D_MIX ** -0.5),
        's5_a_re': -0.5 + nrm((L, G_D, N_STATE), 0.01),
        's5_a_im': jnp.pi * jnp.arange(N_STATE, dtype=f32)[None, None, :] + nrm((L, G_D, N_STATE), 0.01),
        's5_b_re': nrm((L, G_D, N_STATE, GROUP_D), (2 * GROUP_D) ** -0.5),
        's5_b_im': nrm((L, G_D, N_STATE, GROUP_D), (2 * GROUP_D) ** -0.5),
        's5_c_re': nrm((L, G_D, GROUP_D, N_STATE), (2 * N_STATE) ** -0.5),
        's5_c_im': nrm((L, G_D, GROUP_D, N_STATE), (2 * N_STATE) ** -0.5),
        's5_d': nrm((L, D_MIX), 1.0),
        's5_log_dt': uni((L, G_D), math.log(1e-3), math.log(1e-1)),
        's5_glu_w': nrm((L, D_MIX, 2 * D_MODEL), D_MIX ** -0.5),
        'w_o': nrm((L, D_MODEL, D_MODEL), D_MODEL ** -0.5),
        'norm_ffn_g': 1.0 + nrm((L, D_MODEL), 0.1),
        'ffn_w1': nrm((L, D_MODEL, D_FF), D_MODEL ** -0.5),
        'ffn_w2': nrm((L, D_FF, D_MODEL), D_FF ** -0.5),
        'final_g': 1.0 + nrm((D_MODEL,), 0.1),
    }


def reference(x, c, ada_w, ada_b, norm_mix_g, w_in, rwkv_mu, rwkv_w0, rwkv_w2, rwkv_a0, rwkv_a2,
              rwkv_g2, rwkv_v0, rwkv_v1, rwkv_v2, rwkv_kk, rwkv_ka, rwkv_rk, rwkv_lnx_w, rwkv_lnx_b,
              rwkv_out, sg_ln_w, sg_ln_b, sg_ws, sg_bs, sg_out, conv_w, conv_out, s5_a_re, s5_a_im,
              s5_b_re, s5_b_im, s5_c_re, s5_c_im, s5_d, s5_log_dt, s5_glu_w, w_o, norm_ffn_g,
              ffn_w1, ffn_w2, final_g):
    in_dtype = x.dtype
    bsz, seq, _ = x.shape
    c_act = jax.nn.silu(c.astype(jnp.float32))
    v_first = None
    for l in range(DEPTH):
        mod = (c_act @ ada_w[l] + ada_b[l])[:, None, :]
        sh1, sc1, gt1, sh2, sc2, gt2 = jnp.split(mod, 6, axis=-1)

        h = rms_norm(x, norm_mix_g[l]) * (1.0 + sc1) + sh1
        p = h @ w_in[l]
        v_mix = None if l == 0 else (rwkv_v0[l - 1], rwkv_v1[l - 1], rwkv_v2[l - 1])
        y_a, v_first = rwkv7_mixer(p[..., :OFF_B], v_first, v_mix, rwkv_mu[l], rwkv_w0[l], rwkv_w2[l],
                                   rwkv_a0[l], rwkv_a2[l], rwkv_g2[l], rwkv_kk[l], rwkv_ka[l],
                                   rwkv_rk[l], rwkv_lnx_w[l], rwkv_lnx_b[l], rwkv_out[l])
        y_b = spatial_gating_mixer(p[..., OFF_B:OFF_C], sg_ln_w[l], sg_ln_b[l], sg_ws[l], sg_bs[l], sg_out[l])
        y_c = short_conv_mixer(p[..., OFF_C:OFF_D], conv_w[l], conv_out[l])
        y_d = s5_mixer(p[..., OFF_D:OFF_G], s5_a_re[l], s5_a_im[l], s5_b_re[l], s5_b_im[l],
                       s5_c_re[l], s5_c_im[l], s5_d[l], s5_log_dt[l], s5_glu_w[l])
        gates = jax.nn.sigmoid(p[..., OFF_G:]).reshape(bsz, seq, N_BRANCH, D_MODEL)
        merged = (gates[:, :, 0] * y_a + gates[:, :, 1] * y_b
                  + gates[:, :, 2] * y_c + gates[:, :, 3] * y_d)
        x = x + gt1 * (merged @ w_o[l])

        h = rms_norm(x, norm_ffn_g[l]) * (1.0 + sc2) + sh2
        x = x + gt2 * (jnp.square(jax.nn.relu(h @ ffn_w1[l])) @ ffn_w2[l])
    return rms_norm(x, final_g).astype(in_dtype)
```

```python
import contextlib
import math
import numpy as np
import concourse.bass as bass
import concourse.mybir as mybir
from concourse.bass_utils import run_bass_kernel_spmd

F32 = mybir.dt.float32
BF16 = mybir.dt.bfloat16
AF = mybir.ActivationFunctionType
ALU = mybir.AluOpType

D = 1024
DM = 256
TS = 512
C = 64
NCH = TS // C
FR = 128
EPS = 1e-6
C0 = math.exp(-0.5)
MAGIC = 12582912.0


class Buf:
    __slots__ = ("name", "t", "lastw", "readers")

    def __init__(self, name, t=None):
        self.name = name
        self.t = t
        self.lastw = None
        self.readers = []

    def __getitem__(self, idx):
        return self.t[idx]


class Sched:
    N_DMA_SEM = 10

    def __init__(self, nc, stack):
        self.nc = nc
        self.stack = stack
        self.engs = {"pe": nc.tensor, "act": nc.scalar, "dve": nc.vector, "pool": nc.gpsimd, "sp": nc.sync}
        self.sem = {}
        self.cnt = {}
        for k in ("pe", "act", "dve", "pool"):
            self.sem[k] = stack.enter_context(nc.semaphore("s_" + k))
            self.cnt[k] = 0
        self.dma_sems = {"sp": [], "pool": []}
        for qn in ("sp", "pool"):
            for i in range(self.N_DMA_SEM):
                key = "dma_%s%d" % (qn, i)
                self.sem[key] = stack.enter_context(nc.semaphore("s_" + key))
                self.cnt[key] = 0
                self.dma_sems[qn].append(key)
        self.dma_rr = {"sp": 0, "pool": 0}
        self.waited = {}
        self.n_inst = 0
        self.n_wait = 0
        self.sb_bytes = 0

    def sb(self, name, shape, dt):
        t = self.stack.enter_context(self.nc.sbuf_tensor("sb_" + name, list(shape), dt))
        n = 1
        for s in shape[1:]:
            n *= s
        self.sb_bytes += n * (2 if dt == BF16 else 4)
        return Buf(name, t)

    def ps(self, name, shape, dt=F32):
        t = self.stack.enter_context(self.nc.psum_tensor("pp_" + name, list(shape), dt))
        return Buf(name, t)

    def _need(self, eng, deps):
        best = {}
        for d in deps:
            if d is None:
                continue
            sk, v, _ = d
            if v > best.get(sk, 0):
                best[sk] = v
        for sk, v in best.items():
            if self.waited.get((eng, sk), 0) >= v:
                continue
            self.engs[eng].wait_ge(self.sem[sk], v)
            self.waited[(eng, sk)] = v
            self.n_wait += 1

    def do(self, eng, fn, R=(), W=()):
        deps = []
        for b in R:
            lw = b.lastw
            if lw is not None and not (lw[2] == eng and eng == "pe"):
                deps.append(lw)
        for b in W:
            lw = b.lastw
            if lw is not None and (lw[2] != eng or eng != "pe"):
                deps.append(lw)
            for r in b.readers:
                if r[2] != eng or eng != "pe":
                    deps.append(r)
        self._need(eng, deps)
        ins = fn(self.engs[eng])
        self.cnt[eng] += 1
        v = self.cnt[eng]
        ins.then_inc(self.sem[eng], 1)
        tok = (eng, v, eng)
        for b in R:
            b.readers = [r for r in b.readers if r[0] != eng] + [tok]
        for b in W:
            b.lastw = tok
            b.readers = []
        self.n_inst += 1
        return ins

    def dma(self, out_ap, in_ap, R=(), W=(), q="sp"):
        sk = self.dma_sems[q][self.dma_rr[q]]
        self.dma_rr[q] = (self.dma_rr[q] + 1) % len(self.dma_sems[q])
        deps = []
        for b in R:
            if b.lastw is not None:
                deps.append(b.lastw)
        for b in W:
            if b.lastw is not None:
                deps.append(b.lastw)
            deps.extend(b.readers)
        if self.cnt[sk] > 0:
            deps.append((sk, self.cnt[sk], "dmaq"))
        self._need(q, deps)
        ins = self.engs[q].dma_start(out=out_ap, in_=in_ap)
        self.cnt[sk] += 16
        ins.then_inc(self.sem[sk], 16)
        tok = (sk, self.cnt[sk], "dmaq")
        for b in R:
            b.readers.append(tok)
        for b in W:
            b.lastw = tok
            b.readers = []
        self.n_inst += 1
        return ins

    def finish(self, bufs, eng="sp"):
        self._need(eng, [b.lastw for b in bufs if b.lastw is not None])


class Ring:
    def __init__(self, S, name, n, shape, dt, psum=False):
        self.bufs = [(S.ps if psum else S.sb)("%s%d" % (name, i), shape, dt) for i in range(n)]
        self.i = 0

    def next(self):
        b = self.bufs[self.i]
        self.i = (self.i + 1) % len(self.bufs)
        return b


def tile_km(W, msz):
    K, M = W.shape
    return np.ascontiguousarray(W.reshape(K // 128, 128, M // msz, msz).transpose(2, 1, 0, 3))


def col_pk(v, p=128):
    return np.ascontiguousarray(v.reshape(-1, p).T)


def prep_inputs(inp, b):
    f = np.float32
    L = 2
    m = {}
    m["xT"] = np.ascontiguousarray(inp["x"][b].T)
    m["cT"] = col_pk(inp["c"][b])
    m["ident"] = np.eye(128, dtype=f)
    m["ones"] = np.ones((128, 128), f)
    su = np.triu(np.ones((C, C), f), 1)
    sl = np.tril(np.ones((C, C), f), -1)
    iu = np.triu(np.ones((C, C), f), 0)
    m["msk5"] = np.ascontiguousarray(np.concatenate([su, sl, iu, sl, iu], axis=1))
    rs = np.ones((64, TS), f)
    rs[:, ::C] = 0.0
    m["reset"] = rs
    m["sgmask"] = np.triu(np.ones((128, 128), f), 0)
    m["iota"] = np.ascontiguousarray(np.broadcast_to(np.arange(FR, dtype=f) + 1.0, (128, FR)))
    sel = np.zeros((128, 2), f)
    sel[(np.arange(128) % 32) < 16, 0] = 1.0
    sel[(np.arange(128) % 32) >= 16, 1] = 1.0
    m["sel16"] = sel
    m["ada_w"] = np.ascontiguousarray(inp["ada_w"].reshape(L, 8, 128, 48, 128).transpose(0, 3, 2, 1, 4))
    m["ada_b"] = np.stack([col_pk(inp["ada_b"][l]) for l in range(L)])
    m["g_mix"] = np.stack([col_pk(inp["norm_mix_g"][l]) for l in range(L)])
    m["g_ffn"] = np.stack([col_pk(inp["norm_ffn_g"][l]) for l in range(L)])
    m["g_fin"] = col_pk(inp["final_g"])
    w_in = inp["w_in"]
    m["w_in64"] = np.stack([tile_km(w_in[l][:, :768], 64) for l in range(L)])
    m["w_in128"] = np.stack([tile_km(w_in[l][:, 768:], 128) for l in range(L)])
    mu = inp["rwkv_mu"]
    m["mu_rkv"] = np.stack([np.concatenate([col_pk(mu[l][q * 256:(q + 1) * 256], 64) for q in range(3)], axis=1)
                            for l in range(L)])
    m["mu_lora"] = np.stack([mu[l][768:896].reshape(128, 1) for l in range(L)])
    hp = lambda k: np.stack([col_pk(inp[k][l], 64) for l in range(inp[k].shape[0])])
    rk = inp["rwkv_rk"].reshape(L, 256)
    m["hpar"] = np.ascontiguousarray(np.stack(
        [hp("rwkv_w0"), hp("rwkv_a0"), hp("rwkv_kk"), hp("rwkv_ka"), hp("rwkv_lnx_w"), hp("rwkv_lnx_b"),
         np.stack([col_pk(rk[l], 64) for l in range(L)])], axis=2))
    m["v0"] = hp("rwkv_v0")
    m["lora_up"] = np.ascontiguousarray(np.concatenate([inp["rwkv_w2"], inp["rwkv_a2"], inp["rwkv_g2"]], axis=1))
    m["v1"] = np.ascontiguousarray(inp["rwkv_v1"].reshape(1, 4, 64, 32).transpose(0, 2, 1, 3))
    m["v2"] = np.ascontiguousarray(inp["rwkv_v2"])
    m["rwkv_out"] = np.ascontiguousarray(inp["rwkv_out"].reshape(L, 4, 64, 1024).transpose(0, 2, 1, 3))
    k2 = lambda k: np.ascontiguousarray(inp[k].reshape(L, 2, 128, -1).transpose(0, 2, 1, 3))
    m["sg_out"] = k2("sg_out")
    m["conv_out"] = k2("conv_out")
    m["glu_w"] = k2("s5_glu_w")
    m["sg_ln"] = np.stack([np.concatenate([col_pk(inp["sg_ln_w"][l]), col_pk(inp["sg_ln_b"][l])], axis=1) for l in range(L)])
    m["sg_wsT"] = np.ascontiguousarray(inp["sg_ws"].transpose(0, 3, 1, 2))
    bs = inp["sg_bs"]
    m["sg_bs"] = np.ascontiguousarray(np.stack(
        [np.stack([np.repeat(bs[l, 2 * ct:2 * ct + 2], 64, axis=0) for ct in range(2)], axis=1) for l in range(L)]))
    m["conv_w"] = np.ascontiguousarray(inp["conv_w"].reshape(L, 3, 2, 128).transpose(0, 3, 2, 1))
    pair = lambda a: np.ascontiguousarray(a.reshape(L, 8, 128).transpose(0, 2, 1))
    m["s5_lam"] = np.ascontiguousarray(np.stack(
        [pair(inp["s5_a_re"]), pair(inp["s5_a_im"]),
         pair(np.repeat(inp["s5_log_dt"][:, :, None], 64, axis=2))], axis=2))
    def bk(a):
        return np.ascontiguousarray(a.reshape(L, 2, 8, 64, 16).transpose(0, 2, 4, 1, 3).reshape(L, 128, 2, 64))
    def lk(a):
        a3 = a if a.ndim == 3 else np.repeat(a[:, :, None], 64, axis=2)
        r = np.repeat(a3.reshape(L, 2, 8, 1, 64), 16, axis=3)
        return np.ascontiguousarray(r.transpose(0, 2, 3, 1, 4).reshape(L, 128, 2, 64))
    m["s5_bk"] = np.ascontiguousarray(np.stack(
        [bk(inp["s5_b_re"]), bk(inp["s5_b_im"]), lk(inp["s5_a_re"]), lk(inp["s5_a_im"]), lk(inp["s5_log_dt"])], axis=2))
    lc = np.zeros((L, 8, 128, 2, 128), f)
    for jp in range(8):
        for gg in range(2):
            g = 2 * jp + gg
            cs = (jp % 4) * 32 + gg * 16
            lc[:, jp, gg * 64:(gg + 1) * 64, 0, cs:cs + 16] = inp["s5_c_re"][:, g].transpose(0, 2, 1)
            lc[:, jp, gg * 64:(gg + 1) * 64, 1, cs:cs + 16] = inp["s5_c_im"][:, g].transpose(0, 2, 1)
    m["s5_lc"] = np.ascontiguousarray(lc.transpose(0, 2, 1, 3, 4))
    m["s5_d"] = np.stack([col_pk(inp["s5_d"][l]) for l in range(L)])
    m["w_o"] = np.stack([tile_km(inp["w_o"][l], 128) for l in range(L)])
    m["ffn_w1"] = np.stack([tile_km(inp["ffn_w1"][l], 128) for l in range(L)])
    m["ffn_w2"] = np.stack([tile_km(inp["ffn_w2"][l], 128) for l in range(L)])
    return {k: np.ascontiguousarray(v, dtype=f) for k, v in m.items()}


def build(shapes, T):
    NS = T // TS
    NF = TS // FR
    nc = bass.Bass("TRN2", target_bir_lowering=False)
    dr = {}
    for k, shp in shapes.items():
        dr[k] = nc.dram_tensor(k, list(shp), F32, kind="ExternalInput").ap()
    outT = nc.dram_tensor("outT", [D, T], F32, kind="ExternalOutput").ap()
    xs_d = nc.dram_tensor("xs_scr", [D, T], F32, kind="Internal").ap()
    vf_d = nc.dram_tensor("vf_scr", [64, 4, T], F32, kind="Internal").ap()

    with contextlib.ExitStack() as st:
        S = Sched(nc, st)
        do, dma = S.do, S.dma
        B_outs = [Buf("outT%d" % i) for i in range(NS)]
        B_xs = [Buf("xs%d" % i) for i in range(NS)]
        B_vf = [Buf("vf%d" % i) for i in range(NS)]

        def cload(name, key, shape):
            b = S.sb(name, shape, F32)
            dma(b[:], dr[key], W=[b])
            return b
        ident = cload("ident", "ident", [128, 128])
        ones = cload("ones", "ones", [128, 128])
        msk5 = cload("msk5", "msk5", [64, 320])
        reset = cload("reset", "reset", [64, TS])
        sgmask = cload("sgmask", "sgmask", [128, 128])
        iota = cload("iota", "iota", [128, FR])
        sel16 = cload("sel16", "sel16", [128, 2])
        cT = cload("cT", "cT", [128, 8])
        g_fin = cload("g_fin", "g_fin", [128, 8])
        ones_bf = S.sb("ones_bf", [128, 128], BF16)
        do("dve", lambda e: e.tensor_copy(ones_bf[:], ones[:]), R=[ones], W=[ones_bf])
        csil = S.sb("csil", [128, 8], F32)
        do("act", lambda e: e.activation(csil[:], cT[:], AF.Silu), R=[cT], W=[csil])
        negpi = S.sb("negpi", [128, 1], F32)
        do("dve", lambda e: e.memset(negpi[:], -math.pi), W=[negpi])

        PS = [S.ps("ps%d" % i, [128, 512]) for i in range(8)]
        ps_rr = [0]

        def psum():
            b = PS[ps_rr[0] % 4]
            ps_rr[0] += 1
            return b

        X = S.sb("X", [128, 8, TS], F32)
        hT = S.sb("hT", [128, 8, TS], BF16)
        aT = S.sb("aT", [128, 16, TS], BF16)
        sqb = S.sb("sqb", [128, TS], BF16)
        rstd = S.sb("rstd", [128, TS], F32)
        xn = S.sb("xn", [128, TS], F32)
        mixA = S.sb("mixA", [64, 4, TS], BF16)
        mixB = S.sb("mixB", [128, 2, TS], BF16)
        mixC = S.sb("mixC", [128, 2, TS], BF16)
        mixD = S.sb("mixD", [128, 2, TS], BF16)
        w128 = Ring(S, "w128_", 5, [128, 8, 128], BF16)
        wsm = Ring(S, "wsm_", 8, [128, 2, 128], BF16)
        w64 = Ring(S, "w64_", 3, [128, 8, 64], BF16)
        wf2 = Ring(S, "wf2_", 2, [128, 16, 128], BF16)
        wra = Ring(S, "wra_", 2, [64, 4, 128], BF16)
        modT = S.sb("modT", [128, 48], F32)
        geff1 = S.sb("geff1", [128, 8], F32)
        geff2 = S.sb("geff2", [128, 8], F32)
        tmp48 = S.sb("tmp48", [128, 48], F32)
        work = Ring(S, "work_", 3, [128, TS], F32)
        Z = [S.sb("Z%d" % i, [128, TS], F32) for i in range(6)]
        PL = S.sb("PL", [128, TS], F32)
        LA = S.sb("LA", [128, TS], F32)
        H = {nm: S.sb("H_" + nm, [64, TS], F32) for nm in ("r", "k", "d", "e", "c", "w", "m", "p", "a", "g", "q")}
        Vall = S.sb("Vall", [64, 4, TS], F32)
        carry = S.sb("carry", [64, 12], F32)
        carryL = S.sb("carryL", [128, 1], F32)
        VV = S.sb("VV", [32, TS], F32)
        STt = S.sb("STt", [64, 4, 64], F32)
        STW = S.sb("STW", [64, 64], F32)
        NB = 4
        SCx = S.sb("SCx", [64, NB, 128], F32)
        SCb = S.sb("SCb", [64, NB, 192], BF16)
        TMb = S.sb("TMb", [64, NB, 256], BF16)
        XS = [S.sb("XS%d" % i, [64, NB, 64], F32) for i in range(2)]
        YS = [S.sb("YS%d" % i, [64, NB, 64], F32) for i in range(2)]
        TT = S.sb("TT", [64, NB, 64], F32)
        TTb = S.sb("TTb", [64, NB, 64], BF16)
        APMb = S.sb("APMb", [64, NB, 128], BF16)
        Atb = S.sb("Atb", [64, TS], BF16)
        Btb = S.sb("Btb", [64, TS], BF16)
        Ktb = S.sb("Ktb", [64, TS], BF16)
        Rtb = S.sb("Rtb", [64, TS], BF16)
        Vb = S.sb("Vb", [64, TS], BF16)
        STb = S.sb("STb", [64, 64], BF16)
        UTb = S.sb("UTb", [64, 64], BF16)
        ident_bf = S.sb("ident_bf", [128, 128], BF16)
        mu_rkv = S.sb("mu_rkv", [64, 12], F32)
        mu_lora = S.sb("mu_lora", [128, 1], F32)
        hpar = S.sb("hpar", [64, 7, 4], F32)
        omka = S.sb("omka", [64, 4], F32)
        v0 = S.sb("v0", [64, 4], F32)
        lora_up = S.sb("lora_up", [128, 256], F32)
        v1 = S.sb("v1", [64, 4, 32], F32)
        v2 = S.sb("v2", [32, 256], F32)
        sg_ln = S.sb("sg_ln", [128, 4], F32)
        sg_wb = S.sb("sg_wb", [128, 4, 128], BF16)
        sg_bs = S.sb("sg_bs", [128, 2, 128], F32)
        conv_w = S.sb("conv_w", [128, 2, 3], F32)
        zc = S.sb("zc", [128, 2, TS + 2], F32)
        vz = [S.sb("vz%d" % i, [128, 128], BF16) for i in range(4)]
        s5_lam = S.sb("s5_lam", [128, 3, 8], F32)
        s5_bk = S.sb("s5_bk", [128, 5, 2, 64], F32)
        s5_lc = S.sb("s5_lc", [128, 8, 2, 128], BF16)
        s5_d = S.sb("s5_d", [128, 2], F32)
        LB = S.sb("LB", [128, 8, 2, 128], BF16)
        cosT = S.sb("cosT", [128, 8, FR], F32)
        sinT = S.sb("sinT", [128, 8, FR], F32)
        rho = S.sb("rho", [128, 8], F32)
        th = S.sb("th", [128, 8], F32)
        s5st = S.sb("s5st", [128, 8, 2], F32)
        s5c = S.sb("s5c", [128, 2], F32)
        small = Ring(S, "small_", 4, [128, 64], F32)
        uT = S.sb("uT", [128, 2, TS], F32)
        uTb = S.sb("uTb", [128, 2, TS], BF16)
        xrb = S.sb("xrb", [128, TS], BF16)
        xib = S.sb("xib", [128, TS], BF16)
        tP = {nm: S.sb("s5p_" + nm, [128, 8], F32) for nm in ("lre", "dt", "mag", "ang", "sn", "cs", "abr", "abi", "den", "qre", "qim", "t1", "t2")}
        tK = {nm: S.sb("s5k_" + nm, [128, 2, 64], F32) for nm in ("lre", "dt", "mag", "ang", "sn", "cs", "abr", "abi", "den", "qre", "qim", "t1", "t2")}
        bbr = S.sb("bbr", [128, 2, 64], F32)
        bbi = S.sb("bbi", [128, 2, 64], F32)
        tq = S.sb("tq", [128, 2, 64], F32)

        def zero(b, ap=None):
            do("dve", lambda e: e.memset(b[:] if ap is None else ap, 0.0), W=[b])
        for b in vz:
            zero(b)
        do("dve", lambda e: e.tensor_copy(ident_bf[:], ident[:]), R=[ident], W=[ident_bf])
        zero(LB)
        print("SBUF bytes/partition:", S.sb_bytes)

        def wload(buf, src, ap=None):
            dma(buf[:] if ap is None else ap, src, W=[buf], q="pool")
            return buf

        def rms_stats():
            pb = psum()
            for ft in range(8):
                do("act", lambda e: e.activation(sqb[:], X[:, ft, :], AF.Square), R=[X], W=[sqb])
                do("pe", lambda e: e.matmul(pb[:], ones_bf[:], sqb[:], start=(ft == 0), stop=(ft == 7)), R=[ones_bf, sqb], W=[pb])
            do("dve", lambda e: e.tensor_scalar(rstd[:], pb[:], 1.0 / D, EPS, ALU.mult, ALU.add), R=[pb], W=[rstd])
            do("act", lambda e: e.activation(rstd[:], rstd[:], AF.Sqrt), R=[rstd], W=[rstd])
            do("dve", lambda e: e.reciprocal(rstd[:], rstd[:]), R=[rstd], W=[rstd])

        def rmsnorm_to(dst, geff, shift_col0):
            rms_stats()
            for ft in range(8):
                do("dve", lambda e: e.tensor_tensor(xn[:], X[:, ft, :], rstd[:], ALU.mult), R=[X, rstd], W=[xn])
                do("act", lambda e: e.activation(dst[:, ft, :], xn[:], AF.Identity, bias=modT[:, shift_col0 + ft:shift_col0 + ft + 1],
                                                 scale=geff[:, ft:ft + 1]), R=[xn, modT, geff], W=[dst])

        def proj128(src_ap, rhs_buf, rhs_fn, nk, ring, evac):
            wb = wload(ring.next(), src_ap)
            pb = psum()
            for kt in range(nk):
                do("pe", lambda e: e.matmul(pb[:], wb[:, kt, :], rhs_fn(kt), start=(kt == 0), stop=(kt == nk - 1)), R=[wb, rhs_buf], W=[pb])
            evac(pb)

        def s5_disc(t, are, aim, ldt, srcb):
            A = lambda nm: t[nm][:]
            B = lambda nm: t[nm]
            do("dve", lambda e: e.tensor_scalar(A("lre"), are, -1e-4, None, ALU.min), R=[srcb], W=[B("lre")])
            do("act", lambda e: e.activation(A("dt"), ldt, AF.Exp), R=[srcb], W=[B("dt")])
            do("dve", lambda e: e.tensor_tensor(A("t1"), A("lre"), A("dt"), ALU.mult), R=[B("lre"), B("dt")], W=[B("t1")])
            do("act", lambda e: e.activation(A("mag"), A("t1"), AF.Exp), R=[B("t1")], W=[B("mag")])
            do("dve", lambda e: e.tensor_tensor(A("ang"), aim, A("dt"), ALU.mult), R=[srcb, B("dt")], W=[B("ang")])
            for off, dst in ((0.0, "sn"), (0.5 * math.pi, "cs")):
                do("dve", lambda e: e.tensor_scalar(A("t1"), A("ang"), off, None, ALU.add), R=[B("ang")], W=[B("t1")])
                do("dve", lambda e: e.tensor_scalar(A("t2"), A("t1"), 1.0 / (2 * math.pi), MAGIC, ALU.mult, ALU.add), R=[B("t1")], W=[B("t2")])
                do("dve", lambda e: e.tensor_scalar(A("t2"), A("t2"), -MAGIC, None, ALU.add), R=[B("t2")], W=[B("t2")])
                do("dve", lambda e: e.scalar_tensor_tensor(A("t1"), A("t2"), -2 * math.pi, A("t1"), ALU.mult, ALU.add), R=[B("t2"), B("t1")], W=[B("t1")])
                do("act", lambda e: e.activation(A(dst), A("t1"), AF.Sin), R=[B("t1")], W=[B(dst)])
            do("dve", lambda e: e.tensor_tensor(A("abr"), A("mag"), A("cs"), ALU.mult), R=[B("mag"), B("cs")], W=[B("abr")])
            do("dve", lambda e: e.tensor_tensor(A("abi"), A("mag"), A("sn"), ALU.mult), R=[B("mag"), B("sn")], W=[B("abi")])
            do("dve", lambda e: e.tensor_tensor(A("den"), A("lre"), A("lre"), ALU.mult), R=[B("lre")], W=[B("den")])
            do("dve", lambda e: e.tensor_tensor(A("t1"), aim, aim, ALU.mult), R=[srcb], W=[B("t1")])
            do("dve", lambda e: e.tensor_tensor(A("den"), A("den"), A("t1"), ALU.add), R=[B("den"), B("t1")], W=[B("den")])
            do("dve", lambda e: e.reciprocal(A("den"), A("den")), R=[B("den")], W=[B("den")])
            do("dve", lambda e: e.tensor_scalar(A("t1"), A("abr"), -1.0, None, ALU.add), R=[B("abr")], W=[B("t1")])
            do("dve", lambda e: e.tensor_tensor(A("qre"), A("t1"), A("lre"), ALU.mult), R=[B("t1"), B("lre")], W=[B("qre")])
            do("dve", lambda e: e.tensor_tensor(A("t2"), A("abi"), aim, ALU.mult), R=[B("abi"), srcb], W=[B("t2")])
            do("dve", lambda e: e.tensor_tensor(A("qre"), A("qre"), A("t2"), ALU.add), R=[B("qre"), B("t2")], W=[B("qre")])
            do("dve", lambda e: e.tensor_tensor(A("qre"), A("qre"), A("den"), ALU.mult), R=[B("qre"), B("den")], W=[B("qre")])
            do("dve", lambda e: e.tensor_tensor(A("qim"), A("abi"), A("lre"), ALU.mult), R=[B("abi"), B("lre")], W=[B("qim")])
            do("dve", lambda e: e.tensor_tensor(A("t2"), A("t1"), aim, ALU.mult), R=[B("t1"), srcb], W=[B("t2")])
            do("dve", lambda e: e.tensor_tensor(A("qim"), A("qim"), A("t2"), ALU.subtract), R=[B("qim"), B("t2")], W=[B("qim")])
            do("dve", lambda e: e.tensor_tensor(A("qim"), A("qim"), A("den"), ALU.mult), R=[B("qim"), B("den")], W=[B("qim")])

        for l in range(2):
            pmod = PS[4]
            for mt in range(48):
                wb = Z[mt % 2 * 2]
                wb2 = Z[mt % 2 * 2 + 1]
                dma(wb[:].rearrange("p (k m) -> p k m", k=4), dr["ada_w"][l, mt][:, 0:4, :], W=[wb])
                dma(wb2[:].rearrange("p (k m) -> p k m", k=4), dr["ada_w"][l, mt][:, 4:8, :], W=[wb2])
                for kt in range(8):
                    wv = (wb if kt < 4 else wb2)
                    do("pe", lambda e: e.matmul(pmod[:, mt:mt + 1], wv[:, (kt % 4) * 128:(kt % 4 + 1) * 128], csil[:, kt:kt + 1],
                                                start=(kt == 0), stop=(kt == 7)), R=[wv, csil], W=[pmod])
            dma(tmp48[:], dr["ada_b"][l], W=[tmp48])
            do("dve", lambda e: e.tensor_tensor(modT[:], pmod[:, 0:48], tmp48[:], ALU.add), R=[pmod, tmp48], W=[modT])
            gm = small.next()
            dma(gm[:, 0:8], dr["g_mix"][l], W=[gm])
            do("dve", lambda e: e.scalar_tensor_tensor(geff1[:], modT[:, 8:16], 1.0, gm[:, 0:8], ALU.add, ALU.mult), R=[modT, gm], W=[geff1])
            gf = small.next()
            dma(gf[:, 0:8], dr["g_ffn"][l], W=[gf])
            do("dve", lambda e: e.scalar_tensor_tensor(geff2[:], modT[:, 32:40], 1.0, gf[:, 0:8], ALU.add, ALU.mult), R=[modT, gf], W=[geff2])

            dma(mu_rkv[:], dr["mu_rkv"][l], W=[mu_rkv])
            dma(mu_lora[:], dr["mu_lora"][l], W=[mu_lora])
            dma(hpar[:], dr["hpar"][l], W=[hpar])
            do("dve", lambda e: e.tensor_scalar(omka[:], hpar[:, 3, :], -1.0, 1.0, ALU.mult, ALU.add), R=[hpar], W=[omka])
            dma(lora_up[:], dr["lora_up"][l], W=[lora_up])
            if l == 1:
                dma(v0[:], dr["v0"][0], W=[v0])
                dma(v1[:], dr["v1"][0], W=[v1])
                dma(v2[:], dr["v2"][0], W=[v2])
            dma(sg_ln[:], dr["sg_ln"][l], W=[sg_ln])
            sgw = Z[4]
            dma(sgw[:].rearrange("p (g t) -> p g t", g=4), dr["sg_wsT"][l], W=[sgw])
            for g in range(4):
                do("dve", lambda e: e.tensor_tensor(sg_wb[:, g, :], sgw[:, g * 128:(g + 1) * 128], sgmask[:], ALU.mult), R=[sgw, sgmask], W=[sg_wb])
            dma(sg_bs[:], dr["sg_bs"][l], W=[sg_bs])
            dma(conv_w[:], dr["conv_w"][l], W=[conv_w])
            dma(s5_lam[:], dr["s5_lam"][l], W=[s5_lam])
            dma(s5_bk[:], dr["s5_bk"][l], W=[s5_bk])
            wload(s5_lc, dr["s5_lc"][l])
            dma(s5_d[:], dr["s5_d"][l], W=[s5_d])
            do("dve", lambda e: e.tensor_scalar(s5_lc[:, :, 1, :], s5_lc[:, :, 1, :], -1.0, None, ALU.mult), R=[s5_lc], W=[s5_lc])

            s5_disc(tP, s5_lam[:, 0, :], s5_lam[:, 1, :], s5_lam[:, 2, :], s5_lam)
            do("dve", lambda e: e.tensor_copy(rho[:], tP["mag"][:]), R=[tP["mag"]], W=[rho])
            do("dve", lambda e: e.tensor_copy(th[:], tP["ang"][:]), R=[tP["ang"]], W=[th])
            for jp in range(8):
                for off, dstT in ((0.0, sinT), (0.5 * math.pi, cosT)):
                    wk = work.next()
                    wk2 = work.next()
                    do("dve", lambda e: e.tensor_scalar(wk[:, 0:FR], iota[:], th[:, jp:jp + 1], off, ALU.mult, ALU.add), R=[iota, th], W=[wk])
                    do("dve", lambda e: e.tensor_scalar(wk2[:, 0:FR], wk[:, 0:FR], 1.0 / (2 * math.pi), MAGIC, ALU.mult, ALU.add), R=[wk], W=[wk2])
                    do("dve", lambda e: e.tensor_scalar(wk2[:, 0:FR], wk2[:, 0:FR], -MAGIC, None, ALU.add), R=[wk2], W=[wk2])
                    do("dve", lambda e: e.scalar_tensor_tensor(wk[:, 0:FR], wk2[:, 0:FR], -2 * math.pi, wk[:, 0:FR], ALU.mult, ALU.add), R=[wk2, wk], W=[wk])
                    do("act", lambda e: e.activation(dstT[:, jp, :], wk[:, 0:FR], AF.Sin), R=[wk], W=[dstT])
            s5_disc(tK, s5_bk[:, 2], s5_bk[:, 3], s5_bk[:, 4], s5_bk)
            do("dve", lambda e: e.tensor_tensor(bbr[:], tK["qre"][:], s5_bk[:, 0], ALU.mult), R=[tK["qre"], s5_bk], W=[bbr])
            do("dve", lambda e: e.tensor_tensor(tq[:], tK["qim"][:], s5_bk[:, 1], ALU.mult), R=[tK["qim"], s5_bk], W=[tq])
            do("dve", lambda e: e.tensor_tensor(bbr[:], bbr[:], tq[:], ALU.subtract), R=[bbr, tq], W=[bbr])
            do("dve", lambda e: e.tensor_tensor(bbi[:], tK["qre"][:], s5_bk[:, 1], ALU.mult), R=[tK["qre"], s5_bk], W=[bbi])
            do("dve", lambda e: e.tensor_tensor(tq[:], tK["qim"][:], s5_bk[:, 0], ALU.mult), R=[tK["qim"], s5_bk], W=[tq])
            do("dve", lambda e: e.tensor_tensor(bbi[:], bbi[:], tq[:], ALU.add), R=[bbi, tq], W=[bbi])
            for jp in range(8):
                kt, r0 = jp // 4, (jp % 4) * 32
                for ri, bb in ((0, bbr), (1, bbi)):
                    for gg in range(2):
                        do("dve", lambda e: e.tensor_scalar(LB[r0:r0 + 32, jp, ri, gg * 64:(gg + 1) * 64], bb[r0:r0 + 32, kt, :],
                                                            sel16[r0:r0 + 32, gg:gg + 1], None, ALU.mult), R=[bb, sel16], W=[LB])

            zero(STt)
            zero(carry)
            zero(carryL)
            zero(s5st)
            zero(zc, zc[:, :, 0:2])

            for si in range(NS):
                t0 = si * TS
                src = dr["xT"] if l == 0 else xs_d
                srcb = [] if l == 0 else [B_xs[si]]
                dma(X[:], src[:, t0:t0 + TS].rearrange("(ft p) t -> p ft t", p=128), R=srcb, W=[X])
                rmsnorm_to(hT, geff1, 0)
                hfn = lambda kt: hT[:, kt, :]

                def ev_lora(pb):
                    do("act", lambda e: e.copy(PL[:], pb[:]), R=[pb], W=[PL])
                proj128(dr["w_in128"][l, 0], hT, hfn, 8, w128, ev_lora)
                wk = work.next()
                do("dve", lambda e: e.tensor_tensor(wk[:, 1:TS], PL[:, 0:TS - 1], PL[:, 1:TS], ALU.subtract), R=[PL], W=[wk])
                do("dve", lambda e: e.tensor_tensor(wk[:, 0:1], carryL[:], PL[:, 0:1], ALU.subtract), R=[PL, carryL], W=[wk])
                do("dve", lambda e: e.tensor_copy(carryL[:], PL[:, TS - 1:TS]), R=[PL, wk], W=[carryL])
                do("dve", lambda e: e.scalar_tensor_tensor(PL[:], wk[:], mu_lora[:, 0:1], PL[:], ALU.mult, ALU.add), R=[wk, mu_lora, PL], W=[PL])
                do("act", lambda e: e.activation(LA[0:32, :], PL[0:32, :], AF.Tanh), R=[PL], W=[LA])
                do("act", lambda e: e.copy(LA[32:64, :], PL[32:64, :]), R=[PL], W=[LA])
                do("act", lambda e: e.activation(LA[64:128, :], PL[64:128, :], AF.Sigmoid), R=[PL], W=[LA])

                def rkv_proj(q, h, dst_buf, dst_ap):
                    wb = wload(w64.next(), dr["w_in64"][l, q * 4 + h])
                    pb = psum()
                    for kt in range(8):
                        do("pe", lambda e: e.matmul(pb[0:64, :], wb[:, kt, :], hT[:, kt, :], start=(kt == 0), stop=(kt == 7)), R=[wb, hT], W=[pb])
                    do("act", lambda e: e.copy(dst_ap, pb[0:64, :]), R=[pb], W=[dst_buf])

                def tshift(Qb, Qap, cc):
                    dq = H["d"]
                    do("dve", lambda e: e.tensor_tensor(dq[:, 1:TS], Qap[:, 0:TS - 1], Qap[:, 1:TS], ALU.subtract), R=[Qb], W=[dq])
                    do("dve", lambda e: e.tensor_tensor(dq[:, 0:1], carry[:, cc:cc + 1], Qap[:, 0:1], ALU.subtract), R=[Qb, carry], W=[dq])
                    do("dve", lambda e: e.tensor_copy(carry[:, cc:cc + 1], Qap[:, TS - 1:TS]), R=[Qb, dq], W=[carry])
                    do("dve", lambda e: e.scalar_tensor_tensor(Qap, dq[:], mu_rkv[:, cc:cc + 1], Qap, ALU.mult, ALU.add), R=[dq, mu_rkv, Qb], W=[Qb])

                for h in range(4):
                    rkv_proj(2, h, Vall, Vall[:, h, :])
                    tshift(Vall, Vall[:, h, :], 8 + h)
                if l == 0:
                    dma(vf_d[:, :, t0:t0 + TS], Vall[:], R=[Vall], W=[B_vf[si]])
                else:
                    pv = psum()
                    for h in range(4):
                        do("pe", lambda e: e.matmul(pv[0:32, :], v1[:, h, :], Vall[:, h, :], start=(h == 0), stop=(h == 3)), R=[v1, Vall], W=[pv])
                    do("act", lambda e: e.copy(VV[:], pv[0:32, :]), R=[pv], W=[VV])
                    for h in range(4):
                        pg = psum()
                        do("pe", lambda e: e.matmul(pg[0:64, :], v2[:, h * 64:(h + 1) * 64], VV[:], start=True, stop=True), R=[v2, VV], W=[pg])
                        wk = work.next()
                        do("act", lambda e: e.activation(wk[0:64, :], pg[0:64, :], AF.Sigmoid, bias=v0[:, h:h + 1]), R=[pg, v0], W=[wk])
                        wk2 = work.next()
                        dma(wk2[0:64, :], vf_d[:, h, t0:t0 + TS], R=[B_vf[si]], W=[wk2])
                        do("dve", lambda e: e.tensor_tensor(wk2[0:64, :], wk2[0:64, :], Vall[:, h, :], ALU.subtract), R=[wk2, Vall], W=[wk2])
                        do("dve", lambda e: e.tensor_tensor(wk2[0:64, :], wk2[0:64, :], wk[0:64, :], ALU.mult), R=[wk2, wk], W=[wk2])
                        do("dve", lambda e: e.tensor_tensor(Vall[:, h, :], Vall[:, h, :], wk2[0:64, :], ALU.add), R=[Vall, wk2], W=[Vall])

                for h in range(4):
                    Hr, Hk, Hd, He, Hc, Hw, Hm, Hp, Ha, Hg, Hq = (H[k] for k in "rkdecwmpagq")
                    rkv_proj(0, h, Hr, Hr[:])
                    rkv_proj(1, h, Hk, Hk[:])
                    tshift(Hr, Hr[:], h)
                    tshift(Hk, Hk[:], 4 + h)
                    cs_ = slice(h * 64, (h + 1) * 64)
                    pw = psum()
                    do("pe", lambda e: e.matmul(pw[0:64, :], lora_up[0:32, cs_], LA[0:32, :], start=True, stop=True), R=[lora_up, LA], W=[pw])
                    do("act", lambda e: e.activation(He[:], pw[0:64, :], AF.Sigmoid, bias=hpar[:, 0, h:h + 1]), R=[pw, hpar], W=[He])
                    pa = psum()
                    do("pe", lambda e: e.matmul(pa[0:64, :], lora_up[32:64, cs_], LA[32:64, :], start=True, stop=True), R=[lora_up, LA], W=[pa])
                    do("act", lambda e: e.activation(Ha[:], pa[0:64, :], AF.Sigmoid, bias=hpar[:, 1, h:h + 1]), R=[pa, hpar], W=[Ha])
                    pg = psum()
                    do("pe", lambda e: e.matmul(pg[0:64, :], lora_up[64:128, cs_], LA[64:128, :], start=True, stop=True), R=[lora_up, LA], W=[pg])
                    do("act", lambda e: e.copy(Hg[:], pg[0:64, :]), R=[pg], W=[Hg])
                    do("dve", lambda e: e.tensor_tensor_scan(Hc[:], reset[:], He[:], 0.0, ALU.mult, ALU.add), R=[reset, He], W=[Hc])
                    do("act", lambda e: e.activation(Hw[:], Hc[:], AF.Exp, scale=-C0), R=[Hc], W=[Hw])
                    do("act", lambda e: e.activation(Hm[:], Hc[:], AF.Exp, scale=C0), R=[Hc], W=[Hm])
                    do("dve", lambda e: e.tensor_tensor(Hd[:], Hc[:], He[:], ALU.subtract), R=[Hc, He], W=[Hd])
                    do("act", lambda e: e.activation(Hp[:], Hd[:], AF.Exp, scale=-C0), R=[Hd], W=[Hp])
                    do("dve", lambda e: e.tensor_scalar(Hq[:], Hk[:], hpar[:, 2, h:h + 1], None, ALU.mult), R=[Hk, hpar], W=[Hq])
                    do("dve", lambda e: e.tensor_tensor(Hd[:], Hq[:], Hq[:], ALU.mult), R=[Hq], W=[Hd])
                    pn = psum()
                    do("pe", lambda e: e.matmul(pn[0:64, :], ones[0:64, 0:64], Hd[:], start=True, stop=True), R=[ones, Hd], W=[pn])
                    do("dve", lambda e: e.tensor_scalar(Hc[:], pn[0:64, :], 1e-24, None, ALU.max), R=[pn], W=[Hc])
                    do("act", lambda e: e.activation(Hc[:], Hc[:], AF.Sqrt), R=[Hc], W=[Hc])
                    do("dve", lambda e: e.reciprocal(Hc[:], Hc[:]), R=[Hc], W=[Hc])
                    do("dve", lambda e: e.tensor_tensor(Hq[:], Hq[:], Hc[:], ALU.mult), R=[Hq, Hc], W=[Hq])
                    do("dve", lambda e: e.tensor_scalar(Hd[:], Ha[:], hpar[:, 3, h:h + 1], omka[:, h:h + 1], ALU.mult, ALU.add), R=[Ha, hpar, omka], W=[Hd])
                    do("dve", lambda e: e.tensor_tensor(Hk[:], Hk[:], Hd[:], ALU.mult), R=[Hk, Hd], W=[Hk])
                    do("dve", lambda e: e.scalar_tensor_tensor(Atb[:], Hq[:], -1.0, Hp[:], ALU.mult, ALU.mult), R=[Hq, Hp], W=[Atb])
                    do("dve", lambda e: e.tensor_tensor(Ha[:], Hq[:], Ha[:], ALU.mult), R=[Hq, Ha], W=[Ha])
                    do("dve", lambda e: e.tensor_tensor(Btb[:], Ha[:], Hm[:], ALU.mult), R=[Ha, Hm], W=[Btb])
                    do("dve", lambda e: e.tensor_tensor(Ktb[:], Hk[:], Hm[:], ALU.mult), R=[Hk, Hm], W=[Ktb])
                    do("act", lambda e: e.copy(Vb[:], Vall[:, h, :]), R=[Vall], W=[Vb])
                    do("dve", lambda e: e.scalar_tensor_tensor(Hd[:], Hr[:], hpar[:, 6, h:h + 1], Hk[:], ALU.mult, ALU.mult), R=[Hr, hpar, Hk], W=[Hd])
                    pn = psum()
                    do("pe", lambda e: e.matmul(pn[0:64, :], ones[0:64, 0:64], Hd[:], start=True, stop=True), R=[ones, Hd], W=[pn])
                    do("dve", lambda e: e.tensor_tensor(Hq[:], pn[0:64, :], Vall[:, h, :], ALU.mult), R=[pn, Vall], W=[Hq])
                    do("dve", lambda e: e.tensor_tensor(Rtb[:], Hr[:], Hw[:], ALU.mult), R=[Hr, Hw], W=[Rtb])
                    At, Bt, Kt, Rt = Atb, Btb, Ktb, Rtb
                    PO = PS[7]
                    do("act", lambda e: e.copy(STb[:], STt[:, h, :]), R=[STt], W=[STb])
                    for bt in range(NCH // NB):
                        for q in range(NB):
                            c = bt * NB + q
                            cs2 = slice(c * C, (c + 1) * C)
                            pb = psum()
                            a_, b_, k_, r_ = At[:, cs2], Bt[:, cs2], Kt[:, cs2], Rt[:, cs2]
                            do("pe", lambda e: e.matmul(pb[0:64, 0:64], b_, a_, start=True, stop=True), R=[Bt, At], W=[pb])
                            do("pe", lambda e: e.matmul(pb[0:64, 64:128], a_, b_, start=True, stop=True), R=[At, Bt], W=[pb])
                            do("pe", lambda e: e.matmul(pb[0:64, 128:192], b_, r_, start=True, stop=True), R=[Bt, Rt], W=[pb])
                            do("pe", lambda e: e.matmul(pb[0:64, 192:256], a_, k_, start=True, stop=True), R=[At, Kt], W=[pb])
                            do("pe", lambda e: e.matmul(pb[0:64, 256:320], k_, r_, start=True, stop=True), R=[Kt, Rt], W=[pb])
                            do("dve", lambda e: e.tensor_tensor(SCx[:, q, :], pb[0:64, 0:128], msk5[:, 0:128], ALU.mult), R=[pb, msk5], W=[SCx])
                            do("dve", lambda e: e.tensor_tensor(SCb[:, q, :], pb[0:64, 128:320], msk5[:, 128:320], ALU.mult), R=[pb, msk5], W=[SCb])
                            pt = psum()
                            v_ = Vb[:, cs2]
                            for qi, (sb_, s_) in enumerate(((At, a_), (Bt, b_), (Kt, k_), (Vb, v_))):
                                do("pe", lambda e: e.matmul(pt[0:64, qi * 64:(qi + 1) * 64], s_, ident_bf[0:64, 0:64], start=True, stop=True),
                                   R=[sb_, ident_bf], W=[pt])
                            do("act", lambda e: e.copy(TMb[:, q, :], pt[0:64, 0:256]), R=[pt], W=[TMb])
                        do("dve", lambda e: e.tensor_tensor(TT[:], SCx[:, :, 0:64], ident[0:64, 0:64].unsqueeze(1).to_broadcast([64, NB, 64]), ALU.add),
                           R=[SCx, ident], W=[TT])
                        Xb, Xf = SCx, (lambda q: SCx[:, q, 0:64])
                        Yb, Yf = SCx, (lambda q: SCx[:, q, 64:128])
                        cur = 0
                        for lev in range(1, 6):
                            Xn_, Yn_ = XS[cur], YS[cur]
                            if lev < 5:
                                pb = PS[4]
                                for q in range(NB):
                                    do("pe", lambda e: e.matmul(pb[0:64, q * 64:(q + 1) * 64], Yf(q), Xf(q), start=True, stop=True), R=[Yb, Xb], W=[pb])
                                do("act", lambda e: e.copy(Xn_[:], pb[0:64, 0:NB * 64].rearrange("p (q t) -> p q t", q=NB)), R=[pb], W=[Xn_])
                            pb = PS[5]
                            for q in range(NB):
                                do("pe", lambda e: e.matmul(pb[0:64, q * 64:(q + 1) * 64], Xf(q), Yf(q), start=True, stop=True), R=[Xb, Yb], W=[pb])
                            do("dve", lambda e: e.tensor_copy(Yn_[:], pb[0:64, 0:NB * 64].rearrange("p (q t) -> p q t", q=NB)), R=[pb], W=[Yn_])
                            pb = PS[6]
                            for q in range(NB):
                                do("pe", lambda e: e.matmul(pb[0:64, q * 64:(q + 1) * 64], Yn_[:, q, :], TT[:, q, :], start=True, stop=True), R=[Yn_, TT], W=[pb])
                            if lev < 5:
                                do("dve", lambda e: e.tensor_tensor(TT[:], TT[:], pb[0:64, 0:NB * 64].rearrange("p (q t) -> p q t", q=NB), ALU.add),
                                   R=[pb, TT], W=[TT])
                            else:
                                do("dve", lambda e: e.tensor_tensor(TTb[:], TT[:], pb[0:64, 0:NB * 64].rearrange("p (q t) -> p q t", q=NB), ALU.add),
                                   R=[pb, TT], W=[TTb])
                            Xb, Yb = Xn_, Yn_
                            Xf = (lambda q, Xn_=Xn_: Xn_[:, q, :])
                            Yf = (lambda q, Yn_=Yn_: Yn_[:, q, :])
                            cur = 1 - cur
                        pb = PS[4]
                        for q in range(NB):
                            do("pe", lambda e: e.matmul(pb[0:64, q * 128:q * 128 + 64], TMb[:, q, 0:64], TTb[:, q, :], start=True, stop=True), R=[TMb, TTb], W=[pb])
                            do("pe", lambda e: e.matmul(pb[0:64, q * 128 + 64:q * 128 + 128], SCb[:, q, 64:128], TTb[:, q, :], start=True, stop=True), R=[SCb, TTb], W=[pb])
                        do("act", lambda e: e.copy(APMb[:], pb[0:64, 0:NB * 128].rearrange("p (q t) -> p q t", q=NB)), R=[pb], W=[APMb])
                        for q in range(NB):
                            c = bt * NB + q
                            cs2 = slice(c * C, (c + 1) * C)
                            pu = PS[5]
                            do("pe", lambda e: e.matmul(pu[0:64, 0:64], APMb[:, q, 64:128], TMb[:, q, 192:256], start=True, stop=False), R=[APMb, TMb], W=[pu])
                            do("pe", lambda e: e.matmul(pu[0:64, 0:64], APMb[:, q, 0:64], STb[:], start=False, stop=True), R=[APMb, STb], W=[pu])
                            do("act", lambda e: e.copy(UTb[:], pu[0:64, 0:64]), R=[pu], W=[UTb])
                            do("pe", lambda e: e.matmul(PO[0:64, cs2], STb[:], Rt[:, cs2], start=True, stop=False), R=[STb, Rt], W=[PO])
                            do("pe", lambda e: e.matmul(PO[0:64, cs2], TMb[:, q, 192:256], SCb[:, q, 128:192], start=False, stop=False), R=[TMb, SCb], W=[PO])
                            do("pe", lambda e: e.matmul(PO[0:64, cs2], UTb[:], SCb[:, q, 0:64], start=False, stop=True), R=[UTb, SCb], W=[PO])
                            pn = PS[6]
                            do("pe", lambda e: e.matmul(pn[0:64, 0:64], TMb[:, q, 64:128], UTb[:], start=True, stop=False), R=[TMb, UTb], W=[pn])
                            do("pe", lambda e: e.matmul(pn[0:64, 0:64], TMb[:, q, 128:192], TMb[:, q, 192:256], start=False, stop=True), R=[TMb], W=[pn])
                            wc = Hw[:, c * C + C - 1:c * C + C]
                            do("act", lambda e: e.activation(STW[:], STt[:, h, :], AF.Copy, scale=wc), R=[STt, Hw], W=[STW])
                            do("dve", lambda e: e.scalar_tensor_tensor(STb[:], pn[0:64, 0:64], wc, STW[:], ALU.mult, ALU.add), R=[pn, Hw, STW], W=[STb])
                            do("dve", lambda e: e.scalar_tensor_tensor(STt[:, h, :], pn[0:64, 0:64], wc, STW[:], ALU.mult, ALU.add), R=[pn, Hw, STW], W=[STt])
                    OS = Hm
                    do("act", lambda e: e.copy(OS[:], PO[0:64, :]), R=[PO], W=[OS])
                    do("dve", lambda e: e.tensor_tensor(Hd[:], OS[:], OS[:], ALU.mult), R=[OS], W=[Hd])
                    pm1 = psum()
                    do("pe", lambda e: e.matmul(pm1[0:64, :], ones[0:64, 0:64], OS[:], start=True, stop=True), R=[ones, OS], W=[pm1])
                    pm2 = psum()
                    do("pe", lambda e: e.matmul(pm2[0:64, :], ones[0:64, 0:64], Hd[:], start=True, stop=True), R=[ones, Hd], W=[pm2])
                    mu_ = work.next()
                    do("act", lambda e: e.activation(mu_[0:64, :], pm1[0:64, :], AF.Copy, scale=1.0 / 64), R=[pm1], W=[mu_])
                    var = work.next()
                    do("dve", lambda e: e.tensor_tensor(var[0:64, :], mu_[0:64, :], mu_[0:64, :], ALU.mult), R=[mu_], W=[var])
                    do("dve", lambda e: e.scalar_tensor_tensor(var[0:64, :], pm2[0:64, :], 1.0 / 64, var[0:64, :], ALU.mult, ALU.subtract), R=[pm2, var], W=[var])
                    do("dve", lambda e: e.tensor_scalar(var[0:64, :], var[0:64, :], 64e-5, None, ALU.add), R=[var], W=[var])
                    do("act", lambda e: e.activation(var[0:64, :], var[0:64, :], AF.Sqrt), R=[var], W=[var])
                    do("dve", lambda e: e.reciprocal(var[0:64, :], var[0:64, :]), R=[var], W=[var])
                    do("dve", lambda e: e.tensor_tensor(Hc[:], OS[:], mu_[0:64, :], ALU.subtract), R=[OS, mu_], W=[Hc])
                    do("dve", lambda e: e.tensor_tensor(Hc[:], Hc[:], var[0:64, :], ALU.mult), R=[Hc, var], W=[Hc])
                    do("dve", lambda e: e.tensor_scalar(Hc[:], Hc[:], hpar[:, 4, h:h + 1], hpar[:, 5, h:h + 1], ALU.mult, ALU.add), R=[Hc, hpar], W=[Hc])
                    do("dve", lambda e: e.tensor_tensor(Hc[:], Hc[:], Hq[:], ALU.add), R=[Hc, Hq], W=[Hc])
                    do("dve", lambda e: e.tensor_tensor(mixA[:, h, :], Hc[:], Hg[:], ALU.mult), R=[Hc, Hg], W=[mixA])

                for mt in range(4):
                    def ev_gelu(pb, mt=mt):
                        do("act", lambda e: e.activation(Z[mt][:], pb[:], AF.Gelu_apprx_tanh), R=[pb], W=[Z[mt]])
                    proj128(dr["w_in128"][l, 1 + mt], hT, hfn, 8, w128, ev_gelu)
                pm1 = psum()
                for i in range(2):
                    do("pe", lambda e: e.matmul(pm1[:], ones[:], Z[2 + i][:], start=(i == 0), stop=(i == 1)), R=[ones, Z[2 + i]], W=[pm1])
                pm2 = psum()
                for i in range(2):
                    sq_ = (PL, LA)[i]
                    do("dve", lambda e: e.tensor_tensor(sq_[:], Z[2 + i][:], Z[2 + i][:], ALU.mult), R=[Z[2 + i]], W=[sq_])
                    do("pe", lambda e: e.matmul(pm2[:], ones[:], sq_[:], start=(i == 0), stop=(i == 1)), R=[ones, sq_], W=[pm2])
                do("act", lambda e: e.activation(LA[:], pm1[:], AF.Copy, scale=1.0 / 256), R=[pm1], W=[LA])
                do("dve", lambda e: e.tensor_tensor(PL[:], LA[:], LA[:], ALU.mult), R=[LA], W=[PL])
                do("dve", lambda e: e.scalar_tensor_tensor(PL[:], pm2[:], 1.0 / 256, PL[:], ALU.mult, ALU.subtract), R=[pm2, PL], W=[PL])
                do("dve", lambda e: e.tensor_scalar(PL[:], PL[:], 1e-5, None, ALU.add), R=[PL], W=[PL])
                do("act", lambda e: e.activation(PL[:], PL[:], AF.Sqrt), R=[PL], W=[PL])
                do("dve", lambda e: e.reciprocal(PL[:], PL[:]), R=[PL], W=[PL])
                for ct in range(2):
                    zv = Z[2 + ct]
                    do("dve", lambda e: e.tensor_tensor(zv[:], zv[:], LA[:], ALU.subtract), R=[zv, LA], W=[zv])
                    do("dve", lambda e: e.tensor_tensor(zv[:], zv[:], PL[:], ALU.mult), R=[zv, PL], W=[zv])
                    do("dve", lambda e: e.tensor_scalar(zv[:], zv[:], sg_ln[:, ct:ct + 1], sg_ln[:, 2 + ct:3 + ct], ALU.mult, ALU.add), R=[zv, sg_ln], W=[zv])
                for ck in range(TS // 128):
                    ks = slice(ck * 128, (ck + 1) * 128)
                    for ct in range(2):
                        pt = psum()
                        do("pe", lambda e: e.matmul(pt[:, 0:128], Z[2 + ct][:, ks], ident[:], start=True, stop=True), R=[Z[2 + ct], ident], W=[pt])
                        do("act", lambda e: e.copy(vz[ct * 2][:, 0:64], pt[:, 0:64]), R=[pt], W=[vz[ct * 2]])
                        do("dve", lambda e: e.tensor_copy(vz[ct * 2 + 1][:, 64:128], pt[:, 64:128]), R=[pt], W=[vz[ct * 2 + 1]])
                        pmx = psum()
                        for par in range(2):
                            g = ct * 2 + par
                            do("pe", lambda e: e.matmul(pmx[:, 0:128], vz[ct * 2 + par][:], sg_wb[:, g, :], start=(par == 0), stop=(par == 1)),
                               R=[vz[ct * 2 + par], sg_wb], W=[pmx])
                        wk = work.next()
                        do("dve", lambda e: e.tensor_tensor(wk[:, 0:128], pmx[:, 0:128], sg_bs[:, ct, :], ALU.add), R=[pmx, sg_bs], W=[wk])
                        do("dve", lambda e: e.tensor_tensor(mixB[:, ct, ks], wk[:, 0:128], Z[ct][:, ks], ALU.mult), R=[wk, Z[ct]], W=[mixB])

                for mt in range(6):
                    def ev_c(pb, mt=mt):
                        do("act", lambda e: e.copy(Z[mt][:], pb[:]), R=[pb], W=[Z[mt]])
                    proj128(dr["w_in128"][l, 5 + mt], hT, hfn, 8, w128, ev_c)
                for ct in range(2):
                    do("dve", lambda e: e.tensor_tensor(zc[:, ct, 2:TS + 2], Z[2 + ct][:], Z[4 + ct][:], ALU.mult), R=[Z[2 + ct], Z[4 + ct]], W=[zc])
                    y = Z[2 + ct]
                    do("dve", lambda e: e.tensor_scalar(y[:], zc[:, ct, 0:TS], conv_w[:, ct, 0:1], None, ALU.mult), R=[zc, conv_w], W=[y])
                    do("dve", lambda e: e.scalar_tensor_tensor(y[:], zc[:, ct, 1:TS + 1], conv_w[:, ct, 1:2], y[:], ALU.mult, ALU.add), R=[zc, conv_w, y], W=[y])
                    do("dve", lambda e: e.scalar_tensor_tensor(y[:], zc[:, ct, 2:TS + 2], conv_w[:, ct, 2:3], y[:], ALU.mult, ALU.add), R=[zc, conv_w, y], W=[y])
                    do("dve", lambda e: e.tensor_tensor(mixC[:, ct, :], y[:], Z[ct][:], ALU.mult), R=[y, Z[ct]], W=[mixC])
                do("dve", lambda e: e.tensor_copy(zc[:, :, 0:2], zc[:, :, TS:TS + 2]), R=[zc], W=[zc])

                for kt in range(2):
                    def ev_u(pb, kt=kt):
                        do("act", lambda e: e.copy(uT[:, kt, :], pb[:]), R=[pb], W=[uT])
                        do("dve", lambda e: e.tensor_copy(uTb[:, kt, :], pb[:]), R=[pb], W=[uTb])
                    proj128(dr["w_in128"][l, 11 + kt], hT, hfn, 8, w128, ev_u)
                py = [PS[4], PS[5]]
                f3 = lambda ap: ap.rearrange("p (f s) -> p f s", f=NF)
                for jp in range(8):
                    kt = jp // 4
                    pre = psum()
                    do("pe", lambda e: e.matmul(pre[:], LB[:, jp, 0, :], uTb[:, kt, :], start=True, stop=True), R=[LB, uTb], W=[pre])
                    pim = psum()
                    do("pe", lambda e: e.matmul(pim[:], LB[:, jp, 1, :], uTb[:, kt, :], start=True, stop=True), R=[LB, uTb], W=[pim])
                    bre, bim, mre, mim, tmp = Z[0], Z[1], Z[2], Z[3], Z[4]
                    do("act", lambda e: e.copy(bre[:], pre[:]), R=[pre], W=[bre])
                    do("act", lambda e: e.copy(bim[:], pim[:]), R=[pim], W=[bim])
                    cb = cosT[:, jp, :].unsqueeze(1).to_broadcast([128, NF, FR])
                    sb_ = sinT[:, jp, :].unsqueeze(1).to_broadcast([128, NF, FR])
                    do("dve", lambda e: e.tensor_tensor(f3(mre[:]), f3(bre[:]), cb, ALU.mult), R=[bre, cosT], W=[mre])
                    do("dve", lambda e: e.tensor_tensor(f3(tmp[:]), f3(bim[:]), sb_, ALU.mult), R=[bim, sinT], W=[tmp])
                    do("dve", lambda e: e.tensor_tensor(mre[:], mre[:], tmp[:], ALU.add), R=[mre, tmp], W=[mre])
                    tmp2 = Z[5]
                    do("dve", lambda e: e.tensor_tensor(f3(mim[:]), f3(bim[:]), cb, ALU.mult), R=[bim, cosT], W=[mim])
                    do("dve", lambda e: e.tensor_tensor(f3(tmp2[:]), f3(bre[:]), sb_, ALU.mult), R=[bre, sinT], W=[tmp2])
                    do("dve", lambda e: e.tensor_tensor(mim[:], mim[:], tmp2[:], ALU.subtract), R=[mim, tmp2], W=[mim])
                    rb = rho[:, jp:jp + 1].to_broadcast([128, FR])
                    cL, sL = cosT[:, jp, FR - 1:FR], sinT[:, jp, FR - 1:FR]
                    for fi in range(NF):
                        fs = slice(fi * FR, (fi + 1) * FR)
                        last = slice(fi * FR + FR - 1, fi * FR + FR)
                        do("dve", lambda e: e.tensor_tensor_scan(bre[:, fs], rb, mre[:, fs], s5st[:, jp, 0:1], ALU.mult, ALU.add), R=[rho, mre, s5st], W=[bre])
                        do("dve", lambda e: e.tensor_tensor_scan(bim[:, fs], rb, mim[:, fs], s5st[:, jp, 1:2], ALU.mult, ALU.add), R=[rho, mim, s5st], W=[bim])
                        do("dve", lambda e: e.tensor_scalar(s5c[:, 0:1], bim[:, last], sL, None, ALU.mult), R=[bim, sinT], W=[s5c])
                        do("dve", lambda e: e.tensor_scalar(s5c[:, 1:2], bre[:, last], sL, None, ALU.mult), R=[bre, sinT], W=[s5c])
                        do("dve", lambda e: e.scalar_tensor_tensor(s5st[:, jp, 0:1], bre[:, last], cL, s5c[:, 0:1], ALU.mult, ALU.subtract), R=[bre, cosT, s5c], W=[s5st])
                        do("dve", lambda e: e.scalar_tensor_tensor(s5st[:, jp, 1:2], bim[:, last], cL, s5c[:, 1:2], ALU.mult, ALU.add), R=[bim, cosT, s5c], W=[s5st])
                    do("dve", lambda e: e.tensor_tensor(f3(mre[:]), f3(bre[:]), cb, ALU.mult), R=[bre, cosT], W=[mre])
                    do("dve", lambda e: e.tensor_tensor(f3(tmp[:]), f3(bim[:]), sb_, ALU.mult), R=[bim, sinT], W=[tmp])
                    do("dve", lambda e: e.tensor_tensor(xrb[:], mre[:], tmp[:], ALU.subtract), R=[mre, tmp], W=[xrb])
                    do("dve", lambda e: e.tensor_tensor(f3(mim[:]), f3(bre[:]), sb_, ALU.mult), R=[bre, sinT], W=[mim])
                    do("dve", lambda e: e.tensor_tensor(f3(tmp2[:]), f3(bim[:]), cb, ALU.mult), R=[bim, cosT], W=[tmp2])
                    do("dve", lambda e: e.tensor_tensor(xib[:], mim[:], tmp2[:], ALU.add), R=[mim, tmp2], W=[xib])
                    ct = jp // 4
                    do("pe", lambda e: e.matmul(py[ct][:], s5_lc[:, jp, 0, :], xrb[:], start=(jp % 4 == 0), stop=False), R=[s5_lc, xrb], W=[py[ct]])
                    do("pe", lambda e: e.matmul(py[ct][:], s5_lc[:, jp, 1, :], xib[:], start=False, stop=(jp % 4 == 3)), R=[s5_lc, xib], W=[py[ct]])
                for ct in range(2):
                    do("dve", lambda e: e.scalar_tensor_tensor(uT[:, ct, :], uT[:, ct, :], s5_d[:, ct:ct + 1], py[ct][:], ALU.mult, ALU.add),
                       R=[uT, s5_d, py[ct]], W=[uT])
                    do("act", lambda e: e.activation(mixD[:, ct, :], uT[:, ct, :], AF.Gelu_apprx_tanh), R=[uT], W=[mixD])

                merged = aT
                for ft in range(8):
                    fsl = slice(ft * 128, (ft + 1) * 128)
                    acc = Z[5]
                    gts = [Z[0], Z[1], Z[2], Z[3]]
                    for br in range(4):
                        def ev_g(pb, br=br):
                            do("act", lambda e: e.activation(gts[br][:], pb[:], AF.Sigmoid), R=[pb], W=[gts[br]])
                        proj128(dr["w_in128"][l, 13 + br * 8 + ft], hT, hfn, 8, w128, ev_g)
                    wq = [wload(wsm.next(), dr["sg_out"][l][:, :, fsl]), wload(wsm.next(), dr["conv_out"][l][:, :, fsl]),
                          wload(wsm.next(), dr["glu_w"][l][:, :, fsl]),
                          wload(wsm.next(), dr["glu_w"][l][:, :, 1024 + ft * 128:1024 + (ft + 1) * 128])]
                    wr_ = wload(wra.next(), dr["rwkv_out"][l][:, :, fsl])
                    pa = psum()
                    for h in range(4):
                        do("pe", lambda e: e.matmul(pa[:], wr_[:, h, :], mixA[:, h, :], start=(h == 0), stop=(h == 3)), R=[wr_, mixA], W=[pa])
                    do("dve", lambda e: e.tensor_tensor(acc[:], pa[:], gts[0][:], ALU.mult), R=[pa, gts[0]], W=[acc])
                    for bi, mixT in ((1, mixB), (2, mixC)):
                        pb_ = psum()
                        for kt in range(2):
                            do("pe", lambda e: e.matmul(pb_[:], wq[bi - 1][:, kt, :], mixT[:, kt, :], start=(kt == 0), stop=(kt == 1)), R=[wq[bi - 1], mixT], W=[pb_])
                        do("dve", lambda e: e.tensor_tensor(gts[bi][:], pb_[:], gts[bi][:], ALU.mult), R=[pb_, gts[bi]], W=[gts[bi]])
                        do("dve", lambda e: e.tensor_tensor(acc[:], acc[:], gts[bi][:], ALU.add), R=[acc, gts[bi]], W=[acc])
                    ph1 = psum()
                    for kt in range(2):
                        do("pe", lambda e: e.matmul(ph1[:], wq[2][:, kt, :], mixD[:, kt, :], start=(kt == 0), stop=(kt == 1)), R=[wq[2], mixD], W=[ph1])
                    ph2 = psum()
                    for kt in range(2):
                        do("pe", lambda e: e.matmul(ph2[:], wq[3][:, kt, :], mixD[:, kt, :], start=(kt == 0), stop=(kt == 1)), R=[wq[3], mixD], W=[ph2])
                    do("act", lambda e: e.activation(xn[:], ph2[:], AF.Sigmoid), R=[ph2], W=[xn])
                    do("dve", lambda e: e.tensor_tensor(xn[:], ph1[:], xn[:], ALU.mult), R=[ph1, xn], W=[xn])
                    do("dve", lambda e: e.tensor_tensor(xn[:], xn[:], gts[3][:], ALU.mult), R=[xn, gts[3]], W=[xn])
                    do("dve", lambda e: e.tensor_tensor(merged[:, ft, :], acc[:], xn[:], ALU.add), R=[acc, xn], W=[merged])

                for mt in range(8):
                    def ev_o(pb, mt=mt):
                        do("dve", lambda e: e.scalar_tensor_tensor(X[:, mt, :], pb[:], modT[:, 16 + mt:17 + mt], X[:, mt, :], ALU.mult, ALU.add),
                           R=[pb, modT, X], W=[X])
                    proj128(dr["w_o"][l, mt], merged, lambda kt: merged[:, kt, :], 8, w128, ev_o)

                rmsnorm_to(hT, geff2, 24)
                for half in range(2):
                    for i in range(16):
                        def ev_f(pb, i=i):
                            wk = work.next()
                            do("act", lambda e: e.activation(wk[:], pb[:], AF.Relu), R=[pb], W=[wk])
                            do("dve", lambda e: e.tensor_tensor(aT[:, i, :], wk[:], wk[:], ALU.mult), R=[wk], W=[aT])
                        proj128(dr["ffn_w1"][l, half * 16 + i], hT, hfn, 8, w128, ev_f)
                    for mt in range(8):
                        def ev_2(pb, mt=mt):
                            do("dve", lambda e: e.scalar_tensor_tensor(X[:, mt, :], pb[:], modT[:, 40 + mt:41 + mt], X[:, mt, :], ALU.mult, ALU.add),
                               R=[pb, modT, X], W=[X])
                        proj128(dr["ffn_w2"][l, mt][:, half * 16:(half + 1) * 16, :], aT, lambda kt: aT[:, kt, :], 16, wf2, ev_2)

                if l == 0:
                    dma(xs_d[:, t0:t0 + TS].rearrange("(ft p) t -> p ft t", p=128), X[:], R=[X], W=[B_xs[si]])
                else:
                    rms_stats()
                    for ft in range(8):
                        do("dve", lambda e: e.scalar_tensor_tensor(X[:, ft, :], X[:, ft, :], g_fin[:, ft:ft + 1], rstd[:], ALU.mult, ALU.mult),
                           R=[X, g_fin, rstd], W=[X])
                    dma(outT[:, t0:t0 + TS].rearrange("(ft p) t -> p ft t", p=128), X[:], R=[X], W=[B_outs[si]])
        S.finish(B_outs)
        print("instructions:", S.n_inst, "waits:", S.n_wait)
    return nc


_CACHE = {}


def run(inputs, T):
    maps = [prep_inputs(inputs, b) for b in range(4)]
    shapes = {k: v.shape for k, v in maps[0].items()}
    key = (T,)
    if key not in _CACHE:
        _CACHE[key] = build(shapes, T)
    nc = _CACHE[key]
    in_maps = [maps[i % 4] for i in range(8)]
    res = run_bass_kernel_spmd(nc, in_maps, core_ids=list(range(8)))
    out = np.stack([np.ascontiguousarray(res.results[b]["outT"].T) for b in range(4)])
    return out.astype(np.float32)


def kernel(**inputs):
    inputs = {k: np.asarray(v, dtype=np.float32) for k, v in inputs.items()}
    T = inputs["x"].shape[1]
    return run(inputs, T)
```

```python
import contextlib
import math
import os
import numpy as np
import concourse.bass as bass
import concourse.mybir as mybir
from concourse.bass_utils import run_bass_kernel_spmd

F32 = mybir.dt.float32
BF16 = mybir.dt.bfloat16
AF = mybir.ActivationFunctionType
ALU = mybir.AluOpType

D = 1024
DM = 256
TS = 512
C = 64
NCH = TS // C
FR = 128
EPS = 1e-6
C0 = math.exp(-0.5)
MAGIC = 12582912.0


class Buf:
    __slots__ = ("name", "t", "lastw", "readers", "psum")

    def __init__(self, name, t=None):
        self.name = name
        self.t = t
        self.lastw = None
        self.readers = []
        self.psum = False

    def __getitem__(self, idx):
        return self.t[idx]


class Sched:
    N_DMA_SEM = 10

    def __init__(self, nc, stack):
        self.nc = nc
        self.stack = stack
        self.engs = {"pe": nc.tensor, "act": nc.scalar, "dve": nc.vector, "pool": nc.gpsimd, "sp": nc.sync}
        self.sem = {}
        self.cnt = {}
        for k in ("pe", "act", "dve", "pool"):
            self.sem[k] = stack.enter_context(nc.semaphore("s_" + k))
            self.cnt[k] = 0
        self.dma_sems = {"sp": [], "pool": []}
        for qn in ("sp", "pool"):
            for i in range(self.N_DMA_SEM):
                key = "dma_%s%d" % (qn, i)
                self.sem[key] = stack.enter_context(nc.semaphore("s_" + key))
                self.cnt[key] = 0
                self.dma_sems[qn].append(key)
        self.dma_rr = {"sp": 0, "pool": 0}
        self.waited = {}
        self.n_inst = 0
        self.n_wait = 0
        self.sb_bytes = 0

    def sb(self, name, shape, dt):
        t = self.stack.enter_context(self.nc.sbuf_tensor("sb_" + name, list(shape), dt))
        n = 1
        for s in shape[1:]:
            n *= s
        self.sb_bytes += n * (2 if dt == BF16 else 4)
        return Buf(name, t)

    def ps(self, name, shape, dt=F32):
        t = self.stack.enter_context(self.nc.psum_tensor("pp_" + name, list(shape), dt))
        b = Buf(name, t)
        b.psum = True
        return b

    def _need(self, eng, deps):
        best = {}
        for d in deps:
            if d is None:
                continue
            sk, v, _ = d
            if v > best.get(sk, 0):
                best[sk] = v
        for sk, v in best.items():
            if self.waited.get((eng, sk), 0) >= v:
                continue
            self.engs[eng].wait_ge(self.sem[sk], v)
            self.waited[(eng, sk)] = v
            self.n_wait += 1

    def do(self, eng, fn, R=(), W=()):
        deps = []
        for b in R:
            lw = b.lastw
            if lw is not None and not (lw[2] == eng and eng == "pe"):
                deps.append(lw)
            if b.psum:
                for r in b.readers:
                    if r[2] != eng:
                        deps.append(r)
        for b in W:
            lw = b.lastw
            if lw is not None and (lw[2] != eng or eng != "pe"):
                deps.append(lw)
            for r in b.readers:
                if r[2] != eng or eng != "pe":
                    deps.append(r)
        self._need(eng, deps)
        ins = fn(self.engs[eng])
        self.cnt[eng] += 1
        v = self.cnt[eng]
        ins.then_inc(self.sem[eng], 1)
        tok = (eng, v, eng)
        for b in R:
            b.readers = [r for r in b.readers if r[0] != eng] + [tok]
        for b in W:
            b.lastw = tok
            b.readers = []
        self.n_inst += 1
        return ins

    def dma(self, out_ap, in_ap, R=(), W=(), q="sp"):
        sk = self.dma_sems[q][self.dma_rr[q]]
        self.dma_rr[q] = (self.dma_rr[q] + 1) % len(self.dma_sems[q])
        deps = []
        for b in R:
            if b.lastw is not None:
                deps.append(b.lastw)
        for b in W:
            if b.lastw is not None:
                deps.append(b.lastw)
            deps.extend(b.readers)
        if self.cnt[sk] > 0:
            deps.append((sk, self.cnt[sk], "dmaq"))
        self._need(q, deps)
        ins = self.engs[q].dma_start(out=out_ap, in_=in_ap)
        self.cnt[sk] += 16
        ins.then_inc(self.sem[sk], 16)
        tok = (sk, self.cnt[sk], "dmaq")
        for b in R:
            b.readers.append(tok)
        for b in W:
            b.lastw = tok
            b.readers = []
        self.n_inst += 1
        return ins

    def finish(self, bufs, eng="sp"):
        self._need(eng, [b.lastw for b in bufs if b.lastw is not None])


class Ring:
    def __init__(self, S, name, n, shape, dt, psum=False):
        self.bufs = [(S.ps if psum else S.sb)("%s%d" % (name, i), shape, dt) for i in range(n)]
        self.i = 0

    def next(self):
        b = self.bufs[self.i]
        self.i = (self.i + 1) % len(self.bufs)
        return b


def tile_km(W, msz):
    K, M = W.shape
    return np.ascontiguousarray(W.reshape(K // 128, 128, M // msz, msz).transpose(2, 1, 0, 3))


def col_pk(v, p=128):
    return np.ascontiguousarray(v.reshape(-1, p).T)


def prep_inputs(inp, b):
    f = np.float32
    L = 2
    m = {}
    m["xT"] = np.ascontiguousarray(inp["x"][b].T)
    m["cT"] = col_pk(inp["c"][b])
    m["ident"] = np.eye(128, dtype=f)
    m["ones"] = np.ones((128, 128), f)
    su = np.triu(np.ones((C, C), f), 1)
    sl = np.tril(np.ones((C, C), f), -1)
    iu = np.triu(np.ones((C, C), f), 0)
    m["msk5"] = np.ascontiguousarray(np.concatenate([su, sl, iu, sl, iu], axis=1))
    rs = np.ones((64, TS), f)
    rs[:, ::C] = 0.0
    m["reset"] = rs
    m["sgmask"] = np.triu(np.ones((128, 128), f), 0)
    m["iota"] = np.ascontiguousarray(np.broadcast_to(np.arange(FR, dtype=f) + 1.0, (128, FR)))
    sel = np.zeros((128, 2), f)
    sel[(np.arange(128) % 32) < 16, 0] = 1.0
    sel[(np.arange(128) % 32) >= 16, 1] = 1.0
    m["sel16"] = sel
    m["ada_w"] = np.ascontiguousarray(inp["ada_w"].reshape(L, 8, 128, 48, 128).transpose(0, 3, 2, 1, 4))
    m["ada_b"] = np.stack([col_pk(inp["ada_b"][l]) for l in range(L)])
    m["g_mix"] = np.stack([col_pk(inp["norm_mix_g"][l]) for l in range(L)])
    m["g_ffn"] = np.stack([col_pk(inp["norm_ffn_g"][l]) for l in range(L)])
    m["g_fin"] = col_pk(inp["final_g"])
    w_in = inp["w_in"]
    m["w_in64"] = np.stack([tile_km(w_in[l][:, :768], 64) for l in range(L)])
    m["w_in128"] = np.stack([tile_km(w_in[l][:, 768:], 128) for l in range(L)])
    mu = inp["rwkv_mu"]
    m["mu_rkv"] = np.stack([np.concatenate([col_pk(mu[l][q * 256:(q + 1) * 256], 64) for q in range(3)], axis=1)
                            for l in range(L)])
    m["mu_lora"] = np.stack([mu[l][768:896].reshape(128, 1) for l in range(L)])
    hp = lambda k: np.stack([col_pk(inp[k][l], 64) for l in range(inp[k].shape[0])])
    rk = inp["rwkv_rk"].reshape(L, 256)
    m["hpar"] = np.ascontiguousarray(np.stack(
        [hp("rwkv_w0"), hp("rwkv_a0"), hp("rwkv_kk"), hp("rwkv_ka"), hp("rwkv_lnx_w"), hp("rwkv_lnx_b"),
         np.stack([col_pk(rk[l], 64) for l in range(L)])], axis=2))
    m["v0"] = hp("rwkv_v0")
    m["lora_up"] = np.ascontiguousarray(np.concatenate([inp["rwkv_w2"], inp["rwkv_a2"], inp["rwkv_g2"]], axis=1))
    m["v1"] = np.ascontiguousarray(inp["rwkv_v1"].reshape(1, 4, 64, 32).transpose(0, 2, 1, 3))
    m["v2"] = np.ascontiguousarray(inp["rwkv_v2"])
    m["rwkv_out"] = np.ascontiguousarray(inp["rwkv_out"].reshape(L, 4, 64, 1024).transpose(0, 2, 1, 3))
    k2 = lambda k: np.ascontiguousarray(inp[k].reshape(L, 2, 128, -1).transpose(0, 2, 1, 3))
    m["sg_out"] = k2("sg_out")
    m["conv_out"] = k2("conv_out")
    m["glu_w"] = k2("s5_glu_w")
    m["sg_ln"] = np.stack([np.concatenate([col_pk(inp["sg_ln_w"][l]), col_pk(inp["sg_ln_b"][l])], axis=1) for l in range(L)])
    m["sg_wsT"] = np.ascontiguousarray(inp["sg_ws"].transpose(0, 3, 1, 2))
    bs = inp["sg_bs"]
    m["sg_bs"] = np.ascontiguousarray(np.stack(
        [np.stack([np.repeat(bs[l, 2 * ct:2 * ct + 2], 64, axis=0) for ct in range(2)], axis=1) for l in range(L)]))
    m["conv_w"] = np.ascontiguousarray(inp["conv_w"].reshape(L, 3, 2, 128).transpose(0, 3, 2, 1))
    pair = lambda a: np.ascontiguousarray(a.reshape(L, 8, 128).transpose(0, 2, 1))
    m["s5_lam"] = np.ascontiguousarray(np.stack(
        [pair(inp["s5_a_re"]), pair(inp["s5_a_im"]),
         pair(np.repeat(inp["s5_log_dt"][:, :, None], 64, axis=2))], axis=2))
    def bk(a):
        return np.ascontiguousarray(a.reshape(L, 2, 8, 64, 16).transpose(0, 2, 4, 1, 3).reshape(L, 128, 2, 64))
    def lk(a):
        a3 = a if a.ndim == 3 else np.repeat(a[:, :, None], 64, axis=2)
        r = np.repeat(a3.reshape(L, 2, 8, 1, 64), 16, axis=3)
        return np.ascontiguousarray(r.transpose(0, 2, 3, 1, 4).reshape(L, 128, 2, 64))
    m["s5_bk"] = np.ascontiguousarray(np.stack(
        [bk(inp["s5_b_re"]), bk(inp["s5_b_im"]), lk(inp["s5_a_re"]), lk(inp["s5_a_im"]), lk(inp["s5_log_dt"])], axis=2))
    lc = np.zeros((L, 8, 128, 2, 128), f)
    for jp in range(8):
        for gg in range(2):
            g = 2 * jp + gg
            cs = (jp % 4) * 32 + gg * 16
            lc[:, jp, gg * 64:(gg + 1) * 64, 0, cs:cs + 16] = inp["s5_c_re"][:, g].transpose(0, 2, 1)
            lc[:, jp, gg * 64:(gg + 1) * 64, 1, cs:cs + 16] = inp["s5_c_im"][:, g].transpose(0, 2, 1)
    m["s5_lc"] = np.ascontiguousarray(lc.transpose(0, 2, 1, 3, 4))
    m["s5_d"] = np.stack([col_pk(inp["s5_d"][l]) for l in range(L)])
    m["w_o"] = np.stack([tile_km(inp["w_o"][l], 128) for l in range(L)])
    m["ffn_w1"] = np.stack([tile_km(inp["ffn_w1"][l], 128) for l in range(L)])
    m["ffn_w2"] = np.stack([tile_km(inp["ffn_w2"][l], 128) for l in range(L)])
    return {k: np.ascontiguousarray(v, dtype=f) for k, v in m.items()}


def build(shapes, T):
    NS = T // TS
    NF = TS // FR
    nc = bass.Bass("TRN2", target_bir_lowering=False)
    dr = {}
    for k, shp in shapes.items():
        dr[k] = nc.dram_tensor(k, list(shp), F32, kind="ExternalInput").ap()
    outT = nc.dram_tensor("outT", [D, T], F32, kind="ExternalOutput").ap()
    xs_d = nc.dram_tensor("xs_scr", [D, T], F32, kind="Internal").ap()
    vf_d = nc.dram_tensor("vf_scr", [64, 4, T], F32, kind="Internal").ap()

    with contextlib.ExitStack() as st:
        S = Sched(nc, st)
        do, dma = S.do, S.dma
        B_outs = [Buf("outT%d" % i) for i in range(NS)]
        B_xs = [Buf("xs%d" % i) for i in range(NS)]
        B_vf = [Buf("vf%d" % i) for i in range(NS)]

        def cload(name, key, shape):
            b = S.sb(name, shape, F32)
            dma(b[:], dr[key], W=[b])
            return b
        ident = cload("ident", "ident", [128, 128])
        ones = cload("ones", "ones", [128, 128])
        msk5 = cload("msk5", "msk5", [64, 320])
        reset = cload("reset", "reset", [64, TS])
        sgmask = cload("sgmask", "sgmask", [128, 128])
        iota = cload("iota", "iota", [128, FR])
        sel16 = cload("sel16", "sel16", [128, 2])
        cT = cload("cT", "cT", [128, 8])
        g_fin = cload("g_fin", "g_fin", [128, 8])
        ones_bf = S.sb("ones_bf", [128, 128], BF16)
        do("dve", lambda e: e.tensor_copy(ones_bf[:], ones[:]), R=[ones], W=[ones_bf])
        csil = S.sb("csil", [128, 8], F32)
        do("act", lambda e: e.activation(csil[:], cT[:], AF.Silu), R=[cT], W=[csil])
        negpi = S.sb("negpi", [128, 1], F32)
        do("dve", lambda e: e.memset(negpi[:], -math.pi), W=[negpi])

        PS = [S.ps("ps%d" % i, [128, 512]) for i in range(8)]
        ps_rr = [0]

        def psum():
            b = PS[ps_rr[0] % 3]
            ps_rr[0] += 1
            return b

        X = S.sb("X", [128, 8, TS], F32)
        hT = S.sb("hT", [128, 8, TS], BF16)
        aT = S.sb("aT", [128, 16, TS], BF16)
        sqb = S.sb("sqb", [128, TS], BF16)
        rstd = S.sb("rstd", [128, TS], F32)
        xn = S.sb("xn", [128, TS], F32)
        mixA = S.sb("mixA", [64, 4, TS], BF16)
        mixB = S.sb("mixB", [128, 2, TS], BF16)
        mixC = S.sb("mixC", [128, 2, TS], BF16)
        mixD = S.sb("mixD", [128, 2, TS], BF16)
        w128 = Ring(S, "w128_", 5, [128, 8, 128], BF16)
        wsm = Ring(S, "wsm_", 8, [128, 2, 128], BF16)
        w64 = Ring(S, "w64_", 3, [128, 8, 64], BF16)
        wf2 = Ring(S, "wf2_", 2, [128, 16, 128], BF16)
        wra = Ring(S, "wra_", 2, [64, 4, 128], BF16)
        modT = S.sb("modT", [128, 48], F32)
        geff1 = S.sb("geff1", [128, 8], F32)
        geff2 = S.sb("geff2", [128, 8], F32)
        tmp48 = S.sb("tmp48", [128, 48], F32)
        work = Ring(S, "work_", 3, [128, TS], F32)
        Z = [S.sb("Z%d" % i, [128, TS], F32) for i in range(6)]
        PL = S.sb("PL", [128, TS], F32)
        LA = S.sb("LA", [128, TS], F32)
        H = {nm: S.sb("H_" + nm, [64, TS], F32) for nm in ("r", "k", "d", "e", "c", "w", "m", "p", "a", "g", "q")}
        Vall = S.sb("Vall", [64, 4, TS], F32)
        carry = S.sb("carry", [64, 12], F32)
        carryL = S.sb("carryL", [128, 1], F32)
        VV = S.sb("VV", [32, TS], F32)
        STt = S.sb("STt", [64, 4, 64], F32)
        STW = S.sb("STW", [64, 64], F32)
        NB = 4
        SCx = S.sb("SCx", [64, NB, 128], F32)
        SCb = S.sb("SCb", [64, NB, 192], BF16)
        TMb = S.sb("TMb", [64, NB, 256], BF16)
        XS = [S.sb("XS%d" % i, [64, NB, 64], F32) for i in range(2)]
        YS = [S.sb("YS%d" % i, [64, NB, 64], F32) for i in range(2)]
        TT = S.sb("TT", [64, NB, 64], F32)
        TTb = S.sb("TTb", [64, NB, 64], BF16)
        APMb = S.sb("APMb", [64, NB, 128], BF16)
        Atb = S.sb("Atb", [64, TS], BF16)
        Btb = S.sb("Btb", [64, TS], BF16)
        Ktb = S.sb("Ktb", [64, TS], BF16)
        Rtb = S.sb("Rtb", [64, TS], BF16)
        Vb = S.sb("Vb", [64, TS], BF16)
        STb = S.sb("STb", [64, 64], BF16)
        UTb = S.sb("UTb", [64, 64], BF16)
        ident_bf = S.sb("ident_bf", [128, 128], BF16)
        mu_rkv = S.sb("mu_rkv", [64, 12], F32)
        mu_lora = S.sb("mu_lora", [128, 1], F32)
        hpar = S.sb("hpar", [64, 7, 4], F32)
        omka = S.sb("omka", [64, 4], F32)
        v0 = S.sb("v0", [64, 4], F32)
        lora_up = S.sb("lora_up", [128, 256], F32)
        v1 = S.sb("v1", [64, 4, 32], F32)
        v2 = S.sb("v2", [32, 256], F32)
        sg_ln = S.sb("sg_ln", [128, 4], F32)
        sg_wb = S.sb("sg_wb", [128, 4, 128], BF16)
        sg_bs = S.sb("sg_bs", [128, 2, 128], F32)
        conv_w = S.sb("conv_w", [128, 2, 3], F32)
        zc = S.sb("zc", [128, 2, TS + 2], F32)
        vz = [S.sb("vz%d" % i, [128, 128], BF16) for i in range(4)]
        s5_lam = S.sb("s5_lam", [128, 3, 8], F32)
        s5_bk = S.sb("s5_bk", [128, 5, 2, 64], F32)
        s5_lc = S.sb("s5_lc", [128, 8, 2, 128], BF16)
        s5_d = S.sb("s5_d", [128, 2], F32)
        LB = S.sb("LB", [128, 8, 2, 128], BF16)
        cosT = S.sb("cosT", [128, 8, FR], F32)
        sinT = S.sb("sinT", [128, 8, FR], F32)
        rho = S.sb("rho", [128, 8], F32)
        th = S.sb("th", [128, 8], F32)
        s5st = S.sb("s5st", [128, 8, 2], F32)
        s5c = S.sb("s5c", [128, 2], F32)
        small = Ring(S, "small_", 4, [128, 64], F32)
        uT = S.sb("uT", [128, 2, TS], F32)
        uTb = S.sb("uTb", [128, 2, TS], BF16)
        xrb = S.sb("xrb", [128, TS], BF16)
        xib = S.sb("xib", [128, TS], BF16)
        tP = {nm: S.sb("s5p_" + nm, [128, 8], F32) for nm in ("lre", "dt", "mag", "ang", "sn", "cs", "abr", "abi", "den", "qre", "qim", "t1", "t2")}
        tK = {nm: S.sb("s5k_" + nm, [128, 2, 64], F32) for nm in ("lre", "dt", "mag", "ang", "sn", "cs", "abr", "abi", "den", "qre", "qim", "t1", "t2")}
        bbr = S.sb("bbr", [128, 2, 64], F32)
        bbi = S.sb("bbi", [128, 2, 64], F32)
        tq = S.sb("tq", [128, 2, 64], F32)

        def zero(b, ap=None):
            do("dve", lambda e: e.memset(b[:] if ap is None else ap, 0.0), W=[b])
        for b in vz:
            zero(b)
        do("dve", lambda e: e.tensor_copy(ident_bf[:], ident[:]), R=[ident], W=[ident_bf])
        zero(LB)
        print("SBUF bytes/partition:", S.sb_bytes)

        def wload(buf, src, ap=None):
            dma(buf[:] if ap is None else ap, src, W=[buf], q="pool")
            return buf

        def rms_stats():
            pb = psum()
            for ft in range(8):
                do("act", lambda e: e.activation(sqb[:], X[:, ft, :], AF.Square), R=[X], W=[sqb])
                do("pe", lambda e: e.matmul(pb[:], ones_bf[:], sqb[:], start=(ft == 0), stop=(ft == 7)), R=[ones_bf, sqb], W=[pb])
            do("dve", lambda e: e.tensor_scalar(rstd[:], pb[:], 1.0 / D, EPS, ALU.mult, ALU.add), R=[pb], W=[rstd])
            do("act", lambda e: e.activation(rstd[:], rstd[:], AF.Sqrt), R=[rstd], W=[rstd])
            do("dve", lambda e: e.reciprocal(rstd[:], rstd[:]), R=[rstd], W=[rstd])

        def rmsnorm_to(dst, geff, shift_col0):
            rms_stats()
            for ft in range(8):
                do("dve", lambda e: e.tensor_tensor(xn[:], X[:, ft, :], rstd[:], ALU.mult), R=[X, rstd], W=[xn])
                do("act", lambda e: e.activation(dst[:, ft, :], xn[:], AF.Identity, bias=modT[:, shift_col0 + ft:shift_col0 + ft + 1],
                                                 scale=geff[:, ft:ft + 1]), R=[xn, modT, geff], W=[dst])

        def proj128(src_ap, rhs_buf, rhs_fn, nk, ring, evac):
            wb = wload(ring.next(), src_ap)
            pb = psum()
            for kt in range(nk):
                do("pe", lambda e: e.matmul(pb[:], wb[:, kt, :], rhs_fn(kt), start=(kt == 0), stop=(kt == nk - 1)), R=[wb, rhs_buf], W=[pb])
            evac(pb)

        def s5_disc(t, are, aim, ldt, srcb):
            A = lambda nm: t[nm][:]
            B = lambda nm: t[nm]
            do("dve", lambda e: e.tensor_scalar(A("lre"), are, -1e-4, None, ALU.min), R=[srcb], W=[B("lre")])
            do("act", lambda e: e.activation(A("dt"), ldt, AF.Exp), R=[srcb], W=[B("dt")])
            do("dve", lambda e: e.tensor_tensor(A("t1"), A("lre"), A("dt"), ALU.mult), R=[B("lre"), B("dt")], W=[B("t1")])
            do("act", lambda e: e.activation(A("mag"), A("t1"), AF.Exp), R=[B("t1")], W=[B("mag")])
            do("dve", lambda e: e.tensor_tensor(A("ang"), aim, A("dt"), ALU.mult), R=[srcb, B("dt")], W=[B("ang")])
            for off, dst in ((0.0, "sn"), (0.5 * math.pi, "cs")):
                do("dve", lambda e: e.tensor_scalar(A("t1"), A("ang"), off, None, ALU.add), R=[B("ang")], W=[B("t1")])
                do("dve", lambda e: e.tensor_scalar(A("t2"), A("t1"), 1.0 / (2 * math.pi), MAGIC, ALU.mult, ALU.add), R=[B("t1")], W=[B("t2")])
                do("dve", lambda e: e.tensor_scalar(A("t2"), A("t2"), -MAGIC, None, ALU.add), R=[B("t2")], W=[B("t2")])
                do("dve", lambda e: e.scalar_tensor_tensor(A("t1"), A("t2"), -2 * math.pi, A("t1"), ALU.mult, ALU.add), R=[B("t2"), B("t1")], W=[B("t1")])
                do("act", lambda e: e.activation(A(dst), A("t1"), AF.Sin), R=[B("t1")], W=[B(dst)])
            do("dve", lambda e: e.tensor_tensor(A("abr"), A("mag"), A("cs"), ALU.mult), R=[B("mag"), B("cs")], W=[B("abr")])
            do("dve", lambda e: e.tensor_tensor(A("abi"), A("mag"), A("sn"), ALU.mult), R=[B("mag"), B("sn")], W=[B("abi")])
            do("dve", lambda e: e.tensor_tensor(A("den"), A("lre"), A("lre"), ALU.mult), R=[B("lre")], W=[B("den")])
            do("dve", lambda e: e.tensor_tensor(A("t1"), aim, aim, ALU.mult), R=[srcb], W=[B("t1")])
            do("dve", lambda e: e.tensor_tensor(A("den"), A("den"), A("t1"), ALU.add), R=[B("den"), B("t1")], W=[B("den")])
            do("dve", lambda e: e.reciprocal(A("den"), A("den")), R=[B("den")], W=[B("den")])
            do("dve", lambda e: e.tensor_scalar(A("t1"), A("abr"), -1.0, None, ALU.add), R=[B("abr")], W=[B("t1")])
            do("dve", lambda e: e.tensor_tensor(A("qre"), A("t1"), A("lre"), ALU.mult), R=[B("t1"), B("lre")], W=[B("qre")])
            do("dve", lambda e: e.tensor_tensor(A("t2"), A("abi"), aim, ALU.mult), R=[B("abi"), srcb], W=[B("t2")])
            do("dve", lambda e: e.tensor_tensor(A("qre"), A("qre"), A("t2"), ALU.add), R=[B("qre"), B("t2")], W=[B("qre")])
            do("dve", lambda e: e.tensor_tensor(A("qre"), A("qre"), A("den"), ALU.mult), R=[B("qre"), B("den")], W=[B("qre")])
            do("dve", lambda e: e.tensor_tensor(A("qim"), A("abi"), A("lre"), ALU.mult), R=[B("abi"), B("lre")], W=[B("qim")])
            do("dve", lambda e: e.tensor_tensor(A("t2"), A("t1"), aim, ALU.mult), R=[B("t1"), srcb], W=[B("t2")])
            do("dve", lambda e: e.tensor_tensor(A("qim"), A("qim"), A("t2"), ALU.subtract), R=[B("qim"), B("t2")], W=[B("qim")])
            do("dve", lambda e: e.tensor_tensor(A("qim"), A("qim"), A("den"), ALU.mult), R=[B("qim"), B("den")], W=[B("qim")])

        for l in range(2):
            pmod = PS[4]
            for mt in range(48):
                wb = Z[mt % 2 * 2]
                wb2 = Z[mt % 2 * 2 + 1]
                dma(wb[:].rearrange("p (k m) -> p k m", k=4), dr["ada_w"][l, mt][:, 0:4, :], W=[wb])
                dma(wb2[:].rearrange("p (k m) -> p k m", k=4), dr["ada_w"][l, mt][:, 4:8, :], W=[wb2])
                for kt in range(8):
                    wv = (wb if kt < 4 else wb2)
                    do("pe", lambda e: e.matmul(pmod[:, mt:mt + 1], wv[:, (kt % 4) * 128:(kt % 4 + 1) * 128], csil[:, kt:kt + 1],
                                                start=(kt == 0), stop=(kt == 7)), R=[wv, csil], W=[pmod])
            dma(tmp48[:], dr["ada_b"][l], W=[tmp48])
            do("dve", lambda e: e.tensor_tensor(modT[:], pmod[:, 0:48], tmp48[:], ALU.add), R=[pmod, tmp48], W=[modT])
            gm = small.next()
            dma(gm[:, 0:8], dr["g_mix"][l], W=[gm])
            do("dve", lambda e: e.scalar_tensor_tensor(geff1[:], modT[:, 8:16], 1.0, gm[:, 0:8], ALU.add, ALU.mult), R=[modT, gm], W=[geff1])
            gf = small.next()
            dma(gf[:, 0:8], dr["g_ffn"][l], W=[gf])
            do("dve", lambda e: e.scalar_tensor_tensor(geff2[:], modT[:, 32:40], 1.0, gf[:, 0:8], ALU.add, ALU.mult), R=[modT, gf], W=[geff2])

            dma(mu_rkv[:], dr["mu_rkv"][l], W=[mu_rkv])
            dma(mu_lora[:], dr["mu_lora"][l], W=[mu_lora])
            dma(hpar[:], dr["hpar"][l], W=[hpar])
            do("dve", lambda e: e.tensor_scalar(omka[:], hpar[:, 3, :], -1.0, 1.0, ALU.mult, ALU.add), R=[hpar], W=[omka])
            dma(lora_up[:], dr["lora_up"][l], W=[lora_up])
            if l == 1:
                dma(v0[:], dr["v0"][0], W=[v0])
                dma(v1[:], dr["v1"][0], W=[v1])
                dma(v2[:], dr["v2"][0], W=[v2])
            dma(sg_ln[:], dr["sg_ln"][l], W=[sg_ln])
            sgw = Z[4]
            dma(sgw[:].rearrange("p (g t) -> p g t", g=4), dr["sg_wsT"][l], W=[sgw])
            for g in range(4):
                do("dve", lambda e: e.tensor_tensor(sg_wb[:, g, :], sgw[:, g * 128:(g + 1) * 128], sgmask[:], ALU.mult), R=[sgw, sgmask], W=[sg_wb])
            dma(sg_bs[:], dr["sg_bs"][l], W=[sg_bs])
            dma(conv_w[:], dr["conv_w"][l], W=[conv_w])
            dma(s5_lam[:], dr["s5_lam"][l], W=[s5_lam])
            dma(s5_bk[:], dr["s5_bk"][l], W=[s5_bk])
            wload(s5_lc, dr["s5_lc"][l])
            dma(s5_d[:], dr["s5_d"][l], W=[s5_d])
            do("dve", lambda e: e.tensor_scalar(s5_lc[:, :, 1, :], s5_lc[:, :, 1, :], -1.0, None, ALU.mult), R=[s5_lc], W=[s5_lc])

            s5_disc(tP, s5_lam[:, 0, :], s5_lam[:, 1, :], s5_lam[:, 2, :], s5_lam)
            do("dve", lambda e: e.tensor_copy(rho[:], tP["mag"][:]), R=[tP["mag"]], W=[rho])
            do("dve", lambda e: e.tensor_copy(th[:], tP["ang"][:]), R=[tP["ang"]], W=[th])
            for jp in range(8):
                for off, dstT in ((0.0, sinT), (0.5 * math.pi, cosT)):
                    wk = work.next()
                    wk2 = work.next()
                    do("dve", lambda e: e.tensor_scalar(wk[:, 0:FR], iota[:], th[:, jp:jp + 1], off, ALU.mult, ALU.add), R=[iota, th], W=[wk])
                    do("dve", lambda e: e.tensor_scalar(wk2[:, 0:FR], wk[:, 0:FR], 1.0 / (2 * math.pi), MAGIC, ALU.mult, ALU.add), R=[wk], W=[wk2])
                    do("dve", lambda e: e.tensor_scalar(wk2[:, 0:FR], wk2[:, 0:FR], -MAGIC, None, ALU.add), R=[wk2], W=[wk2])
                    do("dve", lambda e: e.scalar_tensor_tensor(wk[:, 0:FR], wk2[:, 0:FR], -2 * math.pi, wk[:, 0:FR], ALU.mult, ALU.add), R=[wk2, wk], W=[wk])
                    do("act", lambda e: e.activation(dstT[:, jp, :], wk[:, 0:FR], AF.Sin), R=[wk], W=[dstT])
            s5_disc(tK, s5_bk[:, 2], s5_bk[:, 3], s5_bk[:, 4], s5_bk)
            do("dve", lambda e: e.tensor_tensor(bbr[:], tK["qre"][:], s5_bk[:, 0], ALU.mult), R=[tK["qre"], s5_bk], W=[bbr])
            do("dve", lambda e: e.tensor_tensor(tq[:], tK["qim"][:], s5_bk[:, 1], ALU.mult), R=[tK["qim"], s5_bk], W=[tq])
            do("dve", lambda e: e.tensor_tensor(bbr[:], bbr[:], tq[:], ALU.subtract), R=[bbr, tq], W=[bbr])
            do("dve", lambda e: e.tensor_tensor(bbi[:], tK["qre"][:], s5_bk[:, 1], ALU.mult), R=[tK["qre"], s5_bk], W=[bbi])
            do("dve", lambda e: e.tensor_tensor(tq[:], tK["qim"][:], s5_bk[:, 0], ALU.mult), R=[tK["qim"], s5_bk], W=[tq])
            do("dve", lambda e: e.tensor_tensor(bbi[:], bbi[:], tq[:], ALU.add), R=[bbi, tq], W=[bbi])
            for jp in range(8):
                kt, r0 = jp // 4, (jp % 4) * 32
                for ri, bb in ((0, bbr), (1, bbi)):
                    for gg in range(2):
                        do("dve", lambda e: e.tensor_scalar(LB[r0:r0 + 32, jp, ri, gg * 64:(gg + 1) * 64], bb[r0:r0 + 32, kt, :],
                                                            sel16[r0:r0 + 32, gg:gg + 1], None, ALU.mult), R=[bb, sel16], W=[LB])

            zero(STt)
            zero(carry)
            zero(carryL)
            zero(s5st)
            zero(zc, zc[:, :, 0:2])

            for si in range(NS):
                t0 = si * TS
                src = dr["xT"] if l == 0 else xs_d
                srcb = [] if l == 0 else [B_xs[si]]
                dma(X[:], src[:, t0:t0 + TS].rearrange("(ft p) t -> p ft t", p=128), R=srcb, W=[X])
                rmsnorm_to(hT, geff1, 0)
                hfn = lambda kt: hT[:, kt, :]

                def s5_gen():
                    for kt in range(2):
                        def ev_u(pb, kt=kt):
                            do("act", lambda e: e.copy(uT[:, kt, :], pb[:]), R=[pb], W=[uT])
                            do("dve", lambda e: e.tensor_copy(uTb[:, kt, :], pb[:]), R=[pb], W=[uTb])
                        proj128(dr["w_in128"][l, 11 + kt], hT, hfn, 8, w128, ev_u)
                    py = [PS[3], PS[3]]
                    f3 = lambda ap: ap.rearrange("p (f s) -> p f s", f=NF)
                    for jp in range(8):
                        kt = jp // 4
                        pre = psum()
                        do("pe", lambda e: e.matmul(pre[:], LB[:, jp, 0, :], uTb[:, kt, :], start=True, stop=True), R=[LB, uTb], W=[pre])
                        pim = psum()
                        do("pe", lambda e: e.matmul(pim[:], LB[:, jp, 1, :], uTb[:, kt, :], start=True, stop=True), R=[LB, uTb], W=[pim])
                        bre, bim, mre, mim, tmp = Z[0], Z[1], Z[2], Z[3], Z[4]
                        do("act", lambda e: e.copy(bre[:], pre[:]), R=[pre], W=[bre])
                        do("act", lambda e: e.copy(bim[:], pim[:]), R=[pim], W=[bim])
                        cb = cosT[:, jp, :].unsqueeze(1).to_broadcast([128, NF, FR])
                        sb_ = sinT[:, jp, :].unsqueeze(1).to_broadcast([128, NF, FR])
                        do("dve", lambda e: e.tensor_tensor(f3(mre[:]), f3(bre[:]), cb, ALU.mult), R=[bre, cosT], W=[mre])
                        do("dve", lambda e: e.tensor_tensor(f3(tmp[:]), f3(bim[:]), sb_, ALU.mult), R=[bim, sinT], W=[tmp])
                        yield
                        do("dve", lambda e: e.tensor_tensor(mre[:], mre[:], tmp[:], ALU.add), R=[mre, tmp], W=[mre])
                        yield
                        tmp2 = Z[5]
                        do("dve", lambda e: e.tensor_tensor(f3(mim[:]), f3(bim[:]), cb, ALU.mult), R=[bim, cosT], W=[mim])
                        yield
                        do("dve", lambda e: e.tensor_tensor(f3(tmp2[:]), f3(bre[:]), sb_, ALU.mult), R=[bre, sinT], W=[tmp2])
                        do("dve", lambda e: e.tensor_tensor(mim[:], mim[:], tmp2[:], ALU.subtract), R=[mim, tmp2], W=[mim])
                        yield
                        rb = rho[:, jp:jp + 1].to_broadcast([128, FR])
                        cL, sL = cosT[:, jp, FR - 1:FR], sinT[:, jp, FR - 1:FR]
                        for fi in range(NF):
                            fs = slice(fi * FR, (fi + 1) * FR)
                            last = slice(fi * FR + FR - 1, fi * FR + FR)
                            do("dve", lambda e: e.tensor_tensor_scan(bre[:, fs], rb, mre[:, fs], s5st[:, jp, 0:1], ALU.mult, ALU.add), R=[rho, mre, s5st], W=[bre])
                            do("dve", lambda e: e.tensor_tensor_scan(bim[:, fs], rb, mim[:, fs], s5st[:, jp, 1:2], ALU.mult, ALU.add), R=[rho, mim, s5st], W=[bim])
                            yield
                            do("dve", lambda e: e.tensor_scalar(s5c[:, 0:1], bim[:, last], sL, None, ALU.mult), R=[bim, sinT], W=[s5c])
                            do("dve", lambda e: e.tensor_scalar(s5c[:, 1:2], bre[:, last], sL, None, ALU.mult), R=[bre, sinT], W=[s5c])
                            do("dve", lambda e: e.scalar_tensor_tensor(s5st[:, jp, 0:1], bre[:, last], cL, s5c[:, 0:1], ALU.mult, ALU.subtract), R=[bre, cosT, s5c], W=[s5st])
                            do("dve", lambda e: e.scalar_tensor_tensor(s5st[:, jp, 1:2], bim[:, last], cL, s5c[:, 1:2], ALU.mult, ALU.add), R=[bim, cosT, s5c], W=[s5st])
                            yield
                        do("dve", lambda e: e.tensor_tensor(f3(mre[:]), f3(bre[:]), cb, ALU.mult), R=[bre, cosT], W=[mre])
                        yield
                        do("dve", lambda e: e.tensor_tensor(f3(tmp[:]), f3(bim[:]), sb_, ALU.mult), R=[bim, sinT], W=[tmp])
                        do("dve", lambda e: e.tensor_tensor(xrb[:], mre[:], tmp[:], ALU.subtract), R=[mre, tmp], W=[xrb])
                        yield
                        do("dve", lambda e: e.tensor_tensor(f3(mim[:]), f3(bre[:]), sb_, ALU.mult), R=[bre, sinT], W=[mim])
                        yield
                        do("dve", lambda e: e.tensor_tensor(f3(tmp2[:]), f3(bim[:]), cb, ALU.mult), R=[bim, cosT], W=[tmp2])
                        do("dve", lambda e: e.tensor_tensor(xib[:], mim[:], tmp2[:], ALU.add), R=[mim, tmp2], W=[xib])
                        yield
                        ct = jp // 4
                        do("pe", lambda e: e.matmul(py[ct][:], s5_lc[:, jp, 0, :], xrb[:], start=(jp % 4 == 0), stop=False), R=[s5_lc, xrb], W=[py[ct]])
                        do("pe", lambda e: e.matmul(py[ct][:], s5_lc[:, jp, 1, :], xib[:], start=False, stop=(jp % 4 == 3)), R=[s5_lc, xib], W=[py[ct]])
                        if jp % 4 == 3:
                            do("dve", lambda e: e.scalar_tensor_tensor(uT[:, ct, :], uT[:, ct, :], s5_d[:, ct:ct + 1], py[ct][:], ALU.mult, ALU.add),
                               R=[uT, s5_d, py[ct]], W=[uT])
                            do("act", lambda e: e.activation(mixD[:, ct, :], uT[:, ct, :], AF.Gelu_apprx_tanh), R=[uT], W=[mixD])
                        yield

                s5g_ = s5_gen()

                def pump(n=1):
                    if os.environ.get("NOPUMP"):
                        return
                    for _ in range(n):
                        if next(s5g_, "done") == "done":
                            break

                def ev_lora(pb):
                    do("act", lambda e: e.copy(PL[:], pb[:]), R=[pb], W=[PL])
                proj128(dr["w_in128"][l, 0], hT, hfn, 8, w128, ev_lora)
                wk = work.next()
                do("dve", lambda e: e.tensor_tensor(wk[:, 1:TS], PL[:, 0:TS - 1], PL[:, 1:TS], ALU.subtract), R=[PL], W=[wk])
                do("dve", lambda e: e.tensor_tensor(wk[:, 0:1], carryL[:], PL[:, 0:1], ALU.subtract), R=[PL, carryL], W=[wk])
                do("dve", lambda e: e.tensor_copy(carryL[:], PL[:, TS - 1:TS]), R=[PL, wk], W=[carryL])
                do("dve", lambda e: e.scalar_tensor_tensor(PL[:], wk[:], mu_lora[:, 0:1], PL[:], ALU.mult, ALU.add), R=[wk, mu_lora, PL], W=[PL])
                do("act", lambda e: e.activation(LA[0:32, :], PL[0:32, :], AF.Tanh), R=[PL], W=[LA])
                do("act", lambda e: e.copy(LA[32:64, :], PL[32:64, :]), R=[PL], W=[LA])
                do("act", lambda e: e.activation(LA[64:128, :], PL[64:128, :], AF.Sigmoid), R=[PL], W=[LA])

                def rkv_proj(q, h, dst_buf, dst_ap):
                    wb = wload(w64.next(), dr["w_in64"][l, q * 4 + h])
                    pb = psum()
                    for kt in range(8):
                        do("pe", lambda e: e.matmul(pb[0:64, :], wb[:, kt, :], hT[:, kt, :], start=(kt == 0), stop=(kt == 7)), R=[wb, hT], W=[pb])
                    do("act", lambda e: e.copy(dst_ap, pb[0:64, :]), R=[pb], W=[dst_buf])

                def tshift(Qb, Qap, cc):
                    dq = H["d"]
                    do("dve", lambda e: e.tensor_tensor(dq[:, 1:TS], Qap[:, 0:TS - 1], Qap[:, 1:TS], ALU.subtract), R=[Qb], W=[dq])
                    do("dve", lambda e: e.tensor_tensor(dq[:, 0:1], carry[:, cc:cc + 1], Qap[:, 0:1], ALU.subtract), R=[Qb, carry], W=[dq])
                    do("dve", lambda e: e.tensor_copy(carry[:, cc:cc + 1], Qap[:, TS - 1:TS]), R=[Qb, dq], W=[carry])
                    do("dve", lambda e: e.scalar_tensor_tensor(Qap, dq[:], mu_rkv[:, cc:cc + 1], Qap, ALU.mult, ALU.add), R=[dq, mu_rkv, Qb], W=[Qb])

                for h in range(4):
                    rkv_proj(2, h, Vall, Vall[:, h, :])
                    tshift(Vall, Vall[:, h, :], 8 + h)
                if l == 0:
                    dma(vf_d[:, :, t0:t0 + TS], Vall[:], R=[Vall], W=[B_vf[si]])
                else:
                    pv = psum()
                    for h in range(4):
                        do("pe", lambda e: e.matmul(pv[0:32, :], v1[:, h, :], Vall[:, h, :], start=(h == 0), stop=(h == 3)), R=[v1, Vall], W=[pv])
                    do("act", lambda e: e.copy(VV[:], pv[0:32, :]), R=[pv], W=[VV])
                    for h in range(4):
                        pg = psum()
                        do("pe", lambda e: e.matmul(pg[0:64, :], v2[:, h * 64:(h + 1) * 64], VV[:], start=True, stop=True), R=[v2, VV], W=[pg])
                        wk = work.next()
                        do("act", lambda e: e.activation(wk[0:64, :], pg[0:64, :], AF.Sigmoid, bias=v0[:, h:h + 1]), R=[pg, v0], W=[wk])
                        wk2 = work.next()
                        dma(wk2[0:64, :], vf_d[:, h, t0:t0 + TS], R=[B_vf[si]], W=[wk2])
                        do("dve", lambda e: e.tensor_tensor(wk2[0:64, :], wk2[0:64, :], Vall[:, h, :], ALU.subtract), R=[wk2, Vall], W=[wk2])
                        do("dve", lambda e: e.tensor_tensor(wk2[0:64, :], wk2[0:64, :], wk[0:64, :], ALU.mult), R=[wk2, wk], W=[wk2])
                        do("dve", lambda e: e.tensor_tensor(Vall[:, h, :], Vall[:, h, :], wk2[0:64, :], ALU.add), R=[Vall, wk2], W=[Vall])

                for h in range(4):
                    Hr, Hk, Hd, He, Hc, Hw, Hm, Hp, Ha, Hg, Hq = (H[k] for k in "rkdecwmpagq")
                    rkv_proj(0, h, Hr, Hr[:])
                    rkv_proj(1, h, Hk, Hk[:])
                    tshift(Hr, Hr[:], h)
                    tshift(Hk, Hk[:], 4 + h)
                    cs_ = slice(h * 64, (h + 1) * 64)
                    pw = psum()
                    do("pe", lambda e: e.matmul(pw[0:64, :], lora_up[0:32, cs_], LA[0:32, :], start=True, stop=True), R=[lora_up, LA], W=[pw])
                    do("act", lambda e: e.activation(He[:], pw[0:64, :], AF.Sigmoid, bias=hpar[:, 0, h:h + 1]), R=[pw, hpar], W=[He])
                    pa = psum()
                    do("pe", lambda e: e.matmul(pa[0:64, :], lora_up[32:64, cs_], LA[32:64, :], start=True, stop=True), R=[lora_up, LA], W=[pa])
                    do("act", lambda e: e.activation(Ha[:], pa[0:64, :], AF.Sigmoid, bias=hpar[:, 1, h:h + 1]), R=[pa, hpar], W=[Ha])
                    pg = psum()
                    do("pe", lambda e: e.matmul(pg[0:64, :], lora_up[64:128, cs_], LA[64:128, :], start=True, stop=True), R=[lora_up, LA], W=[pg])
                    do("act", lambda e: e.copy(Hg[:], pg[0:64, :]), R=[pg], W=[Hg])
                    do("dve", lambda e: e.tensor_tensor_scan(Hc[:], reset[:], He[:], 0.0, ALU.mult, ALU.add), R=[reset, He], W=[Hc])
                    do("act", lambda e: e.activation(Hw[:], Hc[:], AF.Exp, scale=-C0), R=[Hc], W=[Hw])
                    do("act", lambda e: e.activation(Hm[:], Hc[:], AF.Exp, scale=C0), R=[Hc], W=[Hm])
                    do("dve", lambda e: e.tensor_tensor(Hd[:], Hc[:], He[:], ALU.subtract), R=[Hc, He], W=[Hd])
                    do("act", lambda e: e.activation(Hp[:], Hd[:], AF.Exp, scale=-C0), R=[Hd], W=[Hp])
                    do("dve", lambda e: e.tensor_scalar(Hq[:], Hk[:], hpar[:, 2, h:h + 1], None, ALU.mult), R=[Hk, hpar], W=[Hq])
                    do("dve", lambda e: e.tensor_tensor(Hd[:], Hq[:], Hq[:], ALU.mult), R=[Hq], W=[Hd])
                    pn = psum()
                    do("pe", lambda e: e.matmul(pn[0:64, :], ones[0:64, 0:64], Hd[:], start=True, stop=True), R=[ones, Hd], W=[pn])
                    do("dve", lambda e: e.tensor_scalar(Hc[:], pn[0:64, :], 1e-24, None, ALU.max), R=[pn], W=[Hc])
                    do("act", lambda e: e.activation(Hc[:], Hc[:], AF.Sqrt), R=[Hc], W=[Hc])
                    do("dve", lambda e: e.reciprocal(Hc[:], Hc[:]), R=[Hc], W=[Hc])
                    do("dve", lambda e: e.tensor_tensor(Hq[:], Hq[:], Hc[:], ALU.mult), R=[Hq, Hc], W=[Hq])
                    do("dve", lambda e: e.tensor_scalar(Hd[:], Ha[:], hpar[:, 3, h:h + 1], omka[:, h:h + 1], ALU.mult, ALU.add), R=[Ha, hpar, omka], W=[Hd])
                    do("dve", lambda e: e.tensor_tensor(Hk[:], Hk[:], Hd[:], ALU.mult), R=[Hk, Hd], W=[Hk])
                    do("dve", lambda e: e.scalar_tensor_tensor(Atb[:], Hq[:], -1.0, Hp[:], ALU.mult, ALU.mult), R=[Hq, Hp], W=[Atb])
                    do("dve", lambda e: e.tensor_tensor(Ha[:], Hq[:], Ha[:], ALU.mult), R=[Hq, Ha], W=[Ha])
                    do("dve", lambda e: e.tensor_tensor(Btb[:], Ha[:], Hm[:], ALU.mult), R=[Ha, Hm], W=[Btb])
                    do("dve", lambda e: e.tensor_tensor(Ktb[:], Hk[:], Hm[:], ALU.mult), R=[Hk, Hm], W=[Ktb])
                    do("act", lambda e: e.copy(Vb[:], Vall[:, h, :]), R=[Vall], W=[Vb])
                    do("dve", lambda e: e.scalar_tensor_tensor(Hd[:], Hr[:], hpar[:, 6, h:h + 1], Hk[:], ALU.mult, ALU.mult), R=[Hr, hpar, Hk], W=[Hd])
                    pn = psum()
                    do("pe", lambda e: e.matmul(pn[0:64, :], ones[0:64, 0:64], Hd[:], start=True, stop=True), R=[ones, Hd], W=[pn])
                    do("dve", lambda e: e.tensor_tensor(Hq[:], pn[0:64, :], Vall[:, h, :], ALU.mult), R=[pn, Vall], W=[Hq])
                    do("dve", lambda e: e.tensor_tensor(Rtb[:], Hr[:], Hw[:], ALU.mult), R=[Hr, Hw], W=[Rtb])
                    At, Bt, Kt, Rt = Atb, Btb, Ktb, Rtb
                    PO = PS[7]
                    do("act", lambda e: e.copy(STb[:], STt[:, h, :]), R=[STt], W=[STb])
                    for bt in range(NCH // NB):
                        for q in range(NB):
                            c = bt * NB + q
                            cs2 = slice(c * C, (c + 1) * C)
                            pb = psum()
                            a_, b_, k_, r_ = At[:, cs2], Bt[:, cs2], Kt[:, cs2], Rt[:, cs2]
                            do("pe", lambda e: e.matmul(pb[0:64, 0:64], b_, a_, start=True, stop=True), R=[Bt, At], W=[pb])
                            do("pe", lambda e: e.matmul(pb[0:64, 64:128], a_, b_, start=True, stop=True), R=[At, Bt], W=[pb])
                            do("pe", lambda e: e.matmul(pb[0:64, 128:192], b_, r_, start=True, stop=True), R=[Bt, Rt], W=[pb])
                            do("pe", lambda e: e.matmul(pb[0:64, 192:256], a_, k_, start=True, stop=True), R=[At, Kt], W=[pb])
                            do("pe", lambda e: e.matmul(pb[0:64, 256:320], k_, r_, start=True, stop=True), R=[Kt, Rt], W=[pb])
                            do("dve", lambda e: e.tensor_tensor(SCx[:, q, :], pb[0:64, 0:128], msk5[:, 0:128], ALU.mult), R=[pb, msk5], W=[SCx])
                            do("dve", lambda e: e.tensor_tensor(SCb[:, q, :], pb[0:64, 128:320], msk5[:, 128:320], ALU.mult), R=[pb, msk5], W=[SCb])
                            pt = psum()
                            v_ = Vb[:, cs2]
                            for qi, (sb_, s_) in enumerate(((At, a_), (Bt, b_), (Kt, k_), (Vb, v_))):
                                do("pe", lambda e: e.matmul(pt[0:64, qi * 64:(qi + 1) * 64], s_, ident_bf[0:64, 0:64], start=True, stop=True),
                                   R=[sb_, ident_bf], W=[pt])
                            do("act", lambda e: e.copy(TMb[:, q, :], pt[0:64, 0:256]), R=[pt], W=[TMb])
                            pump()
                        do("dve", lambda e: e.tensor_tensor(TT[:], SCx[:, :, 0:64], ident[0:64, 0:64].unsqueeze(1).to_broadcast([64, NB, 64]), ALU.add),
                           R=[SCx, ident], W=[TT])
                        Xb, Xf = SCx, (lambda q: SCx[:, q, 0:64])
                        Yb, Yf = SCx, (lambda q: SCx[:, q, 64:128])
                        cur = 0
                        for lev in range(1, 6):
                            Xn_, Yn_ = XS[cur], YS[cur]
                            if lev < 5:
                                pb = PS[4]
                                for q in range(NB):
                                    do("pe", lambda e: e.matmul(pb[0:64, q * 64:(q + 1) * 64], Yf(q), Xf(q), start=True, stop=True), R=[Yb, Xb], W=[pb])
                                do("act", lambda e: e.copy(Xn_[:], pb[0:64, 0:NB * 64].rearrange("p (q t) -> p q t", q=NB)), R=[pb], W=[Xn_])
                                pump()
                            pb = PS[5]
                            for q in range(NB):
                                do("pe", lambda e: e.matmul(pb[0:64, q * 64:(q + 1) * 64], Xf(q), Yf(q), start=True, stop=True), R=[Xb, Yb], W=[pb])
                            do("dve", lambda e: e.tensor_copy(Yn_[:], pb[0:64, 0:NB * 64].rearrange("p (q t) -> p q t", q=NB)), R=[pb], W=[Yn_])
                            pump()
                            pb = PS[6]
                            for q in range(NB):
                                do("pe", lambda e: e.matmul(pb[0:64, q * 64:(q + 1) * 64], Yn_[:, q, :], TT[:, q, :], start=True, stop=True), R=[Yn_, TT], W=[pb])
                            if lev < 5:
                                do("dve", lambda e: e.tensor_tensor(TT[:], TT[:], pb[0:64, 0:NB * 64].rearrange("p (q t) -> p q t", q=NB), ALU.add),
                                   R=[pb, TT], W=[TT])
                            else:
                                do("dve", lambda e: e.tensor_tensor(TTb[:], TT[:], pb[0:64, 0:NB * 64].rearrange("p (q t) -> p q t", q=NB), ALU.add),
                                   R=[pb, TT], W=[TTb])
                            Xb, Yb = Xn_, Yn_
                            pump()
                            Xf = (lambda q, Xn_=Xn_: Xn_[:, q, :])
                            Yf = (lambda q, Yn_=Yn_: Yn_[:, q, :])
                            cur = 1 - cur
                        pb = PS[4]
                        for q in range(NB):
                            do("pe", lambda e: e.matmul(pb[0:64, q * 128:q * 128 + 64], TMb[:, q, 0:64], TTb[:, q, :], start=True, stop=True), R=[TMb, TTb], W=[pb])
                            do("pe", lambda e: e.matmul(pb[0:64, q * 128 + 64:q * 128 + 128], SCb[:, q, 64:128], TTb[:, q, :], start=True, stop=True), R=[SCb, TTb], W=[pb])
                        do("act", lambda e: e.copy(APMb[:], pb[0:64, 0:NB * 128].rearrange("p (q t) -> p q t", q=NB)), R=[pb], W=[APMb])
                        pump()
                        for q in range(NB):
                            c = bt * NB + q
                            cs2 = slice(c * C, (c + 1) * C)
                            pu = PS[5]
                            do("pe", lambda e: e.matmul(pu[0:64, 0:64], APMb[:, q, 64:128], TMb[:, q, 192:256], start=True, stop=False), R=[APMb, TMb], W=[pu])
                            do("pe", lambda e: e.matmul(pu[0:64, 0:64], APMb[:, q, 0:64], STb[:], start=False, stop=True), R=[APMb, STb], W=[pu])
                            do("act", lambda e: e.copy(UTb[:], pu[0:64, 0:64]), R=[pu], W=[UTb])
                            pump()
                            do("pe", lambda e: e.matmul(PO[0:64, cs2], STb[:], Rt[:, cs2], start=True, stop=False), R=[STb, Rt], W=[PO])
                            do("pe", lambda e: e.matmul(PO[0:64, cs2], TMb[:, q, 192:256], SCb[:, q, 128:192], start=False, stop=False), R=[TMb, SCb], W=[PO])
                            do("pe", lambda e: e.matmul(PO[0:64, cs2], UTb[:], SCb[:, q, 0:64], start=False, stop=True), R=[UTb, SCb], W=[PO])
                            pn = PS[6]
                            do("pe", lambda e: e.matmul(pn[0:64, 0:64], TMb[:, q, 64:128], UTb[:], start=True, stop=False), R=[TMb, UTb], W=[pn])
                            do("pe", lambda e: e.matmul(pn[0:64, 0:64], TMb[:, q, 128:192], TMb[:, q, 192:256], start=False, stop=True), R=[TMb], W=[pn])
                            wc = Hw[:, c * C + C - 1:c * C + C]
                            do("act", lambda e: e.activation(STW[:], STt[:, h, :], AF.Copy, scale=wc), R=[STt, Hw], W=[STW])
                            do("dve", lambda e: e.scalar_tensor_tensor(STb[:], pn[0:64, 0:64], wc, STW[:], ALU.mult, ALU.add), R=[pn, Hw, STW], W=[STb])
                            do("dve", lambda e: e.scalar_tensor_tensor(STt[:, h, :], pn[0:64, 0:64], wc, STW[:], ALU.mult, ALU.add), R=[pn, Hw, STW], W=[STt])
                            pump()
                    OS = Hm
                    do("act", lambda e: e.copy(OS[:], PO[0:64, :]), R=[PO], W=[OS])
                    do("dve", lambda e: e.tensor_tensor(Hd[:], OS[:], OS[:], ALU.mult), R=[OS], W=[Hd])
                    pm1 = psum()
                    do("pe", lambda e: e.matmul(pm1[0:64, :], ones[0:64, 0:64], OS[:], start=True, stop=True), R=[ones, OS], W=[pm1])
                    pm2 = psum()
                    do("pe", lambda e: e.matmul(pm2[0:64, :], ones[0:64, 0:64], Hd[:], start=True, stop=True), R=[ones, Hd], W=[pm2])
                    mu_ = work.next()
                    do("act", lambda e: e.activation(mu_[0:64, :], pm1[0:64, :], AF.Copy, scale=1.0 / 64), R=[pm1], W=[mu_])
                    var = work.next()
                    do("dve", lambda e: e.tensor_tensor(var[0:64, :], mu_[0:64, :], mu_[0:64, :], ALU.mult), R=[mu_], W=[var])
                    do("dve", lambda e: e.scalar_tensor_tensor(var[0:64, :], pm2[0:64, :], 1.0 / 64, var[0:64, :], ALU.mult, ALU.subtract), R=[pm2, var], W=[var])
                    do("dve", lambda e: e.tensor_scalar(var[0:64, :], var[0:64, :], 64e-5, None, ALU.add), R=[var], W=[var])
                    do("act", lambda e: e.activation(var[0:64, :], var[0:64, :], AF.Sqrt), R=[var], W=[var])
                    do("dve", lambda e: e.reciprocal(var[0:64, :], var[0:64, :]), R=[var], W=[var])
                    do("dve", lambda e: e.tensor_tensor(Hc[:], OS[:], mu_[0:64, :], ALU.subtract), R=[OS, mu_], W=[Hc])
                    do("dve", lambda e: e.tensor_tensor(Hc[:], Hc[:], var[0:64, :], ALU.mult), R=[Hc, var], W=[Hc])
                    do("dve", lambda e: e.tensor_scalar(Hc[:], Hc[:], hpar[:, 4, h:h + 1], hpar[:, 5, h:h + 1], ALU.mult, ALU.add), R=[Hc, hpar], W=[Hc])
                    do("dve", lambda e: e.tensor_tensor(Hc[:], Hc[:], Hq[:], ALU.add), R=[Hc, Hq], W=[Hc])
                    do("dve", lambda e: e.tensor_tensor(mixA[:, h, :], Hc[:], Hg[:], ALU.mult), R=[Hc, Hg], W=[mixA])

                for _ in s5g_:
                    pass

                for mt in range(4):
                    def ev_gelu(pb, mt=mt):
                        do("act", lambda e: e.activation(Z[mt][:], pb[:], AF.Gelu_apprx_tanh), R=[pb], W=[Z[mt]])
                    proj128(dr["w_in128"][l, 1 + mt], hT, hfn, 8, w128, ev_gelu)
                pm1 = psum()
                for i in range(2):
                    do("pe", lambda e: e.matmul(pm1[:], ones[:], Z[2 + i][:], start=(i == 0), stop=(i == 1)), R=[ones, Z[2 + i]], W=[pm1])
                pm2 = psum()
                for i in range(2):
                    sq_ = (PL, LA)[i]
                    do("dve", lambda e: e.tensor_tensor(sq_[:], Z[2 + i][:], Z[2 + i][:], ALU.mult), R=[Z[2 + i]], W=[sq_])
                    do("pe", lambda e: e.matmul(pm2[:], ones[:], sq_[:], start=(i == 0), stop=(i == 1)), R=[ones, sq_], W=[pm2])
                do("act", lambda e: e.activation(LA[:], pm1[:], AF.Copy, scale=1.0 / 256), R=[pm1], W=[LA])
                do("dve", lambda e: e.tensor_tensor(PL[:], LA[:], LA[:], ALU.mult), R=[LA], W=[PL])
                do("dve", lambda e: e.scalar_tensor_tensor(PL[:], pm2[:], 1.0 / 256, PL[:], ALU.mult, ALU.subtract), R=[pm2, PL], W=[PL])
                do("dve", lambda e: e.tensor_scalar(PL[:], PL[:], 1e-5, None, ALU.add), R=[PL], W=[PL])
                do("act", lambda e: e.activation(PL[:], PL[:], AF.Sqrt), R=[PL], W=[PL])
                do("dve", lambda e: e.reciprocal(PL[:], PL[:]), R=[PL], W=[PL])
                for ct in range(2):
                    zv = Z[2 + ct]
                    do("dve", lambda e: e.tensor_tensor(zv[:], zv[:], LA[:], ALU.subtract), R=[zv, LA], W=[zv])
                    do("dve", lambda e: e.tensor_tensor(zv[:], zv[:], PL[:], ALU.mult), R=[zv, PL], W=[zv])
                    do("dve", lambda e: e.tensor_scalar(zv[:], zv[:], sg_ln[:, ct:ct + 1], sg_ln[:, 2 + ct:3 + ct], ALU.mult, ALU.add), R=[zv, sg_ln], W=[zv])
                for ck in range(TS // 128):
                    ks = slice(ck * 128, (ck + 1) * 128)
                    for ct in range(2):
                        pt = psum()
                        do("pe", lambda e: e.matmul(pt[:, 0:128], Z[2 + ct][:, ks], ident[:], start=True, stop=True), R=[Z[2 + ct], ident], W=[pt])
                        do("act", lambda e: e.copy(vz[ct * 2][:, 0:64], pt[:, 0:64]), R=[pt], W=[vz[ct * 2]])
                        do("dve", lambda e: e.tensor_copy(vz[ct * 2 + 1][:, 64:128], pt[:, 64:128]), R=[pt], W=[vz[ct * 2 + 1]])
                        pmx = psum()
                        for par in range(2):
                            g = ct * 2 + par
                            do("pe", lambda e: e.matmul(pmx[:, 0:128], vz[ct * 2 + par][:], sg_wb[:, g, :], start=(par == 0), stop=(par == 1)),
                               R=[vz[ct * 2 + par], sg_wb], W=[pmx])
                        wk = work.next()
                        do("dve", lambda e: e.tensor_tensor(wk[:, 0:128], pmx[:, 0:128], sg_bs[:, ct, :], ALU.add), R=[pmx, sg_bs], W=[wk])
                        do("dve", lambda e: e.tensor_tensor(mixB[:, ct, ks], wk[:, 0:128], Z[ct][:, ks], ALU.mult), R=[wk, Z[ct]], W=[mixB])

                for mt in range(6):
                    def ev_c(pb, mt=mt):
                        do("act", lambda e: e.copy(Z[mt][:], pb[:]), R=[pb], W=[Z[mt]])
                    proj128(dr["w_in128"][l, 5 + mt], hT, hfn, 8, w128, ev_c)
                for ct in range(2):
                    do("dve", lambda e: e.tensor_tensor(zc[:, ct, 2:TS + 2], Z[2 + ct][:], Z[4 + ct][:], ALU.mult), R=[Z[2 + ct], Z[4 + ct]], W=[zc])
                    y = Z[2 + ct]
                    do("dve", lambda e: e.tensor_scalar(y[:], zc[:, ct, 0:TS], conv_w[:, ct, 0:1], None, ALU.mult), R=[zc, conv_w], W=[y])
                    do("dve", lambda e: e.scalar_tensor_tensor(y[:], zc[:, ct, 1:TS + 1], conv_w[:, ct, 1:2], y[:], ALU.mult, ALU.add), R=[zc, conv_w, y], W=[y])
                    do("dve", lambda e: e.scalar_tensor_tensor(y[:], zc[:, ct, 2:TS + 2], conv_w[:, ct, 2:3], y[:], ALU.mult, ALU.add), R=[zc, conv_w, y], W=[y])
                    do("dve", lambda e: e.tensor_tensor(mixC[:, ct, :], y[:], Z[ct][:], ALU.mult), R=[y, Z[ct]], W=[mixC])
                do("dve", lambda e: e.tensor_copy(zc[:, :, 0:2], zc[:, :, TS:TS + 2]), R=[zc], W=[zc])


                merged = aT
                for ft in range(8):
                    fsl = slice(ft * 128, (ft + 1) * 128)
                    acc = Z[5]
                    gts = [Z[0], Z[1], Z[2], Z[3]]
                    for br in range(4):
                        def ev_g(pb, br=br):
                            do("act", lambda e: e.activation(gts[br][:], pb[:], AF.Sigmoid), R=[pb], W=[gts[br]])
                        proj128(dr["w_in128"][l, 13 + br * 8 + ft], hT, hfn, 8, w128, ev_g)
                    wq = [wload(wsm.next(), dr["sg_out"][l][:, :, fsl]), wload(wsm.next(), dr["conv_out"][l][:, :, fsl]),
                          wload(wsm.next(), dr["glu_w"][l][:, :, fsl]),
                          wload(wsm.next(), dr["glu_w"][l][:, :, 1024 + ft * 128:1024 + (ft + 1) * 128])]
                    wr_ = wload(wra.next(), dr["rwkv_out"][l][:, :, fsl])
                    pa = psum()
                    for h in range(4):
                        do("pe", lambda e: e.matmul(pa[:], wr_[:, h, :], mixA[:, h, :], start=(h == 0), stop=(h == 3)), R=[wr_, mixA], W=[pa])
                    do("dve", lambda e: e.tensor_tensor(acc[:], pa[:], gts[0][:], ALU.mult), R=[pa, gts[0]], W=[acc])
                    for bi, mixT in ((1, mixB), (2, mixC)):
                        pb_ = psum()
                        for kt in range(2):
                            do("pe", lambda e: e.matmul(pb_[:], wq[bi - 1][:, kt, :], mixT[:, kt, :], start=(kt == 0), stop=(kt == 1)), R=[wq[bi - 1], mixT], W=[pb_])
                        do("dve", lambda e: e.tensor_tensor(gts[bi][:], pb_[:], gts[bi][:], ALU.mult), R=[pb_, gts[bi]], W=[gts[bi]])
                        do("dve", lambda e: e.tensor_tensor(acc[:], acc[:], gts[bi][:], ALU.add), R=[acc, gts[bi]], W=[acc])
                    ph1 = psum()
                    for kt in range(2):
                        do("pe", lambda e: e.matmul(ph1[:], wq[2][:, kt, :], mixD[:, kt, :], start=(kt == 0), stop=(kt == 1)), R=[wq[2], mixD], W=[ph1])
                    ph2 = psum()
                    for kt in range(2):
                        do("pe", lambda e: e.matmul(ph2[:], wq[3][:, kt, :], mixD[:, kt, :], start=(kt == 0), stop=(kt == 1)), R=[wq[3], mixD], W=[ph2])
                    do("act", lambda e: e.activation(xn[:], ph2[:], AF.Sigmoid), R=[ph2], W=[xn])
                    do("dve", lambda e: e.tensor_tensor(xn[:], ph1[:], xn[:], ALU.mult), R=[ph1, xn], W=[xn])
                    do("dve", lambda e: e.tensor_tensor(xn[:], xn[:], gts[3][:], ALU.mult), R=[xn, gts[3]], W=[xn])
                    do("dve", lambda e: e.tensor_tensor(merged[:, ft, :], acc[:], xn[:], ALU.add), R=[acc, xn], W=[merged])

                for mt in range(8):
                    def ev_o(pb, mt=mt):
                        do("dve", lambda e: e.scalar_tensor_tensor(X[:, mt, :], pb[:], modT[:, 16 + mt:17 + mt], X[:, mt, :], ALU.mult, ALU.add),
                           R=[pb, modT, X], W=[X])
                    proj128(dr["w_o"][l, mt], merged, lambda kt: merged[:, kt, :], 8, w128, ev_o)

                rmsnorm_to(hT, geff2, 24)
                for half in range(2):
                    for i in range(16):
                        def ev_f(pb, i=i):
                            wk = work.next()
                            do("act", lambda e: e.activation(wk[:], pb[:], AF.Relu), R=[pb], W=[wk])
                            do("dve", lambda e: e.tensor_tensor(aT[:, i, :], wk[:], wk[:], ALU.mult), R=[wk], W=[aT])
                        proj128(dr["ffn_w1"][l, half * 16 + i], hT, hfn, 8, w128, ev_f)
                    for mt in range(8):
                        def ev_2(pb, mt=mt):
                            do("dve", lambda e: e.scalar_tensor_tensor(X[:, mt, :], pb[:], modT[:, 40 + mt:41 + mt], X[:, mt, :], ALU.mult, ALU.add),
                               R=[pb, modT, X], W=[X])
                        proj128(dr["ffn_w2"][l, mt][:, half * 16:(half + 1) * 16, :], aT, lambda kt: aT[:, kt, :], 16, wf2, ev_2)

                if l == 0:
                    dma(xs_d[:, t0:t0 + TS].rearrange("(ft p) t -> p ft t", p=128), X[:], R=[X], W=[B_xs[si]])
                else:
                    rms_stats()
                    for ft in range(8):
                        do("dve", lambda e: e.scalar_tensor_tensor(X[:, ft, :], X[:, ft, :], g_fin[:, ft:ft + 1], rstd[:], ALU.mult, ALU.mult),
                           R=[X, g_fin, rstd], W=[X])
                    dma(outT[:, t0:t0 + TS].rearrange("(ft p) t -> p ft t", p=128), X[:], R=[X], W=[B_outs[si]])
        S.finish(B_outs)
        print("instructions:", S.n_inst, "waits:", S.n_wait)
    return nc


_CACHE = {}


def run(inputs, T):
    maps = [prep_inputs(inputs, b) for b in range(4)]
    shapes = {k: v.shape for k, v in maps[0].items()}
    key = (T,)
    if key not in _CACHE:
        _CACHE[key] = build(shapes, T)
    nc = _CACHE[key]
    in_maps = [maps[i % 4] for i in range(8)]
    res = run_bass_kernel_spmd(nc, in_maps, core_ids=list(range(8)))
    out = np.stack([np.ascontiguousarray(res.results[b]["outT"].T) for b in range(4)])
    return out.astype(np.float32)


def kernel(**inputs):
    inputs = {k: np.asarray(v, dtype=np.float32) for k, v in inputs.items()}
    T = inputs["x"].shape[1]
    return run(inputs, T)
```

```python
import contextlib
import math
import os
import numpy as np
import concourse.bass as bass
import concourse.mybir as mybir
from concourse.bass_utils import run_bass_kernel_spmd

F32 = mybir.dt.float32
BF16 = mybir.dt.bfloat16
AF = mybir.ActivationFunctionType
ALU = mybir.AluOpType

D = 1024
DM = 256
TS = 512
C = 64
NCH = TS // C
FR = 128
EPS = 1e-6
C0 = math.exp(-0.5)
MAGIC = 12582912.0


class Buf:
    __slots__ = ("name", "t", "lastw", "readers", "psum")

    def __init__(self, name, t=None):
        self.name = name
        self.t = t
        self.lastw = None
        self.readers = []
        self.psum = False

    def __getitem__(self, idx):
        return self.t[idx]


class Sched:
    N_DMA_SEM = 10

    def __init__(self, nc, stack):
        self.nc = nc
        self.stack = stack
        self.engs = {"pe": nc.tensor, "act": nc.scalar, "dve": nc.vector, "pool": nc.gpsimd, "sp": nc.sync}
        self.sem = {}
        self.cnt = {}
        for k in ("pe", "act", "dve", "pool"):
            self.sem[k] = stack.enter_context(nc.semaphore("s_" + k))
            self.cnt[k] = 0
        self.dma_sems = {"sp": [], "pool": []}
        for qn in ("sp", "pool"):
            for i in range(self.N_DMA_SEM):
                key = "dma_%s%d" % (qn, i)
                self.sem[key] = stack.enter_context(nc.semaphore("s_" + key))
                self.cnt[key] = 0
                self.dma_sems[qn].append(key)
        self.dma_rr = {"sp": 0, "pool": 0}
        self.waited = {}
        self.n_inst = 0
        self.n_wait = 0
        self.sb_bytes = 0

    def sb(self, name, shape, dt):
        t = self.stack.enter_context(self.nc.sbuf_tensor("sb_" + name, list(shape), dt))
        n = 1
        for s in shape[1:]:
            n *= s
        self.sb_bytes += n * (2 if dt == BF16 else 4)
        return Buf(name, t)

    def ps(self, name, shape, dt=F32):
        t = self.stack.enter_context(self.nc.psum_tensor("pp_" + name, list(shape), dt))
        b = Buf(name, t)
        b.psum = True
        return b

    def _need(self, eng, deps):
        best = {}
        for d in deps:
            if d is None:
                continue
            sk, v, _ = d
            if v > best.get(sk, 0):
                best[sk] = v
        for sk, v in best.items():
            if self.waited.get((eng, sk), 0) >= v:
                continue
            self.engs[eng].wait_ge(self.sem[sk], v)
            self.waited[(eng, sk)] = v
            self.n_wait += 1

    def do(self, eng, fn, R=(), W=()):
        deps = []
        for b in R:
            lw = b.lastw
            if lw is not None and not (lw[2] == eng and eng == "pe"):
                deps.append(lw)
            if b.psum:
                for r in b.readers:
                    if r[2] != eng:
                        deps.append(r)
        for b in W:
            lw = b.lastw
            if lw is not None and (lw[2] != eng or eng != "pe"):
                deps.append(lw)
            for r in b.readers:
                if r[2] != eng or eng != "pe":
                    deps.append(r)
        self._need(eng, deps)
        ins = fn(self.engs[eng])
        self.cnt[eng] += 1
        v = self.cnt[eng]
        ins.then_inc(self.sem[eng], 1)
        tok = (eng, v, eng)
        for b in R:
            b.readers = [r for r in b.readers if r[0] != eng] + [tok]
        for b in W:
            b.lastw = tok
            b.readers = []
        self.n_inst += 1
        return ins

    def dma(self, out_ap, in_ap, R=(), W=(), q="sp"):
        sk = self.dma_sems[q][self.dma_rr[q]]
        self.dma_rr[q] = (self.dma_rr[q] + 1) % len(self.dma_sems[q])
        deps = []
        for b in R:
            if b.lastw is not None:
                deps.append(b.lastw)
        for b in W:
            if b.lastw is not None:
                deps.append(b.lastw)
            deps.extend(b.readers)
        if self.cnt[sk] > 0:
            deps.append((sk, self.cnt[sk], "dmaq"))
        self._need(q, deps)
        ins = self.engs[q].dma_start(out=out_ap, in_=in_ap)
        self.cnt[sk] += 16
        ins.then_inc(self.sem[sk], 16)
        tok = (sk, self.cnt[sk], "dmaq")
        for b in R:
            b.readers.append(tok)
        for b in W:
            b.lastw = tok
            b.readers = []
        self.n_inst += 1
        return ins

    def finish(self, bufs, eng="sp"):
        self._need(eng, [b.lastw for b in bufs if b.lastw is not None])


class Ring:
    def __init__(self, S, name, n, shape, dt, psum=False):
        self.bufs = [(S.ps if psum else S.sb)("%s%d" % (name, i), shape, dt) for i in range(n)]
        self.i = 0

    def next(self):
        b = self.bufs[self.i]
        self.i = (self.i + 1) % len(self.bufs)
        return b


def tile_km(W, msz):
    K, M = W.shape
    return np.ascontiguousarray(W.reshape(K // 128, 128, M // msz, msz).transpose(2, 1, 0, 3))


def col_pk(v, p=128):
    return np.ascontiguousarray(v.reshape(-1, p).T)


def prep_inputs(inp, b):
    f = np.float32
    L = 2
    m = {}
    m["xT"] = np.ascontiguousarray(inp["x"][b].T)
    m["cT"] = col_pk(inp["c"][b])
    m["ident"] = np.eye(128, dtype=f)
    m["ones"] = np.ones((128, 128), f)
    su = np.triu(np.ones((C, C), f), 1)
    sl = np.tril(np.ones((C, C), f), -1)
    iu = np.triu(np.ones((C, C), f), 0)
    m["msk5"] = np.ascontiguousarray(np.concatenate([su, sl, iu, sl, iu], axis=1))
    rs = np.ones((64, TS), f)
    rs[:, ::C] = 0.0
    m["reset"] = rs
    m["sgmask"] = np.triu(np.ones((128, 128), f), 0)
    m["iota"] = np.ascontiguousarray(np.broadcast_to(np.arange(FR, dtype=f) + 1.0, (128, FR)))
    sel = np.zeros((128, 2), f)
    sel[(np.arange(128) % 32) < 16, 0] = 1.0
    sel[(np.arange(128) % 32) >= 16, 1] = 1.0
    m["sel16"] = sel
    m["ada_w"] = np.ascontiguousarray(inp["ada_w"].reshape(L, 8, 128, 48, 128).transpose(0, 3, 2, 1, 4))
    m["ada_b"] = np.stack([col_pk(inp["ada_b"][l]) for l in range(L)])
    m["g_mix"] = np.stack([col_pk(inp["norm_mix_g"][l]) for l in range(L)])
    m["g_ffn"] = np.stack([col_pk(inp["norm_ffn_g"][l]) for l in range(L)])
    m["g_fin"] = col_pk(inp["final_g"])
    w_in = inp["w_in"]
    m["w_in64"] = np.stack([tile_km(w_in[l][:, :768], 64) for l in range(L)])
    m["w_in128"] = np.stack([tile_km(w_in[l][:, 768:], 128) for l in range(L)])
    mu = inp["rwkv_mu"]
    m["mu_rkv"] = np.stack([np.concatenate([col_pk(mu[l][q * 256:(q + 1) * 256], 64) for q in range(3)], axis=1)
                            for l in range(L)])
    m["mu_lora"] = np.stack([mu[l][768:896].reshape(128, 1) for l in range(L)])
    hp = lambda k: np.stack([col_pk(inp[k][l], 64) for l in range(inp[k].shape[0])])
    rk = inp["rwkv_rk"].reshape(L, 256)
    m["hpar"] = np.ascontiguousarray(np.stack(
        [hp("rwkv_w0"), hp("rwkv_a0"), hp("rwkv_kk"), hp("rwkv_ka"), hp("rwkv_lnx_w"), hp("rwkv_lnx_b"),
         np.stack([col_pk(rk[l], 64) for l in range(L)])], axis=2))
    m["v0"] = hp("rwkv_v0")
    m["lora_up"] = np.ascontiguousarray(np.concatenate([inp["rwkv_w2"], inp["rwkv_a2"], inp["rwkv_g2"]], axis=1))
    m["v1"] = np.ascontiguousarray(inp["rwkv_v1"].reshape(1, 4, 64, 32).transpose(0, 2, 1, 3))
    m["v2"] = np.ascontiguousarray(inp["rwkv_v2"])
    m["rwkv_out"] = np.ascontiguousarray(inp["rwkv_out"].reshape(L, 4, 64, 1024).transpose(0, 2, 1, 3))
    k2 = lambda k: np.ascontiguousarray(inp[k].reshape(L, 2, 128, -1).transpose(0, 2, 1, 3))
    m["sg_out"] = k2("sg_out")
    m["conv_out"] = k2("conv_out")
    m["glu_w"] = k2("s5_glu_w")
    m["sg_ln"] = np.stack([np.concatenate([col_pk(inp["sg_ln_w"][l]), col_pk(inp["sg_ln_b"][l])], axis=1) for l in range(L)])
    m["sg_wsT"] = np.ascontiguousarray(inp["sg_ws"].transpose(0, 3, 1, 2))
    bs = inp["sg_bs"]
    m["sg_bs"] = np.ascontiguousarray(np.stack(
        [np.stack([np.repeat(bs[l, 2 * ct:2 * ct + 2], 64, axis=0) for ct in range(2)], axis=1) for l in range(L)]))
    m["conv_w"] = np.ascontiguousarray(inp["conv_w"].reshape(L, 3, 2, 128).transpose(0, 3, 2, 1))
    pair = lambda a: np.ascontiguousarray(a.reshape(L, 8, 128).transpose(0, 2, 1))
    m["s5_lam"] = np.ascontiguousarray(np.stack(
        [pair(inp["s5_a_re"]), pair(inp["s5_a_im"]),
         pair(np.repeat(inp["s5_log_dt"][:, :, None], 64, axis=2))], axis=2))
    def bk(a):
        return np.ascontiguousarray(a.reshape(L, 2, 8, 64, 16).transpose(0, 2, 4, 1, 3).reshape(L, 128, 2, 64))
    def lk(a):
        a3 = a if a.ndim == 3 else np.repeat(a[:, :, None], 64, axis=2)
        r = np.repeat(a3.reshape(L, 2, 8, 1, 64), 16, axis=3)
        return np.ascontiguousarray(r.transpose(0, 2, 3, 1, 4).reshape(L, 128, 2, 64))
    m["s5_bk"] = np.ascontiguousarray(np.stack(
        [bk(inp["s5_b_re"]), bk(inp["s5_b_im"]), lk(inp["s5_a_re"]), lk(inp["s5_a_im"]), lk(inp["s5_log_dt"])], axis=2))
    lc = np.zeros((L, 8, 128, 2, 128), f)
    for jp in range(8):
        for gg in range(2):
            g = 2 * jp + gg
            cs = (jp % 4) * 32 + gg * 16
            lc[:, jp, gg * 64:(gg + 1) * 64, 0, cs:cs + 16] = inp["s5_c_re"][:, g].transpose(0, 2, 1)
            lc[:, jp, gg * 64:(gg + 1) * 64, 1, cs:cs + 16] = inp["s5_c_im"][:, g].transpose(0, 2, 1)
    m["s5_lc"] = np.ascontiguousarray(lc.transpose(0, 2, 1, 3, 4))
    m["s5_d"] = np.stack([col_pk(inp["s5_d"][l]) for l in range(L)])
    m["w_o"] = np.stack([tile_km(inp["w_o"][l], 128) for l in range(L)])
    m["ffn_w1"] = np.stack([tile_km(inp["ffn_w1"][l], 128) for l in range(L)])
    m["ffn_w2"] = np.stack([tile_km(inp["ffn_w2"][l], 128) for l in range(L)])
    return {k: np.ascontiguousarray(v, dtype=f) for k, v in m.items()}


def build(shapes, T):
    NS = T // TS
    NF = TS // FR
    nc = bass.Bass("TRN2", target_bir_lowering=False)
    dr = {}
    for k, shp in shapes.items():
        dr[k] = nc.dram_tensor(k, list(shp), F32, kind="ExternalInput").ap()
    outT = nc.dram_tensor("outT", [D, T], F32, kind="ExternalOutput").ap()
    xs_d = nc.dram_tensor("xs_scr", [D, T], F32, kind="Internal").ap()
    vf_d = nc.dram_tensor("vf_scr", [64, 4, T], F32, kind="Internal").ap()

    with contextlib.ExitStack() as st:
        S = Sched(nc, st)
        do, dma = S.do, S.dma
        B_outs = [Buf("outT%d" % i) for i in range(NS)]
        B_xs = [Buf("xs%d" % i) for i in range(NS)]
        B_vf = [Buf("vf%d" % i) for i in range(NS)]

        def cload(name, key, shape):
            b = S.sb(name, shape, F32)
            dma(b[:], dr[key], W=[b])
            return b
        ident = cload("ident", "ident", [128, 128])
        ones = cload("ones", "ones", [128, 128])
        msk5 = cload("msk5", "msk5", [64, 320])
        reset = cload("reset", "reset", [64, TS])
        sgmask = cload("sgmask", "sgmask", [128, 128])
        iota = cload("iota", "iota", [128, FR])
        sel16 = cload("sel16", "sel16", [128, 2])
        cT = cload("cT", "cT", [128, 8])
        g_fin = cload("g_fin", "g_fin", [128, 8])
        ones_bf = S.sb("ones_bf", [128, 128], BF16)
        do("dve", lambda e: e.tensor_copy(ones_bf[:], ones[:]), R=[ones], W=[ones_bf])
        csil = S.sb("csil", [128, 8], F32)
        do("act", lambda e: e.activation(csil[:], cT[:], AF.Silu), R=[cT], W=[csil])
        negpi = S.sb("negpi", [128, 1], F32)
        do("dve", lambda e: e.memset(negpi[:], -math.pi), W=[negpi])

        PS = [S.ps("ps%d" % i, [128, 512]) for i in range(8)]
        ps_rr = [0]

        def psum():
            b = PS[ps_rr[0] % 3]
            ps_rr[0] += 1
            return b

        X = S.sb("X", [128, 8, TS], F32)
        hT = S.sb("hT", [128, 8, TS], BF16)
        aT = S.sb("aT", [128, 8, TS], BF16)
        sqb = S.sb("sqb", [128, TS], BF16)
        rstd = S.sb("rstd", [128, TS], F32)
        xn = S.sb("xn", [128, TS], F32)
        mixA = S.sb("mixA", [64, 4, TS], BF16)
        mixB = S.sb("mixB", [128, 2, TS], BF16)
        mixC = S.sb("mixC", [128, 2, TS], BF16)
        mixD = S.sb("mixD", [128, 2, TS], BF16)
        w128 = Ring(S, "w128_", 6, [128, 8, 128], BF16)
        wsm = Ring(S, "wsm_", 8, [128, 2, 128], BF16)
        w64 = Ring(S, "w64_", 3, [128, 8, 64], BF16)
        wra = Ring(S, "wra_", 2, [64, 4, 128], BF16)
        modT = S.sb("modT", [128, 48], F32)
        geff1 = S.sb("geff1", [128, 8], F32)
        geff2 = S.sb("geff2", [128, 8], F32)
        tmp48 = S.sb("tmp48", [128, 48], F32)
        work = Ring(S, "work_", 3, [128, TS], F32)
        Z = [S.sb("Z%d" % i, [128, TS], F32) for i in range(6)]
        PL = S.sb("PL", [128, TS], F32)
        LA = S.sb("LA", [128, TS], F32)
        H = {nm: S.sb("H_" + nm, [64, TS], F32) for nm in ("r", "k", "d", "e", "c", "w", "m", "p", "a", "g", "q")}
        Vall = S.sb("Vall", [64, 4, TS], F32)
        carry = S.sb("carry", [64, 12], F32)
        carryL = S.sb("carryL", [128, 1], F32)
        VV = S.sb("VV", [32, TS], F32)
        STt = S.sb("STt", [64, 4, 64], F32)
        STW = S.sb("STW", [64, 64], F32)
        NB = 4
        SCx = S.sb("SCx", [64, NB, 128], F32)
        SCb = S.sb("SCb", [64, NB, 192], BF16)
        TMb = S.sb("TMb", [64, NB, 256], BF16)
        XS = [S.sb("XS%d" % i, [64, NB, 64], F32) for i in range(2)]
        YS = [S.sb("YS%d" % i, [64, NB, 64], F32) for i in range(2)]
        TT = S.sb("TT", [64, NB, 64], F32)
        TTb = S.sb("TTb", [64, NB, 64], BF16)
        APMb = S.sb("APMb", [64, NB, 128], BF16)
        PP = []
        for par in range(2):
            d_ = {k: S.sb("%s_%d" % (k, par), [64, TS], BF16) for k in ("Atb", "Btb", "Ktb", "Rtb", "Vb")}
            if par == 0:
                d_.update({"Hw": H["w"], "Hg": H["g"], "Hq": H["q"]})
            else:
                d_.update({k: S.sb("%s_%d" % (k, par), [64, TS], F32) for k in ("Hw", "Hg", "Hq")})
            PP.append(d_)
        STb = S.sb("STb", [64, 64], BF16)
        UTb = S.sb("UTb", [64, 64], BF16)
        ident_bf = S.sb("ident_bf", [128, 128], BF16)
        mu_rkv = S.sb("mu_rkv", [64, 12], F32)
        mu_lora = S.sb("mu_lora", [128, 1], F32)
        hpar = S.sb("hpar", [64, 7, 4], F32)
        omka = S.sb("omka", [64, 4], F32)
        v0 = S.sb("v0", [64, 4], F32)
        lora_up = S.sb("lora_up", [128, 256], F32)
        v1 = S.sb("v1", [64, 4, 32], F32)
        v2 = S.sb("v2", [32, 256], F32)
        sg_ln = S.sb("sg_ln", [128, 4], F32)
        sg_wb = S.sb("sg_wb", [128, 4, 128], BF16)
        sg_bs = S.sb("sg_bs", [128, 2, 128], F32)
        conv_w = S.sb("conv_w", [128, 2, 3], F32)
        zc = S.sb("zc", [128, 2, TS + 2], F32)
        vz = [S.sb("vz%d" % i, [128, 128], BF16) for i in range(4)]
        s5_lam = S.sb("s5_lam", [128, 3, 8], F32)
        s5_bk = S.sb("s5_bk", [128, 5, 2, 64], F32)
        s5_lc = S.sb("s5_lc", [128, 8, 2, 128], BF16)
        s5_d = S.sb("s5_d", [128, 2], F32)
        LB = S.sb("LB", [128, 8, 2, 128], BF16)
        cosT = S.sb("cosT", [128, 8, FR], F32)
        sinT = S.sb("sinT", [128, 8, FR], F32)
        rho = S.sb("rho", [128, 8], F32)
        th = S.sb("th", [128, 8], F32)
        s5st = S.sb("s5st", [128, 8, 2], F32)
        s5c = S.sb("s5c", [128, 2], F32)
        small = Ring(S, "small_", 4, [128, 64], F32)
        uT = S.sb("uT", [128, 2, TS], F32)
        uTb = S.sb("uTb", [128, 2, TS], BF16)
        xrb = S.sb("xrb", [128, TS], BF16)
        xib = S.sb("xib", [128, TS], BF16)
        tP = {nm: S.sb("s5p_" + nm, [128, 8], F32) for nm in ("lre", "dt", "mag", "ang", "sn", "cs", "abr", "abi", "den", "qre", "qim", "t1", "t2")}
        tK = {nm: S.sb("s5k_" + nm, [128, 2, 64], F32) for nm in ("lre", "dt", "mag", "ang", "sn", "cs", "abr", "abi", "den", "qre", "qim", "t1", "t2")}
        bbr = S.sb("bbr", [128, 2, 64], F32)
        bbi = S.sb("bbi", [128, 2, 64], F32)
        tq = S.sb("tq", [128, 2, 64], F32)

        def zero(b, ap=None):
            do("dve", lambda e: e.memset(b[:] if ap is None else ap, 0.0), W=[b])
        for b in vz:
            zero(b)
        do("dve", lambda e: e.tensor_copy(ident_bf[:], ident[:]), R=[ident], W=[ident_bf])
        zero(LB)
        print("SBUF bytes/partition:", S.sb_bytes)

        def wload(buf, src, ap=None):
            dma(buf[:] if ap is None else ap, src, W=[buf], q="pool")
            return buf

        def rms_stats():
            pb = psum()
            for ft in range(8):
                do("act", lambda e: e.activation(sqb[:], X[:, ft, :], AF.Square), R=[X], W=[sqb])
                do("pe", lambda e: e.matmul(pb[:], ones_bf[:], sqb[:], start=(ft == 0), stop=(ft == 7)), R=[ones_bf, sqb], W=[pb])
            do("dve", lambda e: e.tensor_scalar(rstd[:], pb[:], 1.0 / D, EPS, ALU.mult, ALU.add), R=[pb], W=[rstd])
            do("act", lambda e: e.activation(rstd[:], rstd[:], AF.Sqrt), R=[rstd], W=[rstd])
            do("dve", lambda e: e.reciprocal(rstd[:], rstd[:]), R=[rstd], W=[rstd])

        def rmsnorm_to(dst, geff, shift_col0):
            rms_stats()
            for ft in range(8):
                do("dve", lambda e: e.tensor_tensor(xn[:], X[:, ft, :], rstd[:], ALU.mult), R=[X, rstd], W=[xn])
                do("act", lambda e: e.activation(dst[:, ft, :], xn[:], AF.Identity, bias=modT[:, shift_col0 + ft:shift_col0 + ft + 1],
                                                 scale=geff[:, ft:ft + 1]), R=[xn, modT, geff], W=[dst])

        def proj128(src_ap, rhs_buf, rhs_fn, nk, ring, evac):
            wb = wload(ring.next(), src_ap)
            pb = psum()
            for kt in range(nk):
                do("pe", lambda e: e.matmul(pb[:], wb[:, kt, :], rhs_fn(kt), start=(kt == 0), stop=(kt == nk - 1)), R=[wb, rhs_buf], W=[pb])
            evac(pb)

        def s5_disc(t, are, aim, ldt, srcb):
            A = lambda nm: t[nm][:]
            B = lambda nm: t[nm]
            do("dve", lambda e: e.tensor_scalar(A("lre"), are, -1e-4, None, ALU.min), R=[srcb], W=[B("lre")])
            do("act", lambda e: e.activation(A("dt"), ldt, AF.Exp), R=[srcb], W=[B("dt")])
            do("dve", lambda e: e.tensor_tensor(A("t1"), A("lre"), A("dt"), ALU.mult), R=[B("lre"), B("dt")], W=[B("t1")])
            do("act", lambda e: e.activation(A("mag"), A("t1"), AF.Exp), R=[B("t1")], W=[B("mag")])
            do("dve", lambda e: e.tensor_tensor(A("ang"), aim, A("dt"), ALU.mult), R=[srcb, B("dt")], W=[B("ang")])
            for off, dst in ((0.0, "sn"), (0.5 * math.pi, "cs")):
                do("dve", lambda e: e.tensor_scalar(A("t1"), A("ang"), off, None, ALU.add), R=[B("ang")], W=[B("t1")])
                do("dve", lambda e: e.tensor_scalar(A("t2"), A("t1"), 1.0 / (2 * math.pi), MAGIC, ALU.mult, ALU.add), R=[B("t1")], W=[B("t2")])
                do("dve", lambda e: e.tensor_scalar(A("t2"), A("t2"), -MAGIC, None, ALU.add), R=[B("t2")], W=[B("t2")])
                do("dve", lambda e: e.scalar_tensor_tensor(A("t1"), A("t2"), -2 * math.pi, A("t1"), ALU.mult, ALU.add), R=[B("t2"), B("t1")], W=[B("t1")])
                do("act", lambda e: e.activation(A(dst), A("t1"), AF.Sin), R=[B("t1")], W=[B(dst)])
            do("dve", lambda e: e.tensor_tensor(A("abr"), A("mag"), A("cs"), ALU.mult), R=[B("mag"), B("cs")], W=[B("abr")])
            do("dve", lambda e: e.tensor_tensor(A("abi"), A("mag"), A("sn"), ALU.mult), R=[B("mag"), B("sn")], W=[B("abi")])
            do("dve", lambda e: e.tensor_tensor(A("den"), A("lre"), A("lre"), ALU.mult), R=[B("lre")], W=[B("den")])
            do("dve", lambda e: e.tensor_tensor(A("t1"), aim, aim, ALU.mult), R=[srcb], W=[B("t1")])
            do("dve", lambda e: e.tensor_tensor(A("den"), A("den"), A("t1"), ALU.add), R=[B("den"), B("t1")], W=[B("den")])
            do("dve", lambda e: e.reciprocal(A("den"), A("den")), R=[B("den")], W=[B("den")])
            do("dve", lambda e: e.tensor_scalar(A("t1"), A("abr"), -1.0, None, ALU.add), R=[B("abr")], W=[B("t1")])
            do("dve", lambda e: e.tensor_tensor(A("qre"), A("t1"), A("lre"), ALU.mult), R=[B("t1"), B("lre")], W=[B("qre")])
            do("dve", lambda e: e.tensor_tensor(A("t2"), A("abi"), aim, ALU.mult), R=[B("abi"), srcb], W=[B("t2")])
            do("dve", lambda e: e.tensor_tensor(A("qre"), A("qre"), A("t2"), ALU.add), R=[B("qre"), B("t2")], W=[B("qre")])
            do("dve", lambda e: e.tensor_tensor(A("qre"), A("qre"), A("den"), ALU.mult), R=[B("qre"), B("den")], W=[B("qre")])
            do("dve", lambda e: e.tensor_tensor(A("qim"), A("abi"), A("lre"), ALU.mult), R=[B("abi"), B("lre")], W=[B("qim")])
            do("dve", lambda e: e.tensor_tensor(A("t2"), A("t1"), aim, ALU.mult), R=[B("t1"), srcb], W=[B("t2")])
            do("dve", lambda e: e.tensor_tensor(A("qim"), A("qim"), A("t2"), ALU.subtract), R=[B("qim"), B("t2")], W=[B("qim")])
            do("dve", lambda e: e.tensor_tensor(A("qim"), A("qim"), A("den"), ALU.mult), R=[B("qim"), B("den")], W=[B("qim")])

        for l in range(2):
            pmod = PS[4]
            for mt in range(48):
                wb = Z[mt % 2 * 2]
                wb2 = Z[mt % 2 * 2 + 1]
                dma(wb[:].rearrange("p (k m) -> p k m", k=4), dr["ada_w"][l, mt][:, 0:4, :], W=[wb])
                dma(wb2[:].rearrange("p (k m) -> p k m", k=4), dr["ada_w"][l, mt][:, 4:8, :], W=[wb2])
                for kt in range(8):
                    wv = (wb if kt < 4 else wb2)
                    do("pe", lambda e: e.matmul(pmod[:, mt:mt + 1], wv[:, (kt % 4) * 128:(kt % 4 + 1) * 128], csil[:, kt:kt + 1],
                                                start=(kt == 0), stop=(kt == 7)), R=[wv, csil], W=[pmod])
            dma(tmp48[:], dr["ada_b"][l], W=[tmp48])
            do("dve", lambda e: e.tensor_tensor(modT[:], pmod[:, 0:48], tmp48[:], ALU.add), R=[pmod, tmp48], W=[modT])
            gm = small.next()
            dma(gm[:, 0:8], dr["g_mix"][l], W=[gm])
            do("dve", lambda e: e.scalar_tensor_tensor(geff1[:], modT[:, 8:16], 1.0, gm[:, 0:8], ALU.add, ALU.mult), R=[modT, gm], W=[geff1])
            gf = small.next()
            dma(gf[:, 0:8], dr["g_ffn"][l], W=[gf])
            do("dve", lambda e: e.scalar_tensor_tensor(geff2[:], modT[:, 32:40], 1.0, gf[:, 0:8], ALU.add, ALU.mult), R=[modT, gf], W=[geff2])

            dma(mu_rkv[:], dr["mu_rkv"][l], W=[mu_rkv])
            dma(mu_lora[:], dr["mu_lora"][l], W=[mu_lora])
            dma(hpar[:], dr["hpar"][l], W=[hpar])
            do("dve", lambda e: e.tensor_scalar(omka[:], hpar[:, 3, :], -1.0, 1.0, ALU.mult, ALU.add), R=[hpar], W=[omka])
            dma(lora_up[:], dr["lora_up"][l], W=[lora_up])
            if l == 1:
                dma(v0[:], dr["v0"][0], W=[v0])
                dma(v1[:], dr["v1"][0], W=[v1])
                dma(v2[:], dr["v2"][0], W=[v2])
            dma(sg_ln[:], dr["sg_ln"][l], W=[sg_ln])
            sgw = Z[4]
            dma(sgw[:].rearrange("p (g t) -> p g t", g=4), dr["sg_wsT"][l], W=[sgw])
            for g in range(4):
                do("dve", lambda e: e.tensor_tensor(sg_wb[:, g, :], sgw[:, g * 128:(g + 1) * 128], sgmask[:], ALU.mult), R=[sgw, sgmask], W=[sg_wb])
            dma(sg_bs[:], dr["sg_bs"][l], W=[sg_bs])
            dma(conv_w[:], dr["conv_w"][l], W=[conv_w])
            dma(s5_lam[:], dr["s5_lam"][l], W=[s5_lam])
            dma(s5_bk[:], dr["s5_bk"][l], W=[s5_bk])
            wload(s5_lc, dr["s5_lc"][l])
            dma(s5_d[:], dr["s5_d"][l], W=[s5_d])
            do("dve", lambda e: e.tensor_scalar(s5_lc[:, :, 1, :], s5_lc[:, :, 1, :], -1.0, None, ALU.mult), R=[s5_lc], W=[s5_lc])

            s5_disc(tP, s5_lam[:, 0, :], s5_lam[:, 1, :], s5_lam[:, 2, :], s5_lam)
            do("dve", lambda e: e.tensor_copy(rho[:], tP["mag"][:]), R=[tP["mag"]], W=[rho])
            do("dve", lambda e: e.tensor_copy(th[:], tP["ang"][:]), R=[tP["ang"]], W=[th])
            for jp in range(8):
                for off, dstT in ((0.0, sinT), (0.5 * math.pi, cosT)):
                    wk = work.next()
                    wk2 = work.next()
                    do("dve", lambda e: e.tensor_scalar(wk[:, 0:FR], iota[:], th[:, jp:jp + 1], off, ALU.mult, ALU.add), R=[iota, th], W=[wk])
                    do("dve", lambda e: e.tensor_scalar(wk2[:, 0:FR], wk[:, 0:FR], 1.0 / (2 * math.pi), MAGIC, ALU.mult, ALU.add), R=[wk], W=[wk2])
                    do("dve", lambda e: e.tensor_scalar(wk2[:, 0:FR], wk2[:, 0:FR], -MAGIC, None, ALU.add), R=[wk2], W=[wk2])
                    do("dve", lambda e: e.scalar_tensor_tensor(wk[:, 0:FR], wk2[:, 0:FR], -2 * math.pi, wk[:, 0:FR], ALU.mult, ALU.add), R=[wk2, wk], W=[wk])
                    do("act", lambda e: e.activation(dstT[:, jp, :], wk[:, 0:FR], AF.Sin), R=[wk], W=[dstT])
            s5_disc(tK, s5_bk[:, 2], s5_bk[:, 3], s5_bk[:, 4], s5_bk)
            do("dve", lambda e: e.tensor_tensor(bbr[:], tK["qre"][:], s5_bk[:, 0], ALU.mult), R=[tK["qre"], s5_bk], W=[bbr])
            do("dve", lambda e: e.tensor_tensor(tq[:], tK["qim"][:], s5_bk[:, 1], ALU.mult), R=[tK["qim"], s5_bk], W=[tq])
            do("dve", lambda e: e.tensor_tensor(bbr[:], bbr[:], tq[:], ALU.subtract), R=[bbr, tq], W=[bbr])
            do("dve", lambda e: e.tensor_tensor(bbi[:], tK["qre"][:], s5_bk[:, 1], ALU.mult), R=[tK["qre"], s5_bk], W=[bbi])
            do("dve", lambda e: e.tensor_tensor(tq[:], tK["qim"][:], s5_bk[:, 0], ALU.mult), R=[tK["qim"], s5_bk], W=[tq])
            do("dve", lambda e: e.tensor_tensor(bbi[:], bbi[:], tq[:], ALU.add), R=[bbi, tq], W=[bbi])
            for jp in range(8):
                kt, r0 = jp // 4, (jp % 4) * 32
                for ri, bb in ((0, bbr), (1, bbi)):
                    for gg in range(2):
                        do("dve", lambda e: e.tensor_scalar(LB[r0:r0 + 32, jp, ri, gg * 64:(gg + 1) * 64], bb[r0:r0 + 32, kt, :],
                                                            sel16[r0:r0 + 32, gg:gg + 1], None, ALU.mult), R=[bb, sel16], W=[LB])

            zero(STt)
            zero(carry)
            zero(carryL)
            zero(s5st)
            zero(zc, zc[:, :, 0:2])

            for si in range(NS):
                t0 = si * TS
                src = dr["xT"] if l == 0 else xs_d
                srcb = [] if l == 0 else [B_xs[si]]
                dma(X[:], src[:, t0:t0 + TS].rearrange("(ft p) t -> p ft t", p=128), R=srcb, W=[X])
                rmsnorm_to(hT, geff1, 0)
                hfn = lambda kt: hT[:, kt, :]

                def s5_gen():
                    for kt in range(2):
                        def ev_u(pb, kt=kt):
                            do("act", lambda e: e.copy(uT[:, kt, :], pb[:]), R=[pb], W=[uT])
                            do("dve", lambda e: e.tensor_copy(uTb[:, kt, :], pb[:]), R=[pb], W=[uTb])
                        proj128(dr["w_in128"][l, 11 + kt], hT, hfn, 8, w128, ev_u)
                    py = [PS[3], PS[3]]
                    f3 = lambda ap: ap.rearrange("p (f s) -> p f s", f=NF)
                    for jp in range(8):
                        kt = jp // 4
                        pre = psum()
                        do("pe", lambda e: e.matmul(pre[:], LB[:, jp, 0, :], uTb[:, kt, :], start=True, stop=True), R=[LB, uTb], W=[pre])
                        pim = psum()
                        do("pe", lambda e: e.matmul(pim[:], LB[:, jp, 1, :], uTb[:, kt, :], start=True, stop=True), R=[LB, uTb], W=[pim])
                        bre, bim, mre, mim, tmp = Z[0], Z[1], Z[2], Z[3], Z[4]
                        do("act", lambda e: e.copy(bre[:], pre[:]), R=[pre], W=[bre])
                        do("act", lambda e: e.copy(bim[:], pim[:]), R=[pim], W=[bim])
                        cb = cosT[:, jp, :].unsqueeze(1).to_broadcast([128, NF, FR])
                        sb_ = sinT[:, jp, :].unsqueeze(1).to_broadcast([128, NF, FR])
                        do("dve", lambda e: e.tensor_tensor(f3(mre[:]), f3(bre[:]), cb, ALU.mult), R=[bre, cosT], W=[mre])
                        do("dve", lambda e: e.tensor_tensor(f3(tmp[:]), f3(bim[:]), sb_, ALU.mult), R=[bim, sinT], W=[tmp])
                        yield
                        do("dve", lambda e: e.tensor_tensor(mre[:], mre[:], tmp[:], ALU.add), R=[mre, tmp], W=[mre])
                        yield
                        tmp2 = Z[5]
                        do("dve", lambda e: e.tensor_tensor(f3(mim[:]), f3(bim[:]), cb, ALU.mult), R=[bim, cosT], W=[mim])
                        yield
                        do("dve", lambda e: e.tensor_tensor(f3(tmp2[:]), f3(bre[:]), sb_, ALU.mult), R=[bre, sinT], W=[tmp2])
                        do("dve", lambda e: e.tensor_tensor(mim[:], mim[:], tmp2[:], ALU.subtract), R=[mim, tmp2], W=[mim])
                        yield
                        rb = rho[:, jp:jp + 1].to_broadcast([128, FR])
                        cL, sL = cosT[:, jp, FR - 1:FR], sinT[:, jp, FR - 1:FR]
                        for fi in range(NF):
                            fs = slice(fi * FR, (fi + 1) * FR)
                            last = slice(fi * FR + FR - 1, fi * FR + FR)
                            do("dve", lambda e: e.tensor_tensor_scan(bre[:, fs], rb, mre[:, fs], s5st[:, jp, 0:1], ALU.mult, ALU.add), R=[rho, mre, s5st], W=[bre])
                            do("dve", lambda e: e.tensor_tensor_scan(bim[:, fs], rb, mim[:, fs], s5st[:, jp, 1:2], ALU.mult, ALU.add), R=[rho, mim, s5st], W=[bim])
                            yield
                            do("dve", lambda e: e.tensor_scalar(s5c[:, 0:1], bim[:, last], sL, None, ALU.mult), R=[bim, sinT], W=[s5c])
                            do("dve", lambda e: e.tensor_scalar(s5c[:, 1:2], bre[:, last], sL, None, ALU.mult), R=[bre, sinT], W=[s5c])
                            do("dve", lambda e: e.scalar_tensor_tensor(s5st[:, jp, 0:1], bre[:, last], cL, s5c[:, 0:1], ALU.mult, ALU.subtract), R=[bre, cosT, s5c], W=[s5st])
                            do("dve", lambda e: e.scalar_tensor_tensor(s5st[:, jp, 1:2], bim[:, last], cL, s5c[:, 1:2], ALU.mult, ALU.add), R=[bim, cosT, s5c], W=[s5st])
                            yield
                        do("dve", lambda e: e.tensor_tensor(f3(mre[:]), f3(bre[:]), cb, ALU.mult), R=[bre, cosT], W=[mre])
                        yield
                        do("dve", lambda e: e.tensor_tensor(f3(tmp[:]), f3(bim[:]), sb_, ALU.mult), R=[bim, sinT], W=[tmp])
                        do("dve", lambda e: e.tensor_tensor(xrb[:], mre[:], tmp[:], ALU.subtract), R=[mre, tmp], W=[xrb])
                        yield
                        do("dve", lambda e: e.tensor_tensor(f3(mim[:]), f3(bre[:]), sb_, ALU.mult), R=[bre, sinT], W=[mim])
                        yield
                        do("dve", lambda e: e.tensor_tensor(f3(tmp2[:]), f3(bim[:]), cb, ALU.mult), R=[bim, cosT], W=[tmp2])
                        do("dve", lambda e: e.tensor_tensor(xib[:], mim[:], tmp2[:], ALU.add), R=[mim, tmp2], W=[xib])
                        yield
                        ct = jp // 4
                        do("pe", lambda e: e.matmul(py[ct][:], s5_lc[:, jp, 0, :], xrb[:], start=(jp % 4 == 0), stop=False), R=[s5_lc, xrb], W=[py[ct]])
                        do("pe", lambda e: e.matmul(py[ct][:], s5_lc[:, jp, 1, :], xib[:], start=False, stop=(jp % 4 == 3)), R=[s5_lc, xib], W=[py[ct]])
                        if jp % 4 == 3:
                            do("dve", lambda e: e.scalar_tensor_tensor(uT[:, ct, :], uT[:, ct, :], s5_d[:, ct:ct + 1], py[ct][:], ALU.mult, ALU.add),
                               R=[uT, s5_d, py[ct]], W=[uT])
                            do("act", lambda e: e.activation(mixD[:, ct, :], uT[:, ct, :], AF.Gelu_apprx_tanh), R=[uT], W=[mixD])
                        yield

                s5g_ = s5_gen()

                def pump(n=1):
                    if os.environ.get("NOPUMP"):
                        return
                    for _ in range(n):
                        if next(s5g_, "done") == "done":
                            break

                def ev_lora(pb):
                    do("act", lambda e: e.copy(PL[:], pb[:]), R=[pb], W=[PL])
                proj128(dr["w_in128"][l, 0], hT, hfn, 8, w128, ev_lora)
                wk = work.next()
                do("dve", lambda e: e.tensor_tensor(wk[:, 1:TS], PL[:, 0:TS - 1], PL[:, 1:TS], ALU.subtract), R=[PL], W=[wk])
                do("dve", lambda e: e.tensor_tensor(wk[:, 0:1], carryL[:], PL[:, 0:1], ALU.subtract), R=[PL, carryL], W=[wk])
                do("dve", lambda e: e.tensor_copy(carryL[:], PL[:, TS - 1:TS]), R=[PL, wk], W=[carryL])
                do("dve", lambda e: e.scalar_tensor_tensor(PL[:], wk[:], mu_lora[:, 0:1], PL[:], ALU.mult, ALU.add), R=[wk, mu_lora, PL], W=[PL])
                do("act", lambda e: e.activation(LA[0:32, :], PL[0:32, :], AF.Tanh), R=[PL], W=[LA])
                do("act", lambda e: e.copy(LA[32:64, :], PL[32:64, :]), R=[PL], W=[LA])
                do("act", lambda e: e.activation(LA[64:128, :], PL[64:128, :], AF.Sigmoid), R=[PL], W=[LA])

                def rkv_proj(q, h, dst_buf, dst_ap):
                    wb = wload(w64.next(), dr["w_in64"][l, q * 4 + h])
                    pb = psum()
                    for kt in range(8):
                        do("pe", lambda e: e.matmul(pb[0:64, :], wb[:, kt, :], hT[:, kt, :], start=(kt == 0), stop=(kt == 7)), R=[wb, hT], W=[pb])
                    do("act", lambda e: e.copy(dst_ap, pb[0:64, :]), R=[pb], W=[dst_buf])

                def tshift(Qb, Qap, cc):
                    dq = H["d"]
                    do("dve", lambda e: e.tensor_tensor(dq[:, 1:TS], Qap[:, 0:TS - 1], Qap[:, 1:TS], ALU.subtract), R=[Qb], W=[dq])
                    do("dve", lambda e: e.tensor_tensor(dq[:, 0:1], carry[:, cc:cc + 1], Qap[:, 0:1], ALU.subtract), R=[Qb, carry], W=[dq])
                    do("dve", lambda e: e.tensor_copy(carry[:, cc:cc + 1], Qap[:, TS - 1:TS]), R=[Qb, dq], W=[carry])
                    do("dve", lambda e: e.scalar_tensor_tensor(Qap, dq[:], mu_rkv[:, cc:cc + 1], Qap, ALU.mult, ALU.add), R=[dq, mu_rkv, Qb], W=[Qb])

                for h in range(4):
                    rkv_proj(2, h, Vall, Vall[:, h, :])
                    tshift(Vall, Vall[:, h, :], 8 + h)
                if l == 0:
                    dma(vf_d[:, :, t0:t0 + TS], Vall[:], R=[Vall], W=[B_vf[si]])
                else:
                    pv = psum()
                    for h in range(4):
                        do("pe", lambda e: e.matmul(pv[0:32, :], v1[:, h, :], Vall[:, h, :], start=(h == 0), stop=(h == 3)), R=[v1, Vall], W=[pv])
                    do("act", lambda e: e.copy(VV[:], pv[0:32, :]), R=[pv], W=[VV])
                    for h in range(4):
                        pg = psum()
                        do("pe", lambda e: e.matmul(pg[0:64, :], v2[:, h * 64:(h + 1) * 64], VV[:], start=True, stop=True), R=[v2, VV], W=[pg])
                        wk = work.next()
                        do("act", lambda e: e.activation(wk[0:64, :], pg[0:64, :], AF.Sigmoid, bias=v0[:, h:h + 1]), R=[pg, v0], W=[wk])
                        wk2 = work.next()
                        dma(wk2[0:64, :], vf_d[:, h, t0:t0 + TS], R=[B_vf[si]], W=[wk2])
                        do("dve", lambda e: e.tensor_tensor(wk2[0:64, :], wk2[0:64, :], Vall[:, h, :], ALU.subtract), R=[wk2, Vall], W=[wk2])
                        do("dve", lambda e: e.tensor_tensor(wk2[0:64, :], wk2[0:64, :], wk[0:64, :], ALU.mult), R=[wk2, wk], W=[wk2])
                        do("dve", lambda e: e.tensor_tensor(Vall[:, h, :], Vall[:, h, :], wk2[0:64, :], ALU.add), R=[Vall, wk2], W=[Vall])

                def prep_gen(h, P):
                    Hr, Hk, Hd, He, Hc, Hm, Hp, Ha = (H[k] for k in "rkdecmpa")
                    Hw, Hg, Hq, Atb, Btb, Ktb, Rtb, Vb = (P[k] for k in ("Hw", "Hg", "Hq", "Atb", "Btb", "Ktb", "Rtb", "Vb"))
                    rkv_proj(0, h, Hr, Hr[:])
                    yield
                    rkv_proj(1, h, Hk, Hk[:])
                    yield
                    tshift(Hr, Hr[:], h)
                    yield
                    tshift(Hk, Hk[:], 4 + h)
                    yield
                    cs_ = slice(h * 64, (h + 1) * 64)
                    pw = psum()
                    do("pe", lambda e: e.matmul(pw[0:64, :], lora_up[0:32, cs_], LA[0:32, :], start=True, stop=True), R=[lora_up, LA], W=[pw])
                    do("act", lambda e: e.activation(He[:], pw[0:64, :], AF.Sigmoid, bias=hpar[:, 0, h:h + 1]), R=[pw, hpar], W=[He])
                    yield
                    pa = psum()
                    do("pe", lambda e: e.matmul(pa[0:64, :], lora_up[32:64, cs_], LA[32:64, :], start=True, stop=True), R=[lora_up, LA], W=[pa])
                    do("act", lambda e: e.activation(Ha[:], pa[0:64, :], AF.Sigmoid, bias=hpar[:, 1, h:h + 1]), R=[pa, hpar], W=[Ha])
                    yield
                    pg = psum()
                    do("pe", lambda e: e.matmul(pg[0:64, :], lora_up[64:128, cs_], LA[64:128, :], start=True, stop=True), R=[lora_up, LA], W=[pg])
                    do("act", lambda e: e.copy(Hg[:], pg[0:64, :]), R=[pg], W=[Hg])
                    yield
                    do("dve", lambda e: e.tensor_tensor_scan(Hc[:], reset[:], He[:], 0.0, ALU.mult, ALU.add), R=[reset, He], W=[Hc])
                    yield
                    do("act", lambda e: e.activation(Hw[:], Hc[:], AF.Exp, scale=-C0), R=[Hc], W=[Hw])
                    yield
                    do("act", lambda e: e.activation(Hm[:], Hc[:], AF.Exp, scale=C0), R=[Hc], W=[Hm])
                    yield
                    do("dve", lambda e: e.tensor_tensor(Hd[:], Hc[:], He[:], ALU.subtract), R=[Hc, He], W=[Hd])
                    yield
                    do("act", lambda e: e.activation(Hp[:], Hd[:], AF.Exp, scale=-C0), R=[Hd], W=[Hp])
                    yield
                    do("dve", lambda e: e.tensor_scalar(Hq[:], Hk[:], hpar[:, 2, h:h + 1], None, ALU.mult), R=[Hk, hpar], W=[Hq])
                    yield
                    do("dve", lambda e: e.tensor_tensor(Hd[:], Hq[:], Hq[:], ALU.mult), R=[Hq], W=[Hd])
                    yield
                    pn = psum()
                    do("pe", lambda e: e.matmul(pn[0:64, :], ones[0:64, 0:64], Hd[:], start=True, stop=True), R=[ones, Hd], W=[pn])
                    do("dve", lambda e: e.tensor_scalar(Hc[:], pn[0:64, :], 1e-24, None, ALU.max), R=[pn], W=[Hc])
                    yield
                    do("act", lambda e: e.activation(Hc[:], Hc[:], AF.Sqrt), R=[Hc], W=[Hc])
                    yield
                    do("dve", lambda e: e.reciprocal(Hc[:], Hc[:]), R=[Hc], W=[Hc])
                    yield
                    do("dve", lambda e: e.tensor_tensor(Hq[:], Hq[:], Hc[:], ALU.mult), R=[Hq, Hc], W=[Hq])
                    yield
                    do("dve", lambda e: e.tensor_scalar(Hd[:], Ha[:], hpar[:, 3, h:h + 1], omka[:, h:h + 1], ALU.mult, ALU.add), R=[Ha, hpar, omka], W=[Hd])
                    yield
                    do("dve", lambda e: e.tensor_tensor(Hk[:], Hk[:], Hd[:], ALU.mult), R=[Hk, Hd], W=[Hk])
                    yield
                    do("dve", lambda e: e.scalar_tensor_tensor(Atb[:], Hq[:], -1.0, Hp[:], ALU.mult, ALU.mult), R=[Hq, Hp], W=[Atb])
                    yield
                    do("dve", lambda e: e.tensor_tensor(Ha[:], Hq[:], Ha[:], ALU.mult), R=[Hq, Ha], W=[Ha])
                    yield
                    do("dve", lambda e: e.tensor_tensor(Btb[:], Ha[:], Hm[:], ALU.mult), R=[Ha, Hm], W=[Btb])
                    yield
                    do("dve", lambda e: e.tensor_tensor(Ktb[:], Hk[:], Hm[:], ALU.mult), R=[Hk, Hm], W=[Ktb])
                    yield
                    do("act", lambda e: e.copy(Vb[:], Vall[:, h, :]), R=[Vall], W=[Vb])
                    yield
                    do("dve", lambda e: e.scalar_tensor_tensor(Hd[:], Hr[:], hpar[:, 6, h:h + 1], Hk[:], ALU.mult, ALU.mult), R=[Hr, hpar, Hk], W=[Hd])
                    yield
                    pn = psum()
                    do("pe", lambda e: e.matmul(pn[0:64, :], ones[0:64, 0:64], Hd[:], start=True, stop=True), R=[ones, Hd], W=[pn])
                    do("dve", lambda e: e.tensor_tensor(Hq[:], pn[0:64, :], Vall[:, h, :], ALU.mult), R=[pn, Vall], W=[Hq])
                    yield
                    do("dve", lambda e: e.tensor_tensor(Rtb[:], Hr[:], Hw[:], ALU.mult), R=[Hr, Hw], W=[Rtb])
                    yield

                def wkv(h, P, pump):
                    Hw, Atb, Btb, Ktb, Rtb, Vb = (P[k] for k in ("Hw", "Atb", "Btb", "Ktb", "Rtb", "Vb"))
                    At, Bt, Kt, Rt = Atb, Btb, Ktb, Rtb
                    PO = PS[7]
                    do("act", lambda e: e.copy(STb[:], STt[:, h, :]), R=[STt], W=[STb])
                    for bt in range(NCH // NB):
                        for q in range(NB):
                            c = bt * NB + q
                            cs2 = slice(c * C, (c + 1) * C)
                            pb = psum()
                            a_, b_, k_, r_ = At[:, cs2], Bt[:, cs2], Kt[:, cs2], Rt[:, cs2]
                            do("pe", lambda e: e.matmul(pb[0:64, 0:64], b_, a_, start=True, stop=True), R=[Bt, At], W=[pb])
                            do("pe", lambda e: e.matmul(pb[0:64, 64:128], a_, b_, start=True, stop=True), R=[At, Bt], W=[pb])
                            do("pe", lambda e: e.matmul(pb[0:64, 128:192], b_, r_, start=True, stop=True), R=[Bt, Rt], W=[pb])
                            do("pe", lambda e: e.matmul(pb[0:64, 192:256], a_, k_, start=True, stop=True), R=[At, Kt], W=[pb])
                            do("pe", lambda e: e.matmul(pb[0:64, 256:320], k_, r_, start=True, stop=True), R=[Kt, Rt], W=[pb])
                            do("dve", lambda e: e.tensor_tensor(SCx[:, q, :], pb[0:64, 0:128], msk5[:, 0:128], ALU.mult), R=[pb, msk5], W=[SCx])
                            do("dve", lambda e: e.tensor_tensor(SCb[:, q, :], pb[0:64, 128:320], msk5[:, 128:320], ALU.mult), R=[pb, msk5], W=[SCb])
                            pt = psum()
                            v_ = Vb[:, cs2]
                            for qi, (sb_, s_) in enumerate(((At, a_), (Bt, b_), (Kt, k_), (Vb, v_))):
                                do("pe", lambda e: e.matmul(pt[0:64, qi * 64:(qi + 1) * 64], s_, ident_bf[0:64, 0:64], start=True, stop=True),
                                   R=[sb_, ident_bf], W=[pt])
                            do("act", lambda e: e.copy(TMb[:, q, :], pt[0:64, 0:256]), R=[pt], W=[TMb])
                            pump()
                        do("dve", lambda e: e.tensor_tensor(TT[:], SCx[:, :, 0:64], ident[0:64, 0:64].unsqueeze(1).to_broadcast([64, NB, 64]), ALU.add),
                           R=[SCx, ident], W=[TT])
                        Xb, Xf = SCx, (lambda q: SCx[:, q, 0:64])
                        Yb, Yf = SCx, (lambda q: SCx[:, q, 64:128])
                        cur = 0
                        for lev in range(1, 6):
                            Xn_, Yn_ = XS[cur], YS[cur]
                            if lev < 5:
                                pb = PS[4]
                                for q in range(NB):
                                    do("pe", lambda e: e.matmul(pb[0:64, q * 64:(q + 1) * 64], Yf(q), Xf(q), start=True, stop=True), R=[Yb, Xb], W=[pb])
                                do("act", lambda e: e.copy(Xn_[:], pb[0:64, 0:NB * 64].rearrange("p (q t) -> p q t", q=NB)), R=[pb], W=[Xn_])
                                pump()
                            pb = PS[5]
                            for q in range(NB):
                                do("pe", lambda e: e.matmul(pb[0:64, q * 64:(q + 1) * 64], Xf(q), Yf(q), start=True, stop=True), R=[Xb, Yb], W=[pb])
                            do("dve", lambda e: e.tensor_copy(Yn_[:], pb[0:64, 0:NB * 64].rearrange("p (q t) -> p q t", q=NB)), R=[pb], W=[Yn_])
                            pump()
                            pb = PS[6]
                            for q in range(NB):
                                do("pe", lambda e: e.matmul(pb[0:64, q * 64:(q + 1) * 64], Yn_[:, q, :], TT[:, q, :], start=True, stop=True), R=[Yn_, TT], W=[pb])
                            if lev < 5:
                                do("dve", lambda e: e.tensor_tensor(TT[:], TT[:], pb[0:64, 0:NB * 64].rearrange("p (q t) -> p q t", q=NB), ALU.add),
                                   R=[pb, TT], W=[TT])
                            else:
                                do("dve", lambda e: e.tensor_tensor(TTb[:], TT[:], pb[0:64, 0:NB * 64].rearrange("p (q t) -> p q t", q=NB), ALU.add),
                                   R=[pb, TT], W=[TTb])
                            Xb, Yb = Xn_, Yn_
                            pump()
                            Xf = (lambda q, Xn_=Xn_: Xn_[:, q, :])
                            Yf = (lambda q, Yn_=Yn_: Yn_[:, q, :])
                            cur = 1 - cur
                        pb = PS[4]
                        for q in range(NB):
                            do("pe", lambda e: e.matmul(pb[0:64, q * 128:q * 128 + 64], TMb[:, q, 0:64], TTb[:, q, :], start=True, stop=True), R=[TMb, TTb], W=[pb])
                            do("pe", lambda e: e.matmul(pb[0:64, q * 128 + 64:q * 128 + 128], SCb[:, q, 64:128], TTb[:, q, :], start=True, stop=True), R=[SCb, TTb], W=[pb])
                        do("act", lambda e: e.copy(APMb[:], pb[0:64, 0:NB * 128].rearrange("p (q t) -> p q t", q=NB)), R=[pb], W=[APMb])
                        pump()
                        for q in range(NB):
                            c = bt * NB + q
                            cs2 = slice(c * C, (c + 1) * C)
                            pu = PS[5]
                            do("pe", lambda e: e.matmul(pu[0:64, 0:64], APMb[:, q, 64:128], TMb[:, q, 192:256], start=True, stop=False), R=[APMb, TMb], W=[pu])
                            do("pe", lambda e: e.matmul(pu[0:64, 0:64], APMb[:, q, 0:64], STb[:], start=False, stop=True), R=[APMb, STb], W=[pu])
                            do("act", lambda e: e.copy(UTb[:], pu[0:64, 0:64]), R=[pu], W=[UTb])
                            pump()
                            do("pe", lambda e: e.matmul(PO[0:64, cs2], STb[:], Rt[:, cs2], start=True, stop=False), R=[STb, Rt], W=[PO])
                            do("pe", lambda e: e.matmul(PO[0:64, cs2], TMb[:, q, 192:256], SCb[:, q, 128:192], start=False, stop=False), R=[TMb, SCb], W=[PO])
                            do("pe", lambda e: e.matmul(PO[0:64, cs2], UTb[:], SCb[:, q, 0:64], start=False, stop=True), R=[UTb, SCb], W=[PO])
                            pn = PS[6]
                            do("pe", lambda e: e.matmul(pn[0:64, 0:64], TMb[:, q, 64:128], UTb[:], start=True, stop=False), R=[TMb, UTb], W=[pn])
                            do("pe", lambda e: e.matmul(pn[0:64, 0:64], TMb[:, q, 128:192], TMb[:, q, 192:256], start=False, stop=True), R=[TMb], W=[pn])
                            wc = Hw[:, c * C + C - 1:c * C + C]
                            do("act", lambda e: e.activation(STW[:], STt[:, h, :], AF.Copy, scale=wc), R=[STt, Hw], W=[STW])
                            do("dve", lambda e: e.scalar_tensor_tensor(STb[:], pn[0:64, 0:64], wc, STW[:], ALU.mult, ALU.add), R=[pn, Hw, STW], W=[STb])
                            do("dve", lambda e: e.scalar_tensor_tensor(STt[:, h, :], pn[0:64, 0:64], wc, STW[:], ALU.mult, ALU.add), R=[pn, Hw, STW], W=[STt])
                            pump()

                def post(h, P):
                    Hd, Hc, Hm = H["d"], H["c"], H["m"]
                    Hg, Hq = P["Hg"], P["Hq"]
                    PO = PS[7]
                    OS = Hm
                    do("act", lambda e: e.copy(OS[:], PO[0:64, :]), R=[PO], W=[OS])
                    do("dve", lambda e: e.tensor_tensor(Hd[:], OS[:], OS[:], ALU.mult), R=[OS], W=[Hd])
                    pm1 = psum()
                    do("pe", lambda e: e.matmul(pm1[0:64, :], ones[0:64, 0:64], OS[:], start=True, stop=True), R=[ones, OS], W=[pm1])
                    pm2 = psum()
                    do("pe", lambda e: e.matmul(pm2[0:64, :], ones[0:64, 0:64], Hd[:], start=True, stop=True), R=[ones, Hd], W=[pm2])
                    mu_ = work.next()
                    do("act", lambda e: e.activation(mu_[0:64, :], pm1[0:64, :], AF.Copy, scale=1.0 / 64), R=[pm1], W=[mu_])
                    var = work.next()
                    do("dve", lambda e: e.tensor_tensor(var[0:64, :], mu_[0:64, :], mu_[0:64, :], ALU.mult), R=[mu_], W=[var])
                    do("dve", lambda e: e.scalar_tensor_tensor(var[0:64, :], pm2[0:64, :], 1.0 / 64, var[0:64, :], ALU.mult, ALU.subtract), R=[pm2, var], W=[var])
                    do("dve", lambda e: e.tensor_scalar(var[0:64, :], var[0:64, :], 64e-5, None, ALU.add), R=[var], W=[var])
                    do("act", lambda e: e.activation(var[0:64, :], var[0:64, :], AF.Sqrt), R=[var], W=[var])
                    do("dve", lambda e: e.reciprocal(var[0:64, :], var[0:64, :]), R=[var], W=[var])
                    do("dve", lambda e: e.tensor_tensor(Hc[:], OS[:], mu_[0:64, :], ALU.subtract), R=[OS, mu_], W=[Hc])
                    do("dve", lambda e: e.tensor_tensor(Hc[:], Hc[:], var[0:64, :], ALU.mult), R=[Hc, var], W=[Hc])
                    do("dve", lambda e: e.tensor_scalar(Hc[:], Hc[:], hpar[:, 4, h:h + 1], hpar[:, 5, h:h + 1], ALU.mult, ALU.add), R=[Hc, hpar], W=[Hc])
                    do("dve", lambda e: e.tensor_tensor(Hc[:], Hc[:], Hq[:], ALU.add), R=[Hc, Hq], W=[Hc])
                    do("dve", lambda e: e.tensor_tensor(mixA[:, h, :], Hc[:], Hg[:], ALU.mult), R=[Hc, Hg], W=[mixA])

                prepg = prep_gen(0, PP[0])
                for _ in prepg:
                    pass
                for h in range(4):
                    nxt = prep_gen(h + 1, PP[(h + 1) % 2]) if h < 3 else iter(())

                    def pump2(nxt=nxt):
                        pump()
                        next(nxt, None)
                    wkv(h, PP[h % 2], pump2)
                    for _ in nxt:
                        pass
                    post(h, PP[h % 2])

                for _ in s5g_:
                    pass

                for mt in range(4):
                    def ev_gelu(pb, mt=mt):
                        do("act", lambda e: e.activation(Z[mt][:], pb[:], AF.Gelu_apprx_tanh), R=[pb], W=[Z[mt]])
                    proj128(dr["w_in128"][l, 1 + mt], hT, hfn, 8, w128, ev_gelu)
                pm1 = psum()
                for i in range(2):
                    do("pe", lambda e: e.matmul(pm1[:], ones[:], Z[2 + i][:], start=(i == 0), stop=(i == 1)), R=[ones, Z[2 + i]], W=[pm1])
                pm2 = psum()
                for i in range(2):
                    sq_ = (PL, LA)[i]
                    do("dve", lambda e: e.tensor_tensor(sq_[:], Z[2 + i][:], Z[2 + i][:], ALU.mult), R=[Z[2 + i]], W=[sq_])
                    do("pe", lambda e: e.matmul(pm2[:], ones[:], sq_[:], start=(i == 0), stop=(i == 1)), R=[ones, sq_], W=[pm2])
                do("act", lambda e: e.activation(LA[:], pm1[:], AF.Copy, scale=1.0 / 256), R=[pm1], W=[LA])
                do("dve", lambda e: e.tensor_tensor(PL[:], LA[:], LA[:], ALU.mult), R=[LA], W=[PL])
                do("dve", lambda e: e.scalar_tensor_tensor(PL[:], pm2[:], 1.0 / 256, PL[:], ALU.mult, ALU.subtract), R=[pm2, PL], W=[PL])
                do("dve", lambda e: e.tensor_scalar(PL[:], PL[:], 1e-5, None, ALU.add), R=[PL], W=[PL])
                do("act", lambda e: e.activation(PL[:], PL[:], AF.Sqrt), R=[PL], W=[PL])
                do("dve", lambda e: e.reciprocal(PL[:], PL[:]), R=[PL], W=[PL])
                for ct in range(2):
                    zv = Z[2 + ct]
                    do("dve", lambda e: e.tensor_tensor(zv[:], zv[:], LA[:], ALU.subtract), R=[zv, LA], W=[zv])
                    do("dve", lambda e: e.tensor_tensor(zv[:], zv[:], PL[:], ALU.mult), R=[zv, PL], W=[zv])
                    do("dve", lambda e: e.tensor_scalar(zv[:], zv[:], sg_ln[:, ct:ct + 1], sg_ln[:, 2 + ct:3 + ct], ALU.mult, ALU.add), R=[zv, sg_ln], W=[zv])
                for ck in range(TS // 128):
                    ks = slice(ck * 128, (ck + 1) * 128)
                    for ct in range(2):
                        pt = psum()
                        do("pe", lambda e: e.matmul(pt[:, 0:128], Z[2 + ct][:, ks], ident[:], start=True, stop=True), R=[Z[2 + ct], ident], W=[pt])
                        do("act", lambda e: e.copy(vz[ct * 2][:, 0:64], pt[:, 0:64]), R=[pt], W=[vz[ct * 2]])
                        do("dve", lambda e: e.tensor_copy(vz[ct * 2 + 1][:, 64:128], pt[:, 64:128]), R=[pt], W=[vz[ct * 2 + 1]])
                        pmx = psum()
                        for par in range(2):
                            g = ct * 2 + par
                            do("pe", lambda e: e.matmul(pmx[:, 0:128], vz[ct * 2 + par][:], sg_wb[:, g, :], start=(par == 0), stop=(par == 1)),
                               R=[vz[ct * 2 + par], sg_wb], W=[pmx])
                        wk = work.next()
                        do("dve", lambda e: e.tensor_tensor(wk[:, 0:128], pmx[:, 0:128], sg_bs[:, ct, :], ALU.add), R=[pmx, sg_bs], W=[wk])
                        do("dve", lambda e: e.tensor_tensor(mixB[:, ct, ks], wk[:, 0:128], Z[ct][:, ks], ALU.mult), R=[wk, Z[ct]], W=[mixB])

                for mt in range(6):
                    def ev_c(pb, mt=mt):
                        do("act", lambda e: e.copy(Z[mt][:], pb[:]), R=[pb], W=[Z[mt]])
                    proj128(dr["w_in128"][l, 5 + mt], hT, hfn, 8, w128, ev_c)
                for ct in range(2):
                    do("dve", lambda e: e.tensor_tensor(zc[:, ct, 2:TS + 2], Z[2 + ct][:], Z[4 + ct][:], ALU.mult), R=[Z[2 + ct], Z[4 + ct]], W=[zc])
                    y = Z[2 + ct]
                    do("dve", lambda e: e.tensor_scalar(y[:], zc[:, ct, 0:TS], conv_w[:, ct, 0:1], None, ALU.mult), R=[zc, conv_w], W=[y])
                    do("dve", lambda e: e.scalar_tensor_tensor(y[:], zc[:, ct, 1:TS + 1], conv_w[:, ct, 1:2], y[:], ALU.mult, ALU.add), R=[zc, conv_w, y], W=[y])
                    do("dve", lambda e: e.scalar_tensor_tensor(y[:], zc[:, ct, 2:TS + 2], conv_w[:, ct, 2:3], y[:], ALU.mult, ALU.add), R=[zc, conv_w, y], W=[y])
                    do("dve", lambda e: e.tensor_tensor(mixC[:, ct, :], y[:], Z[ct][:], ALU.mult), R=[y, Z[ct]], W=[mixC])
                do("dve", lambda e: e.tensor_copy(zc[:, :, 0:2], zc[:, :, TS:TS + 2]), R=[zc], W=[zc])


                merged = aT
                for ft in range(8):
                    fsl = slice(ft * 128, (ft + 1) * 128)
                    acc = Z[5]
                    gts = [Z[0], Z[1], Z[2], Z[3]]
                    for br in range(4):
                        def ev_g(pb, br=br):
                            do("act", lambda e: e.activation(gts[br][:], pb[:], AF.Sigmoid), R=[pb], W=[gts[br]])
                        proj128(dr["w_in128"][l, 13 + br * 8 + ft], hT, hfn, 8, w128, ev_g)
                    wq = [wload(wsm.next(), dr["sg_out"][l][:, :, fsl]), wload(wsm.next(), dr["conv_out"][l][:, :, fsl]),
                          wload(wsm.next(), dr["glu_w"][l][:, :, fsl]),
                          wload(wsm.next(), dr["glu_w"][l][:, :, 1024 + ft * 128:1024 + (ft + 1) * 128])]
                    wr_ = wload(wra.next(), dr["rwkv_out"][l][:, :, fsl])
                    pa = psum()
                    for h in range(4):
                        do("pe", lambda e: e.matmul(pa[:], wr_[:, h, :], mixA[:, h, :], start=(h == 0), stop=(h == 3)), R=[wr_, mixA], W=[pa])
                    do("dve", lambda e: e.tensor_tensor(acc[:], pa[:], gts[0][:], ALU.mult), R=[pa, gts[0]], W=[acc])
                    for bi, mixT in ((1, mixB), (2, mixC)):
                        pb_ = psum()
                        for kt in range(2):
                            do("pe", lambda e: e.matmul(pb_[:], wq[bi - 1][:, kt, :], mixT[:, kt, :], start=(kt == 0), stop=(kt == 1)), R=[wq[bi - 1], mixT], W=[pb_])
                        do("dve", lambda e: e.tensor_tensor(gts[bi][:], pb_[:], gts[bi][:], ALU.mult), R=[pb_, gts[bi]], W=[gts[bi]])
                        do("dve", lambda e: e.tensor_tensor(acc[:], acc[:], gts[bi][:], ALU.add), R=[acc, gts[bi]], W=[acc])
                    ph1 = psum()
                    for kt in range(2):
                        do("pe", lambda e: e.matmul(ph1[:], wq[2][:, kt, :], mixD[:, kt, :], start=(kt == 0), stop=(kt == 1)), R=[wq[2], mixD], W=[ph1])
                    ph2 = psum()
                    for kt in range(2):
                        do("pe", lambda e: e.matmul(ph2[:], wq[3][:, kt, :], mixD[:, kt, :], start=(kt == 0), stop=(kt == 1)), R=[wq[3], mixD], W=[ph2])
                    do("act", lambda e: e.activation(xn[:], ph2[:], AF.Sigmoid), R=[ph2], W=[xn])
                    do("dve", lambda e: e.tensor_tensor(xn[:], ph1[:], xn[:], ALU.mult), R=[ph1, xn], W=[xn])
                    do("dve", lambda e: e.tensor_tensor(xn[:], xn[:], gts[3][:], ALU.mult), R=[xn, gts[3]], W=[xn])
                    do("dve", lambda e: e.tensor_tensor(merged[:, ft, :], acc[:], xn[:], ALU.add), R=[acc, xn], W=[merged])

                for mt in range(8):
                    def ev_o(pb, mt=mt):
                        do("dve", lambda e: e.scalar_tensor_tensor(X[:, mt, :], pb[:], modT[:, 16 + mt:17 + mt], X[:, mt, :], ALU.mult, ALU.add),
                           R=[pb, modT, X], W=[X])
                    proj128(dr["w_o"][l, mt], merged, lambda kt: merged[:, kt, :], 8, w128, ev_o)

                rmsnorm_to(hT, geff2, 24)
                for qtr in range(4):
                    for i in range(8):
                        def ev_f(pb, i=i):
                            wk = work.next()
                            do("act", lambda e: e.activation(wk[:], pb[:], AF.Relu), R=[pb], W=[wk])
                            do("dve", lambda e: e.tensor_tensor(aT[:, i, :], wk[:], wk[:], ALU.mult), R=[wk], W=[aT])
                        proj128(dr["ffn_w1"][l, qtr * 8 + i], hT, hfn, 8, w128, ev_f)
                    for mt in range(8):
                        def ev_2(pb, mt=mt):
                            do("dve", lambda e: e.scalar_tensor_tensor(X[:, mt, :], pb[:], modT[:, 40 + mt:41 + mt], X[:, mt, :], ALU.mult, ALU.add),
                               R=[pb, modT, X], W=[X])
                        proj128(dr["ffn_w2"][l, mt][:, qtr * 8:(qtr + 1) * 8, :], aT, lambda kt: aT[:, kt, :], 8, w128, ev_2)

                if l == 0:
                    dma(xs_d[:, t0:t0 + TS].rearrange("(ft p) t -> p ft t", p=128), X[:], R=[X], W=[B_xs[si]])
                else:
                    rms_stats()
                    for ft in range(8):
                        do("dve", lambda e: e.scalar_tensor_tensor(X[:, ft, :], X[:, ft, :], g_fin[:, ft:ft + 1], rstd[:], ALU.mult, ALU.mult),
                           R=[X, g_fin, rstd], W=[X])
                    dma(outT[:, t0:t0 + TS].rearrange("(ft p) t -> p ft t", p=128), X[:], R=[X], W=[B_outs[si]])
        S.finish(B_outs)
        print("instructions:", S.n_inst, "waits:", S.n_wait)
    return nc


_CACHE = {}


def run(inputs, T):
    maps = [prep_inputs(inputs, b) for b in range(4)]
    shapes = {k: v.shape for k, v in maps[0].items()}
    key = (T,)
    if key not in _CACHE:
        _CACHE[key] = build(shapes, T)
    nc = _CACHE[key]
    in_maps = [maps[i % 4] for i in range(8)]
    res = run_bass_kernel_spmd(nc, in_maps, core_ids=list(range(8)))
    out = np.stack([np.ascontiguousarray(res.results[b]["outT"].T) for b in range(4)])
    return out.astype(np.float32)


def kernel(**inputs):
    inputs = {k: np.asarray(v, dtype=np.float32) for k, v in inputs.items()}
    T = inputs["x"].shape[1]
    return run(inputs, T)
```

```python
import contextlib
import math
import os
import numpy as np
import concourse.bass as bass
import concourse.mybir as mybir
from concourse.bass_utils import run_bass_kernel_spmd

F32 = mybir.dt.float32
BF16 = mybir.dt.bfloat16
AF = mybir.ActivationFunctionType
ALU = mybir.AluOpType

D = 1024
DM = 256
TS = 512
C = 64
NCH = TS // C
FR = 128
EPS = 1e-6
C0 = math.exp(-0.5)
MAGIC = 12582912.0


class Buf:
    __slots__ = ("name", "t", "lastw", "readers", "psum")

    def __init__(self, name, t=None):
        self.name = name
        self.t = t
        self.lastw = None
        self.readers = []
        self.psum = False

    def __getitem__(self, idx):
        return self.t[idx]


class Sched:
    N_DMA_SEM = 10

    def __init__(self, nc, stack):
        self.nc = nc
        self.stack = stack
        self.engs = {"pe": nc.tensor, "act": nc.scalar, "dve": nc.vector, "pool": nc.gpsimd, "sp": nc.sync}
        self.sem = {}
        self.cnt = {}
        for k in ("pe", "act", "dve", "pool"):
            self.sem[k] = stack.enter_context(nc.semaphore("s_" + k))
            self.cnt[k] = 0
        self.dma_sems = {"sp": [], "pool": []}
        for qn in ("sp", "pool"):
            for i in range(self.N_DMA_SEM):
                key = "dma_%s%d" % (qn, i)
                self.sem[key] = stack.enter_context(nc.semaphore("s_" + key))
                self.cnt[key] = 0
                self.dma_sems[qn].append(key)
        self.dma_rr = {"sp": 0, "pool": 0}
        self.waited = {}
        self.n_inst = 0
        self.n_wait = 0
        self.sb_bytes = 0

    def sb(self, name, shape, dt):
        t = self.stack.enter_context(self.nc.sbuf_tensor("sb_" + name, list(shape), dt))
        n = 1
        for s in shape[1:]:
            n *= s
        self.sb_bytes += n * (2 if dt == BF16 else 4)
        return Buf(name, t)

    def ps(self, name, shape, dt=F32):
        t = self.stack.enter_context(self.nc.psum_tensor("pp_" + name, list(shape), dt))
        b = Buf(name, t)
        b.psum = True
        return b

    def _need(self, eng, deps):
        best = {}
        for d in deps:
            if d is None:
                continue
            sk, v, _ = d
            if v > best.get(sk, 0):
                best[sk] = v
        for sk, v in best.items():
            if self.waited.get((eng, sk), 0) >= v:
                continue
            self.engs[eng].wait_ge(self.sem[sk], v)
            self.waited[(eng, sk)] = v
            self.n_wait += 1

    def do(self, eng, fn, R=(), W=()):
        deps = []
        for b in R:
            lw = b.lastw
            if lw is not None and not (lw[2] == eng and eng == "pe"):
                deps.append(lw)
            if b.psum:
                for r in b.readers:
                    if r[2] != eng:
                        deps.append(r)
        for b in W:
            lw = b.lastw
            if lw is not None and (lw[2] != eng or eng != "pe"):
                deps.append(lw)
            for r in b.readers:
                if r[2] != eng or eng != "pe":
                    deps.append(r)
        self._need(eng, deps)
        ins = fn(self.engs[eng])
        self.cnt[eng] += 1
        v = self.cnt[eng]
        ins.then_inc(self.sem[eng], 1)
        tok = (eng, v, eng)
        for b in R:
            b.readers = [r for r in b.readers if r[0] != eng] + [tok]
        for b in W:
            b.lastw = tok
            b.readers = []
        self.n_inst += 1
        return ins

    def dma(self, out_ap, in_ap, R=(), W=(), q="sp"):
        sk = self.dma_sems[q][self.dma_rr[q]]
        self.dma_rr[q] = (self.dma_rr[q] + 1) % len(self.dma_sems[q])
        deps = []
        for b in R:
            if b.lastw is not None:
                deps.append(b.lastw)
        for b in W:
            if b.lastw is not None:
                deps.append(b.lastw)
            deps.extend(b.readers)
        if self.cnt[sk] > 0:
            deps.append((sk, self.cnt[sk], "dmaq"))
        self._need(q, deps)
        ins = self.engs[q].dma_start(out=out_ap, in_=in_ap)
        self.cnt[sk] += 16
        ins.then_inc(self.sem[sk], 16)
        tok = (sk, self.cnt[sk], "dmaq")
        for b in R:
            b.readers.append(tok)
        for b in W:
            b.lastw = tok
            b.readers = []
        self.n_inst += 1
        return ins

    def finish(self, bufs, eng="sp"):
        self._need(eng, [b.lastw for b in bufs if b.lastw is not None])


class Ring:
    def __init__(self, S, name, n, shape, dt, psum=False):
        self.bufs = [(S.ps if psum else S.sb)("%s%d" % (name, i), shape, dt) for i in range(n)]
        self.i = 0

    def next(self):
        b = self.bufs[self.i]
        self.i = (self.i + 1) % len(self.bufs)
        return b


def tile_km(W, msz):
    K, M = W.shape
    return np.ascontiguousarray(W.reshape(K // 128, 128, M // msz, msz).transpose(2, 1, 0, 3))


def col_pk(v, p=128):
    return np.ascontiguousarray(v.reshape(-1, p).T)


def prep_inputs(inp, b):
    f = np.float32
    L = 2
    m = {}
    m["xT"] = np.ascontiguousarray(inp["x"][b].T)
    m["cT"] = col_pk(inp["c"][b])
    m["ident"] = np.eye(128, dtype=f)
    m["ones"] = np.ones((128, 128), f)
    su = np.triu(np.ones((C, C), f), 1)
    sl = np.tril(np.ones((C, C), f), -1)
    iu = np.triu(np.ones((C, C), f), 0)
    m["msk5"] = np.ascontiguousarray(np.concatenate([su, sl, iu, sl, iu], axis=1))
    rs = np.ones((64, TS), f)
    rs[:, ::C] = 0.0
    m["reset"] = rs
    m["sgmask"] = np.triu(np.ones((128, 128), f), 0)
    m["iota"] = np.ascontiguousarray(np.broadcast_to(np.arange(FR, dtype=f) + 1.0, (128, FR)))
    sel = np.zeros((128, 2), f)
    sel[(np.arange(128) % 32) < 16, 0] = 1.0
    sel[(np.arange(128) % 32) >= 16, 1] = 1.0
    m["sel16"] = sel
    m["ada_w"] = np.ascontiguousarray(inp["ada_w"].reshape(L, 8, 128, 48, 128).transpose(0, 3, 2, 1, 4))
    m["ada_b"] = np.stack([col_pk(inp["ada_b"][l]) for l in range(L)])
    m["g_mix"] = np.stack([col_pk(inp["norm_mix_g"][l]) for l in range(L)])
    m["g_ffn"] = np.stack([col_pk(inp["norm_ffn_g"][l]) for l in range(L)])
    m["g_fin"] = col_pk(inp["final_g"])
    w_in = inp["w_in"]
    m["w_in64"] = np.stack([tile_km(w_in[l][:, :768], 64) for l in range(L)])
    m["w_in128"] = np.stack([tile_km(w_in[l][:, 768:], 128) for l in range(L)])
    mu = inp["rwkv_mu"]
    m["mu_rkv"] = np.stack([np.concatenate([col_pk(mu[l][q * 256:(q + 1) * 256], 64) for q in range(3)], axis=1)
                            for l in range(L)])
    m["mu_lora"] = np.stack([mu[l][768:896].reshape(128, 1) for l in range(L)])
    hp = lambda k: np.stack([col_pk(inp[k][l], 64) for l in range(inp[k].shape[0])])
    rk = inp["rwkv_rk"].reshape(L, 256)
    m["hpar"] = np.ascontiguousarray(np.stack(
        [hp("rwkv_w0"), hp("rwkv_a0"), hp("rwkv_kk"), hp("rwkv_ka"), hp("rwkv_lnx_w"), hp("rwkv_lnx_b"),
         np.stack([col_pk(rk[l], 64) for l in range(L)])], axis=2))
    m["v0"] = hp("rwkv_v0")
    m["lora_up"] = np.ascontiguousarray(np.concatenate([inp["rwkv_w2"], inp["rwkv_a2"], inp["rwkv_g2"]], axis=1))
    m["v1"] = np.ascontiguousarray(inp["rwkv_v1"].reshape(1, 4, 64, 32).transpose(0, 2, 1, 3))
    m["v2"] = np.ascontiguousarray(inp["rwkv_v2"])
    m["rwkv_out"] = np.ascontiguousarray(inp["rwkv_out"].reshape(L, 4, 64, 1024).transpose(0, 2, 1, 3))
    k2 = lambda k: np.ascontiguousarray(inp[k].reshape(L, 2, 128, -1).transpose(0, 2, 1, 3))
    m["sg_out"] = k2("sg_out")
    m["conv_out"] = k2("conv_out")
    m["glu_w"] = k2("s5_glu_w")
    m["sg_ln"] = np.stack([np.concatenate([col_pk(inp["sg_ln_w"][l]), col_pk(inp["sg_ln_b"][l])], axis=1) for l in range(L)])
    m["sg_wsT"] = np.ascontiguousarray(inp["sg_ws"].transpose(0, 3, 1, 2))
    bs = inp["sg_bs"]
    m["sg_bs"] = np.ascontiguousarray(np.stack(
        [np.stack([np.repeat(bs[l, 2 * ct:2 * ct + 2], 64, axis=0) for ct in range(2)], axis=1) for l in range(L)]))
    m["conv_w"] = np.ascontiguousarray(inp["conv_w"].reshape(L, 3, 2, 128).transpose(0, 3, 2, 1))
    pair = lambda a: np.ascontiguousarray(a.reshape(L, 8, 128).transpose(0, 2, 1))
    m["s5_lam"] = np.ascontiguousarray(np.stack(
        [pair(inp["s5_a_re"]), pair(inp["s5_a_im"]),
         pair(np.repeat(inp["s5_log_dt"][:, :, None], 64, axis=2))], axis=2))
    def bk(a):
        return np.ascontiguousarray(a.reshape(L, 2, 8, 64, 16).transpose(0, 2, 4, 1, 3).reshape(L, 128, 2, 64))
    def lk(a):
        a3 = a if a.ndim == 3 else np.repeat(a[:, :, None], 64, axis=2)
        r = np.repeat(a3.reshape(L, 2, 8, 1, 64), 16, axis=3)
        return np.ascontiguousarray(r.transpose(0, 2, 3, 1, 4).reshape(L, 128, 2, 64))
    m["s5_bk"] = np.ascontiguousarray(np.stack(
        [bk(inp["s5_b_re"]), bk(inp["s5_b_im"]), lk(inp["s5_a_re"]), lk(inp["s5_a_im"]), lk(inp["s5_log_dt"])], axis=2))
    lc = np.zeros((L, 8, 128, 2, 128), f)
    for jp in range(8):
        for gg in range(2):
            g = 2 * jp + gg
            cs = (jp % 4) * 32 + gg * 16
            lc[:, jp, gg * 64:(gg + 1) * 64, 0, cs:cs + 16] = inp["s5_c_re"][:, g].transpose(0, 2, 1)
            lc[:, jp, gg * 64:(gg + 1) * 64, 1, cs:cs + 16] = inp["s5_c_im"][:, g].transpose(0, 2, 1)
    m["s5_lc"] = np.ascontiguousarray(lc.transpose(0, 2, 1, 3, 4))
    m["s5_d"] = np.stack([col_pk(inp["s5_d"][l]) for l in range(L)])
    m["w_o"] = np.stack([tile_km(inp["w_o"][l], 128) for l in range(L)])
    m["ffn_w1"] = np.stack([tile_km(inp["ffn_w1"][l], 128) for l in range(L)])
    m["ffn_w2"] = np.stack([tile_km(inp["ffn_w2"][l], 128) for l in range(L)])
    return {k: np.ascontiguousarray(v, dtype=f) for k, v in m.items()}


def build(shapes, T):
    NS = T // TS
    NF = TS // FR
    nc = bass.Bass("TRN2", target_bir_lowering=False)
    dr = {}
    for k, shp in shapes.items():
        dr[k] = nc.dram_tensor(k, list(shp), F32, kind="ExternalInput").ap()
    outT = nc.dram_tensor("outT", [D, T], F32, kind="ExternalOutput").ap()
    xs_d = nc.dram_tensor("xs_scr", [D, T], F32, kind="Internal").ap()
    vf_d = nc.dram_tensor("vf_scr", [64, 4, T], F32, kind="Internal").ap()

    with contextlib.ExitStack() as st:
        S = Sched(nc, st)
        do, dma = S.do, S.dma
        B_outs = [Buf("outT%d" % i) for i in range(NS)]
        B_xs = [Buf("xs%d" % i) for i in range(NS)]
        B_vf = [Buf("vf%d" % i) for i in range(NS)]

        def cload(name, key, shape):
            b = S.sb(name, shape, F32)
            dma(b[:], dr[key], W=[b])
            return b
        ident = cload("ident", "ident", [128, 128])
        ones = cload("ones", "ones", [128, 128])
        msk5 = cload("msk5", "msk5", [64, 320])
        reset = cload("reset", "reset", [64, TS])
        sgmask = cload("sgmask", "sgmask", [128, 128])
        iota = cload("iota", "iota", [128, FR])
        sel16 = cload("sel16", "sel16", [128, 2])
        cT = cload("cT", "cT", [128, 8])
        g_fin = cload("g_fin", "g_fin", [128, 8])
        ones_bf = S.sb("ones_bf", [128, 128], BF16)
        do("dve", lambda e: e.tensor_copy(ones_bf[:], ones[:]), R=[ones], W=[ones_bf])
        csil = S.sb("csil", [128, 8], F32)
        do("act", lambda e: e.activation(csil[:], cT[:], AF.Silu), R=[cT], W=[csil])
        negpi = S.sb("negpi", [128, 1], F32)
        do("dve", lambda e: e.memset(negpi[:], -math.pi), W=[negpi])

        PS = [S.ps("ps%d" % i, [128, 512]) for i in range(8)]
        ps_rr = [0]

        def psum():
            b = PS[ps_rr[0] % 3]
            ps_rr[0] += 1
            return b

        X = S.sb("X", [128, 8, TS], F32)
        hT = S.sb("hT", [128, 8, TS], BF16)
        aT = S.sb("aT", [128, 8, TS], BF16)
        sqb = S.sb("sqb", [128, TS], BF16)
        rstd = S.sb("rstd", [128, TS], F32)
        xn = S.sb("xn", [128, TS], F32)
        mixA = S.sb("mixA", [64, 4, TS], BF16)
        mixB = S.sb("mixB", [128, 2, TS], BF16)
        mixC = S.sb("mixC", [128, 2, TS], BF16)
        mixD = S.sb("mixD", [128, 2, TS], BF16)
        w128 = Ring(S, "w128_", 6, [128, 8, 128], BF16)
        wsm = Ring(S, "wsm_", 8, [128, 2, 128], BF16)
        w64 = Ring(S, "w64_", 3, [128, 8, 64], BF16)
        wra = Ring(S, "wra_", 2, [64, 4, 128], BF16)
        modT = S.sb("modT", [128, 48], F32)
        geff1 = S.sb("geff1", [128, 8], F32)
        geff2 = S.sb("geff2", [128, 8], F32)
        tmp48 = S.sb("tmp48", [128, 48], F32)
        work = Ring(S, "work_", 3, [128, TS], F32)
        Z = [S.sb("Z%d" % i, [128, TS], F32) for i in range(6)]
        PL = S.sb("PL", [128, TS], F32)
        LA = S.sb("LA", [128, TS], F32)
        H = {nm: S.sb("H_" + nm, [64, TS], F32) for nm in ("r", "k", "d", "e", "c", "w", "m", "p", "a", "g", "q")}
        Vall = S.sb("Vall", [64, 4, TS], F32)
        carry = S.sb("carry", [64, 12], F32)
        carryL = S.sb("carryL", [128, 1], F32)
        VV = S.sb("VV", [32, TS], F32)
        STt = S.sb("STt", [64, 4, 64], F32)
        STW = S.sb("STW", [64, 64], F32)
        NB = 4
        SCx = S.sb("SCx", [64, NB, 128], F32)
        SCb = S.sb("SCb", [64, NB, 192], BF16)
        TMb = S.sb("TMb", [64, NB, 256], BF16)
        XS = [S.sb("XS%d" % i, [64, NB, 64], F32) for i in range(2)]
        YS = [S.sb("YS%d" % i, [64, NB, 64], F32) for i in range(2)]
        TT = S.sb("TT", [64, NB, 64], F32)
        TTb = S.sb("TTb", [64, NB, 64], BF16)
        APMb = S.sb("APMb", [64, NB, 128], BF16)
        PP = []
        for par in range(2):
            d_ = {k: S.sb("%s_%d" % (k, par), [64, TS], BF16) for k in ("Atb", "Btb", "Ktb", "Rtb", "Vb")}
            if par == 0:
                d_.update({"Hw": H["w"], "Hg": H["g"], "Hq": H["q"]})
            else:
                d_.update({k: S.sb("%s_%d" % (k, par), [64, TS], F32) for k in ("Hw", "Hg", "Hq")})
            PP.append(d_)
        STb = S.sb("STb", [64, 64], BF16)
        UTb = S.sb("UTb", [64, 64], BF16)
        ident_bf = S.sb("ident_bf", [128, 128], BF16)
        mu_rkv = S.sb("mu_rkv", [64, 12], F32)
        mu_lora = S.sb("mu_lora", [128, 1], F32)
        hpar = S.sb("hpar", [64, 7, 4], F32)
        omka = S.sb("omka", [64, 4], F32)
        v0 = S.sb("v0", [64, 4], F32)
        lora_up = S.sb("lora_up", [128, 256], F32)
        v1 = S.sb("v1", [64, 4, 32], F32)
        v2 = S.sb("v2", [32, 256], F32)
        sg_ln = S.sb("sg_ln", [128, 4], F32)
        sg_wb = S.sb("sg_wb", [128, 4, 128], BF16)
        sg_bs = S.sb("sg_bs", [128, 2, 128], F32)
        conv_w = S.sb("conv_w", [128, 2, 3], F32)
        zc = S.sb("zc", [128, 2, TS + 2], F32)
        vz = [S.sb("vz%d" % i, [128, 128], BF16) for i in range(4)]
        s5_lam = S.sb("s5_lam", [128, 3, 8], F32)
        s5_bk = S.sb("s5_bk", [128, 5, 2, 64], F32)
        s5_lc = S.sb("s5_lc", [128, 8, 2, 128], BF16)
        s5_d = S.sb("s5_d", [128, 2], F32)
        LB = S.sb("LB", [128, 8, 2, 128], BF16)
        cosT = S.sb("cosT", [128, 8, FR], F32)
        sinT = S.sb("sinT", [128, 8, FR], F32)
        rho = S.sb("rho", [128, 8], F32)
        th = S.sb("th", [128, 8], F32)
        s5st = S.sb("s5st", [128, 8, 2], F32)
        s5c = S.sb("s5c", [128, 2], F32)
        small = Ring(S, "small_", 4, [128, 64], F32)
        uT = S.sb("uT", [128, 2, TS], F32)
        uTb = S.sb("uTb", [128, 2, TS], BF16)
        xrb = S.sb("xrb", [128, TS], BF16)
        xib = S.sb("xib", [128, TS], BF16)
        tP = {nm: S.sb("s5p_" + nm, [128, 8], F32) for nm in ("lre", "dt", "mag", "ang", "sn", "cs", "abr", "abi", "den", "qre", "qim", "t1", "t2")}
        tK = {nm: S.sb("s5k_" + nm, [128, 2, 64], F32) for nm in ("lre", "dt", "mag", "ang", "sn", "cs", "abr", "abi", "den", "qre", "qim", "t1", "t2")}
        bbr = S.sb("bbr", [128, 2, 64], F32)
        bbi = S.sb("bbi", [128, 2, 64], F32)
        tq = S.sb("tq", [128, 2, 64], F32)

        def zero(b, ap=None):
            do("dve", lambda e: e.memset(b[:] if ap is None else ap, 0.0), W=[b])
        for b in vz:
            zero(b)
        do("dve", lambda e: e.tensor_copy(ident_bf[:], ident[:]), R=[ident], W=[ident_bf])
        zero(LB)
        print("SBUF bytes/partition:", S.sb_bytes)

        def wload(buf, src, ap=None):
            dma(buf[:] if ap is None else ap, src, W=[buf], q="pool")
            return buf

        def rms_stats():
            pb = psum()
            for ft in range(8):
                do("act", lambda e: e.activation(sqb[:], X[:, ft, :], AF.Square), R=[X], W=[sqb])
                do("pe", lambda e: e.matmul(pb[:], ones_bf[:], sqb[:], start=(ft == 0), stop=(ft == 7)), R=[ones_bf, sqb], W=[pb])
            do("dve", lambda e: e.tensor_scalar(rstd[:], pb[:], 1.0 / D, EPS, ALU.mult, ALU.add), R=[pb], W=[rstd])
            do("act", lambda e: e.activation(rstd[:], rstd[:], AF.Sqrt), R=[rstd], W=[rstd])
            do("dve", lambda e: e.reciprocal(rstd[:], rstd[:]), R=[rstd], W=[rstd])

        def rmsnorm_to(dst, geff, shift_col0):
            rms_stats()
            for ft in range(8):
                do("dve", lambda e: e.tensor_tensor(xn[:], X[:, ft, :], rstd[:], ALU.mult), R=[X, rstd], W=[xn])
                do("act", lambda e: e.activation(dst[:, ft, :], xn[:], AF.Identity, bias=modT[:, shift_col0 + ft:shift_col0 + ft + 1],
                                                 scale=geff[:, ft:ft + 1]), R=[xn, modT, geff], W=[dst])

        def proj128(src_ap, rhs_buf, rhs_fn, nk, ring, evac):
            wb = wload(ring.next(), src_ap)
            pb = psum()
            for kt in range(nk):
                do("pe", lambda e: e.matmul(pb[:], wb[:, kt, :], rhs_fn(kt), start=(kt == 0), stop=(kt == nk - 1)), R=[wb, rhs_buf], W=[pb])
            evac(pb)

        def s5_disc(t, are, aim, ldt, srcb):
            A = lambda nm: t[nm][:]
            B = lambda nm: t[nm]
            do("dve", lambda e: e.tensor_scalar(A("lre"), are, -1e-4, None, ALU.min), R=[srcb], W=[B("lre")])
            do("act", lambda e: e.activation(A("dt"), ldt, AF.Exp), R=[srcb], W=[B("dt")])
            do("dve", lambda e: e.tensor_tensor(A("t1"), A("lre"), A("dt"), ALU.mult), R=[B("lre"), B("dt")], W=[B("t1")])
            do("act", lambda e: e.activation(A("mag"), A("t1"), AF.Exp), R=[B("t1")], W=[B("mag")])
            do("dve", lambda e: e.tensor_tensor(A("ang"), aim, A("dt"), ALU.mult), R=[srcb, B("dt")], W=[B("ang")])
            for off, dst in ((0.0, "sn"), (0.5 * math.pi, "cs")):
                do("dve", lambda e: e.tensor_scalar(A("t1"), A("ang"), off, None, ALU.add), R=[B("ang")], W=[B("t1")])
                do("dve", lambda e: e.tensor_scalar(A("t2"), A("t1"), 1.0 / (2 * math.pi), MAGIC, ALU.mult, ALU.add), R=[B("t1")], W=[B("t2")])
                do("dve", lambda e: e.tensor_scalar(A("t2"), A("t2"), -MAGIC, None, ALU.add), R=[B("t2")], W=[B("t2")])
                do("dve", lambda e: e.scalar_tensor_tensor(A("t1"), A("t2"), -2 * math.pi, A("t1"), ALU.mult, ALU.add), R=[B("t2"), B("t1")], W=[B("t1")])
                do("act", lambda e: e.activation(A(dst), A("t1"), AF.Sin), R=[B("t1")], W=[B(dst)])
            do("dve", lambda e: e.tensor_tensor(A("abr"), A("mag"), A("cs"), ALU.mult), R=[B("mag"), B("cs")], W=[B("abr")])
            do("dve", lambda e: e.tensor_tensor(A("abi"), A("mag"), A("sn"), ALU.mult), R=[B("mag"), B("sn")], W=[B("abi")])
            do("dve", lambda e: e.tensor_tensor(A("den"), A("lre"), A("lre"), ALU.mult), R=[B("lre")], W=[B("den")])
            do("dve", lambda e: e.tensor_tensor(A("t1"), aim, aim, ALU.mult), R=[srcb], W=[B("t1")])
            do("dve", lambda e: e.tensor_tensor(A("den"), A("den"), A("t1"), ALU.add), R=[B("den"), B("t1")], W=[B("den")])
            do("dve", lambda e: e.reciprocal(A("den"), A("den")), R=[B("den")], W=[B("den")])
            do("dve", lambda e: e.tensor_scalar(A("t1"), A("abr"), -1.0, None, ALU.add), R=[B("abr")], W=[B("t1")])
            do("dve", lambda e: e.tensor_tensor(A("qre"), A("t1"), A("lre"), ALU.mult), R=[B("t1"), B("lre")], W=[B("qre")])
            do("dve", lambda e: e.tensor_tensor(A("t2"), A("abi"), aim, ALU.mult), R=[B("abi"), srcb], W=[B("t2")])
            do("dve", lambda e: e.tensor_tensor(A("qre"), A("qre"), A("t2"), ALU.add), R=[B("qre"), B("t2")], W=[B("qre")])
            do("dve", lambda e: e.tensor_tensor(A("qre"), A("qre"), A("den"), ALU.mult), R=[B("qre"), B("den")], W=[B("qre")])
            do("dve", lambda e: e.tensor_tensor(A("qim"), A("abi"), A("lre"), ALU.mult), R=[B("abi"), B("lre")], W=[B("qim")])
            do("dve", lambda e: e.tensor_tensor(A("t2"), A("t1"), aim, ALU.mult), R=[B("t1"), srcb], W=[B("t2")])
            do("dve", lambda e: e.tensor_tensor(A("qim"), A("qim"), A("t2"), ALU.subtract), R=[B("qim"), B("t2")], W=[B("qim")])
            do("dve", lambda e: e.tensor_tensor(A("qim"), A("qim"), A("den"), ALU.mult), R=[B("qim"), B("den")], W=[B("qim")])

        for l in range(2):
            pmod = PS[4]
            for mt in range(48):
                wb = Z[mt % 2 * 2]
                wb2 = Z[mt % 2 * 2 + 1]
                dma(wb[:].rearrange("p (k m) -> p k m", k=4), dr["ada_w"][l, mt][:, 0:4, :], W=[wb])
                dma(wb2[:].rearrange("p (k m) -> p k m", k=4), dr["ada_w"][l, mt][:, 4:8, :], W=[wb2])
                for kt in range(8):
                    wv = (wb if kt < 4 else wb2)
                    do("pe", lambda e: e.matmul(pmod[:, mt:mt + 1], wv[:, (kt % 4) * 128:(kt % 4 + 1) * 128], csil[:, kt:kt + 1],
                                                start=(kt == 0), stop=(kt == 7)), R=[wv, csil], W=[pmod])
            dma(tmp48[:], dr["ada_b"][l], W=[tmp48])
            do("dve", lambda e: e.tensor_tensor(modT[:], pmod[:, 0:48], tmp48[:], ALU.add), R=[pmod, tmp48], W=[modT])
            gm = small.next()
            dma(gm[:, 0:8], dr["g_mix"][l], W=[gm])
            do("dve", lambda e: e.scalar_tensor_tensor(geff1[:], modT[:, 8:16], 1.0, gm[:, 0:8], ALU.add, ALU.mult), R=[modT, gm], W=[geff1])
            gf = small.next()
            dma(gf[:, 0:8], dr["g_ffn"][l], W=[gf])
            do("dve", lambda e: e.scalar_tensor_tensor(geff2[:], modT[:, 32:40], 1.0, gf[:, 0:8], ALU.add, ALU.mult), R=[modT, gf], W=[geff2])

            dma(mu_rkv[:], dr["mu_rkv"][l], W=[mu_rkv])
            dma(mu_lora[:], dr["mu_lora"][l], W=[mu_lora])
            dma(hpar[:], dr["hpar"][l], W=[hpar])
            do("dve", lambda e: e.tensor_scalar(omka[:], hpar[:, 3, :], -1.0, 1.0, ALU.mult, ALU.add), R=[hpar], W=[omka])
            dma(lora_up[:], dr["lora_up"][l], W=[lora_up])
            if l == 1:
                dma(v0[:], dr["v0"][0], W=[v0])
                dma(v1[:], dr["v1"][0], W=[v1])
                dma(v2[:], dr["v2"][0], W=[v2])
            dma(sg_ln[:], dr["sg_ln"][l], W=[sg_ln])
            sgw = Z[4]
            dma(sgw[:].rearrange("p (g t) -> p g t", g=4), dr["sg_wsT"][l], W=[sgw])
            for g in range(4):
                do("dve", lambda e: e.tensor_tensor(sg_wb[:, g, :], sgw[:, g * 128:(g + 1) * 128], sgmask[:], ALU.mult), R=[sgw, sgmask], W=[sg_wb])
            dma(sg_bs[:], dr["sg_bs"][l], W=[sg_bs])
            dma(conv_w[:], dr["conv_w"][l], W=[conv_w])
            dma(s5_lam[:], dr["s5_lam"][l], W=[s5_lam])
            dma(s5_bk[:], dr["s5_bk"][l], W=[s5_bk])
            wload(s5_lc, dr["s5_lc"][l])
            dma(s5_d[:], dr["s5_d"][l], W=[s5_d])
            do("dve", lambda e: e.tensor_scalar(s5_lc[:, :, 1, :], s5_lc[:, :, 1, :], -1.0, None, ALU.mult), R=[s5_lc], W=[s5_lc])

            s5_disc(tP, s5_lam[:, 0, :], s5_lam[:, 1, :], s5_lam[:, 2, :], s5_lam)
            do("dve", lambda e: e.tensor_copy(rho[:], tP["mag"][:]), R=[tP["mag"]], W=[rho])
            do("dve", lambda e: e.tensor_copy(th[:], tP["ang"][:]), R=[tP["ang"]], W=[th])
            for jp in range(8):
                for off, dstT in ((0.0, sinT), (0.5 * math.pi, cosT)):
                    wk = work.next()
                    wk2 = work.next()
                    do("dve", lambda e: e.tensor_scalar(wk[:, 0:FR], iota[:], th[:, jp:jp + 1], off, ALU.mult, ALU.add), R=[iota, th], W=[wk])
                    do("dve", lambda e: e.tensor_scalar(wk2[:, 0:FR], wk[:, 0:FR], 1.0 / (2 * math.pi), MAGIC, ALU.mult, ALU.add), R=[wk], W=[wk2])
                    do("dve", lambda e: e.tensor_scalar(wk2[:, 0:FR], wk2[:, 0:FR], -MAGIC, None, ALU.add), R=[wk2], W=[wk2])
                    do("dve", lambda e: e.scalar_tensor_tensor(wk[:, 0:FR], wk2[:, 0:FR], -2 * math.pi, wk[:, 0:FR], ALU.mult, ALU.add), R=[wk2, wk], W=[wk])
                    do("act", lambda e: e.activation(dstT[:, jp, :], wk[:, 0:FR], AF.Sin), R=[wk], W=[dstT])
            s5_disc(tK, s5_bk[:, 2], s5_bk[:, 3], s5_bk[:, 4], s5_bk)
            do("dve", lambda e: e.tensor_tensor(bbr[:], tK["qre"][:], s5_bk[:, 0], ALU.mult), R=[tK["qre"], s5_bk], W=[bbr])
            do("dve", lambda e: e.tensor_tensor(tq[:], tK["qim"][:], s5_bk[:, 1], ALU.mult), R=[tK["qim"], s5_bk], W=[tq])
            do("dve", lambda e: e.tensor_tensor(bbr[:], bbr[:], tq[:], ALU.subtract), R=[bbr, tq], W=[bbr])
            do("dve", lambda e: e.tensor_tensor(bbi[:], tK["qre"][:], s5_bk[:, 1], ALU.mult), R=[tK["qre"], s5_bk], W=[bbi])
            do("dve", lambda e: e.tensor_tensor(tq[:], tK["qim"][:], s5_bk[:, 0], ALU.mult), R=[tK["qim"], s5_bk], W=[tq])
            do("dve", lambda e: e.tensor_tensor(bbi[:], bbi[:], tq[:], ALU.add), R=[bbi, tq], W=[bbi])
            for jp in range(8):
                kt, r0 = jp // 4, (jp % 4) * 32
                for ri, bb in ((0, bbr), (1, bbi)):
                    for gg in range(2):
                        do("dve", lambda e: e.tensor_scalar(LB[r0:r0 + 32, jp, ri, gg * 64:(gg + 1) * 64], bb[r0:r0 + 32, kt, :],
                                                            sel16[r0:r0 + 32, gg:gg + 1], None, ALU.mult), R=[bb, sel16], W=[LB])

            zero(STt)
            zero(carry)
            zero(carryL)
            zero(s5st)
            zero(zc, zc[:, :, 0:2])

            for si in range(NS):
                t0 = si * TS
                src = dr["xT"] if l == 0 else xs_d
                srcb = [] if l == 0 else [B_xs[si]]
                dma(X[:], src[:, t0:t0 + TS].rearrange("(ft p) t -> p ft t", p=128), R=srcb, W=[X])
                rmsnorm_to(hT, geff1, 0)
                hfn = lambda kt: hT[:, kt, :]

                def s5_gen():
                    for kt in range(2):
                        def ev_u(pb, kt=kt):
                            do("act", lambda e: e.copy(uT[:, kt, :], pb[:]), R=[pb], W=[uT])
                            do("dve", lambda e: e.tensor_copy(uTb[:, kt, :], pb[:]), R=[pb], W=[uTb])
                        proj128(dr["w_in128"][l, 11 + kt], hT, hfn, 8, w128, ev_u)
                    py = [PS[3], PS[3]]
                    f3 = lambda ap: ap.rearrange("p (f s) -> p f s", f=NF)
                    for jp in range(8):
                        kt = jp // 4
                        pre = psum()
                        do("pe", lambda e: e.matmul(pre[:], LB[:, jp, 0, :], uTb[:, kt, :], start=True, stop=True), R=[LB, uTb], W=[pre])
                        pim = psum()
                        do("pe", lambda e: e.matmul(pim[:], LB[:, jp, 1, :], uTb[:, kt, :], start=True, stop=True), R=[LB, uTb], W=[pim])
                        bre, bim, mre, mim, tmp = Z[0], Z[1], Z[2], Z[3], Z[4]
                        do("act", lambda e: e.copy(bre[:], pre[:]), R=[pre], W=[bre])
                        do("act", lambda e: e.copy(bim[:], pim[:]), R=[pim], W=[bim])
                        cb = cosT[:, jp, :].unsqueeze(1).to_broadcast([128, NF, FR])
                        sb_ = sinT[:, jp, :].unsqueeze(1).to_broadcast([128, NF, FR])
                        do("dve", lambda e: e.tensor_tensor(f3(mre[:]), f3(bre[:]), cb, ALU.mult), R=[bre, cosT], W=[mre])
                        do("dve", lambda e: e.tensor_tensor(f3(tmp[:]), f3(bim[:]), sb_, ALU.mult), R=[bim, sinT], W=[tmp])
                        yield
                        do("dve", lambda e: e.tensor_tensor(mre[:], mre[:], tmp[:], ALU.add), R=[mre, tmp], W=[mre])
                        yield
                        tmp2 = Z[5]
                        do("dve", lambda e: e.tensor_tensor(f3(mim[:]), f3(bim[:]), cb, ALU.mult), R=[bim, cosT], W=[mim])
                        yield
                        do("dve", lambda e: e.tensor_tensor(f3(tmp2[:]), f3(bre[:]), sb_, ALU.mult), R=[bre, sinT], W=[tmp2])
                        do("dve", lambda e: e.tensor_tensor(mim[:], mim[:], tmp2[:], ALU.subtract), R=[mim, tmp2], W=[mim])
                        yield
                        rb = rho[:, jp:jp + 1].to_broadcast([128, FR])
                        cL, sL = cosT[:, jp, FR - 1:FR], sinT[:, jp, FR - 1:FR]
                        for fi in range(NF):
                            fs = slice(fi * FR, (fi + 1) * FR)
                            last = slice(fi * FR + FR - 1, fi * FR + FR)
                            do("dve", lambda e: e.tensor_tensor_scan(bre[:, fs], rb, mre[:, fs], s5st[:, jp, 0:1], ALU.mult, ALU.add), R=[rho, mre, s5st], W=[bre])
                            do("dve", lambda e: e.tensor_tensor_scan(bim[:, fs], rb, mim[:, fs], s5st[:, jp, 1:2], ALU.mult, ALU.add), R=[rho, mim, s5st], W=[bim])
                            yield
                            do("dve", lambda e: e.tensor_scalar(s5c[:, 0:1], bim[:, last], sL, None, ALU.mult), R=[bim, sinT], W=[s5c])
                            do("dve", lambda e: e.tensor_scalar(s5c[:, 1:2], bre[:, last], sL, None, ALU.mult), R=[bre, sinT], W=[s5c])
                            do("dve", lambda e: e.scalar_tensor_tensor(s5st[:, jp, 0:1], bre[:, last], cL, s5c[:, 0:1], ALU.mult, ALU.subtract), R=[bre, cosT, s5c], W=[s5st])
                            do("dve", lambda e: e.scalar_tensor_tensor(s5st[:, jp, 1:2], bim[:, last], cL, s5c[:, 1:2], ALU.mult, ALU.add), R=[bim, cosT, s5c], W=[s5st])
                            yield
                        do("dve", lambda e: e.tensor_tensor(f3(mre[:]), f3(bre[:]), cb, ALU.mult), R=[bre, cosT], W=[mre])
                        yield
                        do("dve", lambda e: e.tensor_tensor(f3(tmp[:]), f3(bim[:]), sb_, ALU.mult), R=[bim, sinT], W=[tmp])
                        do("dve", lambda e: e.tensor_tensor(xrb[:], mre[:], tmp[:], ALU.subtract), R=[mre, tmp], W=[xrb])
                        yield
                        do("dve", lambda e: e.tensor_tensor(f3(mim[:]), f3(bre[:]), sb_, ALU.mult), R=[bre, sinT], W=[mim])
                        yield
                        do("dve", lambda e: e.tensor_tensor(f3(tmp2[:]), f3(bim[:]), cb, ALU.mult), R=[bim, cosT], W=[tmp2])
                        do("dve", lambda e: e.tensor_tensor(xib[:], mim[:], tmp2[:], ALU.add), R=[mim, tmp2], W=[xib])
                        yield
                        ct = jp // 4
                        do("pe", lambda e: e.matmul(py[ct][:], s5_lc[:, jp, 0, :], xrb[:], start=(jp % 4 == 0), stop=False), R=[s5_lc, xrb], W=[py[ct]])
                        do("pe", lambda e: e.matmul(py[ct][:], s5_lc[:, jp, 1, :], xib[:], start=False, stop=(jp % 4 == 3)), R=[s5_lc, xib], W=[py[ct]])
                        if jp % 4 == 3:
                            do("dve", lambda e: e.scalar_tensor_tensor(uT[:, ct, :], uT[:, ct, :], s5_d[:, ct:ct + 1], py[ct][:], ALU.mult, ALU.add),
                               R=[uT, s5_d, py[ct]], W=[uT])
                            do("act", lambda e: e.activation(mixD[:, ct, :], uT[:, ct, :], AF.Gelu_apprx_tanh), R=[uT], W=[mixD])
                        yield

                def sgc_gen():
                    for mt in range(4):
                        def ev_gelu(pb, mt=mt):
                            do("act", lambda e: e.activation(Z[mt][:], pb[:], AF.Gelu_apprx_tanh), R=[pb], W=[Z[mt]])
                        proj128(dr["w_in128"][l, 1 + mt], hT, hfn, 8, w128, ev_gelu)
                        yield
                    pm1 = psum()
                    for i in range(2):
                        do("pe", lambda e: e.matmul(pm1[:], ones[:], Z[2 + i][:], start=(i == 0), stop=(i == 1)), R=[ones, Z[2 + i]], W=[pm1])
                    pm2 = psum()
                    for i in range(2):
                        sq_ = (Z[5], Z[4])[i]
                        do("dve", lambda e: e.tensor_tensor(sq_[:], Z[2 + i][:], Z[2 + i][:], ALU.mult), R=[Z[2 + i]], W=[sq_])
                        do("pe", lambda e: e.matmul(pm2[:], ones[:], sq_[:], start=(i == 0), stop=(i == 1)), R=[ones, sq_], W=[pm2])
                    do("act", lambda e: e.activation(Z[4][:], pm1[:], AF.Copy, scale=1.0 / 256), R=[pm1], W=[Z[4]])
                    do("dve", lambda e: e.tensor_tensor(Z[5][:], Z[4][:], Z[4][:], ALU.mult), R=[Z[4]], W=[Z[5]])
                    do("dve", lambda e: e.scalar_tensor_tensor(Z[5][:], pm2[:], 1.0 / 256, Z[5][:], ALU.mult, ALU.subtract), R=[pm2, Z[5]], W=[Z[5]])
                    yield
                    do("dve", lambda e: e.tensor_scalar(Z[5][:], Z[5][:], 1e-5, None, ALU.add), R=[Z[5]], W=[Z[5]])
                    do("act", lambda e: e.activation(Z[5][:], Z[5][:], AF.Sqrt), R=[Z[5]], W=[Z[5]])
                    do("dve", lambda e: e.reciprocal(Z[5][:], Z[5][:]), R=[Z[5]], W=[Z[5]])
                    yield
                    for ct in range(2):
                        zv = Z[2 + ct]
                        do("dve", lambda e: e.tensor_tensor(zv[:], zv[:], Z[4][:], ALU.subtract), R=[zv, Z[4]], W=[zv])
                        do("dve", lambda e: e.tensor_tensor(zv[:], zv[:], Z[5][:], ALU.mult), R=[zv, Z[5]], W=[zv])
                        do("dve", lambda e: e.tensor_scalar(zv[:], zv[:], sg_ln[:, ct:ct + 1], sg_ln[:, 2 + ct:3 + ct], ALU.mult, ALU.add), R=[zv, sg_ln], W=[zv])
                        yield
                    for ck in range(TS // 128):
                        ks = slice(ck * 128, (ck + 1) * 128)
                        for ct in range(2):
                            pt = psum()
                            do("pe", lambda e: e.matmul(pt[:, 0:128], Z[2 + ct][:, ks], ident[:], start=True, stop=True), R=[Z[2 + ct], ident], W=[pt])
                            do("act", lambda e: e.copy(vz[ct * 2][:, 0:64], pt[:, 0:64]), R=[pt], W=[vz[ct * 2]])
                            do("dve", lambda e: e.tensor_copy(vz[ct * 2 + 1][:, 64:128], pt[:, 64:128]), R=[pt], W=[vz[ct * 2 + 1]])
                            pmx = psum()
                            for par in range(2):
                                g = ct * 2 + par
                                do("pe", lambda e: e.matmul(pmx[:, 0:128], vz[ct * 2 + par][:], sg_wb[:, g, :], start=(par == 0), stop=(par == 1)),
                                   R=[vz[ct * 2 + par], sg_wb], W=[pmx])
                            wk = work.next()
                            do("dve", lambda e: e.tensor_tensor(wk[:, 0:128], pmx[:, 0:128], sg_bs[:, ct, :], ALU.add), R=[pmx, sg_bs], W=[wk])
                            do("dve", lambda e: e.tensor_tensor(mixB[:, ct, ks], wk[:, 0:128], Z[ct][:, ks], ALU.mult), R=[wk, Z[ct]], W=[mixB])
                            yield
                    for mt in range(6):
                        def ev_c(pb, mt=mt):
                            do("act", lambda e: e.copy(Z[mt][:], pb[:]), R=[pb], W=[Z[mt]])
                        proj128(dr["w_in128"][l, 5 + mt], hT, hfn, 8, w128, ev_c)
                        yield
                    for ct in range(2):
                        do("dve", lambda e: e.tensor_tensor(zc[:, ct, 2:TS + 2], Z[2 + ct][:], Z[4 + ct][:], ALU.mult), R=[Z[2 + ct], Z[4 + ct]], W=[zc])
                        y = Z[2 + ct]
                        do("dve", lambda e: e.tensor_scalar(y[:], zc[:, ct, 0:TS], conv_w[:, ct, 0:1], None, ALU.mult), R=[zc, conv_w], W=[y])
                        do("dve", lambda e: e.scalar_tensor_tensor(y[:], zc[:, ct, 1:TS + 1], conv_w[:, ct, 1:2], y[:], ALU.mult, ALU.add), R=[zc, conv_w, y], W=[y])
                        do("dve", lambda e: e.scalar_tensor_tensor(y[:], zc[:, ct, 2:TS + 2], conv_w[:, ct, 2:3], y[:], ALU.mult, ALU.add), R=[zc, conv_w, y], W=[y])
                        do("dve", lambda e: e.tensor_tensor(mixC[:, ct, :], y[:], Z[ct][:], ALU.mult), R=[y, Z[ct]], W=[mixC])
                        yield
                    do("dve", lambda e: e.tensor_copy(zc[:, :, 0:2], zc[:, :, TS:TS + 2]), R=[zc], W=[zc])
                    yield

                def mix_gen():
                    yield from s5_gen()
                    yield from sgc_gen()

                s5g_ = mix_gen()

                def pump(n=1):
                    if os.environ.get("NOPUMP"):
                        return
                    for _ in range(n):
                        if next(s5g_, "done") == "done":
                            break

                def ev_lora(pb):
                    do("act", lambda e: e.copy(PL[:], pb[:]), R=[pb], W=[PL])
                proj128(dr["w_in128"][l, 0], hT, hfn, 8, w128, ev_lora)
                wk = work.next()
                do("dve", lambda e: e.tensor_tensor(wk[:, 1:TS], PL[:, 0:TS - 1], PL[:, 1:TS], ALU.subtract), R=[PL], W=[wk])
                do("dve", lambda e: e.tensor_tensor(wk[:, 0:1], carryL[:], PL[:, 0:1], ALU.subtract), R=[PL, carryL], W=[wk])
                do("dve", lambda e: e.tensor_copy(carryL[:], PL[:, TS - 1:TS]), R=[PL, wk], W=[carryL])
                do("dve", lambda e: e.scalar_tensor_tensor(PL[:], wk[:], mu_lora[:, 0:1], PL[:], ALU.mult, ALU.add), R=[wk, mu_lora, PL], W=[PL])
                do("act", lambda e: e.activation(LA[0:32, :], PL[0:32, :], AF.Tanh), R=[PL], W=[LA])
                do("act", lambda e: e.copy(LA[32:64, :], PL[32:64, :]), R=[PL], W=[LA])
                do("act", lambda e: e.activation(LA[64:128, :], PL[64:128, :], AF.Sigmoid), R=[PL], W=[LA])

                def rkv_proj(q, h, dst_buf, dst_ap):
                    wb = wload(w64.next(), dr["w_in64"][l, q * 4 + h])
                    pb = psum()
                    for kt in range(8):
                        do("pe", lambda e: e.matmul(pb[0:64, :], wb[:, kt, :], hT[:, kt, :], start=(kt == 0), stop=(kt == 7)), R=[wb, hT], W=[pb])
                    do("act", lambda e: e.copy(dst_ap, pb[0:64, :]), R=[pb], W=[dst_buf])

                def tshift(Qb, Qap, cc):
                    dq = H["d"]
                    do("dve", lambda e: e.tensor_tensor(dq[:, 1:TS], Qap[:, 0:TS - 1], Qap[:, 1:TS], ALU.subtract), R=[Qb], W=[dq])
                    do("dve", lambda e: e.tensor_tensor(dq[:, 0:1], carry[:, cc:cc + 1], Qap[:, 0:1], ALU.subtract), R=[Qb, carry], W=[dq])
                    do("dve", lambda e: e.tensor_copy(carry[:, cc:cc + 1], Qap[:, TS - 1:TS]), R=[Qb, dq], W=[carry])
                    do("dve", lambda e: e.scalar_tensor_tensor(Qap, dq[:], mu_rkv[:, cc:cc + 1], Qap, ALU.mult, ALU.add), R=[dq, mu_rkv, Qb], W=[Qb])

                for h in range(4):
                    rkv_proj(2, h, Vall, Vall[:, h, :])
                    tshift(Vall, Vall[:, h, :], 8 + h)
                if l == 0:
                    dma(vf_d[:, :, t0:t0 + TS], Vall[:], R=[Vall], W=[B_vf[si]])
                else:
                    pv = psum()
                    for h in range(4):
                        do("pe", lambda e: e.matmul(pv[0:32, :], v1[:, h, :], Vall[:, h, :], start=(h == 0), stop=(h == 3)), R=[v1, Vall], W=[pv])
                    do("act", lambda e: e.copy(VV[:], pv[0:32, :]), R=[pv], W=[VV])
                    for h in range(4):
                        pg = psum()
                        do("pe", lambda e: e.matmul(pg[0:64, :], v2[:, h * 64:(h + 1) * 64], VV[:], start=True, stop=True), R=[v2, VV], W=[pg])
                        wk = work.next()
                        do("act", lambda e: e.activation(wk[0:64, :], pg[0:64, :], AF.Sigmoid, bias=v0[:, h:h + 1]), R=[pg, v0], W=[wk])
                        wk2 = work.next()
                        dma(wk2[0:64, :], vf_d[:, h, t0:t0 + TS], R=[B_vf[si]], W=[wk2])
                        do("dve", lambda e: e.tensor_tensor(wk2[0:64, :], wk2[0:64, :], Vall[:, h, :], ALU.subtract), R=[wk2, Vall], W=[wk2])
                        do("dve", lambda e: e.tensor_tensor(wk2[0:64, :], wk2[0:64, :], wk[0:64, :], ALU.mult), R=[wk2, wk], W=[wk2])
                        do("dve", lambda e: e.tensor_tensor(Vall[:, h, :], Vall[:, h, :], wk2[0:64, :], ALU.add), R=[Vall, wk2], W=[Vall])

                def prep_gen(h, P):
                    Hr, Hk, Hd, He, Hc, Hm, Hp, Ha = (H[k] for k in "rkdecmpa")
                    Hw, Hg, Hq, Atb, Btb, Ktb, Rtb, Vb = (P[k] for k in ("Hw", "Hg", "Hq", "Atb", "Btb", "Ktb", "Rtb", "Vb"))
                    rkv_proj(0, h, Hr, Hr[:])
                    yield
                    rkv_proj(1, h, Hk, Hk[:])
                    yield
                    tshift(Hr, Hr[:], h)
                    yield
                    tshift(Hk, Hk[:], 4 + h)
                    yield
                    cs_ = slice(h * 64, (h + 1) * 64)
                    pw = psum()
                    do("pe", lambda e: e.matmul(pw[0:64, :], lora_up[0:32, cs_], LA[0:32, :], start=True, stop=True), R=[lora_up, LA], W=[pw])
                    do("act", lambda e: e.activation(He[:], pw[0:64, :], AF.Sigmoid, bias=hpar[:, 0, h:h + 1]), R=[pw, hpar], W=[He])
                    yield
                    pa = psum()
                    do("pe", lambda e: e.matmul(pa[0:64, :], lora_up[32:64, cs_], LA[32:64, :], start=True, stop=True), R=[lora_up, LA], W=[pa])
                    do("act", lambda e: e.activation(Ha[:], pa[0:64, :], AF.Sigmoid, bias=hpar[:, 1, h:h + 1]), R=[pa, hpar], W=[Ha])
                    yield
                    pg = psum()
                    do("pe", lambda e: e.matmul(pg[0:64, :], lora_up[64:128, cs_], LA[64:128, :], start=True, stop=True), R=[lora_up, LA], W=[pg])
                    do("act", lambda e: e.copy(Hg[:], pg[0:64, :]), R=[pg], W=[Hg])
                    yield
                    do("dve", lambda e: e.tensor_tensor_scan(Hc[:], reset[:], He[:], 0.0, ALU.mult, ALU.add), R=[reset, He], W=[Hc])
                    yield
                    do("act", lambda e: e.activation(Hw[:], Hc[:], AF.Exp, scale=-C0), R=[Hc], W=[Hw])
                    yield
                    do("act", lambda e: e.activation(Hm[:], Hc[:], AF.Exp, scale=C0), R=[Hc], W=[Hm])
                    yield
                    do("dve", lambda e: e.tensor_tensor(Hd[:], Hc[:], He[:], ALU.subtract), R=[Hc, He], W=[Hd])
                    yield
                    do("act", lambda e: e.activation(Hp[:], Hd[:], AF.Exp, scale=-C0), R=[Hd], W=[Hp])
                    yield
                    do("act", lambda e: e.activation(Hq[:], Hk[:], AF.Copy, scale=hpar[:, 2, h:h + 1]), R=[Hk, hpar], W=[Hq])
                    yield
                    do("act", lambda e: e.activation(Hd[:], Hq[:], AF.Square), R=[Hq], W=[Hd])
                    yield
                    pn = psum()
                    do("pe", lambda e: e.matmul(pn[0:64, :], ones[0:64, 0:64], Hd[:], start=True, stop=True), R=[ones, Hd], W=[pn])
                    do("dve", lambda e: e.tensor_scalar(Hc[:], pn[0:64, :], 1e-24, None, ALU.max), R=[pn], W=[Hc])
                    yield
                    do("act", lambda e: e.activation(Hc[:], Hc[:], AF.Sqrt), R=[Hc], W=[Hc])
                    yield
                    do("dve", lambda e: e.reciprocal(Hc[:], Hc[:]), R=[Hc], W=[Hc])
                    yield
                    do("dve", lambda e: e.tensor_tensor(Hq[:], Hq[:], Hc[:], ALU.mult), R=[Hq, Hc], W=[Hq])
                    yield
                    do("dve", lambda e: e.tensor_scalar(Hd[:], Ha[:], hpar[:, 3, h:h + 1], omka[:, h:h + 1], ALU.mult, ALU.add), R=[Ha, hpar, omka], W=[Hd])
                    yield
                    do("dve", lambda e: e.tensor_tensor(Hk[:], Hk[:], Hd[:], ALU.mult), R=[Hk, Hd], W=[Hk])
                    yield
                    do("dve", lambda e: e.scalar_tensor_tensor(Atb[:], Hq[:], -1.0, Hp[:], ALU.mult, ALU.mult), R=[Hq, Hp], W=[Atb])
                    yield
                    do("dve", lambda e: e.tensor_tensor(Ha[:], Hq[:], Ha[:], ALU.mult), R=[Hq, Ha], W=[Ha])
                    yield
                    do("dve", lambda e: e.tensor_tensor(Btb[:], Ha[:], Hm[:], ALU.mult), R=[Ha, Hm], W=[Btb])
                    yield
                    do("dve", lambda e: e.tensor_tensor(Ktb[:], Hk[:], Hm[:], ALU.mult), R=[Hk, Hm], W=[Ktb])
                    yield
                    do("act", lambda e: e.copy(Vb[:], Vall[:, h, :]), R=[Vall], W=[Vb])
                    yield
                    do("dve", lambda e: e.scalar_tensor_tensor(Hd[:], Hr[:], hpar[:, 6, h:h + 1], Hk[:], ALU.mult, ALU.mult), R=[Hr, hpar, Hk], W=[Hd])
                    yield
                    pn = psum()
                    do("pe", lambda e: e.matmul(pn[0:64, :], ones[0:64, 0:64], Hd[:], start=True, stop=True), R=[ones, Hd], W=[pn])
                    do("dve", lambda e: e.tensor_tensor(Hq[:], pn[0:64, :], Vall[:, h, :], ALU.mult), R=[pn, Vall], W=[Hq])
                    yield
                    do("dve", lambda e: e.tensor_tensor(Rtb[:], Hr[:], Hw[:], ALU.mult), R=[Hr, Hw], W=[Rtb])
                    yield

                def wkv(h, P, pump):
                    Hw, Atb, Btb, Ktb, Rtb, Vb = (P[k] for k in ("Hw", "Atb", "Btb", "Ktb", "Rtb", "Vb"))
                    At, Bt, Kt, Rt = Atb, Btb, Ktb, Rtb
                    PO = PS[7]
                    do("act", lambda e: e.copy(STb[:], STt[:, h, :]), R=[STt], W=[STb])
                    for bt in range(NCH // NB):
                        for q in range(NB):
                            c = bt * NB + q
                            cs2 = slice(c * C, (c + 1) * C)
                            pb = psum()
                            a_, b_, k_, r_ = At[:, cs2], Bt[:, cs2], Kt[:, cs2], Rt[:, cs2]
                            do("pe", lambda e: e.matmul(pb[0:64, 0:64], b_, a_, start=True, stop=True), R=[Bt, At], W=[pb])
                            do("pe", lambda e: e.matmul(pb[0:64, 64:128], a_, b_, start=True, stop=True), R=[At, Bt], W=[pb])
                            do("pe", lambda e: e.matmul(pb[0:64, 128:192], b_, r_, start=True, stop=True), R=[Bt, Rt], W=[pb])
                            do("pe", lambda e: e.matmul(pb[0:64, 192:256], a_, k_, start=True, stop=True), R=[At, Kt], W=[pb])
                            do("pe", lambda e: e.matmul(pb[0:64, 256:320], k_, r_, start=True, stop=True), R=[Kt, Rt], W=[pb])
                            do("dve", lambda e: e.tensor_tensor(SCx[:, q, :], pb[0:64, 0:128], msk5[:, 0:128], ALU.mult), R=[pb, msk5], W=[SCx])
                            do("dve", lambda e: e.tensor_tensor(SCb[:, q, :], pb[0:64, 128:320], msk5[:, 128:320], ALU.mult), R=[pb, msk5], W=[SCb])
                            pt = psum()
                            v_ = Vb[:, cs2]
                            for qi, (sb_, s_) in enumerate(((At, a_), (Bt, b_), (Kt, k_), (Vb, v_))):
                                do("pe", lambda e: e.matmul(pt[0:64, qi * 64:(qi + 1) * 64], s_, ident_bf[0:64, 0:64], start=True, stop=True),
                                   R=[sb_, ident_bf], W=[pt])
                            do("act", lambda e: e.copy(TMb[:, q, :], pt[0:64, 0:256]), R=[pt], W=[TMb])
                            pump()
                        do("dve", lambda e: e.tensor_tensor(TT[:], SCx[:, :, 0:64], ident[0:64, 0:64].unsqueeze(1).to_broadcast([64, NB, 64]), ALU.add),
                           R=[SCx, ident], W=[TT])
                        Xb, Xf = SCx, (lambda q: SCx[:, q, 0:64])
                        Yb, Yf = SCx, (lambda q: SCx[:, q, 64:128])
                        cur = 0
                        for lev in range(1, 6):
                            Xn_, Yn_ = XS[cur], YS[cur]
                            if lev < 5:
                                pb = PS[4]
                                for q in range(NB):
                                    do("pe", lambda e: e.matmul(pb[0:64, q * 64:(q + 1) * 64], Yf(q), Xf(q), start=True, stop=True), R=[Yb, Xb], W=[pb])
                                do("act", lambda e: e.copy(Xn_[:], pb[0:64, 0:NB * 64].rearrange("p (q t) -> p q t", q=NB)), R=[pb], W=[Xn_])
                                pump()
                            pb = PS[5]
                            for q in range(NB):
                                do("pe", lambda e: e.matmul(pb[0:64, q * 64:(q + 1) * 64], Xf(q), Yf(q), start=True, stop=True), R=[Xb, Yb], W=[pb])
                            do("act", lambda e: e.copy(Yn_[:], pb[0:64, 0:NB * 64].rearrange("p (q t) -> p q t", q=NB)), R=[pb], W=[Yn_])
                            pump()
                            pb = PS[6]
                            for q in range(NB):
                                do("pe", lambda e: e.matmul(pb[0:64, q * 64:(q + 1) * 64], Yn_[:, q, :], TT[:, q, :], start=True, stop=True), R=[Yn_, TT], W=[pb])
                            if lev < 5:
                                do("dve", lambda e: e.tensor_tensor(TT[:], TT[:], pb[0:64, 0:NB * 64].rearrange("p (q t) -> p q t", q=NB), ALU.add),
                                   R=[pb, TT], W=[TT])
                            else:
                                do("dve", lambda e: e.tensor_tensor(TTb[:], TT[:], pb[0:64, 0:NB * 64].rearrange("p (q t) -> p q t", q=NB), ALU.add),
                                   R=[pb, TT], W=[TTb])
                            Xb, Yb = Xn_, Yn_
                            pump()
                            Xf = (lambda q, Xn_=Xn_: Xn_[:, q, :])
                            Yf = (lambda q, Yn_=Yn_: Yn_[:, q, :])
                            cur = 1 - cur
                        pb = PS[4]
                        for q in range(NB):
                            do("pe", lambda e: e.matmul(pb[0:64, q * 128:q * 128 + 64], TMb[:, q, 0:64], TTb[:, q, :], start=True, stop=True), R=[TMb, TTb], W=[pb])
                            do("pe", lambda e: e.matmul(pb[0:64, q * 128 + 64:q * 128 + 128], SCb[:, q, 64:128], TTb[:, q, :], start=True, stop=True), R=[SCb, TTb], W=[pb])
                        do("act", lambda e: e.copy(APMb[:], pb[0:64, 0:NB * 128].rearrange("p (q t) -> p q t", q=NB)), R=[pb], W=[APMb])
                        pump()
                        for q in range(NB):
                            c = bt * NB + q
                            cs2 = slice(c * C, (c + 1) * C)
                            pu = PS[5]
                            do("pe", lambda e: e.matmul(pu[0:64, 0:64], APMb[:, q, 64:128], TMb[:, q, 192:256], start=True, stop=False), R=[APMb, TMb], W=[pu])
                            do("pe", lambda e: e.matmul(pu[0:64, 0:64], APMb[:, q, 0:64], STb[:], start=False, stop=True), R=[APMb, STb], W=[pu])
                            do("act", lambda e: e.copy(UTb[:], pu[0:64, 0:64]), R=[pu], W=[UTb])
                            pump()
                            do("pe", lambda e: e.matmul(PO[0:64, cs2], STb[:], Rt[:, cs2], start=True, stop=False), R=[STb, Rt], W=[PO])
                            do("pe", lambda e: e.matmul(PO[0:64, cs2], TMb[:, q, 192:256], SCb[:, q, 128:192], start=False, stop=False), R=[TMb, SCb], W=[PO])
                            do("pe", lambda e: e.matmul(PO[0:64, cs2], UTb[:], SCb[:, q, 0:64], start=False, stop=True), R=[UTb, SCb], W=[PO])
                            pn = PS[6]
                            do("pe", lambda e: e.matmul(pn[0:64, 0:64], TMb[:, q, 64:128], UTb[:], start=True, stop=False), R=[TMb, UTb], W=[pn])
                            do("pe", lambda e: e.matmul(pn[0:64, 0:64], TMb[:, q, 128:192], TMb[:, q, 192:256], start=False, stop=True), R=[TMb], W=[pn])
                            wc = Hw[:, c * C + C - 1:c * C + C]
                            do("act", lambda e: e.activation(STW[:], STt[:, h, :], AF.Copy, scale=wc), R=[STt, Hw], W=[STW])
                            do("dve", lambda e: e.scalar_tensor_tensor(STb[:], pn[0:64, 0:64], wc, STW[:], ALU.mult, ALU.add), R=[pn, Hw, STW], W=[STb])
                            do("dve", lambda e: e.scalar_tensor_tensor(STt[:, h, :], pn[0:64, 0:64], wc, STW[:], ALU.mult, ALU.add), R=[pn, Hw, STW], W=[STt])
                            pump()

                def post(h, P):
                    Hd, Hc, Hm = H["d"], H["c"], H["m"]
                    Hg, Hq = P["Hg"], P["Hq"]
                    PO = PS[7]
                    OS = Hm
                    do("act", lambda e: e.copy(OS[:], PO[0:64, :]), R=[PO], W=[OS])
                    do("act", lambda e: e.activation(Hd[:], OS[:], AF.Square), R=[OS], W=[Hd])
                    pm1 = psum()
                    do("pe", lambda e: e.matmul(pm1[0:64, :], ones[0:64, 0:64], OS[:], start=True, stop=True), R=[ones, OS], W=[pm1])
                    pm2 = psum()
                    do("pe", lambda e: e.matmul(pm2[0:64, :], ones[0:64, 0:64], Hd[:], start=True, stop=True), R=[ones, Hd], W=[pm2])
                    mu_ = work.next()
                    do("act", lambda e: e.activation(mu_[0:64, :], pm1[0:64, :], AF.Copy, scale=1.0 / 64), R=[pm1], W=[mu_])
                    var = work.next()
                    do("dve", lambda e: e.tensor_tensor(var[0:64, :], mu_[0:64, :], mu_[0:64, :], ALU.mult), R=[mu_], W=[var])
                    do("dve", lambda e: e.scalar_tensor_tensor(var[0:64, :], pm2[0:64, :], 1.0 / 64, var[0:64, :], ALU.mult, ALU.subtract), R=[pm2, var], W=[var])
                    do("dve", lambda e: e.tensor_scalar(var[0:64, :], var[0:64, :], 64e-5, None, ALU.add), R=[var], W=[var])
                    do("act", lambda e: e.activation(var[0:64, :], var[0:64, :], AF.Sqrt), R=[var], W=[var])
                    do("dve", lambda e: e.reciprocal(var[0:64, :], var[0:64, :]), R=[var], W=[var])
                    do("dve", lambda e: e.tensor_tensor(Hc[:], OS[:], mu_[0:64, :], ALU.subtract), R=[OS, mu_], W=[Hc])
                    do("dve", lambda e: e.tensor_tensor(Hc[:], Hc[:], var[0:64, :], ALU.mult), R=[Hc, var], W=[Hc])
                    do("dve", lambda e: e.tensor_scalar(Hc[:], Hc[:], hpar[:, 4, h:h + 1], hpar[:, 5, h:h + 1], ALU.mult, ALU.add), R=[Hc, hpar], W=[Hc])
                    do("dve", lambda e: e.tensor_tensor(Hc[:], Hc[:], Hq[:], ALU.add), R=[Hc, Hq], W=[Hc])
                    do("dve", lambda e: e.tensor_tensor(mixA[:, h, :], Hc[:], Hg[:], ALU.mult), R=[Hc, Hg], W=[mixA])

                prepg = prep_gen(0, PP[0])
                for _ in prepg:
                    pass
                for h in range(4):
                    nxt = prep_gen(h + 1, PP[(h + 1) % 2]) if h < 3 else iter(())

                    def pump2(nxt=nxt):
                        pump()
                        next(nxt, None)
                    wkv(h, PP[h % 2], pump2)
                    for _ in nxt:
                        pass
                    post(h, PP[h % 2])

                for _ in s5g_:
                    pass


                merged = aT
                for ft in range(8):
                    fsl = slice(ft * 128, (ft + 1) * 128)
                    acc = Z[5]
                    gts = [Z[0], Z[1], Z[2], Z[3]]
                    for br in range(4):
                        def ev_g(pb, br=br):
                            do("act", lambda e: e.activation(gts[br][:], pb[:], AF.Sigmoid), R=[pb], W=[gts[br]])
                        proj128(dr["w_in128"][l, 13 + br * 8 + ft], hT, hfn, 8, w128, ev_g)
                    wq = [wload(wsm.next(), dr["sg_out"][l][:, :, fsl]), wload(wsm.next(), dr["conv_out"][l][:, :, fsl]),
                          wload(wsm.next(), dr["glu_w"][l][:, :, fsl]),
                          wload(wsm.next(), dr["glu_w"][l][:, :, 1024 + ft * 128:1024 + (ft + 1) * 128])]
                    wr_ = wload(wra.next(), dr["rwkv_out"][l][:, :, fsl])
                    pa = psum()
                    for h in range(4):
                        do("pe", lambda e: e.matmul(pa[:], wr_[:, h, :], mixA[:, h, :], start=(h == 0), stop=(h == 3)), R=[wr_, mixA], W=[pa])
                    do("dve", lambda e: e.tensor_tensor(acc[:], pa[:], gts[0][:], ALU.mult), R=[pa, gts[0]], W=[acc])
                    for bi, mixT in ((1, mixB), (2, mixC)):
                        pb_ = psum()
                        for kt in range(2):
                            do("pe", lambda e: e.matmul(pb_[:], wq[bi - 1][:, kt, :], mixT[:, kt, :], start=(kt == 0), stop=(kt == 1)), R=[wq[bi - 1], mixT], W=[pb_])
                        do("dve", lambda e: e.tensor_tensor(gts[bi][:], pb_[:], gts[bi][:], ALU.mult), R=[pb_, gts[bi]], W=[gts[bi]])
                        do("dve", lambda e: e.tensor_tensor(acc[:], acc[:], gts[bi][:], ALU.add), R=[acc, gts[bi]], W=[acc])
                    ph1 = psum()
                    for kt in range(2):
                        do("pe", lambda e: e.matmul(ph1[:], wq[2][:, kt, :], mixD[:, kt, :], start=(kt == 0), stop=(kt == 1)), R=[wq[2], mixD], W=[ph1])
                    ph2 = psum()
                    for kt in range(2):
                        do("pe", lambda e: e.matmul(ph2[:], wq[3][:, kt, :], mixD[:, kt, :], start=(kt == 0), stop=(kt == 1)), R=[wq[3], mixD], W=[ph2])
                    do("act", lambda e: e.activation(xn[:], ph2[:], AF.Sigmoid), R=[ph2], W=[xn])
                    do("dve", lambda e: e.tensor_tensor(xn[:], ph1[:], xn[:], ALU.mult), R=[ph1, xn], W=[xn])
                    do("dve", lambda e: e.tensor_tensor(xn[:], xn[:], gts[3][:], ALU.mult), R=[xn, gts[3]], W=[xn])
                    do("dve", lambda e: e.tensor_tensor(merged[:, ft, :], acc[:], xn[:], ALU.add), R=[acc, xn], W=[merged])

                for mt in range(8):
                    def ev_o(pb, mt=mt):
                        do("dve", lambda e: e.scalar_tensor_tensor(X[:, mt, :], pb[:], modT[:, 16 + mt:17 + mt], X[:, mt, :], ALU.mult, ALU.add),
                           R=[pb, modT, X], W=[X])
                    proj128(dr["w_o"][l, mt], merged, lambda kt: merged[:, kt, :], 8, w128, ev_o)

                rmsnorm_to(hT, geff2, 24)
                for qtr in range(4):
                    for i in range(8):
                        def ev_f(pb, i=i):
                            wk = work.next()
                            do("act", lambda e: e.activation(wk[:], pb[:], AF.Relu), R=[pb], W=[wk])
                            do("dve", lambda e: e.tensor_tensor(aT[:, i, :], wk[:], wk[:], ALU.mult), R=[wk], W=[aT])
                        proj128(dr["ffn_w1"][l, qtr * 8 + i], hT, hfn, 8, w128, ev_f)
                    for mt in range(8):
                        def ev_2(pb, mt=mt):
                            do("dve", lambda e: e.scalar_tensor_tensor(X[:, mt, :], pb[:], modT[:, 40 + mt:41 + mt], X[:, mt, :], ALU.mult, ALU.add),
                               R=[pb, modT, X], W=[X])
                        proj128(dr["ffn_w2"][l, mt][:, qtr * 8:(qtr + 1) * 8, :], aT, lambda kt: aT[:, kt, :], 8, w128, ev_2)

                if l == 0:
                    dma(xs_d[:, t0:t0 + TS].rearrange("(ft p) t -> p ft t", p=128), X[:], R=[X], W=[B_xs[si]])
                else:
                    rms_stats()
                    for ft in range(8):
                        do("dve", lambda e: e.scalar_tensor_tensor(X[:, ft, :], X[:, ft, :], g_fin[:, ft:ft + 1], rstd[:], ALU.mult, ALU.mult),
                           R=[X, g_fin, rstd], W=[X])
                    dma(outT[:, t0:t0 + TS].rearrange("(ft p) t -> p ft t", p=128), X[:], R=[X], W=[B_outs[si]])
        S.finish(B_outs)
        print("instructions:", S.n_inst, "waits:", S.n_wait)
    return nc


_CACHE = {}


def run(inputs, T):
    maps = [prep_inputs(inputs, b) for b in range(4)]
    shapes = {k: v.shape for k, v in maps[0].items()}
    key = (T,)
    if key not in _CACHE:
        _CACHE[key] = build(shapes, T)
    nc = _CACHE[key]
    in_maps = [maps[i % 4] for i in range(8)]
    res = run_bass_kernel_spmd(nc, in_maps, core_ids=list(range(8)))
    out = np.stack([np.ascontiguousarray(res.results[b]["outT"].T) for b in range(4)])
    return out.astype(np.float32)


def kernel(**inputs):
    inputs = {k: np.asarray(v, dtype=np.float32) for k, v in inputs.items()}
    T = inputs["x"].shape[1]
    return run(inputs, T)
```

```python
import contextlib
import math
import os
import numpy as np
import concourse.bass as bass
import concourse.mybir as mybir
from concourse.bass_utils import run_bass_kernel_spmd

F32 = mybir.dt.float32
BF16 = mybir.dt.bfloat16
AF = mybir.ActivationFunctionType
ALU = mybir.AluOpType

D = 1024
DM = 256
TS = 512
C = 64
NCH = TS // C
FR = 128
EPS = 1e-6
C0 = math.exp(-0.5)
MAGIC = 12582912.0


class Buf:
    __slots__ = ("name", "t", "lastw", "readers", "psum")

    def __init__(self, name, t=None):
        self.name = name
        self.t = t
        self.lastw = None
        self.readers = []
        self.psum = False

    def __getitem__(self, idx):
        return self.t[idx]


class Sched:
    N_DMA_SEM = 10

    def __init__(self, nc, stack):
        self.nc = nc
        self.stack = stack
        self.engs = {"pe": nc.tensor, "act": nc.scalar, "dve": nc.vector, "pool": nc.gpsimd, "sp": nc.sync}
        self.sem = {}
        self.cnt = {}
        for k in ("pe", "act", "dve", "pool"):
            self.sem[k] = stack.enter_context(nc.semaphore("s_" + k))
            self.cnt[k] = 0
        self.dma_sems = {"sp": [], "pool": []}
        for qn in ("sp", "pool"):
            for i in range(self.N_DMA_SEM):
                key = "dma_%s%d" % (qn, i)
                self.sem[key] = stack.enter_context(nc.semaphore("s_" + key))
                self.cnt[key] = 0
                self.dma_sems[qn].append(key)
        self.dma_rr = {"sp": 0, "pool": 0}
        self.waited = {}
        self.n_inst = 0
        self.n_wait = 0
        self.sb_bytes = 0

    def sb(self, name, shape, dt):
        t = self.stack.enter_context(self.nc.sbuf_tensor("sb_" + name, list(shape), dt))
        n = 1
        for s in shape[1:]:
            n *= s
        self.sb_bytes += n * (2 if dt == BF16 else 4)
        return Buf(name, t)

    def ps(self, name, shape, dt=F32):
        t = self.stack.enter_context(self.nc.psum_tensor("pp_" + name, list(shape), dt))
        b = Buf(name, t)
        b.psum = True
        return b

    def _need(self, eng, deps):
        best = {}
        for d in deps:
            if d is None:
                continue
            sk, v, _ = d
            if v > best.get(sk, 0):
                best[sk] = v
        for sk, v in best.items():
            if self.waited.get((eng, sk), 0) >= v:
                continue
            self.engs[eng].wait_ge(self.sem[sk], v)
            self.waited[(eng, sk)] = v
            self.n_wait += 1

    def do(self, eng, fn, R=(), W=()):
        deps = []
        for b in R:
            lw = b.lastw
            if lw is not None and not (lw[2] == eng and eng == "pe"):
                deps.append(lw)
            if b.psum:
                for r in b.readers:
                    if r[2] != eng:
                        deps.append(r)
        for b in W:
            lw = b.lastw
            if lw is not None and (lw[2] != eng or eng != "pe"):
                deps.append(lw)
            for r in b.readers:
                if r[2] != eng or eng != "pe":
                    deps.append(r)
        self._need(eng, deps)
        ins = fn(self.engs[eng])
        self.cnt[eng] += 1
        v = self.cnt[eng]
        ins.then_inc(self.sem[eng], 1)
        tok = (eng, v, eng)
        for b in R:
            b.readers = [r for r in b.readers if r[0] != eng] + [tok]
        for b in W:
            b.lastw = tok
            b.readers = []
        self.n_inst += 1
        return ins

    def dma(self, out_ap, in_ap, R=(), W=(), q="sp"):
        sk = self.dma_sems[q][self.dma_rr[q]]
        self.dma_rr[q] = (self.dma_rr[q] + 1) % len(self.dma_sems[q])
        deps = []
        for b in R:
            if b.lastw is not None:
                deps.append(b.lastw)
        for b in W:
            if b.lastw is not None:
                deps.append(b.lastw)
            deps.extend(b.readers)
        if self.cnt[sk] > 0:
            deps.append((sk, self.cnt[sk], "dmaq"))
        self._need(q, deps)
        ins = self.engs[q].dma_start(out=out_ap, in_=in_ap)
        self.cnt[sk] += 16
        ins.then_inc(self.sem[sk], 16)
        tok = (sk, self.cnt[sk], "dmaq")
        for b in R:
            b.readers.append(tok)
        for b in W:
            b.lastw = tok
            b.readers = []
        self.n_inst += 1
        return ins

    def finish(self, bufs, eng="sp"):
        self._need(eng, [b.lastw for b in bufs if b.lastw is not None])


class Ring:
    def __init__(self, S, name, n, shape, dt, psum=False):
        self.bufs = [(S.ps if psum else S.sb)("%s%d" % (name, i), shape, dt) for i in range(n)]
        self.i = 0

    def next(self):
        b = self.bufs[self.i]
        self.i = (self.i + 1) % len(self.bufs)
        return b


def tile_km(W, msz):
    K, M = W.shape
    return np.ascontiguousarray(W.reshape(K // 128, 128, M // msz, msz).transpose(2, 1, 0, 3))


def col_pk(v, p=128):
    return np.ascontiguousarray(v.reshape(-1, p).T)


def prep_inputs(inp, b):
    f = np.float32
    L = 2
    m = {}
    m["xT"] = np.ascontiguousarray(inp["x"][b].T)
    m["cT"] = col_pk(inp["c"][b])
    m["ident"] = np.eye(128, dtype=f)
    m["ones"] = np.ones((128, 128), f)
    su = np.triu(np.ones((C, C), f), 1)
    sl = np.tril(np.ones((C, C), f), -1)
    iu = np.triu(np.ones((C, C), f), 0)
    m["msk5"] = np.ascontiguousarray(np.concatenate([su, sl, iu, sl, iu], axis=1))
    rs = np.ones((64, TS), f)
    rs[:, ::C] = 0.0
    m["reset"] = rs
    m["sgmask"] = np.triu(np.ones((128, 128), f), 0)
    m["iota"] = np.ascontiguousarray(np.broadcast_to(np.arange(FR, dtype=f) + 1.0, (128, FR)))
    sel = np.zeros((128, 2), f)
    sel[(np.arange(128) % 32) < 16, 0] = 1.0
    sel[(np.arange(128) % 32) >= 16, 1] = 1.0
    m["sel16"] = sel
    m["ada_w"] = np.ascontiguousarray(inp["ada_w"].reshape(L, 8, 128, 48, 128).transpose(0, 3, 2, 1, 4))
    m["ada_b"] = np.stack([col_pk(inp["ada_b"][l]) for l in range(L)])
    m["g_mix"] = np.stack([col_pk(inp["norm_mix_g"][l]) for l in range(L)])
    m["g_ffn"] = np.stack([col_pk(inp["norm_ffn_g"][l]) for l in range(L)])
    m["g_fin"] = col_pk(inp["final_g"])
    w_in = inp["w_in"]
    m["w_in64"] = np.stack([tile_km(w_in[l][:, :768], 64) for l in range(L)])
    m["w_in128"] = np.stack([tile_km(w_in[l][:, 768:], 128) for l in range(L)])
    mu = inp["rwkv_mu"]
    m["mu_rkv"] = np.stack([np.concatenate([col_pk(mu[l][q * 256:(q + 1) * 256], 64) for q in range(3)], axis=1)
                            for l in range(L)])
    m["mu_lora"] = np.stack([mu[l][768:896].reshape(128, 1) for l in range(L)])
    hp = lambda k: np.stack([col_pk(inp[k][l], 64) for l in range(inp[k].shape[0])])
    rk = inp["rwkv_rk"].reshape(L, 256)
    m["hpar"] = np.ascontiguousarray(np.stack(
        [hp("rwkv_w0"), hp("rwkv_a0"), hp("rwkv_kk"), hp("rwkv_ka"), hp("rwkv_lnx_w"), hp("rwkv_lnx_b"),
         np.stack([col_pk(rk[l], 64) for l in range(L)])], axis=2))
    m["v0"] = hp("rwkv_v0")
    m["lora_up"] = np.ascontiguousarray(np.concatenate([inp["rwkv_w2"], inp["rwkv_a2"], inp["rwkv_g2"]], axis=1))
    m["v1"] = np.ascontiguousarray(inp["rwkv_v1"].reshape(1, 4, 64, 32).transpose(0, 2, 1, 3))
    m["v2"] = np.ascontiguousarray(inp["rwkv_v2"])
    m["rwkv_out"] = np.ascontiguousarray(inp["rwkv_out"].reshape(L, 4, 64, 1024).transpose(0, 2, 1, 3))
    k2 = lambda k: np.ascontiguousarray(inp[k].reshape(L, 2, 128, -1).transpose(0, 2, 1, 3))
    m["sg_out"] = k2("sg_out")
    m["conv_out"] = k2("conv_out")
    m["glu_w"] = k2("s5_glu_w")
    m["sg_ln"] = np.stack([np.concatenate([col_pk(inp["sg_ln_w"][l]), col_pk(inp["sg_ln_b"][l])], axis=1) for l in range(L)])
    m["sg_wsT"] = np.ascontiguousarray(inp["sg_ws"].transpose(0, 3, 1, 2))
    bs = inp["sg_bs"]
    m["sg_bs"] = np.ascontiguousarray(np.stack(
        [np.stack([np.repeat(bs[l, 2 * ct:2 * ct + 2], 64, axis=0) for ct in range(2)], axis=1) for l in range(L)]))
    m["conv_w"] = np.ascontiguousarray(inp["conv_w"].reshape(L, 3, 2, 128).transpose(0, 3, 2, 1))
    pair = lambda a: np.ascontiguousarray(a.reshape(L, 8, 128).transpose(0, 2, 1))
    m["s5_lam"] = np.ascontiguousarray(np.stack(
        [pair(inp["s5_a_re"]), pair(inp["s5_a_im"]),
         pair(np.repeat(inp["s5_log_dt"][:, :, None], 64, axis=2))], axis=2))
    def bk(a):
        return np.ascontiguousarray(a.reshape(L, 2, 8, 64, 16).transpose(0, 2, 4, 1, 3).reshape(L, 128, 2, 64))
    def lk(a):
        a3 = a if a.ndim == 3 else np.repeat(a[:, :, None], 64, axis=2)
        r = np.repeat(a3.reshape(L, 2, 8, 1, 64), 16, axis=3)
        return np.ascontiguousarray(r.transpose(0, 2, 3, 1, 4).reshape(L, 128, 2, 64))
    m["s5_bk"] = np.ascontiguousarray(np.stack(
        [bk(inp["s5_b_re"]), bk(inp["s5_b_im"]), lk(inp["s5_a_re"]), lk(inp["s5_a_im"]), lk(inp["s5_log_dt"])], axis=2))
    lc = np.zeros((L, 8, 128, 2, 128), f)
    for jp in range(8):
        for gg in range(2):
            g = 2 * jp + gg
            cs = (jp % 4) * 32 + gg * 16
            lc[:, jp, gg * 64:(gg + 1) * 64, 0, cs:cs + 16] = inp["s5_c_re"][:, g].transpose(0, 2, 1)
            lc[:, jp, gg * 64:(gg + 1) * 64, 1, cs:cs + 16] = inp["s5_c_im"][:, g].transpose(0, 2, 1)
    m["s5_lc"] = np.ascontiguousarray(lc.transpose(0, 2, 1, 3, 4))
    m["s5_d"] = np.stack([col_pk(inp["s5_d"][l]) for l in range(L)])
    m["w_o"] = np.stack([tile_km(inp["w_o"][l], 128) for l in range(L)])
    m["ffn_w1"] = np.stack([tile_km(inp["ffn_w1"][l], 128) for l in range(L)])
    m["ffn_w2"] = np.stack([tile_km(inp["ffn_w2"][l], 128) for l in range(L)])
    return {k: np.ascontiguousarray(v, dtype=f) for k, v in m.items()}


def build(shapes, T):
    NS = T // TS
    NF = TS // FR
    nc = bass.Bass("TRN2", target_bir_lowering=False)
    dr = {}
    for k, shp in shapes.items():
        dr[k] = nc.dram_tensor(k, list(shp), F32, kind="ExternalInput").ap()
    outT = nc.dram_tensor("outT", [D, T], F32, kind="ExternalOutput").ap()
    xs_d = nc.dram_tensor("xs_scr", [D, T], F32, kind="Internal").ap()
    vf_d = nc.dram_tensor("vf_scr", [64, 4, T], F32, kind="Internal").ap()

    with contextlib.ExitStack() as st:
        S = Sched(nc, st)
        do, dma = S.do, S.dma
        B_outs = [Buf("outT%d" % i) for i in range(NS)]
        B_xs = [Buf("xs%d" % i) for i in range(NS)]
        B_vf = [Buf("vf%d" % i) for i in range(NS)]

        def cload(name, key, shape):
            b = S.sb(name, shape, F32)
            dma(b[:], dr[key], W=[b])
            return b
        ident = cload("ident", "ident", [128, 128])
        ones = cload("ones", "ones", [128, 128])
        msk5 = cload("msk5", "msk5", [64, 320])
        reset = cload("reset", "reset", [64, TS])
        sgmask = cload("sgmask", "sgmask", [128, 128])
        iota = cload("iota", "iota", [128, FR])
        sel16 = cload("sel16", "sel16", [128, 2])
        cT = cload("cT", "cT", [128, 8])
        g_fin = cload("g_fin", "g_fin", [128, 8])
        ones_bf = S.sb("ones_bf", [128, 128], BF16)
        do("dve", lambda e: e.tensor_copy(ones_bf[:], ones[:]), R=[ones], W=[ones_bf])
        csil = S.sb("csil", [128, 8], F32)
        do("act", lambda e: e.activation(csil[:], cT[:], AF.Silu), R=[cT], W=[csil])
        negpi = S.sb("negpi", [128, 1], F32)
        do("dve", lambda e: e.memset(negpi[:], -math.pi), W=[negpi])

        PS = [S.ps("ps%d" % i, [128, 512]) for i in range(8)]
        ps_rr = [0]

        def psum():
            b = PS[ps_rr[0] % 3]
            ps_rr[0] += 1
            return b

        X = S.sb("X", [128, 8, TS], F32)
        hT = S.sb("hT", [128, 8, TS], BF16)
        hTB = S.sb("hTB", [128, 8, TS], BF16)
        aT = S.sb("aT", [128, 8, TS], BF16)
        sqb = S.sb("sqb", [128, TS], BF16)
        rstd = S.sb("rstd", [128, TS], F32)
        xn = S.sb("xn", [128, TS], F32)
        mixA = S.sb("mixA", [64, 4, TS], BF16)
        mixB = S.sb("mixB", [128, 2, TS], BF16)
        mixC = S.sb("mixC", [128, 2, TS], BF16)
        mixD = S.sb("mixD", [128, 2, TS], BF16)
        w128 = Ring(S, "w128_", 4, [128, 8, 128], BF16)
        wsm = Ring(S, "wsm_", 6, [128, 2, 128], BF16)
        w64 = Ring(S, "w64_", 2, [128, 8, 64], BF16)
        wra = Ring(S, "wra_", 2, [64, 4, 128], BF16)
        modT = S.sb("modT", [128, 48], F32)
        geff1 = S.sb("geff1", [128, 8], F32)
        geff2 = S.sb("geff2", [128, 8], F32)
        tmp48 = S.sb("tmp48", [128, 48], F32)
        work = Ring(S, "work_", 3, [128, TS], F32)
        Z = [S.sb("Z%d" % i, [128, TS], F32) for i in range(6)]
        PL = S.sb("PL", [128, TS], F32)
        LA = S.sb("LA", [128, TS], F32)
        H = {nm: S.sb("H_" + nm, [64, TS], F32) for nm in ("r", "k", "d", "e", "c", "w", "m", "p", "a", "g", "q")}
        Vall = S.sb("Vall", [64, 4, TS], F32)
        carry = S.sb("carry", [64, 12], F32)
        carryL = S.sb("carryL", [128, 1], F32)
        VV = S.sb("VV", [32, TS], F32)
        STt = S.sb("STt", [64, 4, 64], F32)
        STW = S.sb("STW", [64, 64], F32)
        NB = 4
        SCx = S.sb("SCx", [64, NB, 128], F32)
        SCb = S.sb("SCb", [64, NB, 192], BF16)
        TMb = S.sb("TMb", [64, NB, 256], BF16)
        XS = [S.sb("XS%d" % i, [64, NB, 64], F32) for i in range(2)]
        YS = [S.sb("YS%d" % i, [64, NB, 64], F32) for i in range(2)]
        TT = S.sb("TT", [64, NB, 64], F32)
        TTb = S.sb("TTb", [64, NB, 64], BF16)
        APMb = S.sb("APMb", [64, NB, 128], BF16)
        PP = []
        for par in range(2):
            d_ = {k: S.sb("%s_%d" % (k, par), [64, TS], BF16) for k in ("Atb", "Btb", "Ktb", "Rtb", "Vb")}
            if par == 0:
                d_.update({"Hw": H["w"], "Hg": H["g"], "Hq": H["q"]})
            else:
                d_.update({k: S.sb("%s_%d" % (k, par), [64, TS], F32) for k in ("Hw", "Hg", "Hq")})
            PP.append(d_)
        STb = S.sb("STb", [64, 64], BF16)
        UTb = S.sb("UTb", [64, 64], BF16)
        ident_bf = S.sb("ident_bf", [128, 128], BF16)
        mu_rkv = S.sb("mu_rkv", [64, 12], F32)
        mu_lora = S.sb("mu_lora", [128, 1], F32)
        hpar = S.sb("hpar", [64, 7, 4], F32)
        omka = S.sb("omka", [64, 4], F32)
        v0 = S.sb("v0", [64, 4], F32)
        lora_up = S.sb("lora_up", [128, 256], F32)
        v1 = S.sb("v1", [64, 4, 32], F32)
        v2 = S.sb("v2", [32, 256], F32)
        sg_ln = S.sb("sg_ln", [128, 4], F32)
        sg_wb = S.sb("sg_wb", [128, 4, 128], BF16)
        sg_bs = S.sb("sg_bs", [128, 2, 128], F32)
        conv_w = S.sb("conv_w", [128, 2, 3], F32)
        zc = S.sb("zc", [128, 2, TS + 2], F32)
        vz = [S.sb("vz%d" % i, [128, 128], BF16) for i in range(4)]
        s5_lam = S.sb("s5_lam", [128, 3, 8], F32)
        s5_bk = S.sb("s5_bk", [128, 5, 2, 64], F32)
        s5_lc = S.sb("s5_lc", [128, 8, 2, 128], BF16)
        s5_d = S.sb("s5_d", [128, 2], F32)
        LB = S.sb("LB", [128, 8, 2, 128], BF16)
        cosT = S.sb("cosT", [128, 8, FR], F32)
        sinT = S.sb("sinT", [128, 8, FR], F32)
        rho = S.sb("rho", [128, 8], F32)
        th = S.sb("th", [128, 8], F32)
        s5st = S.sb("s5st", [128, 8, 2], F32)
        s5c = S.sb("s5c", [128, 2], F32)
        small = Ring(S, "small_", 4, [128, 64], F32)
        uT = S.sb("uT", [128, 2, TS], F32)
        uTb = S.sb("uTb", [128, 2, TS], BF16)
        xrb = S.sb("xrb", [128, TS], BF16)
        xib = S.sb("xib", [128, TS], BF16)
        tP = {nm: S.sb("s5p_" + nm, [128, 8], F32) for nm in ("lre", "dt", "mag", "ang", "sn", "cs", "abr", "abi", "den", "qre", "qim", "t1", "t2")}
        tK = {nm: S.sb("s5k_" + nm, [128, 2, 64], F32) for nm in ("lre", "dt", "mag", "ang", "sn", "cs", "abr", "abi", "den", "qre", "qim", "t1", "t2")}
        bbr = S.sb("bbr", [128, 2, 64], F32)
        bbi = S.sb("bbi", [128, 2, 64], F32)
        tq = S.sb("tq", [128, 2, 64], F32)

        def zero(b, ap=None):
            do("dve", lambda e: e.memset(b[:] if ap is None else ap, 0.0), W=[b])
        for b in vz:
            zero(b)
        do("dve", lambda e: e.tensor_copy(ident_bf[:], ident[:]), R=[ident], W=[ident_bf])
        zero(LB)
        print("SBUF bytes/partition:", S.sb_bytes)

        def wload(buf, src, ap=None):
            dma(buf[:] if ap is None else ap, src, W=[buf], q="pool")
            return buf

        def rms_stats():
            pb = psum()
            for ft in range(8):
                do("act", lambda e: e.activation(sqb[:], X[:, ft, :], AF.Square), R=[X], W=[sqb])
                do("pe", lambda e: e.matmul(pb[:], ones_bf[:], sqb[:], start=(ft == 0), stop=(ft == 7)), R=[ones_bf, sqb], W=[pb])
            do("dve", lambda e: e.tensor_scalar(rstd[:], pb[:], 1.0 / D, EPS, ALU.mult, ALU.add), R=[pb], W=[rstd])
            do("act", lambda e: e.activation(rstd[:], rstd[:], AF.Sqrt), R=[rstd], W=[rstd])
            do("dve", lambda e: e.reciprocal(rstd[:], rstd[:]), R=[rstd], W=[rstd])

        def rmsnorm_to(dst, geff, shift_col0):
            rms_stats()
            for ft in range(8):
                do("dve", lambda e: e.tensor_tensor(xn[:], X[:, ft, :], rstd[:], ALU.mult), R=[X, rstd], W=[xn])
                do("act", lambda e: e.activation(dst[:, ft, :], xn[:], AF.Identity, bias=modT[:, shift_col0 + ft:shift_col0 + ft + 1],
                                                 scale=geff[:, ft:ft + 1]), R=[xn, modT, geff], W=[dst])

        def proj128(src_ap, rhs_buf, rhs_fn, nk, ring, evac):
            wb = wload(ring.next(), src_ap)
            pb = psum()
            for kt in range(nk):
                do("pe", lambda e: e.matmul(pb[:], wb[:, kt, :], rhs_fn(kt), start=(kt == 0), stop=(kt == nk - 1)), R=[wb, rhs_buf], W=[pb])
            evac(pb)

        def s5_disc(t, are, aim, ldt, srcb):
            A = lambda nm: t[nm][:]
            B = lambda nm: t[nm]
            do("dve", lambda e: e.tensor_scalar(A("lre"), are, -1e-4, None, ALU.min), R=[srcb], W=[B("lre")])
            do("act", lambda e: e.activation(A("dt"), ldt, AF.Exp), R=[srcb], W=[B("dt")])
            do("dve", lambda e: e.tensor_tensor(A("t1"), A("lre"), A("dt"), ALU.mult), R=[B("lre"), B("dt")], W=[B("t1")])
            do("act", lambda e: e.activation(A("mag"), A("t1"), AF.Exp), R=[B("t1")], W=[B("mag")])
            do("dve", lambda e: e.tensor_tensor(A("ang"), aim, A("dt"), ALU.mult), R=[srcb, B("dt")], W=[B("ang")])
            for off, dst in ((0.0, "sn"), (0.5 * math.pi, "cs")):
                do("dve", lambda e: e.tensor_scalar(A("t1"), A("ang"), off, None, ALU.add), R=[B("ang")], W=[B("t1")])
                do("dve", lambda e: e.tensor_scalar(A("t2"), A("t1"), 1.0 / (2 * math.pi), MAGIC, ALU.mult, ALU.add), R=[B("t1")], W=[B("t2")])
                do("dve", lambda e: e.tensor_scalar(A("t2"), A("t2"), -MAGIC, None, ALU.add), R=[B("t2")], W=[B("t2")])
                do("dve", lambda e: e.scalar_tensor_tensor(A("t1"), A("t2"), -2 * math.pi, A("t1"), ALU.mult, ALU.add), R=[B("t2"), B("t1")], W=[B("t1")])
                do("act", lambda e: e.activation(A(dst), A("t1"), AF.Sin), R=[B("t1")], W=[B(dst)])
            do("dve", lambda e: e.tensor_tensor(A("abr"), A("mag"), A("cs"), ALU.mult), R=[B("mag"), B("cs")], W=[B("abr")])
            do("dve", lambda e: e.tensor_tensor(A("abi"), A("mag"), A("sn"), ALU.mult), R=[B("mag"), B("sn")], W=[B("abi")])
            do("dve", lambda e: e.tensor_tensor(A("den"), A("lre"), A("lre"), ALU.mult), R=[B("lre")], W=[B("den")])
            do("dve", lambda e: e.tensor_tensor(A("t1"), aim, aim, ALU.mult), R=[srcb], W=[B("t1")])
            do("dve", lambda e: e.tensor_tensor(A("den"), A("den"), A("t1"), ALU.add), R=[B("den"), B("t1")], W=[B("den")])
            do("dve", lambda e: e.reciprocal(A("den"), A("den")), R=[B("den")], W=[B("den")])
            do("dve", lambda e: e.tensor_scalar(A("t1"), A("abr"), -1.0, None, ALU.add), R=[B("abr")], W=[B("t1")])
            do("dve", lambda e: e.tensor_tensor(A("qre"), A("t1"), A("lre"), ALU.mult), R=[B("t1"), B("lre")], W=[B("qre")])
            do("dve", lambda e: e.tensor_tensor(A("t2"), A("abi"), aim, ALU.mult), R=[B("abi"), srcb], W=[B("t2")])
            do("dve", lambda e: e.tensor_tensor(A("qre"), A("qre"), A("t2"), ALU.add), R=[B("qre"), B("t2")], W=[B("qre")])
            do("dve", lambda e: e.tensor_tensor(A("qre"), A("qre"), A("den"), ALU.mult), R=[B("qre"), B("den")], W=[B("qre")])
            do("dve", lambda e: e.tensor_tensor(A("qim"), A("abi"), A("lre"), ALU.mult), R=[B("abi"), B("lre")], W=[B("qim")])
            do("dve", lambda e: e.tensor_tensor(A("t2"), A("t1"), aim, ALU.mult), R=[B("t1"), srcb], W=[B("t2")])
            do("dve", lambda e: e.tensor_tensor(A("qim"), A("qim"), A("t2"), ALU.subtract), R=[B("qim"), B("t2")], W=[B("qim")])
            do("dve", lambda e: e.tensor_tensor(A("qim"), A("qim"), A("den"), ALU.mult), R=[B("qim"), B("den")], W=[B("qim")])

        for l in range(2):
            pmod = PS[4]
            for mt in range(48):
                wb = Z[mt % 2 * 2]
                wb2 = Z[mt % 2 * 2 + 1]
                dma(wb[:].rearrange("p (k m) -> p k m", k=4), dr["ada_w"][l, mt][:, 0:4, :], W=[wb])
                dma(wb2[:].rearrange("p (k m) -> p k m", k=4), dr["ada_w"][l, mt][:, 4:8, :], W=[wb2])
                for kt in range(8):
                    wv = (wb if kt < 4 else wb2)
                    do("pe", lambda e: e.matmul(pmod[:, mt:mt + 1], wv[:, (kt % 4) * 128:(kt % 4 + 1) * 128], csil[:, kt:kt + 1],
                                                start=(kt == 0), stop=(kt == 7)), R=[wv, csil], W=[pmod])
            dma(tmp48[:], dr["ada_b"][l], W=[tmp48])
            do("dve", lambda e: e.tensor_tensor(modT[:], pmod[:, 0:48], tmp48[:], ALU.add), R=[pmod, tmp48], W=[modT])
            gm = small.next()
            dma(gm[:, 0:8], dr["g_mix"][l], W=[gm])
            do("dve", lambda e: e.scalar_tensor_tensor(geff1[:], modT[:, 8:16], 1.0, gm[:, 0:8], ALU.add, ALU.mult), R=[modT, gm], W=[geff1])
            gf = small.next()
            dma(gf[:, 0:8], dr["g_ffn"][l], W=[gf])
            do("dve", lambda e: e.scalar_tensor_tensor(geff2[:], modT[:, 32:40], 1.0, gf[:, 0:8], ALU.add, ALU.mult), R=[modT, gf], W=[geff2])

            dma(mu_rkv[:], dr["mu_rkv"][l], W=[mu_rkv])
            dma(mu_lora[:], dr["mu_lora"][l], W=[mu_lora])
            dma(hpar[:], dr["hpar"][l], W=[hpar])
            do("dve", lambda e: e.tensor_scalar(omka[:], hpar[:, 3, :], -1.0, 1.0, ALU.mult, ALU.add), R=[hpar], W=[omka])
            dma(lora_up[:], dr["lora_up"][l], W=[lora_up])
            if l == 1:
                dma(v0[:], dr["v0"][0], W=[v0])
                dma(v1[:], dr["v1"][0], W=[v1])
                dma(v2[:], dr["v2"][0], W=[v2])
            dma(sg_ln[:], dr["sg_ln"][l], W=[sg_ln])
            sgw = Z[4]
            dma(sgw[:].rearrange("p (g t) -> p g t", g=4), dr["sg_wsT"][l], W=[sgw])
            for g in range(4):
                do("dve", lambda e: e.tensor_tensor(sg_wb[:, g, :], sgw[:, g * 128:(g + 1) * 128], sgmask[:], ALU.mult), R=[sgw, sgmask], W=[sg_wb])
            dma(sg_bs[:], dr["sg_bs"][l], W=[sg_bs])
            dma(conv_w[:], dr["conv_w"][l], W=[conv_w])
            dma(s5_lam[:], dr["s5_lam"][l], W=[s5_lam])
            dma(s5_bk[:], dr["s5_bk"][l], W=[s5_bk])
            wload(s5_lc, dr["s5_lc"][l])
            dma(s5_d[:], dr["s5_d"][l], W=[s5_d])
            do("dve", lambda e: e.tensor_scalar(s5_lc[:, :, 1, :], s5_lc[:, :, 1, :], -1.0, None, ALU.mult), R=[s5_lc], W=[s5_lc])

            s5_disc(tP, s5_lam[:, 0, :], s5_lam[:, 1, :], s5_lam[:, 2, :], s5_lam)
            do("dve", lambda e: e.tensor_copy(rho[:], tP["mag"][:]), R=[tP["mag"]], W=[rho])
            do("dve", lambda e: e.tensor_copy(th[:], tP["ang"][:]), R=[tP["ang"]], W=[th])
            for jp in range(8):
                for off, dstT in ((0.0, sinT), (0.5 * math.pi, cosT)):
                    wk = work.next()
                    wk2 = work.next()
                    do("dve", lambda e: e.tensor_scalar(wk[:, 0:FR], iota[:], th[:, jp:jp + 1], off, ALU.mult, ALU.add), R=[iota, th], W=[wk])
                    do("dve", lambda e: e.tensor_scalar(wk2[:, 0:FR], wk[:, 0:FR], 1.0 / (2 * math.pi), MAGIC, ALU.mult, ALU.add), R=[wk], W=[wk2])
                    do("dve", lambda e: e.tensor_scalar(wk2[:, 0:FR], wk2[:, 0:FR], -MAGIC, None, ALU.add), R=[wk2], W=[wk2])
                    do("dve", lambda e: e.scalar_tensor_tensor(wk[:, 0:FR], wk2[:, 0:FR], -2 * math.pi, wk[:, 0:FR], ALU.mult, ALU.add), R=[wk2, wk], W=[wk])
                    do("act", lambda e: e.activation(dstT[:, jp, :], wk[:, 0:FR], AF.Sin), R=[wk], W=[dstT])
            s5_disc(tK, s5_bk[:, 2], s5_bk[:, 3], s5_bk[:, 4], s5_bk)
            do("dve", lambda e: e.tensor_tensor(bbr[:], tK["qre"][:], s5_bk[:, 0], ALU.mult), R=[tK["qre"], s5_bk], W=[bbr])
            do("dve", lambda e: e.tensor_tensor(tq[:], tK["qim"][:], s5_bk[:, 1], ALU.mult), R=[tK["qim"], s5_bk], W=[tq])
            do("dve", lambda e: e.tensor_tensor(bbr[:], bbr[:], tq[:], ALU.subtract), R=[bbr, tq], W=[bbr])
            do("dve", lambda e: e.tensor_tensor(bbi[:], tK["qre"][:], s5_bk[:, 1], ALU.mult), R=[tK["qre"], s5_bk], W=[bbi])
            do("dve", lambda e: e.tensor_tensor(tq[:], tK["qim"][:], s5_bk[:, 0], ALU.mult), R=[tK["qim"], s5_bk], W=[tq])
            do("dve", lambda e: e.tensor_tensor(bbi[:], bbi[:], tq[:], ALU.add), R=[bbi, tq], W=[bbi])
            for jp in range(8):
                kt, r0 = jp // 4, (jp % 4) * 32
                for ri, bb in ((0, bbr), (1, bbi)):
                    for gg in range(2):
                        do("dve", lambda e: e.tensor_scalar(LB[r0:r0 + 32, jp, ri, gg * 64:(gg + 1) * 64], bb[r0:r0 + 32, kt, :],
                                                            sel16[r0:r0 + 32, gg:gg + 1], None, ALU.mult), R=[bb, sel16], W=[LB])

            zero(STt)
            zero(carry)
            zero(carryL)
            zero(s5st)
            zero(zc, zc[:, :, 0:2])

            ffn_pending = [None]
            for si in range(NS):
                t0 = si * TS
                src = dr["xT"] if l == 0 else xs_d
                srcb = [] if l == 0 else [B_xs[si]]
                pbn = psum()
                for ft in range(8):
                    stg = work.next()
                    dma(stg[:], src[ft * 128:(ft + 1) * 128, t0:t0 + TS], R=srcb, W=[stg])
                    do("act", lambda e: e.activation(sqb[:], stg[:], AF.Square), R=[stg], W=[sqb])
                    do("pe", lambda e: e.matmul(pbn[:], ones_bf[:], sqb[:], start=(ft == 0), stop=(ft == 7)), R=[ones_bf, sqb], W=[pbn])
                do("dve", lambda e: e.tensor_scalar(rstd[:], pbn[:], 1.0 / D, EPS, ALU.mult, ALU.add), R=[pbn], W=[rstd])
                do("act", lambda e: e.activation(rstd[:], rstd[:], AF.Sqrt), R=[rstd], W=[rstd])
                do("dve", lambda e: e.reciprocal(rstd[:], rstd[:]), R=[rstd], W=[rstd])
                for ft in range(8):
                    stg = work.next()
                    dma(stg[:], src[ft * 128:(ft + 1) * 128, t0:t0 + TS], R=srcb, W=[stg])
                    do("dve", lambda e: e.tensor_tensor(xn[:], stg[:], rstd[:], ALU.mult), R=[stg, rstd], W=[xn])
                    do("act", lambda e: e.activation(hT[:, ft, :], xn[:], AF.Identity, bias=modT[:, ft:ft + 1], scale=geff1[:, ft:ft + 1]),
                       R=[xn, modT, geff1], W=[hT])
                hfn = lambda kt: hT[:, kt, :]

                def s5_gen():
                    for kt in range(2):
                        def ev_u(pb, kt=kt):
                            do("act", lambda e: e.copy(uT[:, kt, :], pb[:]), R=[pb], W=[uT])
                            do("dve", lambda e: e.tensor_copy(uTb[:, kt, :], pb[:]), R=[pb], W=[uTb])
                        proj128(dr["w_in128"][l, 11 + kt], hT, hfn, 8, w128, ev_u)
                    py = [PS[3], PS[3]]
                    f3 = lambda ap: ap.rearrange("p (f s) -> p f s", f=NF)
                    for jp in range(8):
                        kt = jp // 4
                        pre = psum()
                        do("pe", lambda e: e.matmul(pre[:], LB[:, jp, 0, :], uTb[:, kt, :], start=True, stop=True), R=[LB, uTb], W=[pre])
                        pim = psum()
                        do("pe", lambda e: e.matmul(pim[:], LB[:, jp, 1, :], uTb[:, kt, :], start=True, stop=True), R=[LB, uTb], W=[pim])
                        bre, bim, mre, mim, tmp = Z[0], Z[1], Z[2], Z[3], Z[4]
                        do("act", lambda e: e.copy(bre[:], pre[:]), R=[pre], W=[bre])
                        do("act", lambda e: e.copy(bim[:], pim[:]), R=[pim], W=[bim])
                        cb = cosT[:, jp, :].unsqueeze(1).to_broadcast([128, NF, FR])
                        sb_ = sinT[:, jp, :].unsqueeze(1).to_broadcast([128, NF, FR])
                        do("dve", lambda e: e.tensor_tensor(f3(mre[:]), f3(bre[:]), cb, ALU.mult), R=[bre, cosT], W=[mre])
                        do("dve", lambda e: e.tensor_tensor(f3(tmp[:]), f3(bim[:]), sb_, ALU.mult), R=[bim, sinT], W=[tmp])
                        yield
                        do("dve", lambda e: e.tensor_tensor(mre[:], mre[:], tmp[:], ALU.add), R=[mre, tmp], W=[mre])
                        yield
                        tmp2 = Z[5]
                        do("dve", lambda e: e.tensor_tensor(f3(mim[:]), f3(bim[:]), cb, ALU.mult), R=[bim, cosT], W=[mim])
                        yield
                        do("dve", lambda e: e.tensor_tensor(f3(tmp2[:]), f3(bre[:]), sb_, ALU.mult), R=[bre, sinT], W=[tmp2])
                        do("dve", lambda e: e.tensor_tensor(mim[:], mim[:], tmp2[:], ALU.subtract), R=[mim, tmp2], W=[mim])
                        yield
                        rb = rho[:, jp:jp + 1].to_broadcast([128, FR])
                        cL, sL = cosT[:, jp, FR - 1:FR], sinT[:, jp, FR - 1:FR]
                        for fi in range(NF):
                            fs = slice(fi * FR, (fi + 1) * FR)
                            last = slice(fi * FR + FR - 1, fi * FR + FR)
                            do("dve", lambda e: e.tensor_tensor_scan(bre[:, fs], rb, mre[:, fs], s5st[:, jp, 0:1], ALU.mult, ALU.add), R=[rho, mre, s5st], W=[bre])
                            do("dve", lambda e: e.tensor_tensor_scan(bim[:, fs], rb, mim[:, fs], s5st[:, jp, 1:2], ALU.mult, ALU.add), R=[rho, mim, s5st], W=[bim])
                            yield
                            do("dve", lambda e: e.tensor_scalar(s5c[:, 0:1], bim[:, last], sL, None, ALU.mult), R=[bim, sinT], W=[s5c])
                            do("dve", lambda e: e.tensor_scalar(s5c[:, 1:2], bre[:, last], sL, None, ALU.mult), R=[bre, sinT], W=[s5c])
                            do("dve", lambda e: e.scalar_tensor_tensor(s5st[:, jp, 0:1], bre[:, last], cL, s5c[:, 0:1], ALU.mult, ALU.subtract), R=[bre, cosT, s5c], W=[s5st])
                            do("dve", lambda e: e.scalar_tensor_tensor(s5st[:, jp, 1:2], bim[:, last], cL, s5c[:, 1:2], ALU.mult, ALU.add), R=[bim, cosT, s5c], W=[s5st])
                            yield
                        do("dve", lambda e: e.tensor_tensor(f3(mre[:]), f3(bre[:]), cb, ALU.mult), R=[bre, cosT], W=[mre])
                        yield
                        do("dve", lambda e: e.tensor_tensor(f3(tmp[:]), f3(bim[:]), sb_, ALU.mult), R=[bim, sinT], W=[tmp])
                        do("dve", lambda e: e.tensor_tensor(xrb[:], mre[:], tmp[:], ALU.subtract), R=[mre, tmp], W=[xrb])
                        yield
                        do("dve", lambda e: e.tensor_tensor(f3(mim[:]), f3(bre[:]), sb_, ALU.mult), R=[bre, sinT], W=[mim])
                        yield
                        do("dve", lambda e: e.tensor_tensor(f3(tmp2[:]), f3(bim[:]), cb, ALU.mult), R=[bim, cosT], W=[tmp2])
                        do("dve", lambda e: e.tensor_tensor(xib[:], mim[:], tmp2[:], ALU.add), R=[mim, tmp2], W=[xib])
                        yield
                        ct = jp // 4
                        do("pe", lambda e: e.matmul(py[ct][:], s5_lc[:, jp, 0, :], xrb[:], start=(jp % 4 == 0), stop=False), R=[s5_lc, xrb], W=[py[ct]])
                        do("pe", lambda e: e.matmul(py[ct][:], s5_lc[:, jp, 1, :], xib[:], start=False, stop=(jp % 4 == 3)), R=[s5_lc, xib], W=[py[ct]])
                        if jp % 4 == 3:
                            do("dve", lambda e: e.scalar_tensor_tensor(uT[:, ct, :], uT[:, ct, :], s5_d[:, ct:ct + 1], py[ct][:], ALU.mult, ALU.add),
                               R=[uT, s5_d, py[ct]], W=[uT])
                            do("act", lambda e: e.activation(mixD[:, ct, :], uT[:, ct, :], AF.Gelu_apprx_tanh), R=[uT], W=[mixD])
                        yield

                def sgc_gen():
                    for mt in range(4):
                        def ev_gelu(pb, mt=mt):
                            do("act", lambda e: e.activation(Z[mt][:], pb[:], AF.Gelu_apprx_tanh), R=[pb], W=[Z[mt]])
                        proj128(dr["w_in128"][l, 1 + mt], hT, hfn, 8, w128, ev_gelu)
                        yield
                    pm1 = psum()
                    for i in range(2):
                        do("pe", lambda e: e.matmul(pm1[:], ones[:], Z[2 + i][:], start=(i == 0), stop=(i == 1)), R=[ones, Z[2 + i]], W=[pm1])
                    pm2 = psum()
                    for i in range(2):
                        sq_ = (Z[5], Z[4])[i]
                        do("dve", lambda e: e.tensor_tensor(sq_[:], Z[2 + i][:], Z[2 + i][:], ALU.mult), R=[Z[2 + i]], W=[sq_])
                        do("pe", lambda e: e.matmul(pm2[:], ones[:], sq_[:], start=(i == 0), stop=(i == 1)), R=[ones, sq_], W=[pm2])
                    do("act", lambda e: e.activation(Z[4][:], pm1[:], AF.Copy, scale=1.0 / 256), R=[pm1], W=[Z[4]])
                    do("dve", lambda e: e.tensor_tensor(Z[5][:], Z[4][:], Z[4][:], ALU.mult), R=[Z[4]], W=[Z[5]])
                    do("dve", lambda e: e.scalar_tensor_tensor(Z[5][:], pm2[:], 1.0 / 256, Z[5][:], ALU.mult, ALU.subtract), R=[pm2, Z[5]], W=[Z[5]])
                    yield
                    do("dve", lambda e: e.tensor_scalar(Z[5][:], Z[5][:], 1e-5, None, ALU.add), R=[Z[5]], W=[Z[5]])
                    do("act", lambda e: e.activation(Z[5][:], Z[5][:], AF.Sqrt), R=[Z[5]], W=[Z[5]])
                    do("dve", lambda e: e.reciprocal(Z[5][:], Z[5][:]), R=[Z[5]], W=[Z[5]])
                    yield
                    for ct in range(2):
                        zv = Z[2 + ct]
                        do("dve", lambda e: e.tensor_tensor(zv[:], zv[:], Z[4][:], ALU.subtract), R=[zv, Z[4]], W=[zv])
                        do("dve", lambda e: e.tensor_tensor(zv[:], zv[:], Z[5][:], ALU.mult), R=[zv, Z[5]], W=[zv])
                        do("dve", lambda e: e.tensor_scalar(zv[:], zv[:], sg_ln[:, ct:ct + 1], sg_ln[:, 2 + ct:3 + ct], ALU.mult, ALU.add), R=[zv, sg_ln], W=[zv])
                        yield
                    for ck in range(TS // 128):
                        ks = slice(ck * 128, (ck + 1) * 128)
                        for ct in range(2):
                            pt = psum()
                            do("pe", lambda e: e.matmul(pt[:, 0:128], Z[2 + ct][:, ks], ident[:], start=True, stop=True), R=[Z[2 + ct], ident], W=[pt])
                            do("act", lambda e: e.copy(vz[ct * 2][:, 0:64], pt[:, 0:64]), R=[pt], W=[vz[ct * 2]])
                            do("dve", lambda e: e.tensor_copy(vz[ct * 2 + 1][:, 64:128], pt[:, 64:128]), R=[pt], W=[vz[ct * 2 + 1]])
                            pmx = psum()
                            for par in range(2):
                                g = ct * 2 + par
                                do("pe", lambda e: e.matmul(pmx[:, 0:128], vz[ct * 2 + par][:], sg_wb[:, g, :], start=(par == 0), stop=(par == 1)),
                                   R=[vz[ct * 2 + par], sg_wb], W=[pmx])
                            wk = work.next()
                            do("dve", lambda e: e.tensor_tensor(wk[:, 0:128], pmx[:, 0:128], sg_bs[:, ct, :], ALU.add), R=[pmx, sg_bs], W=[wk])
                            do("dve", lambda e: e.tensor_tensor(mixB[:, ct, ks], wk[:, 0:128], Z[ct][:, ks], ALU.mult), R=[wk, Z[ct]], W=[mixB])
                            yield
                    for mt in range(6):
                        def ev_c(pb, mt=mt):
                            do("act", lambda e: e.copy(Z[mt][:], pb[:]), R=[pb], W=[Z[mt]])
                        proj128(dr["w_in128"][l, 5 + mt], hT, hfn, 8, w128, ev_c)
                        yield
                    for ct in range(2):
                        do("dve", lambda e: e.tensor_tensor(zc[:, ct, 2:TS + 2], Z[2 + ct][:], Z[4 + ct][:], ALU.mult), R=[Z[2 + ct], Z[4 + ct]], W=[zc])
                        y = Z[2 + ct]
                        do("dve", lambda e: e.tensor_scalar(y[:], zc[:, ct, 0:TS], conv_w[:, ct, 0:1], None, ALU.mult), R=[zc, conv_w], W=[y])
                        do("dve", lambda e: e.scalar_tensor_tensor(y[:], zc[:, ct, 1:TS + 1], conv_w[:, ct, 1:2], y[:], ALU.mult, ALU.add), R=[zc, conv_w, y], W=[y])
                        do("dve", lambda e: e.scalar_tensor_tensor(y[:], zc[:, ct, 2:TS + 2], conv_w[:, ct, 2:3], y[:], ALU.mult, ALU.add), R=[zc, conv_w, y], W=[y])
                        do("dve", lambda e: e.tensor_tensor(mixC[:, ct, :], y[:], Z[ct][:], ALU.mult), R=[y, Z[ct]], W=[mixC])
                        yield
                    do("dve", lambda e: e.tensor_copy(zc[:, :, 0:2], zc[:, :, TS:TS + 2]), R=[zc], W=[zc])
                    yield

                def mix_gen():
                    yield from s5_gen()
                    yield from sgc_gen()

                s5g_ = mix_gen()

                def pump(n=1):
                    if os.environ.get("NOPUMP"):
                        return
                    for _ in range(n):
                        next(s5g_, None)
                        if ffn_pending[0] is not None:
                            next(ffn_pending[0], None)

                def ev_lora(pb):
                    do("act", lambda e: e.copy(PL[:], pb[:]), R=[pb], W=[PL])
                proj128(dr["w_in128"][l, 0], hT, hfn, 8, w128, ev_lora)
                wk = work.next()
                do("dve", lambda e: e.tensor_tensor(wk[:, 1:TS], PL[:, 0:TS - 1], PL[:, 1:TS], ALU.subtract), R=[PL], W=[wk])
                do("dve", lambda e: e.tensor_tensor(wk[:, 0:1], carryL[:], PL[:, 0:1], ALU.subtract), R=[PL, carryL], W=[wk])
                do("dve", lambda e: e.tensor_copy(carryL[:], PL[:, TS - 1:TS]), R=[PL, wk], W=[carryL])
                do("dve", lambda e: e.scalar_tensor_tensor(PL[:], wk[:], mu_lora[:, 0:1], PL[:], ALU.mult, ALU.add), R=[wk, mu_lora, PL], W=[PL])
                do("act", lambda e: e.activation(LA[0:32, :], PL[0:32, :], AF.Tanh), R=[PL], W=[LA])
                do("act", lambda e: e.copy(LA[32:64, :], PL[32:64, :]), R=[PL], W=[LA])
                do("act", lambda e: e.activation(LA[64:128, :], PL[64:128, :], AF.Sigmoid), R=[PL], W=[LA])

                def rkv_proj(q, h, dst_buf, dst_ap):
                    wb = wload(w64.next(), dr["w_in64"][l, q * 4 + h])
                    pb = psum()
                    for kt in range(8):
                        do("pe", lambda e: e.matmul(pb[0:64, :], wb[:, kt, :], hT[:, kt, :], start=(kt == 0), stop=(kt == 7)), R=[wb, hT], W=[pb])
                    do("act", lambda e: e.copy(dst_ap, pb[0:64, :]), R=[pb], W=[dst_buf])

                def tshift(Qb, Qap, cc):
                    dq = H["d"]
                    do("dve", lambda e: e.tensor_tensor(dq[:, 1:TS], Qap[:, 0:TS - 1], Qap[:, 1:TS], ALU.subtract), R=[Qb], W=[dq])
                    do("dve", lambda e: e.tensor_tensor(dq[:, 0:1], carry[:, cc:cc + 1], Qap[:, 0:1], ALU.subtract), R=[Qb, carry], W=[dq])
                    do("dve", lambda e: e.tensor_copy(carry[:, cc:cc + 1], Qap[:, TS - 1:TS]), R=[Qb, dq], W=[carry])
                    do("dve", lambda e: e.scalar_tensor_tensor(Qap, dq[:], mu_rkv[:, cc:cc + 1], Qap, ALU.mult, ALU.add), R=[dq, mu_rkv, Qb], W=[Qb])

                for h in range(4):
                    rkv_proj(2, h, Vall, Vall[:, h, :])
                    tshift(Vall, Vall[:, h, :], 8 + h)
                if l == 0:
                    dma(vf_d[:, :, t0:t0 + TS], Vall[:], R=[Vall], W=[B_vf[si]])
                else:
                    pv = psum()
                    for h in range(4):
                        do("pe", lambda e: e.matmul(pv[0:32, :], v1[:, h, :], Vall[:, h, :], start=(h == 0), stop=(h == 3)), R=[v1, Vall], W=[pv])
                    do("act", lambda e: e.copy(VV[:], pv[0:32, :]), R=[pv], W=[VV])
                    for h in range(4):
                        pg = psum()
                        do("pe", lambda e: e.matmul(pg[0:64, :], v2[:, h * 64:(h + 1) * 64], VV[:], start=True, stop=True), R=[v2, VV], W=[pg])
                        wk = work.next()
                        do("act", lambda e: e.activation(wk[0:64, :], pg[0:64, :], AF.Sigmoid, bias=v0[:, h:h + 1]), R=[pg, v0], W=[wk])
                        wk2 = work.next()
                        dma(wk2[0:64, :], vf_d[:, h, t0:t0 + TS], R=[B_vf[si]], W=[wk2])
                        do("dve", lambda e: e.tensor_tensor(wk2[0:64, :], wk2[0:64, :], Vall[:, h, :], ALU.subtract), R=[wk2, Vall], W=[wk2])
                        do("dve", lambda e: e.tensor_tensor(wk2[0:64, :], wk2[0:64, :], wk[0:64, :], ALU.mult), R=[wk2, wk], W=[wk2])
                        do("dve", lambda e: e.tensor_tensor(Vall[:, h, :], Vall[:, h, :], wk2[0:64, :], ALU.add), R=[Vall, wk2], W=[Vall])

                def prep_gen(h, P):
                    Hr, Hk, Hd, He, Hc, Hm, Hp, Ha = (H[k] for k in "rkdecmpa")
                    Hw, Hg, Hq, Atb, Btb, Ktb, Rtb, Vb = (P[k] for k in ("Hw", "Hg", "Hq", "Atb", "Btb", "Ktb", "Rtb", "Vb"))
                    rkv_proj(0, h, Hr, Hr[:])
                    yield
                    rkv_proj(1, h, Hk, Hk[:])
                    yield
                    tshift(Hr, Hr[:], h)
                    yield
                    tshift(Hk, Hk[:], 4 + h)
                    yield
                    cs_ = slice(h * 64, (h + 1) * 64)
                    pw = psum()
                    do("pe", lambda e: e.matmul(pw[0:64, :], lora_up[0:32, cs_], LA[0:32, :], start=True, stop=True), R=[lora_up, LA], W=[pw])
                    do("act", lambda e: e.activation(He[:], pw[0:64, :], AF.Sigmoid, bias=hpar[:, 0, h:h + 1]), R=[pw, hpar], W=[He])
                    yield
                    pa = psum()
                    do("pe", lambda e: e.matmul(pa[0:64, :], lora_up[32:64, cs_], LA[32:64, :], start=True, stop=True), R=[lora_up, LA], W=[pa])
                    do("act", lambda e: e.activation(Ha[:], pa[0:64, :], AF.Sigmoid, bias=hpar[:, 1, h:h + 1]), R=[pa, hpar], W=[Ha])
                    yield
                    pg = psum()
                    do("pe", lambda e: e.matmul(pg[0:64, :], lora_up[64:128, cs_], LA[64:128, :], start=True, stop=True), R=[lora_up, LA], W=[pg])
                    do("act", lambda e: e.copy(Hg[:], pg[0:64, :]), R=[pg], W=[Hg])
                    yield
                    do("dve", lambda e: e.tensor_tensor_scan(Hc[:], reset[:], He[:], 0.0, ALU.mult, ALU.add), R=[reset, He], W=[Hc])
                    yield
                    do("act", lambda e: e.activation(Hw[:], Hc[:], AF.Exp, scale=-C0), R=[Hc], W=[Hw])
                    yield
                    do("act", lambda e: e.activation(Hm[:], Hc[:], AF.Exp, scale=C0), R=[Hc], W=[Hm])
                    yield
                    do("dve", lambda e: e.tensor_tensor(Hd[:], Hc[:], He[:], ALU.subtract), R=[Hc, He], W=[Hd])
                    yield
                    do("act", lambda e: e.activation(Hp[:], Hd[:], AF.Exp, scale=-C0), R=[Hd], W=[Hp])
                    yield
                    do("act", lambda e: e.activation(Hq[:], Hk[:], AF.Copy, scale=hpar[:, 2, h:h + 1]), R=[Hk, hpar], W=[Hq])
                    yield
                    do("act", lambda e: e.activation(Hd[:], Hq[:], AF.Square), R=[Hq], W=[Hd])
                    yield
                    pn = psum()
                    do("pe", lambda e: e.matmul(pn[0:64, :], ones[0:64, 0:64], Hd[:], start=True, stop=True), R=[ones, Hd], W=[pn])
                    do("dve", lambda e: e.tensor_scalar(Hc[:], pn[0:64, :], 1e-24, None, ALU.max), R=[pn], W=[Hc])
                    yield
                    do("act", lambda e: e.activation(Hc[:], Hc[:], AF.Sqrt), R=[Hc], W=[Hc])
                    yield
                    do("dve", lambda e: e.reciprocal(Hc[:], Hc[:]), R=[Hc], W=[Hc])
                    yield
                    do("dve", lambda e: e.tensor_tensor(Hq[:], Hq[:], Hc[:], ALU.mult), R=[Hq, Hc], W=[Hq])
                    yield
                    do("dve", lambda e: e.tensor_scalar(Hd[:], Ha[:], hpar[:, 3, h:h + 1], omka[:, h:h + 1], ALU.mult, ALU.add), R=[Ha, hpar, omka], W=[Hd])
                    yield
                    do("dve", lambda e: e.tensor_tensor(Hk[:], Hk[:], Hd[:], ALU.mult), R=[Hk, Hd], W=[Hk])
                    yield
                    do("dve", lambda e: e.scalar_tensor_tensor(Atb[:], Hq[:], -1.0, Hp[:], ALU.mult, ALU.mult), R=[Hq, Hp], W=[Atb])
                    yield
                    do("dve", lambda e: e.tensor_tensor(Ha[:], Hq[:], Ha[:], ALU.mult), R=[Hq, Ha], W=[Ha])
                    yield
                    do("dve", lambda e: e.tensor_tensor(Btb[:], Ha[:], Hm[:], ALU.mult), R=[Ha, Hm], W=[Btb])
                    yield
                    do("dve", lambda e: e.tensor_tensor(Ktb[:], Hk[:], Hm[:], ALU.mult), R=[Hk, Hm], W=[Ktb])
                    yield
                    do("act", lambda e: e.copy(Vb[:], Vall[:, h, :]), R=[Vall], W=[Vb])
                    yield
                    do("dve", lambda e: e.scalar_tensor_tensor(Hd[:], Hr[:], hpar[:, 6, h:h + 1], Hk[:], ALU.mult, ALU.mult), R=[Hr, hpar, Hk], W=[Hd])
                    yield
                    pn = psum()
                    do("pe", lambda e: e.matmul(pn[0:64, :], ones[0:64, 0:64], Hd[:], start=True, stop=True), R=[ones, Hd], W=[pn])
                    do("dve", lambda e: e.tensor_tensor(Hq[:], pn[0:64, :], Vall[:, h, :], ALU.mult), R=[pn, Vall], W=[Hq])
                    yield
                    do("dve", lambda e: e.tensor_tensor(Rtb[:], Hr[:], Hw[:], ALU.mult), R=[Hr, Hw], W=[Rtb])
                    yield

                def wkv(h, P, pump):
                    Hw, Atb, Btb, Ktb, Rtb, Vb = (P[k] for k in ("Hw", "Atb", "Btb", "Ktb", "Rtb", "Vb"))
                    At, Bt, Kt, Rt = Atb, Btb, Ktb, Rtb
                    PO = PS[7]
                    do("act", lambda e: e.copy(STb[:], STt[:, h, :]), R=[STt], W=[STb])
                    for bt in range(NCH // NB):
                        for q in range(NB):
                            c = bt * NB + q
                            cs2 = slice(c * C, (c + 1) * C)
                            pb = psum()
                            a_, b_, k_, r_ = At[:, cs2], Bt[:, cs2], Kt[:, cs2], Rt[:, cs2]
                            do("pe", lambda e: e.matmul(pb[0:64, 0:64], b_, a_, start=True, stop=True), R=[Bt, At], W=[pb])
                            do("pe", lambda e: e.matmul(pb[0:64, 64:128], a_, b_, start=True, stop=True), R=[At, Bt], W=[pb])
                            do("pe", lambda e: e.matmul(pb[0:64, 128:192], b_, r_, start=True, stop=True), R=[Bt, Rt], W=[pb])
                            do("pe", lambda e: e.matmul(pb[0:64, 192:256], a_, k_, start=True, stop=True), R=[At, Kt], W=[pb])
                            do("pe", lambda e: e.matmul(pb[0:64, 256:320], k_, r_, start=True, stop=True), R=[Kt, Rt], W=[pb])
                            do("dve", lambda e: e.tensor_tensor(SCx[:, q, :], pb[0:64, 0:128], msk5[:, 0:128], ALU.mult), R=[pb, msk5], W=[SCx])
                            do("dve", lambda e: e.tensor_tensor(SCb[:, q, :], pb[0:64, 128:320], msk5[:, 128:320], ALU.mult), R=[pb, msk5], W=[SCb])
                            pt = psum()
                            v_ = Vb[:, cs2]
                            for qi, (sb_, s_) in enumerate(((At, a_), (Bt, b_), (Kt, k_), (Vb, v_))):
                                do("pe", lambda e: e.matmul(pt[0:64, qi * 64:(qi + 1) * 64], s_, ident_bf[0:64, 0:64], start=True, stop=True),
                                   R=[sb_, ident_bf], W=[pt])
                            do("act", lambda e: e.copy(TMb[:, q, :], pt[0:64, 0:256]), R=[pt], W=[TMb])
                            pump()
                        do("dve", lambda e: e.tensor_tensor(TT[:], SCx[:, :, 0:64], ident[0:64, 0:64].unsqueeze(1).to_broadcast([64, NB, 64]), ALU.add),
                           R=[SCx, ident], W=[TT])
                        Xb, Xf = SCx, (lambda q: SCx[:, q, 0:64])
                        Yb, Yf = SCx, (lambda q: SCx[:, q, 64:128])
                        cur = 0
                        for lev in range(1, 6):
                            Xn_, Yn_ = XS[cur], YS[cur]
                            if lev < 5:
                                pb = PS[4]
                                for q in range(NB):
                                    do("pe", lambda e: e.matmul(pb[0:64, q * 64:(q + 1) * 64], Yf(q), Xf(q), start=True, stop=True), R=[Yb, Xb], W=[pb])
                                do("act", lambda e: e.copy(Xn_[:], pb[0:64, 0:NB * 64].rearrange("p (q t) -> p q t", q=NB)), R=[pb], W=[Xn_])
                                pump()
                            pb = PS[5]
                            for q in range(NB):
                                do("pe", lambda e: e.matmul(pb[0:64, q * 64:(q + 1) * 64], Xf(q), Yf(q), start=True, stop=True), R=[Xb, Yb], W=[pb])
                            do("act", lambda e: e.copy(Yn_[:], pb[0:64, 0:NB * 64].rearrange("p (q t) -> p q t", q=NB)), R=[pb], W=[Yn_])
                            pump()
                            pb = PS[6]
                            for q in range(NB):
                                do("pe", lambda e: e.matmul(pb[0:64, q * 64:(q + 1) * 64], Yn_[:, q, :], TT[:, q, :], start=True, stop=True), R=[Yn_, TT], W=[pb])
                            if lev < 5:
                                do("dve", lambda e: e.tensor_tensor(TT[:], TT[:], pb[0:64, 0:NB * 64].rearrange("p (q t) -> p q t", q=NB), ALU.add),
                                   R=[pb, TT], W=[TT])
                            else:
                                do("dve", lambda e: e.tensor_tensor(TTb[:], TT[:], pb[0:64, 0:NB * 64].rearrange("p (q t) -> p q t", q=NB), ALU.add),
                                   R=[pb, TT], W=[TTb])
                            Xb, Yb = Xn_, Yn_
                            pump()
                            Xf = (lambda q, Xn_=Xn_: Xn_[:, q, :])
                            Yf = (lambda q, Yn_=Yn_: Yn_[:, q, :])
                            cur = 1 - cur
                        pb = PS[4]
                        for q in range(NB):
                            do("pe", lambda e: e.matmul(pb[0:64, q * 128:q * 128 + 64], TMb[:, q, 0:64], TTb[:, q, :], start=True, stop=True), R=[TMb, TTb], W=[pb])
                            do("pe", lambda e: e.matmul(pb[0:64, q * 128 + 64:q * 128 + 128], SCb[:, q, 64:128], TTb[:, q, :], start=True, stop=True), R=[SCb, TTb], W=[pb])
                        do("act", lambda e: e.copy(APMb[:], pb[0:64, 0:NB * 128].rearrange("p (q t) -> p q t", q=NB)), R=[pb], W=[APMb])
                        pump()
                        for q in range(NB):
                            c = bt * NB + q
                            cs2 = slice(c * C, (c + 1) * C)
                            pu = PS[5]
                            do("pe", lambda e: e.matmul(pu[0:64, 0:64], APMb[:, q, 64:128], TMb[:, q, 192:256], start=True, stop=False), R=[APMb, TMb], W=[pu])
                            do("pe", lambda e: e.matmul(pu[0:64, 0:64], APMb[:, q, 0:64], STb[:], start=False, stop=True), R=[APMb, STb], W=[pu])
                            do("act", lambda e: e.copy(UTb[:], pu[0:64, 0:64]), R=[pu], W=[UTb])
                            pump()
                            do("pe", lambda e: e.matmul(PO[0:64, cs2], STb[:], Rt[:, cs2], start=True, stop=False), R=[STb, Rt], W=[PO])
                            do("pe", lambda e: e.matmul(PO[0:64, cs2], TMb[:, q, 192:256], SCb[:, q, 128:192], start=False, stop=False), R=[TMb, SCb], W=[PO])
                            do("pe", lambda e: e.matmul(PO[0:64, cs2], UTb[:], SCb[:, q, 0:64], start=False, stop=True), R=[UTb, SCb], W=[PO])
                            pn = PS[6]
                            do("pe", lambda e: e.matmul(pn[0:64, 0:64], TMb[:, q, 64:128], UTb[:], start=True, stop=False), R=[TMb, UTb], W=[pn])
                            do("pe", lambda e: e.matmul(pn[0:64, 0:64], TMb[:, q, 128:192], TMb[:, q, 192:256], start=False, stop=True), R=[TMb], W=[pn])
                            wc = Hw[:, c * C + C - 1:c * C + C]
                            do("act", lambda e: e.activation(STW[:], STt[:, h, :], AF.Copy, scale=wc), R=[STt, Hw], W=[STW])
                            do("dve", lambda e: e.scalar_tensor_tensor(STb[:], pn[0:64, 0:64], wc, STW[:], ALU.mult, ALU.add), R=[pn, Hw, STW], W=[STb])
                            do("dve", lambda e: e.scalar_tensor_tensor(STt[:, h, :], pn[0:64, 0:64], wc, STW[:], ALU.mult, ALU.add), R=[pn, Hw, STW], W=[STt])
                            pump()

                def post(h, P):
                    Hd, Hc, Hm = H["d"], H["c"], H["m"]
                    Hg, Hq = P["Hg"], P["Hq"]
                    PO = PS[7]
                    OS = Hm
                    do("act", lambda e: e.copy(OS[:], PO[0:64, :]), R=[PO], W=[OS])
                    do("act", lambda e: e.activation(Hd[:], OS[:], AF.Square), R=[OS], W=[Hd])
                    pm1 = psum()
                    do("pe", lambda e: e.matmul(pm1[0:64, :], ones[0:64, 0:64], OS[:], start=True, stop=True), R=[ones, OS], W=[pm1])
                    pm2 = psum()
                    do("pe", lambda e: e.matmul(pm2[0:64, :], ones[0:64, 0:64], Hd[:], start=True, stop=True), R=[ones, Hd], W=[pm2])
                    mu_ = work.next()
                    do("act", lambda e: e.activation(mu_[0:64, :], pm1[0:64, :], AF.Copy, scale=1.0 / 64), R=[pm1], W=[mu_])
                    var = work.next()
                    do("dve", lambda e: e.tensor_tensor(var[0:64, :], mu_[0:64, :], mu_[0:64, :], ALU.mult), R=[mu_], W=[var])
                    do("dve", lambda e: e.scalar_tensor_tensor(var[0:64, :], pm2[0:64, :], 1.0 / 64, var[0:64, :], ALU.mult, ALU.subtract), R=[pm2, var], W=[var])
                    do("dve", lambda e: e.tensor_scalar(var[0:64, :], var[0:64, :], 64e-5, None, ALU.add), R=[var], W=[var])
                    do("act", lambda e: e.activation(var[0:64, :], var[0:64, :], AF.Sqrt), R=[var], W=[var])
                    do("dve", lambda e: e.reciprocal(var[0:64, :], var[0:64, :]), R=[var], W=[var])
                    do("dve", lambda e: e.tensor_tensor(Hc[:], OS[:], mu_[0:64, :], ALU.subtract), R=[OS, mu_], W=[Hc])
                    do("dve", lambda e: e.tensor_tensor(Hc[:], Hc[:], var[0:64, :], ALU.mult), R=[Hc, var], W=[Hc])
                    do("dve", lambda e: e.tensor_scalar(Hc[:], Hc[:], hpar[:, 4, h:h + 1], hpar[:, 5, h:h + 1], ALU.mult, ALU.add), R=[Hc, hpar], W=[Hc])
                    do("dve", lambda e: e.tensor_tensor(Hc[:], Hc[:], Hq[:], ALU.add), R=[Hc, Hq], W=[Hc])
                    do("dve", lambda e: e.tensor_tensor(mixA[:, h, :], Hc[:], Hg[:], ALU.mult), R=[Hc, Hg], W=[mixA])

                prepg = prep_gen(0, PP[0])
                for _ in prepg:
                    pass
                for h in range(4):
                    nxt = prep_gen(h + 1, PP[(h + 1) % 2]) if h < 3 else iter(())

                    def pump2(nxt=nxt):
                        pump()
                        next(nxt, None)
                    wkv(h, PP[h % 2], pump2)
                    for _ in nxt:
                        pass
                    post(h, PP[h % 2])

                for _ in s5g_:
                    pass
                if ffn_pending[0] is not None:
                    for _ in ffn_pending[0]:
                        pass
                    ffn_pending[0] = None


                merged = aT
                for ft in range(8):
                    fsl = slice(ft * 128, (ft + 1) * 128)
                    acc = Z[5]
                    gts = [Z[0], Z[1], Z[2], Z[3]]
                    for br in range(4):
                        def ev_g(pb, br=br):
                            do("act", lambda e: e.activation(gts[br][:], pb[:], AF.Sigmoid), R=[pb], W=[gts[br]])
                        proj128(dr["w_in128"][l, 13 + br * 8 + ft], hT, hfn, 8, w128, ev_g)
                    wq = [wload(wsm.next(), dr["sg_out"][l][:, :, fsl]), wload(wsm.next(), dr["conv_out"][l][:, :, fsl]),
                          wload(wsm.next(), dr["glu_w"][l][:, :, fsl]),
                          wload(wsm.next(), dr["glu_w"][l][:, :, 1024 + ft * 128:1024 + (ft + 1) * 128])]
                    wr_ = wload(wra.next(), dr["rwkv_out"][l][:, :, fsl])
                    pa = psum()
                    for h in range(4):
                        do("pe", lambda e: e.matmul(pa[:], wr_[:, h, :], mixA[:, h, :], start=(h == 0), stop=(h == 3)), R=[wr_, mixA], W=[pa])
                    do("dve", lambda e: e.tensor_tensor(acc[:], pa[:], gts[0][:], ALU.mult), R=[pa, gts[0]], W=[acc])
                    for bi, mixT in ((1, mixB), (2, mixC)):
                        pb_ = psum()
                        for kt in range(2):
                            do("pe", lambda e: e.matmul(pb_[:], wq[bi - 1][:, kt, :], mixT[:, kt, :], start=(kt == 0), stop=(kt == 1)), R=[wq[bi - 1], mixT], W=[pb_])
                        do("dve", lambda e: e.tensor_tensor(gts[bi][:], pb_[:], gts[bi][:], ALU.mult), R=[pb_, gts[bi]], W=[gts[bi]])
                        do("dve", lambda e: e.tensor_tensor(acc[:], acc[:], gts[bi][:], ALU.add), R=[acc, gts[bi]], W=[acc])
                    ph1 = psum()
                    for kt in range(2):
                        do("pe", lambda e: e.matmul(ph1[:], wq[2][:, kt, :], mixD[:, kt, :], start=(kt == 0), stop=(kt == 1)), R=[wq[2], mixD], W=[ph1])
                    ph2 = psum()
                    for kt in range(2):
                        do("pe", lambda e: e.matmul(ph2[:], wq[3][:, kt, :], mixD[:, kt, :], start=(kt == 0), stop=(kt == 1)), R=[wq[3], mixD], W=[ph2])
                    do("act", lambda e: e.activation(xn[:], ph2[:], AF.Sigmoid), R=[ph2], W=[xn])
                    do("dve", lambda e: e.tensor_tensor(xn[:], ph1[:], xn[:], ALU.mult), R=[ph1, xn], W=[xn])
                    do("dve", lambda e: e.tensor_tensor(xn[:], xn[:], gts[3][:], ALU.mult), R=[xn, gts[3]], W=[xn])
                    do("dve", lambda e: e.tensor_tensor(merged[:, ft, :], acc[:], xn[:], ALU.add), R=[acc, xn], W=[merged])

                dma(X[:], src[:, t0:t0 + TS].rearrange("(ft p) t -> p ft t", p=128), R=srcb, W=[X])
                for mt in range(8):
                    def ev_o(pb, mt=mt):
                        do("dve", lambda e: e.scalar_tensor_tensor(X[:, mt, :], pb[:], modT[:, 16 + mt:17 + mt], X[:, mt, :], ALU.mult, ALU.add),
                           R=[pb, modT, X], W=[X])
                    proj128(dr["w_o"][l, mt], merged, lambda kt: merged[:, kt, :], 8, w128, ev_o)

                rmsnorm_to(hTB, geff2, 24)
                def ffn_gen(l=l, si=si, t0=t0):
                    for qtr in range(4):
                        for i in range(8):
                            def ev_f(pb, i=i):
                                wk = work.next()
                                do("act", lambda e: e.activation(wk[:], pb[:], AF.Relu), R=[pb], W=[wk])
                                do("dve", lambda e: e.tensor_tensor(aT[:, i, :], wk[:], wk[:], ALU.mult), R=[wk], W=[aT])
                            proj128(dr["ffn_w1"][l, qtr * 8 + i], hTB, lambda kt: hTB[:, kt, :], 8, w128, ev_f)
                            yield
                        for mt in range(8):
                            def ev_2(pb, mt=mt):
                                do("dve", lambda e: e.scalar_tensor_tensor(X[:, mt, :], pb[:], modT[:, 40 + mt:41 + mt], X[:, mt, :], ALU.mult, ALU.add),
                                   R=[pb, modT, X], W=[X])
                            proj128(dr["ffn_w2"][l, mt][:, qtr * 8:(qtr + 1) * 8, :], aT, lambda kt: aT[:, kt, :], 8, w128, ev_2)
                            yield
                    if l == 0:
                        dma(xs_d[:, t0:t0 + TS].rearrange("(ft p) t -> p ft t", p=128), X[:], R=[X], W=[B_xs[si]])
                    else:
                        rms_stats()
                        for ft in range(8):
                            do("dve", lambda e: e.scalar_tensor_tensor(X[:, ft, :], X[:, ft, :], g_fin[:, ft:ft + 1], rstd[:], ALU.mult, ALU.mult),
                               R=[X, g_fin, rstd], W=[X])
                        dma(outT[:, t0:t0 + TS].rearrange("(ft p) t -> p ft t", p=128), X[:], R=[X], W=[B_outs[si]])
                    yield

                ffn_pending[0] = ffn_gen()
                if si == NS - 1:
                    for _ in ffn_pending[0]:
                        pass
                    ffn_pending[0] = None
        S.finish(B_outs)
        print("instructions:", S.n_inst, "waits:", S.n_wait)
    return nc


_CACHE = {}


def run(inputs, T):
    maps = [prep_inputs(inputs, b) for b in range(4)]
    shapes = {k: v.shape for k, v in maps[0].items()}
    key = (T,)
    if key not in _CACHE:
        _CACHE[key] = build(shapes, T)
    nc = _CACHE[key]
    in_maps = [maps[i % 4] for i in range(8)]
    res = run_bass_kernel_spmd(nc, in_maps, core_ids=list(range(8)))
    out = np.stack([np.ascontiguousarray(res.results[b]["outT"].T) for b in range(4)])
    return out.astype(np.float32)


def kernel(**inputs):
    inputs = {k: np.asarray(v, dtype=np.float32) for k, v in inputs.items()}
    T = inputs["x"].shape[1]
    return run(inputs, T)
```

```python
import contextlib
import math
import os
import numpy as np
import concourse.bass as bass
import concourse.mybir as mybir
from concourse.bass_utils import run_bass_kernel_spmd

F32 = mybir.dt.float32
BF16 = mybir.dt.bfloat16
AF = mybir.ActivationFunctionType
ALU = mybir.AluOpType

D = 1024
DM = 256
TS = 512
C = 64
NCH = TS // C
FR = 128
EPS = 1e-6
C0 = math.exp(-0.5)
MAGIC = 12582912.0


class Buf:
    __slots__ = ("name", "t", "lastw", "readers", "psum")

    def __init__(self, name, t=None):
        self.name = name
        self.t = t
        self.lastw = None
        self.readers = []
        self.psum = False

    def __getitem__(self, idx):
        return self.t[idx]


class Sched:
    N_DMA_SEM = 10

    def __init__(self, nc, stack):
        self.nc = nc
        self.stack = stack
        self.engs = {"pe": nc.tensor, "act": nc.scalar, "dve": nc.vector, "pool": nc.gpsimd, "sp": nc.sync}
        self.sem = {}
        self.cnt = {}
        for k in ("pe", "act", "dve", "pool"):
            self.sem[k] = stack.enter_context(nc.semaphore("s_" + k))
            self.cnt[k] = 0
        self.dma_sems = {"sp": [], "pool": []}
        for qn in ("sp", "pool"):
            for i in range(self.N_DMA_SEM):
                key = "dma_%s%d" % (qn, i)
                self.sem[key] = stack.enter_context(nc.semaphore("s_" + key))
                self.cnt[key] = 0
                self.dma_sems[qn].append(key)
        self.dma_rr = {"sp": 0, "pool": 0}
        self.waited = {}
        self.n_inst = 0
        self.n_wait = 0
        self.sb_bytes = 0

    def sb(self, name, shape, dt):
        t = self.stack.enter_context(self.nc.sbuf_tensor("sb_" + name, list(shape), dt))
        n = 1
        for s in shape[1:]:
            n *= s
        self.sb_bytes += n * (2 if dt == BF16 else 4)
        return Buf(name, t)

    def ps(self, name, shape, dt=F32):
        t = self.stack.enter_context(self.nc.psum_tensor("pp_" + name, list(shape), dt))
        b = Buf(name, t)
        b.psum = True
        return b

    def _need(self, eng, deps):
        best = {}
        for d in deps:
            if d is None:
                continue
            sk, v, _ = d
            if v > best.get(sk, 0):
                best[sk] = v
        for sk, v in best.items():
            if self.waited.get((eng, sk), 0) >= v:
                continue
            self.engs[eng].wait_ge(self.sem[sk], v)
            self.waited[(eng, sk)] = v
            self.n_wait += 1

    def do(self, eng, fn, R=(), W=()):
        deps = []
        for b in R:
            lw = b.lastw
            if lw is not None and not (lw[2] == eng and eng == "pe"):
                deps.append(lw)
            if b.psum:
                for r in b.readers:
                    if r[2] != eng:
                        deps.append(r)
        for b in W:
            lw = b.lastw
            if lw is not None and (lw[2] != eng or eng != "pe"):
                deps.append(lw)
            for r in b.readers:
                if r[2] != eng or eng != "pe":
                    deps.append(r)
        self._need(eng, deps)
        ins = fn(self.engs[eng])
        self.cnt[eng] += 1
        v = self.cnt[eng]
        ins.then_inc(self.sem[eng], 1)
        tok = (eng, v, eng)
        for b in R:
            b.readers = [r for r in b.readers if r[0] != eng] + [tok]
        for b in W:
            b.lastw = tok
            b.readers = []
        self.n_inst += 1
        return ins

    def dma(self, out_ap, in_ap, R=(), W=(), q="sp"):
        sk = self.dma_sems[q][self.dma_rr[q]]
        self.dma_rr[q] = (self.dma_rr[q] + 1) % len(self.dma_sems[q])
        deps = []
        for b in R:
            if b.lastw is not None:
                deps.append(b.lastw)
        for b in W:
            if b.lastw is not None:
                deps.append(b.lastw)
            deps.extend(b.readers)
        if self.cnt[sk] > 0:
            deps.append((sk, self.cnt[sk], "dmaq"))
        self._need(q, deps)
        ins = self.engs[q].dma_start(out=out_ap, in_=in_ap)
        self.cnt[sk] += 16
        ins.then_inc(self.sem[sk], 16)
        tok = (sk, self.cnt[sk], "dmaq")
        for b in R:
            b.readers.append(tok)
        for b in W:
            b.lastw = tok
            b.readers = []
        self.n_inst += 1
        return ins

    def finish(self, bufs, eng="sp"):
        self._need(eng, [b.lastw for b in bufs if b.lastw is not None])


class Ring:
    def __init__(self, S, name, n, shape, dt, psum=False):
        self.bufs = [(S.ps if psum else S.sb)("%s%d" % (name, i), shape, dt) for i in range(n)]
        self.i = 0

    def next(self):
        b = self.bufs[self.i]
        self.i = (self.i + 1) % len(self.bufs)
        return b


def tile_km(W, msz):
    K, M = W.shape
    return np.ascontiguousarray(W.reshape(K // 128, 128, M // msz, msz).transpose(2, 1, 0, 3))


def col_pk(v, p=128):
    return np.ascontiguousarray(v.reshape(-1, p).T)


def prep_inputs(inp, b):
    f = np.float32
    L = 2
    m = {}
    m["xT"] = np.ascontiguousarray(inp["x"][b].T)
    m["cT"] = col_pk(inp["c"][b])
    m["ident"] = np.eye(128, dtype=f)
    m["ones"] = np.ones((128, 128), f)
    su = np.triu(np.ones((C, C), f), 1)
    sl = np.tril(np.ones((C, C), f), -1)
    iu = np.triu(np.ones((C, C), f), 0)
    m["msk5"] = np.ascontiguousarray(np.concatenate([su, sl, iu, sl, iu], axis=1))
    rs = np.ones((64, TS), f)
    rs[:, ::C] = 0.0
    m["reset"] = rs
    m["sgmask"] = np.triu(np.ones((128, 128), f), 0)
    m["iota"] = np.ascontiguousarray(np.broadcast_to(np.arange(FR, dtype=f) + 1.0, (128, FR)))
    sel = np.zeros((128, 2), f)
    sel[(np.arange(128) % 32) < 16, 0] = 1.0
    sel[(np.arange(128) % 32) >= 16, 1] = 1.0
    m["sel16"] = sel
    m["ada_w"] = np.ascontiguousarray(inp["ada_w"].reshape(L, 8, 128, 48, 128).transpose(0, 3, 2, 1, 4))
    m["ada_b"] = np.stack([col_pk(inp["ada_b"][l]) for l in range(L)])
    m["g_mix"] = np.stack([col_pk(inp["norm_mix_g"][l]) for l in range(L)])
    m["g_ffn"] = np.stack([col_pk(inp["norm_ffn_g"][l]) for l in range(L)])
    m["g_fin"] = col_pk(inp["final_g"])
    w_in = inp["w_in"]
    m["w_in64"] = np.stack([tile_km(w_in[l][:, :768], 64) for l in range(L)])
    m["w_in128"] = np.stack([tile_km(w_in[l][:, 768:], 128) for l in range(L)])
    mu = inp["rwkv_mu"]
    m["mu_rkv"] = np.stack([np.concatenate([col_pk(mu[l][q * 256:(q + 1) * 256], 64) for q in range(3)], axis=1)
                            for l in range(L)])
    m["mu_lora"] = np.stack([mu[l][768:896].reshape(128, 1) for l in range(L)])
    hp = lambda k: np.stack([col_pk(inp[k][l], 64) for l in range(inp[k].shape[0])])
    rk = inp["rwkv_rk"].reshape(L, 256)
    m["hpar"] = np.ascontiguousarray(np.stack(
        [hp("rwkv_w0"), hp("rwkv_a0"), hp("rwkv_kk"), hp("rwkv_ka"), hp("rwkv_lnx_w"), hp("rwkv_lnx_b"),
         np.stack([col_pk(rk[l], 64) for l in range(L)])], axis=2))
    m["v0"] = hp("rwkv_v0")
    m["lora_up"] = np.ascontiguousarray(np.concatenate([inp["rwkv_w2"], inp["rwkv_a2"], inp["rwkv_g2"]], axis=1))
    m["v1"] = np.ascontiguousarray(inp["rwkv_v1"].reshape(1, 4, 64, 32).transpose(0, 2, 1, 3))
    m["v2"] = np.ascontiguousarray(inp["rwkv_v2"])
    m["rwkv_out"] = np.ascontiguousarray(inp["rwkv_out"].reshape(L, 4, 64, 1024).transpose(0, 2, 1, 3))
    k2 = lambda k: np.ascontiguousarray(inp[k].reshape(L, 2, 128, -1).transpose(0, 2, 1, 3))
    m["sg_out"] = k2("sg_out")
    m["conv_out"] = k2("conv_out")
    m["glu_w"] = k2("s5_glu_w")
    m["sg_ln"] = np.stack([np.concatenate([col_pk(inp["sg_ln_w"][l]), col_pk(inp["sg_ln_b"][l])], axis=1) for l in range(L)])
    m["sg_wsT"] = np.ascontiguousarray(inp["sg_ws"].transpose(0, 3, 1, 2))
    bs = inp["sg_bs"]
    m["sg_bs"] = np.ascontiguousarray(np.stack(
        [np.stack([np.repeat(bs[l, 2 * ct:2 * ct + 2], 64, axis=0) for ct in range(2)], axis=1) for l in range(L)]))
    m["conv_w"] = np.ascontiguousarray(inp["conv_w"].reshape(L, 3, 2, 128).transpose(0, 3, 2, 1))
    pair = lambda a: np.ascontiguousarray(a.reshape(L, 8, 128).transpose(0, 2, 1))
    m["s5_lam"] = np.ascontiguousarray(np.stack(
        [pair(inp["s5_a_re"]), pair(inp["s5_a_im"]),
         pair(np.repeat(inp["s5_log_dt"][:, :, None], 64, axis=2))], axis=2))
    def bk(a):
        return np.ascontiguousarray(a.reshape(L, 2, 8, 64, 16).transpose(0, 2, 4, 1, 3).reshape(L, 128, 2, 64))
    def lk(a):
        a3 = a if a.ndim == 3 else np.repeat(a[:, :, None], 64, axis=2)
        r = np.repeat(a3.reshape(L, 2, 8, 1, 64), 16, axis=3)
        return np.ascontiguousarray(r.transpose(0, 2, 3, 1, 4).reshape(L, 128, 2, 64))
    m["s5_bk"] = np.ascontiguousarray(np.stack(
        [bk(inp["s5_b_re"]), bk(inp["s5_b_im"]), lk(inp["s5_a_re"]), lk(inp["s5_a_im"]), lk(inp["s5_log_dt"])], axis=2))
    lc = np.zeros((L, 8, 128, 2, 128), f)
    for jp in range(8):
        for gg in range(2):
            g = 2 * jp + gg
            cs = (jp % 4) * 32 + gg * 16
            lc[:, jp, gg * 64:(gg + 1) * 64, 0, cs:cs + 16] = inp["s5_c_re"][:, g].transpose(0, 2, 1)
            lc[:, jp, gg * 64:(gg + 1) * 64, 1, cs:cs + 16] = inp["s5_c_im"][:, g].transpose(0, 2, 1)
    m["s5_lc"] = np.ascontiguousarray(lc.transpose(0, 2, 1, 3, 4))
    m["s5_d"] = np.stack([col_pk(inp["s5_d"][l]) for l in range(L)])
    m["w_o"] = np.stack([tile_km(inp["w_o"][l], 128) for l in range(L)])
    m["ffn_w1"] = np.stack([tile_km(inp["ffn_w1"][l], 128) for l in range(L)])
    m["ffn_w2"] = np.stack([tile_km(inp["ffn_w2"][l], 128) for l in range(L)])
    return {k: np.ascontiguousarray(v, dtype=f) for k, v in m.items()}


def build(shapes, T):
    NS = T // TS
    NF = TS // FR
    nc = bass.Bass("TRN2", target_bir_lowering=False)
    dr = {}
    for k, shp in shapes.items():
        dr[k] = nc.dram_tensor(k, list(shp), F32, kind="ExternalInput").ap()
    outT = nc.dram_tensor("outT", [D, T], F32, kind="ExternalOutput").ap()
    xs_d = nc.dram_tensor("xs_scr", [D, T], F32, kind="Internal").ap()
    vf_d = nc.dram_tensor("vf_scr", [64, 4, T], F32, kind="Internal").ap()

    with contextlib.ExitStack() as st:
        S = Sched(nc, st)
        do, dma = S.do, S.dma
        B_outs = [Buf("outT%d" % i) for i in range(NS)]
        B_xs = [Buf("xs%d" % i) for i in range(NS)]
        B_vf = [Buf("vf%d" % i) for i in range(NS)]

        def cload(name, key, shape):
            b = S.sb(name, shape, F32)
            dma(b[:], dr[key], W=[b])
            return b
        ident = cload("ident", "ident", [128, 128])
        ones = cload("ones", "ones", [128, 128])
        msk5 = cload("msk5", "msk5", [64, 320])
        reset = cload("reset", "reset", [64, TS])
        sgmask = cload("sgmask", "sgmask", [128, 128])
        iota = cload("iota", "iota", [128, FR])
        sel16 = cload("sel16", "sel16", [128, 2])
        cT = cload("cT", "cT", [128, 8])
        g_fin = cload("g_fin", "g_fin", [128, 8])
        ones_bf = S.sb("ones_bf", [128, 128], BF16)
        do("dve", lambda e: e.tensor_copy(ones_bf[:], ones[:]), R=[ones], W=[ones_bf])
        csil = S.sb("csil", [128, 8], F32)
        do("act", lambda e: e.activation(csil[:], cT[:], AF.Silu), R=[cT], W=[csil])
        csil_bf = S.sb("csil_bf", [128, 8], BF16)
        do("dve", lambda e: e.tensor_copy(csil_bf[:], csil[:]), R=[csil], W=[csil_bf])
        negpi = S.sb("negpi", [128, 1], F32)
        do("dve", lambda e: e.memset(negpi[:], -math.pi), W=[negpi])

        PS = [S.ps("ps%d" % i, [128, 512]) for i in range(8)]
        ps_rr = [0]

        def psum():
            b = PS[ps_rr[0] % 3]
            ps_rr[0] += 1
            return b

        X = S.sb("X", [128, 8, TS], F32)
        hT = S.sb("hT", [128, 8, TS], BF16)
        hTB = S.sb("hTB", [128, 8, TS], BF16)
        aT = S.sb("aT", [128, 8, TS], BF16)
        sqb = S.sb("sqb", [128, TS], BF16)
        rstd = S.sb("rstd", [128, TS], F32)
        xn = S.sb("xn", [128, TS], F32)
        mixA = S.sb("mixA", [64, 4, TS], BF16)
        mixB = S.sb("mixB", [128, 2, TS], BF16)
        mixC = S.sb("mixC", [128, 2, TS], BF16)
        mixD = S.sb("mixD", [128, 2, TS], BF16)
        w128 = Ring(S, "w128_", 4, [128, 8, 128], BF16)
        wsm = Ring(S, "wsm_", 6, [128, 2, 128], BF16)
        w64 = Ring(S, "w64_", 2, [128, 8, 64], BF16)
        wra = Ring(S, "wra_", 2, [64, 4, 128], BF16)
        modT = S.sb("modT", [128, 48], F32)
        geff1 = S.sb("geff1", [128, 8], F32)
        geff2 = S.sb("geff2", [128, 8], F32)
        tmp48 = S.sb("tmp48", [128, 48], F32)
        work = Ring(S, "work_", 3, [128, TS], F32)
        Z = [S.sb("Z%d" % i, [128, TS], F32) for i in range(6)]
        PL = S.sb("PL", [128, TS], F32)
        LA = S.sb("LA", [128, TS], F32)
        H = {nm: S.sb("H_" + nm, [64, TS], F32) for nm in ("r", "k", "d", "e", "c", "w", "m", "p", "a", "g", "q")}
        Vall = S.sb("Vall", [64, 4, TS], F32)
        carry = S.sb("carry", [64, 12], F32)
        carryL = S.sb("carryL", [128, 1], F32)
        VV = S.sb("VV", [32, TS], F32)
        STt = S.sb("STt", [64, 4, 64], F32)
        STW = S.sb("STW", [64, 64], F32)
        NB = 4
        SCx = S.sb("SCx", [64, NB, 128], F32)
        SCb = S.sb("SCb", [64, NB, 192], BF16)
        TMb = S.sb("TMb", [64, NB, 256], BF16)
        XS = [S.sb("XS%d" % i, [64, NB, 64], F32) for i in range(2)]
        YS = [S.sb("YS%d" % i, [64, NB, 64], F32) for i in range(2)]
        TT = S.sb("TT", [64, NB, 64], F32)
        TTb = S.sb("TTb", [64, NB, 64], BF16)
        APMb = S.sb("APMb", [64, NB, 128], BF16)
        PP = []
        for par in range(2):
            d_ = {k: S.sb("%s_%d" % (k, par), [64, TS], BF16) for k in ("Atb", "Btb", "Ktb", "Rtb", "Vb")}
            if par == 0:
                d_.update({"Hw": H["w"], "Hg": H["g"], "Hq": H["q"]})
            else:
                d_.update({k: S.sb("%s_%d" % (k, par), [64, TS], F32) for k in ("Hw", "Hg", "Hq")})
            PP.append(d_)
        STb = S.sb("STb", [64, 64], BF16)
        UTb = S.sb("UTb", [64, 64], BF16)
        ident_bf = S.sb("ident_bf", [128, 128], BF16)
        mu_rkv = S.sb("mu_rkv", [64, 12], F32)
        mu_lora = S.sb("mu_lora", [128, 1], F32)
        hpar = S.sb("hpar", [64, 7, 4], F32)
        omka = S.sb("omka", [64, 4], F32)
        v0 = S.sb("v0", [64, 4], F32)
        lora_up = S.sb("lora_up", [128, 256], F32)
        v1 = S.sb("v1", [64, 4, 32], F32)
        v2 = S.sb("v2", [32, 256], F32)
        sg_ln = S.sb("sg_ln", [128, 4], F32)
        sg_wb = S.sb("sg_wb", [128, 4, 128], BF16)
        sg_bs = S.sb("sg_bs", [128, 2, 128], F32)
        conv_w = S.sb("conv_w", [128, 2, 3], F32)
        zc = S.sb("zc", [128, 2, TS + 2], F32)
        vz = [S.sb("vz%d" % i, [128, 128], BF16) for i in range(4)]
        s5_lam = S.sb("s5_lam", [128, 3, 8], F32)
        s5_bk = S.sb("s5_bk", [128, 5, 2, 64], F32)
        s5_lc = S.sb("s5_lc", [128, 8, 2, 128], BF16)
        s5_d = S.sb("s5_d", [128, 2], F32)
        LB = S.sb("LB", [128, 8, 2, 128], BF16)
        cosT = S.sb("cosT", [128, 8, FR], F32)
        sinT = S.sb("sinT", [128, 8, FR], F32)
        rho = S.sb("rho", [128, 8], F32)
        th = S.sb("th", [128, 8], F32)
        s5st = S.sb("s5st", [128, 8, 2], F32)
        s5c = S.sb("s5c", [128, 2], F32)
        small = Ring(S, "small_", 4, [128, 64], F32)
        uT = S.sb("uT", [128, 2, TS], F32)
        uTb = S.sb("uTb", [128, 2, TS], BF16)
        xrb = S.sb("xrb", [128, TS], BF16)
        xib = S.sb("xib", [128, TS], BF16)
        tP = {nm: S.sb("s5p_" + nm, [128, 8], F32) for nm in ("lre", "dt", "mag", "ang", "sn", "cs", "abr", "abi", "den", "qre", "qim", "t1", "t2")}
        tK = {nm: S.sb("s5k_" + nm, [128, 2, 64], F32) for nm in ("lre", "dt", "mag", "ang", "sn", "cs", "abr", "abi", "den", "qre", "qim", "t1", "t2")}
        bbr = S.sb("bbr", [128, 2, 64], F32)
        bbi = S.sb("bbi", [128, 2, 64], F32)
        tq = S.sb("tq", [128, 2, 64], F32)

        def zero(b, ap=None):
            do("dve", lambda e: e.memset(b[:] if ap is None else ap, 0.0), W=[b])
        for b in vz:
            zero(b)
        do("dve", lambda e: e.tensor_copy(ident_bf[:], ident[:]), R=[ident], W=[ident_bf])
        zero(LB)
        print("SBUF bytes/partition:", S.sb_bytes)

        def wload(buf, src, ap=None):
            dma(buf[:] if ap is None else ap, src, W=[buf], q="pool")
            return buf

        def rms_stats():
            pb = psum()
            for ft in range(8):
                do("act", lambda e: e.activation(sqb[:], X[:, ft, :], AF.Square), R=[X], W=[sqb])
                do("pe", lambda e: e.matmul(pb[:], ones_bf[:], sqb[:], start=(ft == 0), stop=(ft == 7)), R=[ones_bf, sqb], W=[pb])
            do("dve", lambda e: e.tensor_scalar(rstd[:], pb[:], 1.0 / D, EPS, ALU.mult, ALU.add), R=[pb], W=[rstd])
            do("act", lambda e: e.activation(rstd[:], rstd[:], AF.Sqrt), R=[rstd], W=[rstd])
            do("dve", lambda e: e.reciprocal(rstd[:], rstd[:]), R=[rstd], W=[rstd])

        def rmsnorm_to(dst, geff, shift_col0):
            rms_stats()
            for ft in range(8):
                do("dve", lambda e: e.tensor_tensor(xn[:], X[:, ft, :], rstd[:], ALU.mult), R=[X, rstd], W=[xn])
                do("act", lambda e: e.activation(dst[:, ft, :], xn[:], AF.Identity, bias=modT[:, shift_col0 + ft:shift_col0 + ft + 1],
                                                 scale=geff[:, ft:ft + 1]), R=[xn, modT, geff], W=[dst])

        def proj128(src_ap, rhs_buf, rhs_fn, nk, ring, evac):
            wb = wload(ring.next(), src_ap)
            pb = psum()
            for kt in range(nk):
                do("pe", lambda e: e.matmul(pb[:], wb[:, kt, :], rhs_fn(kt), start=(kt == 0), stop=(kt == nk - 1)), R=[wb, rhs_buf], W=[pb])
            evac(pb)

        def s5_disc(t, are, aim, ldt, srcb):
            A = lambda nm: t[nm][:]
            B = lambda nm: t[nm]
            do("dve", lambda e: e.tensor_scalar(A("lre"), are, -1e-4, None, ALU.min), R=[srcb], W=[B("lre")])
            do("act", lambda e: e.activation(A("dt"), ldt, AF.Exp), R=[srcb], W=[B("dt")])
            do("dve", lambda e: e.tensor_tensor(A("t1"), A("lre"), A("dt"), ALU.mult), R=[B("lre"), B("dt")], W=[B("t1")])
            do("act", lambda e: e.activation(A("mag"), A("t1"), AF.Exp), R=[B("t1")], W=[B("mag")])
            do("dve", lambda e: e.tensor_tensor(A("ang"), aim, A("dt"), ALU.mult), R=[srcb, B("dt")], W=[B("ang")])
            for off, dst in ((0.0, "sn"), (0.5 * math.pi, "cs")):
                do("dve", lambda e: e.tensor_scalar(A("t1"), A("ang"), off, None, ALU.add), R=[B("ang")], W=[B("t1")])
                do("dve", lambda e: e.tensor_scalar(A("t2"), A("t1"), 1.0 / (2 * math.pi), MAGIC, ALU.mult, ALU.add), R=[B("t1")], W=[B("t2")])
                do("dve", lambda e: e.tensor_scalar(A("t2"), A("t2"), -MAGIC, None, ALU.add), R=[B("t2")], W=[B("t2")])
                do("dve", lambda e: e.scalar_tensor_tensor(A("t1"), A("t2"), -2 * math.pi, A("t1"), ALU.mult, ALU.add), R=[B("t2"), B("t1")], W=[B("t1")])
                do("act", lambda e: e.activation(A(dst), A("t1"), AF.Sin), R=[B("t1")], W=[B(dst)])
            do("dve", lambda e: e.tensor_tensor(A("abr"), A("mag"), A("cs"), ALU.mult), R=[B("mag"), B("cs")], W=[B("abr")])
            do("dve", lambda e: e.tensor_tensor(A("abi"), A("mag"), A("sn"), ALU.mult), R=[B("mag"), B("sn")], W=[B("abi")])
            do("dve", lambda e: e.tensor_tensor(A("den"), A("lre"), A("lre"), ALU.mult), R=[B("lre")], W=[B("den")])
            do("dve", lambda e: e.tensor_tensor(A("t1"), aim, aim, ALU.mult), R=[srcb], W=[B("t1")])
            do("dve", lambda e: e.tensor_tensor(A("den"), A("den"), A("t1"), ALU.add), R=[B("den"), B("t1")], W=[B("den")])
            do("dve", lambda e: e.reciprocal(A("den"), A("den")), R=[B("den")], W=[B("den")])
            do("dve", lambda e: e.tensor_scalar(A("t1"), A("abr"), -1.0, None, ALU.add), R=[B("abr")], W=[B("t1")])
            do("dve", lambda e: e.tensor_tensor(A("qre"), A("t1"), A("lre"), ALU.mult), R=[B("t1"), B("lre")], W=[B("qre")])
            do("dve", lambda e: e.tensor_tensor(A("t2"), A("abi"), aim, ALU.mult), R=[B("abi"), srcb], W=[B("t2")])
            do("dve", lambda e: e.tensor_tensor(A("qre"), A("qre"), A("t2"), ALU.add), R=[B("qre"), B("t2")], W=[B("qre")])
            do("dve", lambda e: e.tensor_tensor(A("qre"), A("qre"), A("den"), ALU.mult), R=[B("qre"), B("den")], W=[B("qre")])
            do("dve", lambda e: e.tensor_tensor(A("qim"), A("abi"), A("lre"), ALU.mult), R=[B("abi"), B("lre")], W=[B("qim")])
            do("dve", lambda e: e.tensor_tensor(A("t2"), A("t1"), aim, ALU.mult), R=[B("t1"), srcb], W=[B("t2")])
            do("dve", lambda e: e.tensor_tensor(A("qim"), A("qim"), A("t2"), ALU.subtract), R=[B("qim"), B("t2")], W=[B("qim")])
            do("dve", lambda e: e.tensor_tensor(A("qim"), A("qim"), A("den"), ALU.mult), R=[B("qim"), B("den")], W=[B("qim")])

        for l in range(2):
            pmod = PS[4]
            for mt in range(48):
                wb = wload(w128.next(), dr["ada_w"][l, mt])
                for kt in range(8):
                    do("pe", lambda e: e.matmul(pmod[:, mt:mt + 1], wb[:, kt, :], csil_bf[:, kt:kt + 1], start=(kt == 0), stop=(kt == 7)),
                       R=[wb, csil_bf], W=[pmod])
            dma(tmp48[:], dr["ada_b"][l], W=[tmp48])
            do("dve", lambda e: e.tensor_tensor(modT[:], pmod[:, 0:48], tmp48[:], ALU.add), R=[pmod, tmp48], W=[modT])
            gm = small.next()
            dma(gm[:, 0:8], dr["g_mix"][l], W=[gm])
            do("dve", lambda e: e.scalar_tensor_tensor(geff1[:], modT[:, 8:16], 1.0, gm[:, 0:8], ALU.add, ALU.mult), R=[modT, gm], W=[geff1])
            gf = small.next()
            dma(gf[:, 0:8], dr["g_ffn"][l], W=[gf])
            do("dve", lambda e: e.scalar_tensor_tensor(geff2[:], modT[:, 32:40], 1.0, gf[:, 0:8], ALU.add, ALU.mult), R=[modT, gf], W=[geff2])

            dma(mu_rkv[:], dr["mu_rkv"][l], W=[mu_rkv])
            dma(mu_lora[:], dr["mu_lora"][l], W=[mu_lora])
            dma(hpar[:], dr["hpar"][l], W=[hpar])
            do("dve", lambda e: e.tensor_scalar(omka[:], hpar[:, 3, :], -1.0, 1.0, ALU.mult, ALU.add), R=[hpar], W=[omka])
            dma(lora_up[:], dr["lora_up"][l], W=[lora_up])
            if l == 1:
                dma(v0[:], dr["v0"][0], W=[v0])
                dma(v1[:], dr["v1"][0], W=[v1])
                dma(v2[:], dr["v2"][0], W=[v2])
            dma(sg_ln[:], dr["sg_ln"][l], W=[sg_ln])
            sgw = Z[4]
            dma(sgw[:].rearrange("p (g t) -> p g t", g=4), dr["sg_wsT"][l], W=[sgw])
            for g in range(4):
                do("dve", lambda e: e.tensor_tensor(sg_wb[:, g, :], sgw[:, g * 128:(g + 1) * 128], sgmask[:], ALU.mult), R=[sgw, sgmask], W=[sg_wb])
            dma(sg_bs[:], dr["sg_bs"][l], W=[sg_bs])
            dma(conv_w[:], dr["conv_w"][l], W=[conv_w])
            dma(s5_lam[:], dr["s5_lam"][l], W=[s5_lam])
            dma(s5_bk[:], dr["s5_bk"][l], W=[s5_bk])
            wload(s5_lc, dr["s5_lc"][l])
            dma(s5_d[:], dr["s5_d"][l], W=[s5_d])
            do("dve", lambda e: e.tensor_scalar(s5_lc[:, :, 1, :], s5_lc[:, :, 1, :], -1.0, None, ALU.mult), R=[s5_lc], W=[s5_lc])

            s5_disc(tP, s5_lam[:, 0, :], s5_lam[:, 1, :], s5_lam[:, 2, :], s5_lam)
            do("dve", lambda e: e.tensor_copy(rho[:], tP["mag"][:]), R=[tP["mag"]], W=[rho])
            do("dve", lambda e: e.tensor_copy(th[:], tP["ang"][:]), R=[tP["ang"]], W=[th])
            for jp in range(8):
                for off, dstT in ((0.0, sinT), (0.5 * math.pi, cosT)):
                    wk = work.next()
                    wk2 = work.next()
                    do("dve", lambda e: e.tensor_scalar(wk[:, 0:FR], iota[:], th[:, jp:jp + 1], off, ALU.mult, ALU.add), R=[iota, th], W=[wk])
                    do("dve", lambda e: e.tensor_scalar(wk2[:, 0:FR], wk[:, 0:FR], 1.0 / (2 * math.pi), MAGIC, ALU.mult, ALU.add), R=[wk], W=[wk2])
                    do("dve", lambda e: e.tensor_scalar(wk2[:, 0:FR], wk2[:, 0:FR], -MAGIC, None, ALU.add), R=[wk2], W=[wk2])
                    do("dve", lambda e: e.scalar_tensor_tensor(wk[:, 0:FR], wk2[:, 0:FR], -2 * math.pi, wk[:, 0:FR], ALU.mult, ALU.add), R=[wk2, wk], W=[wk])
                    do("act", lambda e: e.activation(dstT[:, jp, :], wk[:, 0:FR], AF.Sin), R=[wk], W=[dstT])
            s5_disc(tK, s5_bk[:, 2], s5_bk[:, 3], s5_bk[:, 4], s5_bk)
            do("dve", lambda e: e.tensor_tensor(bbr[:], tK["qre"][:], s5_bk[:, 0], ALU.mult), R=[tK["qre"], s5_bk], W=[bbr])
            do("dve", lambda e: e.tensor_tensor(tq[:], tK["qim"][:], s5_bk[:, 1], ALU.mult), R=[tK["qim"], s5_bk], W=[tq])
            do("dve", lambda e: e.tensor_tensor(bbr[:], bbr[:], tq[:], ALU.subtract), R=[bbr, tq], W=[bbr])
            do("dve", lambda e: e.tensor_tensor(bbi[:], tK["qre"][:], s5_bk[:, 1], ALU.mult), R=[tK["qre"], s5_bk], W=[bbi])
            do("dve", lambda e: e.tensor_tensor(tq[:], tK["qim"][:], s5_bk[:, 0], ALU.mult), R=[tK["qim"], s5_bk], W=[tq])
            do("dve", lambda e: e.tensor_tensor(bbi[:], bbi[:], tq[:], ALU.add), R=[bbi, tq], W=[bbi])
            for jp in range(8):
                kt, r0 = jp // 4, (jp % 4) * 32
                for ri, bb in ((0, bbr), (1, bbi)):
                    for gg in range(2):
                        do("dve", lambda e: e.tensor_scalar(LB[r0:r0 + 32, jp, ri, gg * 64:(gg + 1) * 64], bb[r0:r0 + 32, kt, :],
                                                            sel16[r0:r0 + 32, gg:gg + 1], None, ALU.mult), R=[bb, sel16], W=[LB])

            zero(STt)
            zero(carry)
            zero(carryL)
            zero(s5st)
            zero(zc, zc[:, :, 0:2])

            ffn_pending = [None]
            for si in range(NS):
                t0 = si * TS
                src = dr["xT"] if l == 0 else xs_d
                srcb = [] if l == 0 else [B_xs[si]]
                pbn = psum()
                for ft in range(8):
                    stg = work.next()
                    dma(stg[:], src[ft * 128:(ft + 1) * 128, t0:t0 + TS], R=srcb, W=[stg])
                    do("act", lambda e: e.activation(sqb[:], stg[:], AF.Square), R=[stg], W=[sqb])
                    do("pe", lambda e: e.matmul(pbn[:], ones_bf[:], sqb[:], start=(ft == 0), stop=(ft == 7)), R=[ones_bf, sqb], W=[pbn])
                do("dve", lambda e: e.tensor_scalar(rstd[:], pbn[:], 1.0 / D, EPS, ALU.mult, ALU.add), R=[pbn], W=[rstd])
                do("act", lambda e: e.activation(rstd[:], rstd[:], AF.Sqrt), R=[rstd], W=[rstd])
                do("dve", lambda e: e.reciprocal(rstd[:], rstd[:]), R=[rstd], W=[rstd])
                for ft in range(8):
                    stg = work.next()
                    dma(stg[:], src[ft * 128:(ft + 1) * 128, t0:t0 + TS], R=srcb, W=[stg])
                    do("dve", lambda e: e.tensor_tensor(xn[:], stg[:], rstd[:], ALU.mult), R=[stg, rstd], W=[xn])
                    do("act", lambda e: e.activation(hT[:, ft, :], xn[:], AF.Identity, bias=modT[:, ft:ft + 1], scale=geff1[:, ft:ft + 1]),
                       R=[xn, modT, geff1], W=[hT])
                hfn = lambda kt: hT[:, kt, :]

                def s5_gen():
                    for kt in range(2):
                        def ev_u(pb, kt=kt):
                            do("act", lambda e: e.copy(uT[:, kt, :], pb[:]), R=[pb], W=[uT])
                            do("dve", lambda e: e.tensor_copy(uTb[:, kt, :], pb[:]), R=[pb], W=[uTb])
                        proj128(dr["w_in128"][l, 11 + kt], hT, hfn, 8, w128, ev_u)
                    py = [PS[3], PS[3]]
                    f3 = lambda ap: ap.rearrange("p (f s) -> p f s", f=NF)
                    for jp in range(8):
                        kt = jp // 4
                        pre = psum()
                        do("pe", lambda e: e.matmul(pre[:], LB[:, jp, 0, :], uTb[:, kt, :], start=True, stop=True), R=[LB, uTb], W=[pre])
                        pim = psum()
                        do("pe", lambda e: e.matmul(pim[:], LB[:, jp, 1, :], uTb[:, kt, :], start=True, stop=True), R=[LB, uTb], W=[pim])
                        bre, bim, mre, mim, tmp = Z[0], Z[1], Z[2], Z[3], Z[4]
                        do("act", lambda e: e.copy(bre[:], pre[:]), R=[pre], W=[bre])
                        do("act", lambda e: e.copy(bim[:], pim[:]), R=[pim], W=[bim])
                        cb = cosT[:, jp, :].unsqueeze(1).to_broadcast([128, NF, FR])
                        sb_ = sinT[:, jp, :].unsqueeze(1).to_broadcast([128, NF, FR])
                        do("dve", lambda e: e.tensor_tensor(f3(mre[:]), f3(bre[:]), cb, ALU.mult), R=[bre, cosT], W=[mre])
                        do("dve", lambda e: e.tensor_tensor(f3(tmp[:]), f3(bim[:]), sb_, ALU.mult), R=[bim, sinT], W=[tmp])
                        yield
                        do("dve", lambda e: e.tensor_tensor(mre[:], mre[:], tmp[:], ALU.add), R=[mre, tmp], W=[mre])
                        yield
                        tmp2 = Z[5]
                        do("dve", lambda e: e.tensor_tensor(f3(mim[:]), f3(bim[:]), cb, ALU.mult), R=[bim, cosT], W=[mim])
                        yield
                        do("dve", lambda e: e.tensor_tensor(f3(tmp2[:]), f3(bre[:]), sb_, ALU.mult), R=[bre, sinT], W=[tmp2])
                        do("dve", lambda e: e.tensor_tensor(mim[:], mim[:], tmp2[:], ALU.subtract), R=[mim, tmp2], W=[mim])
                        yield
                        rb = rho[:, jp:jp + 1].to_broadcast([128, FR])
                        cL, sL = cosT[:, jp, FR - 1:FR], sinT[:, jp, FR - 1:FR]
                        for fi in range(NF):
                            fs = slice(fi * FR, (fi + 1) * FR)
                            last = slice(fi * FR + FR - 1, fi * FR + FR)
                            do("dve", lambda e: e.tensor_tensor_scan(bre[:, fs], rb, mre[:, fs], s5st[:, jp, 0:1], ALU.mult, ALU.add), R=[rho, mre, s5st], W=[bre])
                            do("dve", lambda e: e.tensor_tensor_scan(bim[:, fs], rb, mim[:, fs], s5st[:, jp, 1:2], ALU.mult, ALU.add), R=[rho, mim, s5st], W=[bim])
                            yield
                            do("dve", lambda e: e.tensor_scalar(s5c[:, 0:1], bim[:, last], sL, None, ALU.mult), R=[bim, sinT], W=[s5c])
                            do("dve", lambda e: e.tensor_scalar(s5c[:, 1:2], bre[:, last], sL, None, ALU.mult), R=[bre, sinT], W=[s5c])
                            do("dve", lambda e: e.scalar_tensor_tensor(s5st[:, jp, 0:1], bre[:, last], cL, s5c[:, 0:1], ALU.mult, ALU.subtract), R=[bre, cosT, s5c], W=[s5st])
                            do("dve", lambda e: e.scalar_tensor_tensor(s5st[:, jp, 1:2], bim[:, last], cL, s5c[:, 1:2], ALU.mult, ALU.add), R=[bim, cosT, s5c], W=[s5st])
                            yield
                        do("dve", lambda e: e.tensor_tensor(f3(mre[:]), f3(bre[:]), cb, ALU.mult), R=[bre, cosT], W=[mre])
                        yield
                        do("dve", lambda e: e.tensor_tensor(f3(tmp[:]), f3(bim[:]), sb_, ALU.mult), R=[bim, sinT], W=[tmp])
                        do("dve", lambda e: e.tensor_tensor(xrb[:], mre[:], tmp[:], ALU.subtract), R=[mre, tmp], W=[xrb])
                        yield
                        do("dve", lambda e: e.tensor_tensor(f3(mim[:]), f3(bre[:]), sb_, ALU.mult), R=[bre, sinT], W=[mim])
                        yield
                        do("dve", lambda e: e.tensor_tensor(f3(tmp2[:]), f3(bim[:]), cb, ALU.mult), R=[bim, cosT], W=[tmp2])
                        do("dve", lambda e: e.tensor_tensor(xib[:], mim[:], tmp2[:], ALU.add), R=[mim, tmp2], W=[xib])
                        yield
                        ct = jp // 4
                        do("pe", lambda e: e.matmul(py[ct][:], s5_lc[:, jp, 0, :], xrb[:], start=(jp % 4 == 0), stop=False), R=[s5_lc, xrb], W=[py[ct]])
                        do("pe", lambda e: e.matmul(py[ct][:], s5_lc[:, jp, 1, :], xib[:], start=False, stop=(jp % 4 == 3)), R=[s5_lc, xib], W=[py[ct]])
                        if jp % 4 == 3:
                            do("dve", lambda e: e.scalar_tensor_tensor(uT[:, ct, :], uT[:, ct, :], s5_d[:, ct:ct + 1], py[ct][:], ALU.mult, ALU.add),
                               R=[uT, s5_d, py[ct]], W=[uT])
                            do("act", lambda e: e.activation(mixD[:, ct, :], uT[:, ct, :], AF.Gelu_apprx_tanh), R=[uT], W=[mixD])
                        yield

                def sgc_gen():
                    for mt in range(4):
                        def ev_gelu(pb, mt=mt):
                            do("act", lambda e: e.activation(Z[mt][:], pb[:], AF.Gelu_apprx_tanh), R=[pb], W=[Z[mt]])
                        proj128(dr["w_in128"][l, 1 + mt], hT, hfn, 8, w128, ev_gelu)
                        yield
                    pm1 = psum()
                    for i in range(2):
                        do("pe", lambda e: e.matmul(pm1[:], ones[:], Z[2 + i][:], start=(i == 0), stop=(i == 1)), R=[ones, Z[2 + i]], W=[pm1])
                    pm2 = psum()
                    for i in range(2):
                        sq_ = (Z[5], Z[4])[i]
                        do("dve", lambda e: e.tensor_tensor(sq_[:], Z[2 + i][:], Z[2 + i][:], ALU.mult), R=[Z[2 + i]], W=[sq_])
                        do("pe", lambda e: e.matmul(pm2[:], ones[:], sq_[:], start=(i == 0), stop=(i == 1)), R=[ones, sq_], W=[pm2])
                    do("act", lambda e: e.activation(Z[4][:], pm1[:], AF.Copy, scale=1.0 / 256), R=[pm1], W=[Z[4]])
                    do("dve", lambda e: e.tensor_tensor(Z[5][:], Z[4][:], Z[4][:], ALU.mult), R=[Z[4]], W=[Z[5]])
                    do("dve", lambda e: e.scalar_tensor_tensor(Z[5][:], pm2[:], 1.0 / 256, Z[5][:], ALU.mult, ALU.subtract), R=[pm2, Z[5]], W=[Z[5]])
                    yield
                    do("dve", lambda e: e.tensor_scalar(Z[5][:], Z[5][:], 1e-5, None, ALU.add), R=[Z[5]], W=[Z[5]])
                    do("act", lambda e: e.activation(Z[5][:], Z[5][:], AF.Sqrt), R=[Z[5]], W=[Z[5]])
                    do("dve", lambda e: e.reciprocal(Z[5][:], Z[5][:]), R=[Z[5]], W=[Z[5]])
                    yield
                    for ct in range(2):
                        zv = Z[2 + ct]
                        do("dve", lambda e: e.tensor_tensor(zv[:], zv[:], Z[4][:], ALU.subtract), R=[zv, Z[4]], W=[zv])
                        do("dve", lambda e: e.tensor_tensor(zv[:], zv[:], Z[5][:], ALU.mult), R=[zv, Z[5]], W=[zv])
                        do("dve", lambda e: e.tensor_scalar(zv[:], zv[:], sg_ln[:, ct:ct + 1], sg_ln[:, 2 + ct:3 + ct], ALU.mult, ALU.add), R=[zv, sg_ln], W=[zv])
                        yield
                    for ck in range(TS // 128):
                        ks = slice(ck * 128, (ck + 1) * 128)
                        for ct in range(2):
                            pt = psum()
                            do("pe", lambda e: e.matmul(pt[:, 0:128], Z[2 + ct][:, ks], ident[:], start=True, stop=True), R=[Z[2 + ct], ident], W=[pt])
                            do("act", lambda e: e.copy(vz[ct * 2][:, 0:64], pt[:, 0:64]), R=[pt], W=[vz[ct * 2]])
                            do("dve", lambda e: e.tensor_copy(vz[ct * 2 + 1][:, 64:128], pt[:, 64:128]), R=[pt], W=[vz[ct * 2 + 1]])
                            pmx = psum()
                            for par in range(2):
                                g = ct * 2 + par
                                do("pe", lambda e: e.matmul(pmx[:, 0:128], vz[ct * 2 + par][:], sg_wb[:, g, :], start=(par == 0), stop=(par == 1)),
                                   R=[vz[ct * 2 + par], sg_wb], W=[pmx])
                            wk = work.next()
                            do("dve", lambda e: e.tensor_tensor(wk[:, 0:128], pmx[:, 0:128], sg_bs[:, ct, :], ALU.add), R=[pmx, sg_bs], W=[wk])
                            do("dve", lambda e: e.tensor_tensor(mixB[:, ct, ks], wk[:, 0:128], Z[ct][:, ks], ALU.mult), R=[wk, Z[ct]], W=[mixB])
                            yield
                    for mt in range(6):
                        def ev_c(pb, mt=mt):
                            do("act", lambda e: e.copy(Z[mt][:], pb[:]), R=[pb], W=[Z[mt]])
                        proj128(dr["w_in128"][l, 5 + mt], hT, hfn, 8, w128, ev_c)
                        yield
                    for ct in range(2):
                        do("dve", lambda e: e.tensor_tensor(zc[:, ct, 2:TS + 2], Z[2 + ct][:], Z[4 + ct][:], ALU.mult), R=[Z[2 + ct], Z[4 + ct]], W=[zc])
                        y = Z[2 + ct]
                        do("dve", lambda e: e.tensor_scalar(y[:], zc[:, ct, 0:TS], conv_w[:, ct, 0:1], None, ALU.mult), R=[zc, conv_w], W=[y])
                        do("dve", lambda e: e.scalar_tensor_tensor(y[:], zc[:, ct, 1:TS + 1], conv_w[:, ct, 1:2], y[:], ALU.mult, ALU.add), R=[zc, conv_w, y], W=[y])
                        do("dve", lambda e: e.scalar_tensor_tensor(y[:], zc[:, ct, 2:TS + 2], conv_w[:, ct, 2:3], y[:], ALU.mult, ALU.add), R=[zc, conv_w, y], W=[y])
                        do("dve", lambda e: e.tensor_tensor(mixC[:, ct, :], y[:], Z[ct][:], ALU.mult), R=[y, Z[ct]], W=[mixC])
                        yield
                    do("dve", lambda e: e.tensor_copy(zc[:, :, 0:2], zc[:, :, TS:TS + 2]), R=[zc], W=[zc])
                    yield

                def mix_gen():
                    yield from s5_gen()
                    yield from sgc_gen()

                s5g_ = mix_gen()

                def pump(n=1):
                    if os.environ.get("NOPUMP"):
                        return
                    for _ in range(n):
                        next(s5g_, None)
                        if ffn_pending[0] is not None:
                            next(ffn_pending[0], None)

                def ev_lora(pb):
                    do("act", lambda e: e.copy(PL[:], pb[:]), R=[pb], W=[PL])
                proj128(dr["w_in128"][l, 0], hT, hfn, 8, w128, ev_lora)
                wk = work.next()
                do("dve", lambda e: e.tensor_tensor(wk[:, 1:TS], PL[:, 0:TS - 1], PL[:, 1:TS], ALU.subtract), R=[PL], W=[wk])
                do("dve", lambda e: e.tensor_tensor(wk[:, 0:1], carryL[:], PL[:, 0:1], ALU.subtract), R=[PL, carryL], W=[wk])
                do("dve", lambda e: e.tensor_copy(carryL[:], PL[:, TS - 1:TS]), R=[PL, wk], W=[carryL])
                do("dve", lambda e: e.scalar_tensor_tensor(PL[:], wk[:], mu_lora[:, 0:1], PL[:], ALU.mult, ALU.add), R=[wk, mu_lora, PL], W=[PL])
                do("act", lambda e: e.activation(LA[0:32, :], PL[0:32, :], AF.Tanh), R=[PL], W=[LA])
                do("act", lambda e: e.copy(LA[32:64, :], PL[32:64, :]), R=[PL], W=[LA])
                do("act", lambda e: e.activation(LA[64:128, :], PL[64:128, :], AF.Sigmoid), R=[PL], W=[LA])

                def rkv_proj(q, h, dst_buf, dst_ap):
                    wb = wload(w64.next(), dr["w_in64"][l, q * 4 + h])
                    pb = psum()
                    for kt in range(8):
                        do("pe", lambda e: e.matmul(pb[0:64, :], wb[:, kt, :], hT[:, kt, :], start=(kt == 0), stop=(kt == 7)), R=[wb, hT], W=[pb])
                    do("act", lambda e: e.copy(dst_ap, pb[0:64, :]), R=[pb], W=[dst_buf])

                def tshift(Qb, Qap, cc):
                    dq = H["d"]
                    do("dve", lambda e: e.tensor_tensor(dq[:, 1:TS], Qap[:, 0:TS - 1], Qap[:, 1:TS], ALU.subtract), R=[Qb], W=[dq])
                    do("dve", lambda e: e.tensor_tensor(dq[:, 0:1], carry[:, cc:cc + 1], Qap[:, 0:1], ALU.subtract), R=[Qb, carry], W=[dq])
                    do("dve", lambda e: e.tensor_copy(carry[:, cc:cc + 1], Qap[:, TS - 1:TS]), R=[Qb, dq], W=[carry])
                    do("dve", lambda e: e.scalar_tensor_tensor(Qap, dq[:], mu_rkv[:, cc:cc + 1], Qap, ALU.mult, ALU.add), R=[dq, mu_rkv, Qb], W=[Qb])

                for h in range(4):
                    rkv_proj(2, h, Vall, Vall[:, h, :])
                    tshift(Vall, Vall[:, h, :], 8 + h)
                if l == 0:
                    dma(vf_d[:, :, t0:t0 + TS], Vall[:], R=[Vall], W=[B_vf[si]])
                else:
                    pv = psum()
                    for h in range(4):
                        do("pe", lambda e: e.matmul(pv[0:32, :], v1[:, h, :], Vall[:, h, :], start=(h == 0), stop=(h == 3)), R=[v1, Vall], W=[pv])
                    do("act", lambda e: e.copy(VV[:], pv[0:32, :]), R=[pv], W=[VV])
                    for h in range(4):
                        pg = psum()
                        do("pe", lambda e: e.matmul(pg[0:64, :], v2[:, h * 64:(h + 1) * 64], VV[:], start=True, stop=True), R=[v2, VV], W=[pg])
                        wk = work.next()
                        do("act", lambda e: e.activation(wk[0:64, :], pg[0:64, :], AF.Sigmoid, bias=v0[:, h:h + 1]), R=[pg, v0], W=[wk])
                        wk2 = work.next()
                        dma(wk2[0:64, :], vf_d[:, h, t0:t0 + TS], R=[B_vf[si]], W=[wk2])
                        do("dve", lambda e: e.tensor_tensor(wk2[0:64, :], wk2[0:64, :], Vall[:, h, :], ALU.subtract), R=[wk2, Vall], W=[wk2])
                        do("dve", lambda e: e.tensor_tensor(wk2[0:64, :], wk2[0:64, :], wk[0:64, :], ALU.mult), R=[wk2, wk], W=[wk2])
                        do("dve", lambda e: e.tensor_tensor(Vall[:, h, :], Vall[:, h, :], wk2[0:64, :], ALU.add), R=[Vall, wk2], W=[Vall])

                def prep_gen(h, P):
                    Hr, Hk, Hd, He, Hc, Hm, Hp, Ha = (H[k] for k in "rkdecmpa")
                    Hw, Hg, Hq, Atb, Btb, Ktb, Rtb, Vb = (P[k] for k in ("Hw", "Hg", "Hq", "Atb", "Btb", "Ktb", "Rtb", "Vb"))
                    rkv_proj(0, h, Hr, Hr[:])
                    yield
                    rkv_proj(1, h, Hk, Hk[:])
                    yield
                    tshift(Hr, Hr[:], h)
                    yield
                    tshift(Hk, Hk[:], 4 + h)
                    yield
                    cs_ = slice(h * 64, (h + 1) * 64)
                    pw = psum()
                    do("pe", lambda e: e.matmul(pw[0:64, :], lora_up[0:32, cs_], LA[0:32, :], start=True, stop=True), R=[lora_up, LA], W=[pw])
                    do("act", lambda e: e.activation(He[:], pw[0:64, :], AF.Sigmoid, bias=hpar[:, 0, h:h + 1]), R=[pw, hpar], W=[He])
                    yield
                    pa = psum()
                    do("pe", lambda e: e.matmul(pa[0:64, :], lora_up[32:64, cs_], LA[32:64, :], start=True, stop=True), R=[lora_up, LA], W=[pa])
                    do("act", lambda e: e.activation(Ha[:], pa[0:64, :], AF.Sigmoid, bias=hpar[:, 1, h:h + 1]), R=[pa, hpar], W=[Ha])
                    yield
                    pg = psum()
                    do("pe", lambda e: e.matmul(pg[0:64, :], lora_up[64:128, cs_], LA[64:128, :], start=True, stop=True), R=[lora_up, LA], W=[pg])
                    do("act", lambda e: e.copy(Hg[:], pg[0:64, :]), R=[pg], W=[Hg])
                    yield
                    do("dve", lambda e: e.tensor_tensor_scan(Hc[:], reset[:], He[:], 0.0, ALU.mult, ALU.add), R=[reset, He], W=[Hc])
                    yield
                    do("act", lambda e: e.activation(Hw[:], Hc[:], AF.Exp, scale=-C0), R=[Hc], W=[Hw])
                    yield
                    do("act", lambda e: e.activation(Hm[:], Hc[:], AF.Exp, scale=C0), R=[Hc], W=[Hm])
                    yield
                    do("dve", lambda e: e.tensor_tensor(Hd[:], Hc[:], He[:], ALU.subtract), R=[Hc, He], W=[Hd])
                    yield
                    do("act", lambda e: e.activation(Hp[:], Hd[:], AF.Exp, scale=-C0), R=[Hd], W=[Hp])
                    yield
                    do("act", lambda e: e.activation(Hq[:], Hk[:], AF.Copy, scale=hpar[:, 2, h:h + 1]), R=[Hk, hpar], W=[Hq])
                    yield
                    do("act", lambda e: e.activation(Hd[:], Hq[:], AF.Square), R=[Hq], W=[Hd])
                    yield
                    pn = psum()
                    do("pe", lambda e: e.matmul(pn[0:64, :], ones[0:64, 0:64], Hd[:], start=True, stop=True), R=[ones, Hd], W=[pn])
                    do("dve", lambda e: e.tensor_scalar(Hc[:], pn[0:64, :], 1e-24, None, ALU.max), R=[pn], W=[Hc])
                    yield
                    do("act", lambda e: e.activation(Hc[:], Hc[:], AF.Sqrt), R=[Hc], W=[Hc])
                    yield
                    do("dve", lambda e: e.reciprocal(Hc[:], Hc[:]), R=[Hc], W=[Hc])
                    yield
                    do("dve", lambda e: e.tensor_tensor(Hq[:], Hq[:], Hc[:], ALU.mult), R=[Hq, Hc], W=[Hq])
                    yield
                    do("dve", lambda e: e.tensor_scalar(Hd[:], Ha[:], hpar[:, 3, h:h + 1], omka[:, h:h + 1], ALU.mult, ALU.add), R=[Ha, hpar, omka], W=[Hd])
                    yield
                    do("dve", lambda e: e.tensor_tensor(Hk[:], Hk[:], Hd[:], ALU.mult), R=[Hk, Hd], W=[Hk])
                    yield
                    do("dve", lambda e: e.scalar_tensor_tensor(Atb[:], Hq[:], -1.0, Hp[:], ALU.mult, ALU.mult), R=[Hq, Hp], W=[Atb])
                    yield
                    do("dve", lambda e: e.tensor_tensor(Ha[:], Hq[:], Ha[:], ALU.mult), R=[Hq, Ha], W=[Ha])
                    yield
                    do("dve", lambda e: e.tensor_tensor(Btb[:], Ha[:], Hm[:], ALU.mult), R=[Ha, Hm], W=[Btb])
                    yield
                    do("dve", lambda e: e.tensor_tensor(Ktb[:], Hk[:], Hm[:], ALU.mult), R=[Hk, Hm], W=[Ktb])
                    yield
                    do("act", lambda e: e.copy(Vb[:], Vall[:, h, :]), R=[Vall], W=[Vb])
                    yield
                    do("dve", lambda e: e.scalar_tensor_tensor(Hd[:], Hr[:], hpar[:, 6, h:h + 1], Hk[:], ALU.mult, ALU.mult), R=[Hr, hpar, Hk], W=[Hd])
                    yield
                    pn = psum()
                    do("pe", lambda e: e.matmul(pn[0:64, :], ones[0:64, 0:64], Hd[:], start=True, stop=True), R=[ones, Hd], W=[pn])
                    do("dve", lambda e: e.tensor_tensor(Hq[:], pn[0:64, :], Vall[:, h, :], ALU.mult), R=[pn, Vall], W=[Hq])
                    yield
                    do("dve", lambda e: e.tensor_tensor(Rtb[:], Hr[:], Hw[:], ALU.mult), R=[Hr, Hw], W=[Rtb])
                    yield

                def wkv(h, P, pump):
                    Hw, Atb, Btb, Ktb, Rtb, Vb = (P[k] for k in ("Hw", "Atb", "Btb", "Ktb", "Rtb", "Vb"))
                    At, Bt, Kt, Rt = Atb, Btb, Ktb, Rtb
                    PO = PS[7]
                    do("act", lambda e: e.copy(STb[:], STt[:, h, :]), R=[STt], W=[STb])
                    for bt in range(NCH // NB):
                        for q in range(NB):
                            c = bt * NB + q
                            cs2 = slice(c * C, (c + 1) * C)
                            pb = psum()
                            a_, b_, k_, r_ = At[:, cs2], Bt[:, cs2], Kt[:, cs2], Rt[:, cs2]
                            do("pe", lambda e: e.matmul(pb[0:64, 0:64], b_, a_, start=True, stop=True), R=[Bt, At], W=[pb])
                            do("pe", lambda e: e.matmul(pb[0:64, 64:128], a_, b_, start=True, stop=True), R=[At, Bt], W=[pb])
                            do("pe", lambda e: e.matmul(pb[0:64, 128:192], b_, r_, start=True, stop=True), R=[Bt, Rt], W=[pb])
                            do("pe", lambda e: e.matmul(pb[0:64, 192:256], a_, k_, start=True, stop=True), R=[At, Kt], W=[pb])
                            do("pe", lambda e: e.matmul(pb[0:64, 256:320], k_, r_, start=True, stop=True), R=[Kt, Rt], W=[pb])
                            do("dve", lambda e: e.tensor_tensor(SCx[:, q, :], pb[0:64, 0:128], msk5[:, 0:128], ALU.mult), R=[pb, msk5], W=[SCx])
                            do("dve", lambda e: e.tensor_tensor(SCb[:, q, :], pb[0:64, 128:320], msk5[:, 128:320], ALU.mult), R=[pb, msk5], W=[SCb])
                            pt = psum()
                            v_ = Vb[:, cs2]
                            for qi, (sb_, s_) in enumerate(((At, a_), (Bt, b_), (Kt, k_), (Vb, v_))):
                                do("pe", lambda e: e.matmul(pt[0:64, qi * 64:(qi + 1) * 64], s_, ident_bf[0:64, 0:64], start=True, stop=True),
                                   R=[sb_, ident_bf], W=[pt])
                            do("act", lambda e: e.copy(TMb[:, q, :], pt[0:64, 0:256]), R=[pt], W=[TMb])
                            pump()
                        do("dve", lambda e: e.tensor_tensor(TT[:], SCx[:, :, 0:64], ident[0:64, 0:64].unsqueeze(1).to_broadcast([64, NB, 64]), ALU.add),
                           R=[SCx, ident], W=[TT])
                        Xb, Xf = SCx, (lambda q: SCx[:, q, 0:64])
                        Yb, Yf = SCx, (lambda q: SCx[:, q, 64:128])
                        cur = 0
                        for lev in range(1, 6):
                            Xn_, Yn_ = XS[cur], YS[cur]
                            if lev < 5:
                                pb = PS[4]
                                for q in range(NB):
                                    do("pe", lambda e: e.matmul(pb[0:64, q * 64:(q + 1) * 64], Yf(q), Xf(q), start=True, stop=True), R=[Yb, Xb], W=[pb])
                                do("act", lambda e: e.copy(Xn_[:], pb[0:64, 0:NB * 64].rearrange("p (q t) -> p q t", q=NB)), R=[pb], W=[Xn_])
                                pump()
                            pb = PS[5]
                            for q in range(NB):
                                do("pe", lambda e: e.matmul(pb[0:64, q * 64:(q + 1) * 64], Xf(q), Yf(q), start=True, stop=True), R=[Xb, Yb], W=[pb])
                            do("act", lambda e: e.copy(Yn_[:], pb[0:64, 0:NB * 64].rearrange("p (q t) -> p q t", q=NB)), R=[pb], W=[Yn_])
                            pump()
                            pb = PS[6]
                            for q in range(NB):
                                do("pe", lambda e: e.matmul(pb[0:64, q * 64:(q + 1) * 64], Yn_[:, q, :], TT[:, q, :], start=True, stop=True), R=[Yn_, TT], W=[pb])
                            if lev < 5:
                                do("dve", lambda e: e.tensor_tensor(TT[:], TT[:], pb[0:64, 0:NB * 64].rearrange("p (q t) -> p q t", q=NB), ALU.add),
                                   R=[pb, TT], W=[TT])
                            else:
                                do("dve", lambda e: e.tensor_tensor(TTb[:], TT[:], pb[0:64, 0:NB * 64].rearrange("p (q t) -> p q t", q=NB), ALU.add),
                                   R=[pb, TT], W=[TTb])
                            Xb, Yb = Xn_, Yn_
                            pump()
                            Xf = (lambda q, Xn_=Xn_: Xn_[:, q, :])
                            Yf = (lambda q, Yn_=Yn_: Yn_[:, q, :])
                            cur = 1 - cur
                        pb = PS[4]
                        for q in range(NB):
                            do("pe", lambda e: e.matmul(pb[0:64, q * 128:q * 128 + 64], TMb[:, q, 0:64], TTb[:, q, :], start=True, stop=True), R=[TMb, TTb], W=[pb])
                            do("pe", lambda e: e.matmul(pb[0:64, q * 128 + 64:q * 128 + 128], SCb[:, q, 64:128], TTb[:, q, :], start=True, stop=True), R=[SCb, TTb], W=[pb])
                        do("act", lambda e: e.copy(APMb[:], pb[0:64, 0:NB * 128].rearrange("p (q t) -> p q t", q=NB)), R=[pb], W=[APMb])
                        pump()
                        for q in range(NB):
                            c = bt * NB + q
                            cs2 = slice(c * C, (c + 1) * C)
                            pu = PS[5]
                            do("pe", lambda e: e.matmul(pu[0:64, 0:64], APMb[:, q, 64:128], TMb[:, q, 192:256], start=True, stop=False), R=[APMb, TMb], W=[pu])
                            do("pe", lambda e: e.matmul(pu[0:64, 0:64], APMb[:, q, 0:64], STb[:], start=False, stop=True), R=[APMb, STb], W=[pu])
                            do("act", lambda e: e.copy(UTb[:], pu[0:64, 0:64]), R=[pu], W=[UTb])
                            pump()
                            do("pe", lambda e: e.matmul(PO[0:64, cs2], STb[:], Rt[:, cs2], start=True, stop=False), R=[STb, Rt], W=[PO])
                            do("pe", lambda e: e.matmul(PO[0:64, cs2], TMb[:, q, 192:256], SCb[:, q, 128:192], start=False, stop=False), R=[TMb, SCb], W=[PO])
                            do("pe", lambda e: e.matmul(PO[0:64, cs2], UTb[:], SCb[:, q, 0:64], start=False, stop=True), R=[UTb, SCb], W=[PO])
                            pn = PS[6]
                            do("pe", lambda e: e.matmul(pn[0:64, 0:64], TMb[:, q, 64:128], UTb[:], start=True, stop=False), R=[TMb, UTb], W=[pn])
                            do("pe", lambda e: e.matmul(pn[0:64, 0:64], TMb[:, q, 128:192], TMb[:, q, 192:256], start=False, stop=True), R=[TMb], W=[pn])
                            wc = Hw[:, c * C + C - 1:c * C + C]
                            do("act", lambda e: e.activation(STW[:], STt[:, h, :], AF.Copy, scale=wc), R=[STt, Hw], W=[STW])
                            do("dve", lambda e: e.scalar_tensor_tensor(STb[:], pn[0:64, 0:64], wc, STW[:], ALU.mult, ALU.add), R=[pn, Hw, STW], W=[STb])
                            do("dve", lambda e: e.scalar_tensor_tensor(STt[:, h, :], pn[0:64, 0:64], wc, STW[:], ALU.mult, ALU.add), R=[pn, Hw, STW], W=[STt])
                            pump()

                def post(h, P):
                    Hd, Hc, Hm = H["d"], H["c"], H["m"]
                    Hg, Hq = P["Hg"], P["Hq"]
                    PO = PS[7]
                    OS = Hm
                    do("act", lambda e: e.copy(OS[:], PO[0:64, :]), R=[PO], W=[OS])
                    do("act", lambda e: e.activation(Hd[:], OS[:], AF.Square), R=[OS], W=[Hd])
                    pm1 = psum()
                    do("pe", lambda e: e.matmul(pm1[0:64, :], ones[0:64, 0:64], OS[:], start=True, stop=True), R=[ones, OS], W=[pm1])
                    pm2 = psum()
                    do("pe", lambda e: e.matmul(pm2[0:64, :], ones[0:64, 0:64], Hd[:], start=True, stop=True), R=[ones, Hd], W=[pm2])
                    mu_ = work.next()
                    do("act", lambda e: e.activation(mu_[0:64, :], pm1[0:64, :], AF.Copy, scale=1.0 / 64), R=[pm1], W=[mu_])
                    var = work.next()
                    do("dve", lambda e: e.tensor_tensor(var[0:64, :], mu_[0:64, :], mu_[0:64, :], ALU.mult), R=[mu_], W=[var])
                    do("dve", lambda e: e.scalar_tensor_tensor(var[0:64, :], pm2[0:64, :], 1.0 / 64, var[0:64, :], ALU.mult, ALU.subtract), R=[pm2, var], W=[var])
                    do("dve", lambda e: e.tensor_scalar(var[0:64, :], var[0:64, :], 64e-5, None, ALU.add), R=[var], W=[var])
                    do("act", lambda e: e.activation(var[0:64, :], var[0:64, :], AF.Sqrt), R=[var], W=[var])
                    do("dve", lambda e: e.reciprocal(var[0:64, :], var[0:64, :]), R=[var], W=[var])
                    do("dve", lambda e: e.tensor_tensor(Hc[:], OS[:], mu_[0:64, :], ALU.subtract), R=[OS, mu_], W=[Hc])
                    do("dve", lambda e: e.tensor_tensor(Hc[:], Hc[:], var[0:64, :], ALU.mult), R=[Hc, var], W=[Hc])
                    do("dve", lambda e: e.tensor_scalar(Hc[:], Hc[:], hpar[:, 4, h:h + 1], hpar[:, 5, h:h + 1], ALU.mult, ALU.add), R=[Hc, hpar], W=[Hc])
                    do("dve", lambda e: e.tensor_tensor(Hc[:], Hc[:], Hq[:], ALU.add), R=[Hc, Hq], W=[Hc])
                    do("dve", lambda e: e.tensor_tensor(mixA[:, h, :], Hc[:], Hg[:], ALU.mult), R=[Hc, Hg], W=[mixA])

                prepg = prep_gen(0, PP[0])
                for _ in prepg:
                    pass
                for h in range(4):
                    nxt = prep_gen(h + 1, PP[(h + 1) % 2]) if h < 3 else iter(())

                    def pump2(nxt=nxt):
                        pump()
                        next(nxt, None)
                    wkv(h, PP[h % 2], pump2)
                    for _ in nxt:
                        pass
                    post(h, PP[h % 2])

                for _ in s5g_:
                    pass
                if ffn_pending[0] is not None:
                    for _ in ffn_pending[0]:
                        pass
                    ffn_pending[0] = None


                merged = aT
                for ft in range(8):
                    fsl = slice(ft * 128, (ft + 1) * 128)
                    acc = Z[5]
                    gts = [Z[0], Z[1], Z[2], Z[3]]
                    for br in range(4):
                        def ev_g(pb, br=br):
                            do("act", lambda e: e.activation(gts[br][:], pb[:], AF.Sigmoid), R=[pb], W=[gts[br]])
                        proj128(dr["w_in128"][l, 13 + br * 8 + ft], hT, hfn, 8, w128, ev_g)
                    wq = [wload(wsm.next(), dr["sg_out"][l][:, :, fsl]), wload(wsm.next(), dr["conv_out"][l][:, :, fsl]),
                          wload(wsm.next(), dr["glu_w"][l][:, :, fsl]),
                          wload(wsm.next(), dr["glu_w"][l][:, :, 1024 + ft * 128:1024 + (ft + 1) * 128])]
                    wr_ = wload(wra.next(), dr["rwkv_out"][l][:, :, fsl])
                    pa = psum()
                    for h in range(4):
                        do("pe", lambda e: e.matmul(pa[:], wr_[:, h, :], mixA[:, h, :], start=(h == 0), stop=(h == 3)), R=[wr_, mixA], W=[pa])
                    do("dve", lambda e: e.tensor_tensor(acc[:], pa[:], gts[0][:], ALU.mult), R=[pa, gts[0]], W=[acc])
                    for bi, mixT in ((1, mixB), (2, mixC)):
                        pb_ = psum()
                        for kt in range(2):
                            do("pe", lambda e: e.matmul(pb_[:], wq[bi - 1][:, kt, :], mixT[:, kt, :], start=(kt == 0), stop=(kt == 1)), R=[wq[bi - 1], mixT], W=[pb_])
                        do("dve", lambda e: e.tensor_tensor(gts[bi][:], pb_[:], gts[bi][:], ALU.mult), R=[pb_, gts[bi]], W=[gts[bi]])
                        do("dve", lambda e: e.tensor_tensor(acc[:], acc[:], gts[bi][:], ALU.add), R=[acc, gts[bi]], W=[acc])
                    ph1 = psum()
                    for kt in range(2):
                        do("pe", lambda e: e.matmul(ph1[:], wq[2][:, kt, :], mixD[:, kt, :], start=(kt == 0), stop=(kt == 1)), R=[wq[2], mixD], W=[ph1])
                    ph2 = psum()
                    for kt in range(2):
                        do("pe", lambda e: e.matmul(ph2[:], wq[3][:, kt, :], mixD[:, kt, :], start=(kt == 0), stop=(kt == 1)), R=[wq[3], mixD], W=[ph2])
                    do("act", lambda e: e.activation(xn[:], ph2[:], AF.Sigmoid), R=[ph2], W=[xn])
                    do("dve", lambda e: e.tensor_tensor(xn[:], ph1[:], xn[:], ALU.mult), R=[ph1, xn], W=[xn])
                    do("dve", lambda e: e.tensor_tensor(xn[:], xn[:], gts[3][:], ALU.mult), R=[xn, gts[3]], W=[xn])
                    do("dve", lambda e: e.tensor_tensor(merged[:, ft, :], acc[:], xn[:], ALU.add), R=[acc, xn], W=[merged])

                dma(X[:], src[:, t0:t0 + TS].rearrange("(ft p) t -> p ft t", p=128), R=srcb, W=[X])
                for mt in range(8):
                    def ev_o(pb, mt=mt):
                        do("dve", lambda e: e.scalar_tensor_tensor(X[:, mt, :], pb[:], modT[:, 16 + mt:17 + mt], X[:, mt, :], ALU.mult, ALU.add),
                           R=[pb, modT, X], W=[X])
                    proj128(dr["w_o"][l, mt], merged, lambda kt: merged[:, kt, :], 8, w128, ev_o)

                rmsnorm_to(hTB, geff2, 24)
                def ffn_gen(l=l, si=si, t0=t0):
                    for qtr in range(4):
                        for i in range(8):
                            def ev_f(pb, i=i):
                                wk = work.next()
                                do("act", lambda e: e.activation(wk[:], pb[:], AF.Relu), R=[pb], W=[wk])
                                do("dve", lambda e: e.tensor_tensor(aT[:, i, :], wk[:], wk[:], ALU.mult), R=[wk], W=[aT])
                            proj128(dr["ffn_w1"][l, qtr * 8 + i], hTB, lambda kt: hTB[:, kt, :], 8, w128, ev_f)
                            yield
                        for mt in range(8):
                            def ev_2(pb, mt=mt):
                                do("dve", lambda e: e.scalar_tensor_tensor(X[:, mt, :], pb[:], modT[:, 40 + mt:41 + mt], X[:, mt, :], ALU.mult, ALU.add),
                                   R=[pb, modT, X], W=[X])
                            proj128(dr["ffn_w2"][l, mt][:, qtr * 8:(qtr + 1) * 8, :], aT, lambda kt: aT[:, kt, :], 8, w128, ev_2)
                            yield
                    if l == 0:
                        dma(xs_d[:, t0:t0 + TS].rearrange("(ft p) t -> p ft t", p=128), X[:], R=[X], W=[B_xs[si]])
                    else:
                        rms_stats()
                        for ft in range(8):
                            do("dve", lambda e: e.scalar_tensor_tensor(X[:, ft, :], X[:, ft, :], g_fin[:, ft:ft + 1], rstd[:], ALU.mult, ALU.mult),
                               R=[X, g_fin, rstd], W=[X])
                        dma(outT[:, t0:t0 + TS].rearrange("(ft p) t -> p ft t", p=128), X[:], R=[X], W=[B_outs[si]])
                    yield

                ffn_pending[0] = ffn_gen()
                if si == NS - 1:
                    for _ in ffn_pending[0]:
                        pass
                    ffn_pending[0] = None
        S.finish(B_outs)
        print("instructions:", S.n_inst, "waits:", S.n_wait)
    return nc


_CACHE = {}


def run(inputs, T):
    maps = [prep_inputs(inputs, b) for b in range(4)]
    shapes = {k: v.shape for k, v in maps[0].items()}
    key = (T,)
    if key not in _CACHE:
        _CACHE[key] = build(shapes, T)
    nc = _CACHE[key]
    in_maps = [maps[i % 4] for i in range(8)]
    res = run_bass_kernel_spmd(nc, in_maps, core_ids=list(range(8)))
    out = np.stack([np.ascontiguousarray(res.results[b]["outT"].T) for b in range(4)])
    return out.astype(np.float32)


def kernel(**inputs):
    inputs = {k: np.asarray(v, dtype=np.float32) for k, v in inputs.items()}
    T = inputs["x"].shape[1]
    return run(inputs, T)
```

```python
import contextlib
import math
import os
import numpy as np
import concourse.bass as bass
import concourse.mybir as mybir
from concourse.bass_utils import run_bass_kernel_spmd

F32 = mybir.dt.float32
BF16 = mybir.dt.bfloat16
AF = mybir.ActivationFunctionType
ALU = mybir.AluOpType

D = 1024
DM = 256
TS = 512
C = 64
NCH = TS // C
FR = 128
EPS = 1e-6
C0 = math.exp(-0.5)
MAGIC = 12582912.0


class Buf:
    __slots__ = ("name", "t", "lastw", "readers", "psum")

    def __init__(self, name, t=None):
        self.name = name
        self.t = t
        self.lastw = None
        self.readers = []
        self.psum = False

    def __getitem__(self, idx):
        return self.t[idx]


class Sched:
    N_DMA_SEM = 10

    def __init__(self, nc, stack):
        self.nc = nc
        self.stack = stack
        self.engs = {"pe": nc.tensor, "act": nc.scalar, "dve": nc.vector, "pool": nc.gpsimd, "sp": nc.sync}
        self.sem = {}
        self.cnt = {}
        for k in ("pe", "act", "dve", "pool"):
            self.sem[k] = stack.enter_context(nc.semaphore("s_" + k))
            self.cnt[k] = 0
        self.dma_sems = {"sp": [], "pool": []}
        for qn in ("sp", "pool"):
            for i in range(self.N_DMA_SEM):
                key = "dma_%s%d" % (qn, i)
                self.sem[key] = stack.enter_context(nc.semaphore("s_" + key))
                self.cnt[key] = 0
                self.dma_sems[qn].append(key)
        self.dma_rr = {"sp": 0, "pool": 0}
        self.waited = {}
        self.n_inst = 0
        self.n_wait = 0
        self.sb_bytes = 0

    def sb(self, name, shape, dt):
        t = self.stack.enter_context(self.nc.sbuf_tensor("sb_" + name, list(shape), dt))
        n = 1
        for s in shape[1:]:
            n *= s
        self.sb_bytes += n * (2 if dt == BF16 else 4)
        return Buf(name, t)

    def ps(self, name, shape, dt=F32):
        t = self.stack.enter_context(self.nc.psum_tensor("pp_" + name, list(shape), dt))
        b = Buf(name, t)
        b.psum = True
        return b

    def _need(self, eng, deps):
        best = {}
        for d in deps:
            if d is None:
                continue
            sk, v, _ = d
            if v > best.get(sk, 0):
                best[sk] = v
        for sk, v in best.items():
            if self.waited.get((eng, sk), 0) >= v:
                continue
            self.engs[eng].wait_ge(self.sem[sk], v)
            self.waited[(eng, sk)] = v
            self.n_wait += 1

    def do(self, eng, fn, R=(), W=()):
        deps = []
        for b in R:
            lw = b.lastw
            if lw is not None and not (lw[2] == eng and eng == "pe"):
                deps.append(lw)
            if b.psum:
                for r in b.readers:
                    if r[2] != eng:
                        deps.append(r)
        for b in W:
            lw = b.lastw
            if lw is not None and (lw[2] != eng or eng != "pe"):
                deps.append(lw)
            for r in b.readers:
                if r[2] != eng or eng != "pe":
                    deps.append(r)
        self._need(eng, deps)
        ins = fn(self.engs[eng])
        self.cnt[eng] += 1
        v = self.cnt[eng]
        ins.then_inc(self.sem[eng], 1)
        tok = (eng, v, eng)
        for b in R:
            b.readers = [r for r in b.readers if r[0] != eng] + [tok]
        for b in W:
            b.lastw = tok
            b.readers = []
        self.n_inst += 1
        return ins

    def dma(self, out_ap, in_ap, R=(), W=(), q="sp"):
        sk = self.dma_sems[q][self.dma_rr[q]]
        self.dma_rr[q] = (self.dma_rr[q] + 1) % len(self.dma_sems[q])
        deps = []
        for b in R:
            if b.lastw is not None:
                deps.append(b.lastw)
        for b in W:
            if b.lastw is not None:
                deps.append(b.lastw)
            deps.extend(b.readers)
        if self.cnt[sk] > 0:
            deps.append((sk, self.cnt[sk], "dmaq"))
        self._need(q, deps)
        ins = self.engs[q].dma_start(out=out_ap, in_=in_ap)
        self.cnt[sk] += 16
        ins.then_inc(self.sem[sk], 16)
        tok = (sk, self.cnt[sk], "dmaq")
        for b in R:
            b.readers.append(tok)
        for b in W:
            b.lastw = tok
            b.readers = []
        self.n_inst += 1
        return ins

    def finish(self, bufs, eng="sp"):
        self._need(eng, [b.lastw for b in bufs if b.lastw is not None])


class Ring:
    def __init__(self, S, name, n, shape, dt, psum=False):
        self.bufs = [(S.ps if psum else S.sb)("%s%d" % (name, i), shape, dt) for i in range(n)]
        self.i = 0

    def next(self):
        b = self.bufs[self.i]
        self.i = (self.i + 1) % len(self.bufs)
        return b


def tile_km(W, msz):
    K, M = W.shape
    return np.ascontiguousarray(W.reshape(K // 128, 128, M // msz, msz).transpose(2, 1, 0, 3))


def col_pk(v, p=128):
    return np.ascontiguousarray(v.reshape(-1, p).T)


def prep_inputs(inp, b):
    f = np.float32
    L = 2
    m = {}
    m["xT"] = np.ascontiguousarray(inp["x"][b].T)
    m["cT"] = col_pk(inp["c"][b])
    m["ident"] = np.eye(128, dtype=f)
    m["ones"] = np.ones((128, 128), f)
    su = np.triu(np.ones((C, C), f), 1)
    sl = np.tril(np.ones((C, C), f), -1)
    iu = np.triu(np.ones((C, C), f), 0)
    m["msk5"] = np.ascontiguousarray(np.concatenate([su, sl, iu, sl, iu], axis=1))
    rs = np.ones((64, TS), f)
    rs[:, ::C] = 0.0
    m["reset"] = rs
    m["sgmask"] = np.triu(np.ones((128, 128), f), 0)
    m["iota"] = np.ascontiguousarray(np.broadcast_to(np.arange(FR, dtype=f) + 1.0, (128, FR)))
    sel = np.zeros((128, 2), f)
    sel[(np.arange(128) % 32) < 16, 0] = 1.0
    sel[(np.arange(128) % 32) >= 16, 1] = 1.0
    m["sel16"] = sel
    m["ada_w"] = np.ascontiguousarray(inp["ada_w"].reshape(L, 8, 128, 48, 128).transpose(0, 3, 2, 1, 4))
    m["ada_b"] = np.stack([col_pk(inp["ada_b"][l]) for l in range(L)])
    m["g_mix"] = np.stack([col_pk(inp["norm_mix_g"][l]) for l in range(L)])
    m["g_ffn"] = np.stack([col_pk(inp["norm_ffn_g"][l]) for l in range(L)])
    m["g_fin"] = col_pk(inp["final_g"])
    w_in = inp["w_in"]
    m["w_in64"] = np.stack([tile_km(w_in[l][:, :768], 64) for l in range(L)])
    m["w_in128"] = np.stack([tile_km(w_in[l][:, 768:], 128) for l in range(L)])
    mu = inp["rwkv_mu"]
    m["mu_rkv"] = np.stack([np.concatenate([col_pk(mu[l][q * 256:(q + 1) * 256], 64) for q in range(3)], axis=1)
                            for l in range(L)])
    m["mu_lora"] = np.stack([mu[l][768:896].reshape(128, 1) for l in range(L)])
    hp = lambda k: np.stack([col_pk(inp[k][l], 64) for l in range(inp[k].shape[0])])
    rk = inp["rwkv_rk"].reshape(L, 256)
    m["hpar"] = np.ascontiguousarray(np.stack(
        [hp("rwkv_w0"), hp("rwkv_a0"), hp("rwkv_kk"), hp("rwkv_ka"), hp("rwkv_lnx_w"), hp("rwkv_lnx_b"),
         np.stack([col_pk(rk[l], 64) for l in range(L)])], axis=2))
    m["v0"] = hp("rwkv_v0")
    m["lora_up"] = np.ascontiguousarray(np.concatenate([inp["rwkv_w2"], inp["rwkv_a2"], inp["rwkv_g2"]], axis=1))
    m["v1"] = np.ascontiguousarray(inp["rwkv_v1"].reshape(1, 4, 64, 32).transpose(0, 2, 1, 3))
    m["v2"] = np.ascontiguousarray(inp["rwkv_v2"])
    m["rwkv_out"] = np.ascontiguousarray(inp["rwkv_out"].reshape(L, 4, 64, 1024).transpose(0, 2, 1, 3))
    k2 = lambda k: np.ascontiguousarray(inp[k].reshape(L, 2, 128, -1).transpose(0, 2, 1, 3))
    m["sg_out"] = k2("sg_out")
    m["conv_out"] = k2("conv_out")
    m["glu_w"] = k2("s5_glu_w")
    m["sg_ln"] = np.stack([np.concatenate([col_pk(inp["sg_ln_w"][l]), col_pk(inp["sg_ln_b"][l])], axis=1) for l in range(L)])
    m["sg_wsT"] = np.ascontiguousarray(inp["sg_ws"].transpose(0, 3, 1, 2))
    bs = inp["sg_bs"]
    m["sg_bs"] = np.ascontiguousarray(np.stack(
        [np.stack([np.repeat(bs[l, 2 * ct:2 * ct + 2], 64, axis=0) for ct in range(2)], axis=1) for l in range(L)]))
    m["conv_w"] = np.ascontiguousarray(inp["conv_w"].reshape(L, 3, 2, 128).transpose(0, 3, 2, 1))
    pair = lambda a: np.ascontiguousarray(a.reshape(L, 8, 128).transpose(0, 2, 1))
    m["s5_lam"] = np.ascontiguousarray(np.stack(
        [pair(inp["s5_a_re"]), pair(inp["s5_a_im"]),
         pair(np.repeat(inp["s5_log_dt"][:, :, None], 64, axis=2))], axis=2))
    def bk(a):
        return np.ascontiguousarray(a.reshape(L, 2, 8, 64, 16).transpose(0, 2, 4, 1, 3).reshape(L, 128, 2, 64))
    def lk(a):
        a3 = a if a.ndim == 3 else np.repeat(a[:, :, None], 64, axis=2)
        r = np.repeat(a3.reshape(L, 2, 8, 1, 64), 16, axis=3)
        return np.ascontiguousarray(r.transpose(0, 2, 3, 1, 4).reshape(L, 128, 2, 64))
    m["s5_bk"] = np.ascontiguousarray(np.stack(
        [bk(inp["s5_b_re"]), bk(inp["s5_b_im"]), lk(inp["s5_a_re"]), lk(inp["s5_a_im"]), lk(inp["s5_log_dt"])], axis=2))
    lc = np.zeros((L, 8, 128, 2, 128), f)
    for jp in range(8):
        for gg in range(2):
            g = 2 * jp + gg
            cs = (jp % 4) * 32 + gg * 16
            lc[:, jp, gg * 64:(gg + 1) * 64, 0, cs:cs + 16] = inp["s5_c_re"][:, g].transpose(0, 2, 1)
            lc[:, jp, gg * 64:(gg + 1) * 64, 1, cs:cs + 16] = inp["s5_c_im"][:, g].transpose(0, 2, 1)
    m["s5_lc"] = np.ascontiguousarray(lc.transpose(0, 2, 1, 3, 4))
    m["s5_d"] = np.stack([col_pk(inp["s5_d"][l]) for l in range(L)])
    m["w_o"] = np.stack([tile_km(inp["w_o"][l], 128) for l in range(L)])
    m["ffn_w1"] = np.stack([tile_km(inp["ffn_w1"][l], 128) for l in range(L)])
    m["ffn_w2"] = np.stack([tile_km(inp["ffn_w2"][l], 128) for l in range(L)])
    return {k: np.ascontiguousarray(v, dtype=f) for k, v in m.items()}


def build(shapes, T):
    NS = T // TS
    NF = TS // FR
    nc = bass.Bass("TRN2", target_bir_lowering=False)
    dr = {}
    for k, shp in shapes.items():
        dr[k] = nc.dram_tensor(k, list(shp), F32, kind="ExternalInput").ap()
    outT = nc.dram_tensor("outT", [D, T], F32, kind="ExternalOutput").ap()
    xs_d = nc.dram_tensor("xs_scr", [D, T], F32, kind="Internal").ap()
    vf_d = nc.dram_tensor("vf_scr", [64, 4, T], F32, kind="Internal").ap()

    with contextlib.ExitStack() as st:
        S = Sched(nc, st)
        do, dma = S.do, S.dma
        B_outs = [Buf("outT%d" % i) for i in range(NS)]
        B_xs = [Buf("xs%d" % i) for i in range(NS)]
        B_vf = [Buf("vf%d" % i) for i in range(NS)]

        def cload(name, key, shape):
            b = S.sb(name, shape, F32)
            dma(b[:], dr[key], W=[b])
            return b
        ident = cload("ident", "ident", [128, 128])
        ones = cload("ones", "ones", [128, 128])
        msk5 = cload("msk5", "msk5", [64, 320])
        reset = cload("reset", "reset", [64, TS])
        sgmask = cload("sgmask", "sgmask", [128, 128])
        iota = cload("iota", "iota", [128, FR])
        sel16 = cload("sel16", "sel16", [128, 2])
        cT = cload("cT", "cT", [128, 8])
        g_fin = cload("g_fin", "g_fin", [128, 8])
        ones_bf = S.sb("ones_bf", [128, 128], BF16)
        do("dve", lambda e: e.tensor_copy(ones_bf[:], ones[:]), R=[ones], W=[ones_bf])
        csil = S.sb("csil", [128, 8], F32)
        do("act", lambda e: e.activation(csil[:], cT[:], AF.Silu), R=[cT], W=[csil])
        csil_bf = S.sb("csil_bf", [128, 8], BF16)
        do("dve", lambda e: e.tensor_copy(csil_bf[:], csil[:]), R=[csil], W=[csil_bf])
        negpi = S.sb("negpi", [128, 1], F32)
        do("dve", lambda e: e.memset(negpi[:], -math.pi), W=[negpi])

        PS = [S.ps("ps%d" % i, [128, 512]) for i in range(8)]
        ps_rr = [0]

        def psum():
            b = PS[ps_rr[0] % 3]
            ps_rr[0] += 1
            return b

        X = S.sb("X", [128, 8, TS], F32)
        hT = S.sb("hT", [128, 8, TS], BF16)
        hTB = S.sb("hTB", [128, 8, TS], BF16)
        aT = S.sb("aT", [128, 8, TS], BF16)
        sqb = S.sb("sqb", [128, TS], BF16)
        rstd = S.sb("rstd", [128, TS], F32)
        xn = S.sb("xn", [128, TS], F32)
        mixA = S.sb("mixA", [64, 4, TS], BF16)
        mixB = S.sb("mixB", [128, 2, TS], BF16)
        mixC = S.sb("mixC", [128, 2, TS], BF16)
        mixD = S.sb("mixD", [128, 2, TS], BF16)
        w128 = Ring(S, "w128_", 4, [128, 8, 128], BF16)
        wsm = Ring(S, "wsm_", 6, [128, 2, 128], BF16)
        w64 = Ring(S, "w64_", 2, [128, 8, 64], BF16)
        wra = Ring(S, "wra_", 2, [64, 4, 128], BF16)
        modT = S.sb("modT", [128, 48], F32)
        geff1 = S.sb("geff1", [128, 8], F32)
        geff2 = S.sb("geff2", [128, 8], F32)
        tmp48 = S.sb("tmp48", [128, 48], F32)
        work = Ring(S, "work_", 3, [128, TS], F32)
        Z = [S.sb("Z%d" % i, [128, TS], F32) for i in range(6)]
        PL = S.sb("PL", [128, TS], F32)
        LA = S.sb("LA", [128, TS], F32)
        H = {nm: S.sb("H_" + nm, [64, TS], F32) for nm in ("r", "k", "d", "e", "c", "w", "m", "p", "a", "g", "q")}
        Vall = S.sb("Vall", [64, 4, TS], F32)
        carry = S.sb("carry", [64, 12], F32)
        carryL = S.sb("carryL", [128, 1], F32)
        VV = S.sb("VV", [32, TS], F32)
        STt = S.sb("STt", [64, 4, 64], F32)
        STW = S.sb("STW", [64, 64], F32)
        NB = 4
        SCx = S.sb("SCx", [64, NB, 128], F32)
        SCb = S.sb("SCb", [64, NB, 192], BF16)
        TMb = S.sb("TMb", [64, NB, 256], BF16)
        XS = [S.sb("XS%d" % i, [64, NB, 64], F32) for i in range(2)]
        YS = [S.sb("YS%d" % i, [64, NB, 64], F32) for i in range(2)]
        TT = S.sb("TT", [64, NB, 64], F32)
        TTb = S.sb("TTb", [64, NB, 64], BF16)
        APMb = S.sb("APMb", [64, NB, 128], BF16)
        PP = []
        for par in range(2):
            d_ = {k: S.sb("%s_%d" % (k, par), [64, TS], BF16) for k in ("Atb", "Btb", "Ktb", "Rtb", "Vb")}
            if par == 0:
                d_.update({"Hw": H["w"], "Hg": H["g"], "Hq": H["q"]})
            else:
                d_.update({k: S.sb("%s_%d" % (k, par), [64, TS], F32) for k in ("Hw", "Hg", "Hq")})
            PP.append(d_)
        STb = S.sb("STb", [64, 64], BF16)
        UTb = S.sb("UTb", [64, 64], BF16)
        ident_bf = S.sb("ident_bf", [128, 128], BF16)
        mu_rkv = S.sb("mu_rkv", [64, 12], F32)
        mu_lora = S.sb("mu_lora", [128, 1], F32)
        hpar = S.sb("hpar", [64, 7, 4], F32)
        omka = S.sb("omka", [64, 4], F32)
        v0 = S.sb("v0", [64, 4], F32)
        lora_up = S.sb("lora_up", [128, 256], F32)
        v1 = S.sb("v1", [64, 4, 32], F32)
        v2 = S.sb("v2", [32, 256], F32)
        sg_ln = S.sb("sg_ln", [128, 4], F32)
        sg_wb = S.sb("sg_wb", [128, 4, 128], BF16)
        sg_bs = S.sb("sg_bs", [128, 2, 128], F32)
        conv_w = S.sb("conv_w", [128, 2, 3], F32)
        zc = S.sb("zc", [128, 2, TS + 2], F32)
        vz = [S.sb("vz%d" % i, [128, 128], BF16) for i in range(4)]
        s5_lam = S.sb("s5_lam", [128, 3, 8], F32)
        s5_bk = S.sb("s5_bk", [128, 5, 2, 64], F32)
        s5_lc = S.sb("s5_lc", [128, 8, 2, 128], BF16)
        s5_d = S.sb("s5_d", [128, 2], F32)
        LB = S.sb("LB", [128, 8, 2, 128], BF16)
        cosT = S.sb("cosT", [128, 8, FR], F32)
        sinT = S.sb("sinT", [128, 8, FR], F32)
        rho = S.sb("rho", [128, 8], F32)
        th = S.sb("th", [128, 8], F32)
        s5st = S.sb("s5st", [128, 8, 2], F32)
        s5c = S.sb("s5c", [128, 2], F32)
        small = Ring(S, "small_", 4, [128, 64], F32)
        uT = S.sb("uT", [128, 2, TS], F32)
        uTb = S.sb("uTb", [128, 2, TS], BF16)
        xrb = S.sb("xrb", [128, TS], BF16)
        xib = S.sb("xib", [128, TS], BF16)
        tP = {nm: S.sb("s5p_" + nm, [128, 8], F32) for nm in ("lre", "dt", "mag", "ang", "sn", "cs", "abr", "abi", "den", "qre", "qim", "t1", "t2")}
        tK = {nm: S.sb("s5k_" + nm, [128, 2, 64], F32) for nm in ("lre", "dt", "mag", "ang", "sn", "cs", "abr", "abi", "den", "qre", "qim", "t1", "t2")}
        bbr = S.sb("bbr", [128, 2, 64], F32)
        bbi = S.sb("bbi", [128, 2, 64], F32)
        tq = S.sb("tq", [128, 2, 64], F32)

        def zero(b, ap=None):
            do("dve", lambda e: e.memset(b[:] if ap is None else ap, 0.0), W=[b])
        for b in vz:
            zero(b)
        do("dve", lambda e: e.tensor_copy(ident_bf[:], ident[:]), R=[ident], W=[ident_bf])
        zero(LB)
        print("SBUF bytes/partition:", S.sb_bytes)

        def wload(buf, src, ap=None):
            dma(buf[:] if ap is None else ap, src, W=[buf], q="pool")
            return buf

        def rms_stats():
            pb = psum()
            for ft in range(8):
                do("act", lambda e: e.activation(sqb[:], X[:, ft, :], AF.Square), R=[X], W=[sqb])
                do("pe", lambda e: e.matmul(pb[:], ones_bf[:], sqb[:], start=(ft == 0), stop=(ft == 7)), R=[ones_bf, sqb], W=[pb])
            do("dve", lambda e: e.tensor_scalar(rstd[:], pb[:], 1.0 / D, EPS, ALU.mult, ALU.add), R=[pb], W=[rstd])
            do("act", lambda e: e.activation(rstd[:], rstd[:], AF.Sqrt), R=[rstd], W=[rstd])
            do("dve", lambda e: e.reciprocal(rstd[:], rstd[:]), R=[rstd], W=[rstd])

        def rmsnorm_to(dst, geff, shift_col0):
            rms_stats()
            for ft in range(8):
                do("dve", lambda e: e.tensor_tensor(xn[:], X[:, ft, :], rstd[:], ALU.mult), R=[X, rstd], W=[xn])
                do("act", lambda e: e.activation(dst[:, ft, :], xn[:], AF.Identity, bias=modT[:, shift_col0 + ft:shift_col0 + ft + 1],
                                                 scale=geff[:, ft:ft + 1]), R=[xn, modT, geff], W=[dst])

        def proj128(src_ap, rhs_buf, rhs_fn, nk, ring, evac):
            wb = wload(ring.next(), src_ap)
            pb = psum()
            for kt in range(nk):
                do("pe", lambda e: e.matmul(pb[:], wb[:, kt, :], rhs_fn(kt), start=(kt == 0), stop=(kt == nk - 1)), R=[wb, rhs_buf], W=[pb])
            evac(pb)

        def s5_disc(t, are, aim, ldt, srcb):
            A = lambda nm: t[nm][:]
            B = lambda nm: t[nm]
            do("dve", lambda e: e.tensor_scalar(A("lre"), are, -1e-4, None, ALU.min), R=[srcb], W=[B("lre")])
            do("act", lambda e: e.activation(A("dt"), ldt, AF.Exp), R=[srcb], W=[B("dt")])
            do("dve", lambda e: e.tensor_tensor(A("t1"), A("lre"), A("dt"), ALU.mult), R=[B("lre"), B("dt")], W=[B("t1")])
            do("act", lambda e: e.activation(A("mag"), A("t1"), AF.Exp), R=[B("t1")], W=[B("mag")])
            do("dve", lambda e: e.tensor_tensor(A("ang"), aim, A("dt"), ALU.mult), R=[srcb, B("dt")], W=[B("ang")])
            for off, dst in ((0.0, "sn"), (0.5 * math.pi, "cs")):
                do("dve", lambda e: e.tensor_scalar(A("t1"), A("ang"), off, None, ALU.add), R=[B("ang")], W=[B("t1")])
                do("dve", lambda e: e.tensor_scalar(A("t2"), A("t1"), 1.0 / (2 * math.pi), MAGIC, ALU.mult, ALU.add), R=[B("t1")], W=[B("t2")])
                do("dve", lambda e: e.tensor_scalar(A("t2"), A("t2"), -MAGIC, None, ALU.add), R=[B("t2")], W=[B("t2")])
                do("dve", lambda e: e.scalar_tensor_tensor(A("t1"), A("t2"), -2 * math.pi, A("t1"), ALU.mult, ALU.add), R=[B("t2"), B("t1")], W=[B("t1")])
                do("act", lambda e: e.activation(A(dst), A("t1"), AF.Sin), R=[B("t1")], W=[B(dst)])
            do("dve", lambda e: e.tensor_tensor(A("abr"), A("mag"), A("cs"), ALU.mult), R=[B("mag"), B("cs")], W=[B("abr")])
            do("dve", lambda e: e.tensor_tensor(A("abi"), A("mag"), A("sn"), ALU.mult), R=[B("mag"), B("sn")], W=[B("abi")])
            do("dve", lambda e: e.tensor_tensor(A("den"), A("lre"), A("lre"), ALU.mult), R=[B("lre")], W=[B("den")])
            do("dve", lambda e: e.tensor_tensor(A("t1"), aim, aim, ALU.mult), R=[srcb], W=[B("t1")])
            do("dve", lambda e: e.tensor_tensor(A("den"), A("den"), A("t1"), ALU.add), R=[B("den"), B("t1")], W=[B("den")])
            do("dve", lambda e: e.reciprocal(A("den"), A("den")), R=[B("den")], W=[B("den")])
            do("dve", lambda e: e.tensor_scalar(A("t1"), A("abr"), -1.0, None, ALU.add), R=[B("abr")], W=[B("t1")])
            do("dve", lambda e: e.tensor_tensor(A("qre"), A("t1"), A("lre"), ALU.mult), R=[B("t1"), B("lre")], W=[B("qre")])
            do("dve", lambda e: e.tensor_tensor(A("t2"), A("abi"), aim, ALU.mult), R=[B("abi"), srcb], W=[B("t2")])
            do("dve", lambda e: e.tensor_tensor(A("qre"), A("qre"), A("t2"), ALU.add), R=[B("qre"), B("t2")], W=[B("qre")])
            do("dve", lambda e: e.tensor_tensor(A("qre"), A("qre"), A("den"), ALU.mult), R=[B("qre"), B("den")], W=[B("qre")])
            do("dve", lambda e: e.tensor_tensor(A("qim"), A("abi"), A("lre"), ALU.mult), R=[B("abi"), B("lre")], W=[B("qim")])
            do("dve", lambda e: e.tensor_tensor(A("t2"), A("t1"), aim, ALU.mult), R=[B("t1"), srcb], W=[B("t2")])
            do("dve", lambda e: e.tensor_tensor(A("qim"), A("qim"), A("t2"), ALU.subtract), R=[B("qim"), B("t2")], W=[B("qim")])
            do("dve", lambda e: e.tensor_tensor(A("qim"), A("qim"), A("den"), ALU.mult), R=[B("qim"), B("den")], W=[B("qim")])

        for l in range(2):
            pmod = PS[4]
            for mt in range(48):
                wb = wload(w128.next(), dr["ada_w"][l, mt])
                for kt in range(8):
                    do("pe", lambda e: e.matmul(pmod[:, mt:mt + 1], wb[:, kt, :], csil_bf[:, kt:kt + 1], start=(kt == 0), stop=(kt == 7)),
                       R=[wb, csil_bf], W=[pmod])
            dma(tmp48[:], dr["ada_b"][l], W=[tmp48])
            do("dve", lambda e: e.tensor_tensor(modT[:], pmod[:, 0:48], tmp48[:], ALU.add), R=[pmod, tmp48], W=[modT])
            gm = small.next()
            dma(gm[:, 0:8], dr["g_mix"][l], W=[gm])
            do("dve", lambda e: e.scalar_tensor_tensor(geff1[:], modT[:, 8:16], 1.0, gm[:, 0:8], ALU.add, ALU.mult), R=[modT, gm], W=[geff1])
            gf = small.next()
            dma(gf[:, 0:8], dr["g_ffn"][l], W=[gf])
            do("dve", lambda e: e.scalar_tensor_tensor(geff2[:], modT[:, 32:40], 1.0, gf[:, 0:8], ALU.add, ALU.mult), R=[modT, gf], W=[geff2])

            dma(mu_rkv[:], dr["mu_rkv"][l], W=[mu_rkv])
            dma(mu_lora[:], dr["mu_lora"][l], W=[mu_lora])
            dma(hpar[:], dr["hpar"][l], W=[hpar])
            do("dve", lambda e: e.tensor_scalar(omka[:], hpar[:, 3, :], -1.0, 1.0, ALU.mult, ALU.add), R=[hpar], W=[omka])
            dma(lora_up[:], dr["lora_up"][l], W=[lora_up])
            if l == 1:
                dma(v0[:], dr["v0"][0], W=[v0])
                dma(v1[:], dr["v1"][0], W=[v1])
                dma(v2[:], dr["v2"][0], W=[v2])
            dma(sg_ln[:], dr["sg_ln"][l], W=[sg_ln])
            sgw = Z[4]
            dma(sgw[:].rearrange("p (g t) -> p g t", g=4), dr["sg_wsT"][l], W=[sgw])
            for g in range(4):
                do("dve", lambda e: e.tensor_tensor(sg_wb[:, g, :], sgw[:, g * 128:(g + 1) * 128], sgmask[:], ALU.mult), R=[sgw, sgmask], W=[sg_wb])
            dma(sg_bs[:], dr["sg_bs"][l], W=[sg_bs])
            dma(conv_w[:], dr["conv_w"][l], W=[conv_w])
            dma(s5_lam[:], dr["s5_lam"][l], W=[s5_lam])
            dma(s5_bk[:], dr["s5_bk"][l], W=[s5_bk])
            wload(s5_lc, dr["s5_lc"][l])
            dma(s5_d[:], dr["s5_d"][l], W=[s5_d])
            do("dve", lambda e: e.tensor_scalar(s5_lc[:, :, 1, :], s5_lc[:, :, 1, :], -1.0, None, ALU.mult), R=[s5_lc], W=[s5_lc])

            s5_disc(tP, s5_lam[:, 0, :], s5_lam[:, 1, :], s5_lam[:, 2, :], s5_lam)
            do("dve", lambda e: e.tensor_copy(rho[:], tP["mag"][:]), R=[tP["mag"]], W=[rho])
            do("dve", lambda e: e.tensor_copy(th[:], tP["ang"][:]), R=[tP["ang"]], W=[th])
            for jp in range(8):
                for off, dstT in ((0.0, sinT), (0.5 * math.pi, cosT)):
                    wk = work.next()
                    wk2 = work.next()
                    do("dve", lambda e: e.tensor_scalar(wk[:, 0:FR], iota[:], th[:, jp:jp + 1], off, ALU.mult, ALU.add), R=[iota, th], W=[wk])
                    do("dve", lambda e: e.tensor_scalar(wk2[:, 0:FR], wk[:, 0:FR], 1.0 / (2 * math.pi), MAGIC, ALU.mult, ALU.add), R=[wk], W=[wk2])
                    do("dve", lambda e: e.tensor_scalar(wk2[:, 0:FR], wk2[:, 0:FR], -MAGIC, None, ALU.add), R=[wk2], W=[wk2])
                    do("dve", lambda e: e.scalar_tensor_tensor(wk[:, 0:FR], wk2[:, 0:FR], -2 * math.pi, wk[:, 0:FR], ALU.mult, ALU.add), R=[wk2, wk], W=[wk])
                    do("act", lambda e: e.activation(dstT[:, jp, :], wk[:, 0:FR], AF.Sin), R=[wk], W=[dstT])
            s5_disc(tK, s5_bk[:, 2], s5_bk[:, 3], s5_bk[:, 4], s5_bk)
            do("dve", lambda e: e.tensor_tensor(bbr[:], tK["qre"][:], s5_bk[:, 0], ALU.mult), R=[tK["qre"], s5_bk], W=[bbr])
            do("dve", lambda e: e.tensor_tensor(tq[:], tK["qim"][:], s5_bk[:, 1], ALU.mult), R=[tK["qim"], s5_bk], W=[tq])
            do("dve", lambda e: e.tensor_tensor(bbr[:], bbr[:], tq[:], ALU.subtract), R=[bbr, tq], W=[bbr])
            do("dve", lambda e: e.tensor_tensor(bbi[:], tK["qre"][:], s5_bk[:, 1], ALU.mult), R=[tK["qre"], s5_bk], W=[bbi])
            do("dve", lambda e: e.tensor_tensor(tq[:], tK["qim"][:], s5_bk[:, 0], ALU.mult), R=[tK["qim"], s5_bk], W=[tq])
            do("dve", lambda e: e.tensor_tensor(bbi[:], bbi[:], tq[:], ALU.add), R=[bbi, tq], W=[bbi])
            for jp in range(8):
                kt, r0 = jp // 4, (jp % 4) * 32
                for ri, bb in ((0, bbr), (1, bbi)):
                    for gg in range(2):
                        do("dve", lambda e: e.tensor_scalar(LB[r0:r0 + 32, jp, ri, gg * 64:(gg + 1) * 64], bb[r0:r0 + 32, kt, :],
                                                            sel16[r0:r0 + 32, gg:gg + 1], None, ALU.mult), R=[bb, sel16], W=[LB])

            zero(STt)
            zero(carry)
            zero(carryL)
            zero(s5st)
            zero(zc, zc[:, :, 0:2])

            ffn_pending = [None]
            for si in range(NS):
                t0 = si * TS
                src = dr["xT"] if l == 0 else xs_d
                srcb = [] if l == 0 else [B_xs[si]]
                pbn = psum()
                for ft in range(8):
                    stg = work.next()
                    dma(stg[:], src[ft * 128:(ft + 1) * 128, t0:t0 + TS], R=srcb, W=[stg])
                    do("act", lambda e: e.activation(sqb[:], stg[:], AF.Square), R=[stg], W=[sqb])
                    do("pe", lambda e: e.matmul(pbn[:], ones_bf[:], sqb[:], start=(ft == 0), stop=(ft == 7)), R=[ones_bf, sqb], W=[pbn])
                do("dve", lambda e: e.tensor_scalar(rstd[:], pbn[:], 1.0 / D, EPS, ALU.mult, ALU.add), R=[pbn], W=[rstd])
                do("act", lambda e: e.activation(rstd[:], rstd[:], AF.Sqrt), R=[rstd], W=[rstd])
                do("dve", lambda e: e.reciprocal(rstd[:], rstd[:]), R=[rstd], W=[rstd])
                for ft in range(8):
                    stg = work.next()
                    dma(stg[:], src[ft * 128:(ft + 1) * 128, t0:t0 + TS], R=srcb, W=[stg])
                    do("dve", lambda e: e.tensor_tensor(xn[:], stg[:], rstd[:], ALU.mult), R=[stg, rstd], W=[xn])
                    do("act", lambda e: e.activation(hT[:, ft, :], xn[:], AF.Identity, bias=modT[:, ft:ft + 1], scale=geff1[:, ft:ft + 1]),
                       R=[xn, modT, geff1], W=[hT])
                hfn = lambda kt: hT[:, kt, :]

                def s5_gen():
                    for kt in range(2):
                        def ev_u(pb, kt=kt):
                            do("act", lambda e: e.copy(uT[:, kt, :], pb[:]), R=[pb], W=[uT])
                            do("dve", lambda e: e.tensor_copy(uTb[:, kt, :], pb[:]), R=[pb], W=[uTb])
                        proj128(dr["w_in128"][l, 11 + kt], hT, hfn, 8, w128, ev_u)
                    py = [PS[3], PS[3]]
                    f3 = lambda ap: ap.rearrange("p (f s) -> p f s", f=NF)
                    for jp in range(8):
                        kt = jp // 4
                        pre = psum()
                        do("pe", lambda e: e.matmul(pre[:], LB[:, jp, 0, :], uTb[:, kt, :], start=True, stop=True), R=[LB, uTb], W=[pre])
                        pim = psum()
                        do("pe", lambda e: e.matmul(pim[:], LB[:, jp, 1, :], uTb[:, kt, :], start=True, stop=True), R=[LB, uTb], W=[pim])
                        bre, bim, mre, mim, tmp = Z[0], Z[1], Z[2], Z[3], Z[4]
                        do("act", lambda e: e.copy(bre[:], pre[:]), R=[pre], W=[bre])
                        do("act", lambda e: e.copy(bim[:], pim[:]), R=[pim], W=[bim])
                        cb = cosT[:, jp, :].unsqueeze(1).to_broadcast([128, NF, FR])
                        sb_ = sinT[:, jp, :].unsqueeze(1).to_broadcast([128, NF, FR])
                        do("dve", lambda e: e.tensor_tensor(f3(mre[:]), f3(bre[:]), cb, ALU.mult), R=[bre, cosT], W=[mre])
                        do("dve", lambda e: e.tensor_tensor(f3(tmp[:]), f3(bim[:]), sb_, ALU.mult), R=[bim, sinT], W=[tmp])
                        yield
                        do("dve", lambda e: e.tensor_tensor(mre[:], mre[:], tmp[:], ALU.add), R=[mre, tmp], W=[mre])
                        yield
                        tmp2 = Z[5]
                        do("dve", lambda e: e.tensor_tensor(f3(mim[:]), f3(bim[:]), cb, ALU.mult), R=[bim, cosT], W=[mim])
                        yield
                        do("dve", lambda e: e.tensor_tensor(f3(tmp2[:]), f3(bre[:]), sb_, ALU.mult), R=[bre, sinT], W=[tmp2])
                        do("dve", lambda e: e.tensor_tensor(mim[:], mim[:], tmp2[:], ALU.subtract), R=[mim, tmp2], W=[mim])
                        yield
                        rb = rho[:, jp:jp + 1].to_broadcast([128, FR])
                        cL, sL = cosT[:, jp, FR - 1:FR], sinT[:, jp, FR - 1:FR]
                        for fi in range(NF):
                            fs = slice(fi * FR, (fi + 1) * FR)
                            last = slice(fi * FR + FR - 1, fi * FR + FR)
                            do("dve", lambda e: e.tensor_tensor_scan(bre[:, fs], rb, mre[:, fs], s5st[:, jp, 0:1], ALU.mult, ALU.add), R=[rho, mre, s5st], W=[bre])
                            do("dve", lambda e: e.tensor_tensor_scan(bim[:, fs], rb, mim[:, fs], s5st[:, jp, 1:2], ALU.mult, ALU.add), R=[rho, mim, s5st], W=[bim])
                            yield
                            do("dve", lambda e: e.tensor_scalar(s5c[:, 0:1], bim[:, last], sL, None, ALU.mult), R=[bim, sinT], W=[s5c])
                            do("dve", lambda e: e.tensor_scalar(s5c[:, 1:2], bre[:, last], sL, None, ALU.mult), R=[bre, sinT], W=[s5c])
                            do("dve", lambda e: e.scalar_tensor_tensor(s5st[:, jp, 0:1], bre[:, last], cL, s5c[:, 0:1], ALU.mult, ALU.subtract), R=[bre, cosT, s5c], W=[s5st])
                            do("dve", lambda e: e.scalar_tensor_tensor(s5st[:, jp, 1:2], bim[:, last], cL, s5c[:, 1:2], ALU.mult, ALU.add), R=[bim, cosT, s5c], W=[s5st])
                            yield
                        do("dve", lambda e: e.tensor_tensor(f3(mre[:]), f3(bre[:]), cb, ALU.mult), R=[bre, cosT], W=[mre])
                        yield
                        do("dve", lambda e: e.tensor_tensor(f3(tmp[:]), f3(bim[:]), sb_, ALU.mult), R=[bim, sinT], W=[tmp])
                        do("dve", lambda e: e.tensor_tensor(xrb[:], mre[:], tmp[:], ALU.subtract), R=[mre, tmp], W=[xrb])
                        yield
                        do("dve", lambda e: e.tensor_tensor(f3(mim[:]), f3(bre[:]), sb_, ALU.mult), R=[bre, sinT], W=[mim])
                        yield
                        do("dve", lambda e: e.tensor_tensor(f3(tmp2[:]), f3(bim[:]), cb, ALU.mult), R=[bim, cosT], W=[tmp2])
                        do("dve", lambda e: e.tensor_tensor(xib[:], mim[:], tmp2[:], ALU.add), R=[mim, tmp2], W=[xib])
                        yield
                        ct = jp // 4
                        do("pe", lambda e: e.matmul(py[ct][:], s5_lc[:, jp, 0, :], xrb[:], start=(jp % 4 == 0), stop=False), R=[s5_lc, xrb], W=[py[ct]])
                        do("pe", lambda e: e.matmul(py[ct][:], s5_lc[:, jp, 1, :], xib[:], start=False, stop=(jp % 4 == 3)), R=[s5_lc, xib], W=[py[ct]])
                        if jp % 4 == 3:
                            do("dve", lambda e: e.scalar_tensor_tensor(uT[:, ct, :], uT[:, ct, :], s5_d[:, ct:ct + 1], py[ct][:], ALU.mult, ALU.add),
                               R=[uT, s5_d, py[ct]], W=[uT])
                            do("act", lambda e: e.activation(mixD[:, ct, :], uT[:, ct, :], AF.Gelu_apprx_tanh), R=[uT], W=[mixD])
                        yield

                def sgc_gen():
                    for mt in range(4):
                        def ev_gelu(pb, mt=mt):
                            do("act", lambda e: e.activation(Z[mt][:], pb[:], AF.Gelu_apprx_tanh), R=[pb], W=[Z[mt]])
                        proj128(dr["w_in128"][l, 1 + mt], hT, hfn, 8, w128, ev_gelu)
                        yield
                    pm1 = psum()
                    for i in range(2):
                        do("pe", lambda e: e.matmul(pm1[:], ones[:], Z[2 + i][:], start=(i == 0), stop=(i == 1)), R=[ones, Z[2 + i]], W=[pm1])
                    pm2 = psum()
                    for i in range(2):
                        sq_ = (Z[5], Z[4])[i]
                        do("dve", lambda e: e.tensor_tensor(sq_[:], Z[2 + i][:], Z[2 + i][:], ALU.mult), R=[Z[2 + i]], W=[sq_])
                        do("pe", lambda e: e.matmul(pm2[:], ones[:], sq_[:], start=(i == 0), stop=(i == 1)), R=[ones, sq_], W=[pm2])
                    do("act", lambda e: e.activation(Z[4][:], pm1[:], AF.Copy, scale=1.0 / 256), R=[pm1], W=[Z[4]])
                    do("dve", lambda e: e.tensor_tensor(Z[5][:], Z[4][:], Z[4][:], ALU.mult), R=[Z[4]], W=[Z[5]])
                    do("dve", lambda e: e.scalar_tensor_tensor(Z[5][:], pm2[:], 1.0 / 256, Z[5][:], ALU.mult, ALU.subtract), R=[pm2, Z[5]], W=[Z[5]])
                    yield
                    do("dve", lambda e: e.tensor_scalar(Z[5][:], Z[5][:], 1e-5, None, ALU.add), R=[Z[5]], W=[Z[5]])
                    do("act", lambda e: e.activation(Z[5][:], Z[5][:], AF.Sqrt), R=[Z[5]], W=[Z[5]])
                    do("dve", lambda e: e.reciprocal(Z[5][:], Z[5][:]), R=[Z[5]], W=[Z[5]])
                    yield
                    for ct in range(2):
                        zv = Z[2 + ct]
                        do("dve", lambda e: e.tensor_tensor(zv[:], zv[:], Z[4][:], ALU.subtract), R=[zv, Z[4]], W=[zv])
                        do("dve", lambda e: e.tensor_tensor(zv[:], zv[:], Z[5][:], ALU.mult), R=[zv, Z[5]], W=[zv])
                        do("dve", lambda e: e.tensor_scalar(zv[:], zv[:], sg_ln[:, ct:ct + 1], sg_ln[:, 2 + ct:3 + ct], ALU.mult, ALU.add), R=[zv, sg_ln], W=[zv])
                        yield
                    for ck in range(TS // 128):
                        ks = slice(ck * 128, (ck + 1) * 128)
                        for ct in range(2):
                            pt = psum()
                            do("pe", lambda e: e.matmul(pt[:, 0:128], Z[2 + ct][:, ks], ident[:], start=True, stop=True), R=[Z[2 + ct], ident], W=[pt])
                            do("act", lambda e: e.copy(vz[ct * 2][:, 0:64], pt[:, 0:64]), R=[pt], W=[vz[ct * 2]])
                            do("dve", lambda e: e.tensor_copy(vz[ct * 2 + 1][:, 64:128], pt[:, 64:128]), R=[pt], W=[vz[ct * 2 + 1]])
                            pmx = psum()
                            for par in range(2):
                                g = ct * 2 + par
                                do("pe", lambda e: e.matmul(pmx[:, 0:128], vz[ct * 2 + par][:], sg_wb[:, g, :], start=(par == 0), stop=(par == 1)),
                                   R=[vz[ct * 2 + par], sg_wb], W=[pmx])
                            wk = work.next()
                            do("dve", lambda e: e.tensor_tensor(wk[:, 0:128], pmx[:, 0:128], sg_bs[:, ct, :], ALU.add), R=[pmx, sg_bs], W=[wk])
                            do("dve", lambda e: e.tensor_tensor(mixB[:, ct, ks], wk[:, 0:128], Z[ct][:, ks], ALU.mult), R=[wk, Z[ct]], W=[mixB])
                            yield
                    for mt in range(6):
                        def ev_c(pb, mt=mt):
                            do("act", lambda e: e.copy(Z[mt][:], pb[:]), R=[pb], W=[Z[mt]])
                        proj128(dr["w_in128"][l, 5 + mt], hT, hfn, 8, w128, ev_c)
                        yield
                    for ct in range(2):
                        do("dve", lambda e: e.tensor_tensor(zc[:, ct, 2:TS + 2], Z[2 + ct][:], Z[4 + ct][:], ALU.mult), R=[Z[2 + ct], Z[4 + ct]], W=[zc])
                        y = Z[2 + ct]
                        do("dve", lambda e: e.tensor_scalar(y[:], zc[:, ct, 0:TS], conv_w[:, ct, 0:1], None, ALU.mult), R=[zc, conv_w], W=[y])
                        do("dve", lambda e: e.scalar_tensor_tensor(y[:], zc[:, ct, 1:TS + 1], conv_w[:, ct, 1:2], y[:], ALU.mult, ALU.add), R=[zc, conv_w, y], W=[y])
                        do("dve", lambda e: e.scalar_tensor_tensor(y[:], zc[:, ct, 2:TS + 2], conv_w[:, ct, 2:3], y[:], ALU.mult, ALU.add), R=[zc, conv_w, y], W=[y])
                        do("dve", lambda e: e.tensor_tensor(mixC[:, ct, :], y[:], Z[ct][:], ALU.mult), R=[y, Z[ct]], W=[mixC])
                        yield
                    do("dve", lambda e: e.tensor_copy(zc[:, :, 0:2], zc[:, :, TS:TS + 2]), R=[zc], W=[zc])
                    yield

                def mix_gen():
                    yield from s5_gen()
                    yield from sgc_gen()

                s5g_ = mix_gen()

                def pump(n=1):
                    if os.environ.get("NOPUMP"):
                        return
                    for _ in range(n):
                        next(s5g_, None)
                        if ffn_pending[0] is not None:
                            next(ffn_pending[0], None)

                def ev_lora(pb):
                    do("act", lambda e: e.copy(PL[:], pb[:]), R=[pb], W=[PL])
                proj128(dr["w_in128"][l, 0], hT, hfn, 8, w128, ev_lora)
                wk = work.next()
                do("dve", lambda e: e.tensor_tensor(wk[:, 1:TS], PL[:, 0:TS - 1], PL[:, 1:TS], ALU.subtract), R=[PL], W=[wk])
                do("dve", lambda e: e.tensor_tensor(wk[:, 0:1], carryL[:], PL[:, 0:1], ALU.subtract), R=[PL, carryL], W=[wk])
                do("dve", lambda e: e.tensor_copy(carryL[:], PL[:, TS - 1:TS]), R=[PL, wk], W=[carryL])
                do("dve", lambda e: e.scalar_tensor_tensor(PL[:], wk[:], mu_lora[:, 0:1], PL[:], ALU.mult, ALU.add), R=[wk, mu_lora, PL], W=[PL])
                do("act", lambda e: e.activation(LA[0:32, :], PL[0:32, :], AF.Tanh), R=[PL], W=[LA])
                do("act", lambda e: e.copy(LA[32:64, :], PL[32:64, :]), R=[PL], W=[LA])
                do("act", lambda e: e.activation(LA[64:128, :], PL[64:128, :], AF.Sigmoid), R=[PL], W=[LA])

                def rkv_proj(q, h, dst_buf, dst_ap):
                    wb = wload(w64.next(), dr["w_in64"][l, q * 4 + h])
                    pb = psum()
                    for kt in range(8):
                        do("pe", lambda e: e.matmul(pb[0:64, :], wb[:, kt, :], hT[:, kt, :], start=(kt == 0), stop=(kt == 7)), R=[wb, hT], W=[pb])
                    do("act", lambda e: e.copy(dst_ap, pb[0:64, :]), R=[pb], W=[dst_buf])

                def tshift(Qb, Qap, cc):
                    dq = H["d"]
                    do("dve", lambda e: e.tensor_tensor(dq[:, 1:TS], Qap[:, 0:TS - 1], Qap[:, 1:TS], ALU.subtract), R=[Qb], W=[dq])
                    do("dve", lambda e: e.tensor_tensor(dq[:, 0:1], carry[:, cc:cc + 1], Qap[:, 0:1], ALU.subtract), R=[Qb, carry], W=[dq])
                    do("dve", lambda e: e.tensor_copy(carry[:, cc:cc + 1], Qap[:, TS - 1:TS]), R=[Qb, dq], W=[carry])
                    do("dve", lambda e: e.scalar_tensor_tensor(Qap, dq[:], mu_rkv[:, cc:cc + 1], Qap, ALU.mult, ALU.add), R=[dq, mu_rkv, Qb], W=[Qb])

                for h in range(4):
                    rkv_proj(2, h, Vall, Vall[:, h, :])
                    tshift(Vall, Vall[:, h, :], 8 + h)
                if l == 0:
                    dma(vf_d[:, :, t0:t0 + TS], Vall[:], R=[Vall], W=[B_vf[si]])
                else:
                    pv = psum()
                    for h in range(4):
                        do("pe", lambda e: e.matmul(pv[0:32, :], v1[:, h, :], Vall[:, h, :], start=(h == 0), stop=(h == 3)), R=[v1, Vall], W=[pv])
                    do("act", lambda e: e.copy(VV[:], pv[0:32, :]), R=[pv], W=[VV])
                    for h in range(4):
                        pg = psum()
                        do("pe", lambda e: e.matmul(pg[0:64, :], v2[:, h * 64:(h + 1) * 64], VV[:], start=True, stop=True), R=[v2, VV], W=[pg])
                        wk = work.next()
                        do("act", lambda e: e.activation(wk[0:64, :], pg[0:64, :], AF.Sigmoid, bias=v0[:, h:h + 1]), R=[pg, v0], W=[wk])
                        wk2 = work.next()
                        dma(wk2[0:64, :], vf_d[:, h, t0:t0 + TS], R=[B_vf[si]], W=[wk2])
                        do("dve", lambda e: e.tensor_tensor(wk2[0:64, :], wk2[0:64, :], Vall[:, h, :], ALU.subtract), R=[wk2, Vall], W=[wk2])
                        do("dve", lambda e: e.tensor_tensor(wk2[0:64, :], wk2[0:64, :], wk[0:64, :], ALU.mult), R=[wk2, wk], W=[wk2])
                        do("dve", lambda e: e.tensor_tensor(Vall[:, h, :], Vall[:, h, :], wk2[0:64, :], ALU.add), R=[Vall, wk2], W=[Vall])

                def prep_gen(h, P):
                    Hr, Hk, Hd, He, Hc, Hm, Hp, Ha = (H[k] for k in "rkdecmpa")
                    Hw, Hg, Hq, Atb, Btb, Ktb, Rtb, Vb = (P[k] for k in ("Hw", "Hg", "Hq", "Atb", "Btb", "Ktb", "Rtb", "Vb"))
                    rkv_proj(0, h, Hr, Hr[:])
                    yield
                    rkv_proj(1, h, Hk, Hk[:])
                    yield
                    tshift(Hr, Hr[:], h)
                    yield
                    tshift(Hk, Hk[:], 4 + h)
                    yield
                    cs_ = slice(h * 64, (h + 1) * 64)
                    pw = psum()
                    do("pe", lambda e: e.matmul(pw[0:64, :], lora_up[0:32, cs_], LA[0:32, :], start=True, stop=True), R=[lora_up, LA], W=[pw])
                    do("act", lambda e: e.activation(He[:], pw[0:64, :], AF.Sigmoid, bias=hpar[:, 0, h:h + 1]), R=[pw, hpar], W=[He])
                    yield
                    pa = psum()
                    do("pe", lambda e: e.matmul(pa[0:64, :], lora_up[32:64, cs_], LA[32:64, :], start=True, stop=True), R=[lora_up, LA], W=[pa])
                    do("act", lambda e: e.activation(Ha[:], pa[0:64, :], AF.Sigmoid, bias=hpar[:, 1, h:h + 1]), R=[pa, hpar], W=[Ha])
                    yield
                    pg = psum()
                    do("pe", lambda e: e.matmul(pg[0:64, :], lora_up[64:128, cs_], LA[64:128, :], start=True, stop=True), R=[lora_up, LA], W=[pg])
                    do("act", lambda e: e.copy(Hg[:], pg[0:64, :]), R=[pg], W=[Hg])
                    yield
                    do("dve", lambda e: e.tensor_tensor_scan(Hc[:], reset[:], He[:], 0.0, ALU.mult, ALU.add), R=[reset, He], W=[Hc])
                    yield
                    do("act", lambda e: e.activation(Hw[:], Hc[:], AF.Exp, scale=-C0), R=[Hc], W=[Hw])
                    yield
                    do("act", lambda e: e.activation(Hm[:], Hc[:], AF.Exp, scale=C0), R=[Hc], W=[Hm])
                    yield
                    do("dve", lambda e: e.tensor_tensor(Hd[:], Hc[:], He[:], ALU.subtract), R=[Hc, He], W=[Hd])
                    yield
                    do("act", lambda e: e.activation(Hp[:], Hd[:], AF.Exp, scale=-C0), R=[Hd], W=[Hp])
                    yield
                    do("act", lambda e: e.activation(Hq[:], Hk[:], AF.Copy, scale=hpar[:, 2, h:h + 1]), R=[Hk, hpar], W=[Hq])
                    yield
                    do("act", lambda e: e.activation(Hd[:], Hq[:], AF.Square), R=[Hq], W=[Hd])
                    yield
                    pn = psum()
                    do("pe", lambda e: e.matmul(pn[0:64, :], ones[0:64, 0:64], Hd[:], start=True, stop=True), R=[ones, Hd], W=[pn])
                    do("dve", lambda e: e.tensor_scalar(Hc[:], pn[0:64, :], 1e-24, None, ALU.max), R=[pn], W=[Hc])
                    yield
                    do("act", lambda e: e.activation(Hc[:], Hc[:], AF.Sqrt), R=[Hc], W=[Hc])
                    yield
                    do("dve", lambda e: e.reciprocal(Hc[:], Hc[:]), R=[Hc], W=[Hc])
                    yield
                    do("dve", lambda e: e.tensor_tensor(Hq[:], Hq[:], Hc[:], ALU.mult), R=[Hq, Hc], W=[Hq])
                    yield
                    do("dve", lambda e: e.tensor_scalar(Hd[:], Ha[:], hpar[:, 3, h:h + 1], omka[:, h:h + 1], ALU.mult, ALU.add), R=[Ha, hpar, omka], W=[Hd])
                    yield
                    do("dve", lambda e: e.tensor_tensor(Hk[:], Hk[:], Hd[:], ALU.mult), R=[Hk, Hd], W=[Hk])
                    yield
                    do("dve", lambda e: e.scalar_tensor_tensor(Atb[:], Hq[:], -1.0, Hp[:], ALU.mult, ALU.mult), R=[Hq, Hp], W=[Atb])
                    yield
                    do("dve", lambda e: e.tensor_tensor(Ha[:], Hq[:], Ha[:], ALU.mult), R=[Hq, Ha], W=[Ha])
                    yield
                    do("dve", lambda e: e.tensor_tensor(Btb[:], Ha[:], Hm[:], ALU.mult), R=[Ha, Hm], W=[Btb])
                    yield
                    do("dve", lambda e: e.tensor_tensor(Ktb[:], Hk[:], Hm[:], ALU.mult), R=[Hk, Hm], W=[Ktb])
                    yield
                    do("act", lambda e: e.copy(Vb[:], Vall[:, h, :]), R=[Vall], W=[Vb])
                    yield
                    do("dve", lambda e: e.scalar_tensor_tensor(Hd[:], Hr[:], hpar[:, 6, h:h + 1], Hk[:], ALU.mult, ALU.mult), R=[Hr, hpar, Hk], W=[Hd])
                    yield
                    pn = psum()
                    do("pe", lambda e: e.matmul(pn[0:64, :], ones[0:64, 0:64], Hd[:], start=True, stop=True), R=[ones, Hd], W=[pn])
                    do("dve", lambda e: e.tensor_tensor(Hq[:], pn[0:64, :], Vall[:, h, :], ALU.mult), R=[pn, Vall], W=[Hq])
                    yield
                    do("dve", lambda e: e.tensor_tensor(Rtb[:], Hr[:], Hw[:], ALU.mult), R=[Hr, Hw], W=[Rtb])
                    yield

                def wkv(h, P, pump):
                    Hw, Atb, Btb, Ktb, Rtb, Vb = (P[k] for k in ("Hw", "Atb", "Btb", "Ktb", "Rtb", "Vb"))
                    At, Bt, Kt, Rt = Atb, Btb, Ktb, Rtb
                    PO = PS[7]
                    do("act", lambda e: e.copy(STb[:], STt[:, h, :]), R=[STt], W=[STb])
                    for bt in range(NCH // NB):
                        for q in range(NB):
                            c = bt * NB + q
                            cs2 = slice(c * C, (c + 1) * C)
                            pb = psum()
                            a_, b_, k_, r_ = At[:, cs2], Bt[:, cs2], Kt[:, cs2], Rt[:, cs2]
                            do("pe", lambda e: e.matmul(pb[0:64, 0:64], b_, a_, start=True, stop=True), R=[Bt, At], W=[pb])
                            do("pe", lambda e: e.matmul(pb[0:64, 64:128], a_, b_, start=True, stop=True), R=[At, Bt], W=[pb])
                            do("pe", lambda e: e.matmul(pb[0:64, 128:192], b_, r_, start=True, stop=True), R=[Bt, Rt], W=[pb])
                            do("pe", lambda e: e.matmul(pb[0:64, 192:256], a_, k_, start=True, stop=True), R=[At, Kt], W=[pb])
                            do("pe", lambda e: e.matmul(pb[0:64, 256:320], k_, r_, start=True, stop=True), R=[Kt, Rt], W=[pb])
                            do("dve", lambda e: e.tensor_tensor(SCx[:, q, :], pb[0:64, 0:128], msk5[:, 0:128], ALU.mult), R=[pb, msk5], W=[SCx])
                            do("dve", lambda e: e.tensor_tensor(SCb[:, q, :], pb[0:64, 128:320], msk5[:, 128:320], ALU.mult), R=[pb, msk5], W=[SCb])
                            pt = psum()
                            v_ = Vb[:, cs2]
                            for qi, (sb_, s_) in enumerate(((At, a_), (Bt, b_), (Kt, k_), (Vb, v_))):
                                do("pe", lambda e: e.matmul(pt[0:64, qi * 64:(qi + 1) * 64], s_, ident_bf[0:64, 0:64], start=True, stop=True),
                                   R=[sb_, ident_bf], W=[pt])
                            do("act", lambda e: e.copy(TMb[:, q, :], pt[0:64, 0:256]), R=[pt], W=[TMb])
                            pump()
                        do("dve", lambda e: e.tensor_tensor(TT[:], SCx[:, :, 0:64], ident[0:64, 0:64].unsqueeze(1).to_broadcast([64, NB, 64]), ALU.add),
                           R=[SCx, ident], W=[TT])
                        Xb, Xf = SCx, (lambda q: SCx[:, q, 0:64])
                        Yb, Yf = SCx, (lambda q: SCx[:, q, 64:128])
                        cur = 0
                        for lev in range(1, 6):
                            Xn_, Yn_ = XS[cur], YS[cur]
                            if lev < 5:
                                pb = PS[4]
                                for q in range(NB):
                                    do("pe", lambda e: e.matmul(pb[0:64, q * 64:(q + 1) * 64], Yf(q), Xf(q), start=True, stop=True), R=[Yb, Xb], W=[pb])
                                do("act", lambda e: e.copy(Xn_[:], pb[0:64, 0:NB * 64].rearrange("p (q t) -> p q t", q=NB)), R=[pb], W=[Xn_])
                                pump()
                            pb = PS[5]
                            for q in range(NB):
                                do("pe", lambda e: e.matmul(pb[0:64, q * 64:(q + 1) * 64], Xf(q), Yf(q), start=True, stop=True), R=[Xb, Yb], W=[pb])
                            do("act", lambda e: e.copy(Yn_[:], pb[0:64, 0:NB * 64].rearrange("p (q t) -> p q t", q=NB)), R=[pb], W=[Yn_])
                            pump()
                            pb = PS[6]
                            for q in range(NB):
                                do("pe", lambda e: e.matmul(pb[0:64, q * 64:(q + 1) * 64], Yn_[:, q, :], TT[:, q, :], start=True, stop=True), R=[Yn_, TT], W=[pb])
                            if lev < 5:
                                do("dve", lambda e: e.tensor_tensor(TT[:], TT[:], pb[0:64, 0:NB * 64].rearrange("p (q t) -> p q t", q=NB), ALU.add),
                                   R=[pb, TT], W=[TT])
                            else:
                                do("dve", lambda e: e.tensor_tensor(TTb[:], TT[:], pb[0:64, 0:NB * 64].rearrange("p (q t) -> p q t", q=NB), ALU.add),
                                   R=[pb, TT], W=[TTb])
                            Xb, Yb = Xn_, Yn_
                            pump()
                            Xf = (lambda q, Xn_=Xn_: Xn_[:, q, :])
                            Yf = (lambda q, Yn_=Yn_: Yn_[:, q, :])
                            cur = 1 - cur
                        pb = PS[4]
                        for q in range(NB):
                            do("pe", lambda e: e.matmul(pb[0:64, q * 128:q * 128 + 64], TMb[:, q, 0:64], TTb[:, q, :], start=True, stop=True), R=[TMb, TTb], W=[pb])
                            do("pe", lambda e: e.matmul(pb[0:64, q * 128 + 64:q * 128 + 128], SCb[:, q, 64:128], TTb[:, q, :], start=True, stop=True), R=[SCb, TTb], W=[pb])
                        do("act", lambda e: e.copy(APMb[:], pb[0:64, 0:NB * 128].rearrange("p (q t) -> p q t", q=NB)), R=[pb], W=[APMb])
                        pump()
                        for q in range(NB):
                            c = bt * NB + q
                            cs2 = slice(c * C, (c + 1) * C)
                            pu = PS[5]
                            do("pe", lambda e: e.matmul(pu[0:64, 0:64], APMb[:, q, 64:128], TMb[:, q, 192:256], start=True, stop=False), R=[APMb, TMb], W=[pu])
                            do("pe", lambda e: e.matmul(pu[0:64, 0:64], APMb[:, q, 0:64], STb[:], start=False, stop=True), R=[APMb, STb], W=[pu])
                            do("act", lambda e: e.copy(UTb[:], pu[0:64, 0:64]), R=[pu], W=[UTb])
                            pump()
                            do("pe", lambda e: e.matmul(PO[0:64, cs2], STb[:], Rt[:, cs2], start=True, stop=False), R=[STb, Rt], W=[PO])
                            do("pe", lambda e: e.matmul(PO[0:64, cs2], TMb[:, q, 192:256], SCb[:, q, 128:192], start=False, stop=False), R=[TMb, SCb], W=[PO])
                            do("pe", lambda e: e.matmul(PO[0:64, cs2], UTb[:], SCb[:, q, 0:64], start=False, stop=True), R=[UTb, SCb], W=[PO])
                            pn = PS[6]
                            do("pe", lambda e: e.matmul(pn[0:64, 0:64], TMb[:, q, 64:128], UTb[:], start=True, stop=False), R=[TMb, UTb], W=[pn])
                            do("pe", lambda e: e.matmul(pn[0:64, 0:64], TMb[:, q, 128:192], TMb[:, q, 192:256], start=False, stop=True), R=[TMb], W=[pn])
                            wc = Hw[:, c * C + C - 1:c * C + C]
                            do("act", lambda e: e.activation(STW[:], STt[:, h, :], AF.Copy, scale=wc), R=[STt, Hw], W=[STW])
                            do("dve", lambda e: e.scalar_tensor_tensor(STb[:], pn[0:64, 0:64], wc, STW[:], ALU.mult, ALU.add), R=[pn, Hw, STW], W=[STb])
                            do("dve", lambda e: e.scalar_tensor_tensor(STt[:, h, :], pn[0:64, 0:64], wc, STW[:], ALU.mult, ALU.add), R=[pn, Hw, STW], W=[STt])
                            pump()

                def post(h, P):
                    Hd, Hc, Hm = H["d"], H["c"], H["m"]
                    Hg, Hq = P["Hg"], P["Hq"]
                    PO = PS[7]
                    OS = Hm
                    do("act", lambda e: e.copy(OS[:], PO[0:64, :]), R=[PO], W=[OS])
                    do("act", lambda e: e.activation(Hd[:], OS[:], AF.Square), R=[OS], W=[Hd])
                    pm1 = psum()
                    do("pe", lambda e: e.matmul(pm1[0:64, :], ones[0:64, 0:64], OS[:], start=True, stop=True), R=[ones, OS], W=[pm1])
                    pm2 = psum()
                    do("pe", lambda e: e.matmul(pm2[0:64, :], ones[0:64, 0:64], Hd[:], start=True, stop=True), R=[ones, Hd], W=[pm2])
                    mu_ = work.next()
                    do("act", lambda e: e.activation(mu_[0:64, :], pm1[0:64, :], AF.Copy, scale=1.0 / 64), R=[pm1], W=[mu_])
                    var = work.next()
                    do("dve", lambda e: e.tensor_tensor(var[0:64, :], mu_[0:64, :], mu_[0:64, :], ALU.mult), R=[mu_], W=[var])
                    do("dve", lambda e: e.scalar_tensor_tensor(var[0:64, :], pm2[0:64, :], 1.0 / 64, var[0:64, :], ALU.mult, ALU.subtract), R=[pm2, var], W=[var])
                    do("dve", lambda e: e.tensor_scalar(var[0:64, :], var[0:64, :], 64e-5, None, ALU.add), R=[var], W=[var])
                    do("act", lambda e: e.activation(var[0:64, :], var[0:64, :], AF.Sqrt), R=[var], W=[var])
                    do("dve", lambda e: e.reciprocal(var[0:64, :], var[0:64, :]), R=[var], W=[var])
                    do("dve", lambda e: e.tensor_tensor(Hc[:], OS[:], mu_[0:64, :], ALU.subtract), R=[OS, mu_], W=[Hc])
                    do("dve", lambda e: e.tensor_tensor(Hc[:], Hc[:], var[0:64, :], ALU.mult), R=[Hc, var], W=[Hc])
                    do("dve", lambda e: e.tensor_scalar(Hc[:], Hc[:], hpar[:, 4, h:h + 1], hpar[:, 5, h:h + 1], ALU.mult, ALU.add), R=[Hc, hpar], W=[Hc])
                    do("dve", lambda e: e.tensor_tensor(Hc[:], Hc[:], Hq[:], ALU.add), R=[Hc, Hq], W=[Hc])
                    do("dve", lambda e: e.tensor_tensor(mixA[:, h, :], Hc[:], Hg[:], ALU.mult), R=[Hc, Hg], W=[mixA])

                prepg = prep_gen(0, PP[0])
                for _ in prepg:
                    pass
                for h in range(4):
                    nxt = prep_gen(h + 1, PP[(h + 1) % 2]) if h < 3 else iter(())

                    def pump2(nxt=nxt):
                        pump()
                        next(nxt, None)
                    wkv(h, PP[h % 2], pump2)
                    for _ in nxt:
                        pass
                    post(h, PP[h % 2])

                for _ in s5g_:
                    pass
                if ffn_pending[0] is not None:
                    for _ in ffn_pending[0]:
                        pass
                    ffn_pending[0] = None


                merged = aT
                for ft in range(8):
                    fsl = slice(ft * 128, (ft + 1) * 128)
                    acc = Z[5]
                    gts = [Z[0], Z[1], Z[2], Z[3]]
                    for br in range(4):
                        def ev_g(pb, br=br):
                            do("act", lambda e: e.activation(gts[br][:], pb[:], AF.Sigmoid), R=[pb], W=[gts[br]])
                        proj128(dr["w_in128"][l, 13 + br * 8 + ft], hT, hfn, 8, w128, ev_g)
                    wq = [wload(wsm.next(), dr["sg_out"][l][:, :, fsl]), wload(wsm.next(), dr["conv_out"][l][:, :, fsl]),
                          wload(wsm.next(), dr["glu_w"][l][:, :, fsl]),
                          wload(wsm.next(), dr["glu_w"][l][:, :, 1024 + ft * 128:1024 + (ft + 1) * 128])]
                    wr_ = wload(wra.next(), dr["rwkv_out"][l][:, :, fsl])
                    pa = psum()
                    for h in range(4):
                        do("pe", lambda e: e.matmul(pa[:], wr_[:, h, :], mixA[:, h, :], start=(h == 0), stop=(h == 3)), R=[wr_, mixA], W=[pa])
                    do("dve", lambda e: e.tensor_tensor(acc[:], pa[:], gts[0][:], ALU.mult), R=[pa, gts[0]], W=[acc])
                    for bi, mixT in ((1, mixB), (2, mixC)):
                        pb_ = psum()
                        for kt in range(2):
                            do("pe", lambda e: e.matmul(pb_[:], wq[bi - 1][:, kt, :], mixT[:, kt, :], start=(kt == 0), stop=(kt == 1)), R=[wq[bi - 1], mixT], W=[pb_])
                        do("dve", lambda e: e.tensor_tensor(gts[bi][:], pb_[:], gts[bi][:], ALU.mult), R=[pb_, gts[bi]], W=[gts[bi]])
                        do("dve", lambda e: e.tensor_tensor(acc[:], acc[:], gts[bi][:], ALU.add), R=[acc, gts[bi]], W=[acc])
                    ph1 = psum()
                    for kt in range(2):
                        do("pe", lambda e: e.matmul(ph1[:], wq[2][:, kt, :], mixD[:, kt, :], start=(kt == 0), stop=(kt == 1)), R=[wq[2], mixD], W=[ph1])
                    ph2 = psum()
                    for kt in range(2):
                        do("pe", lambda e: e.matmul(ph2[:], wq[3][:, kt, :], mixD[:, kt, :], start=(kt == 0), stop=(kt == 1)), R=[wq[3], mixD], W=[ph2])
                    do("act", lambda e: e.activation(xn[:], ph2[:], AF.Sigmoid), R=[ph2], W=[xn])
                    do("dve", lambda e: e.tensor_tensor(xn[:], ph1[:], xn[:], ALU.mult), R=[ph1, xn], W=[xn])
                    do("dve", lambda e: e.tensor_tensor(xn[:], xn[:], gts[3][:], ALU.mult), R=[xn, gts[3]], W=[xn])
                    do("dve", lambda e: e.tensor_tensor(merged[:, ft, :], acc[:], xn[:], ALU.add), R=[acc, xn], W=[merged])

                dma(X[:], src[:, t0:t0 + TS].rearrange("(ft p) t -> p ft t", p=128), R=srcb, W=[X])
                for mt in range(8):
                    def ev_o(pb, mt=mt):
                        do("dve", lambda e: e.scalar_tensor_tensor(X[:, mt, :], pb[:], modT[:, 16 + mt:17 + mt], X[:, mt, :], ALU.mult, ALU.add),
                           R=[pb, modT, X], W=[X])
                    proj128(dr["w_o"][l, mt], merged, lambda kt: merged[:, kt, :], 8, w128, ev_o)

                rmsnorm_to(hTB, geff2, 24)
                def ffn_gen(l=l, si=si, t0=t0):
                    for qtr in range(4):
                        for i in range(8):
                            def ev_f(pb, i=i):
                                wk = work.next()
                                do("act", lambda e: e.activation(wk[:], pb[:], AF.Relu), R=[pb], W=[wk])
                                do("act", lambda e: e.activation(aT[:, i, :], wk[:], AF.Square), R=[wk], W=[aT])
                            proj128(dr["ffn_w1"][l, qtr * 8 + i], hTB, lambda kt: hTB[:, kt, :], 8, w128, ev_f)
                            yield
                        for mt in range(8):
                            def ev_2(pb, mt=mt):
                                do("dve", lambda e: e.scalar_tensor_tensor(X[:, mt, :], pb[:], modT[:, 40 + mt:41 + mt], X[:, mt, :], ALU.mult, ALU.add),
                                   R=[pb, modT, X], W=[X])
                            proj128(dr["ffn_w2"][l, mt][:, qtr * 8:(qtr + 1) * 8, :], aT, lambda kt: aT[:, kt, :], 8, w128, ev_2)
                            yield
                    if l == 0:
                        dma(xs_d[:, t0:t0 + TS].rearrange("(ft p) t -> p ft t", p=128), X[:], R=[X], W=[B_xs[si]])
                    else:
                        rms_stats()
                        for ft in range(8):
                            do("dve", lambda e: e.scalar_tensor_tensor(X[:, ft, :], X[:, ft, :], g_fin[:, ft:ft + 1], rstd[:], ALU.mult, ALU.mult),
                               R=[X, g_fin, rstd], W=[X])
                        dma(outT[:, t0:t0 + TS].rearrange("(ft p) t -> p ft t", p=128), X[:], R=[X], W=[B_outs[si]])
                    yield

                ffn_pending[0] = ffn_gen()
                if si == NS - 1:
                    for _ in ffn_pending[0]:
                        pass
                    ffn_pending[0] = None
        S.finish(B_outs)
        print("instructions:", S.n_inst, "waits:", S.n_wait)
    return nc


_CACHE = {}


def run(inputs, T):
    maps = [prep_inputs(inputs, b) for b in range(4)]
    shapes = {k: v.shape for k, v in maps[0].items()}
    key = (T,)
    if key not in _CACHE:
        _CACHE[key] = build(shapes, T)
    nc = _CACHE[key]
    in_maps = [maps[i % 4] for i in range(8)]
    res = run_bass_kernel_spmd(nc, in_maps, core_ids=list(range(8)))
    out = np.stack([np.ascontiguousarray(res.results[b]["outT"].T) for b in range(4)])
    return out.astype(np.float32)


def kernel(**inputs):
    inputs = {k: np.asarray(v, dtype=np.float32) for k, v in inputs.items()}
    T = inputs["x"].shape[1]
    return run(inputs, T)
```
